# Optimizing a Trainium2 kernel written in Bass

```python
import jax, jax.numpy as jnp
from jax import lax
import numpy as np

D_MODEL = 1024
BATCH = 32
SEQ = 256
DEPTH = 2
DEC_BATCH = 8
DEC_SEQ = 1024
PAST_LEN = 256

GRID_W = 64
POOL_WIDTH = 256
POOL_GROUPS = 4
POOL_GROUP_DIM = POOL_WIDTH // POOL_GROUPS
POOL_WINDOWS = (2, 4, 8, 16)
NA_HEADS = 8
HEAD_DIM = 64
NA_WIDTH = NA_HEADS * HEAD_DIM
NA_ROWS_MAX = 8
NA_COLS = 16
SG_WIDTH = 256
SG_GROUPS = 4
SG_GROUP_DIM = SG_WIDTH // SG_GROUPS
CHUNK = 128
D_FF = 2816
N_MOD = 9
MIX_WIDTH = POOL_WIDTH + NA_WIDTH + SG_WIDTH
IN_WIDTH = POOL_WIDTH + 3 * NA_WIDTH + 2 * SG_WIDTH
IN_SPLITS = (POOL_WIDTH, POOL_WIDTH + NA_WIDTH, POOL_WIDTH + 2 * NA_WIDTH,
             POOL_WIDTH + 3 * NA_WIDTH, POOL_WIDTH + 3 * NA_WIDTH + SG_WIDTH)
EPS = 1e-6

kernel_name = "hybrid_pool_natten_sgmlp_diffusion_step"


def _rms(x, g):
    x32 = x.astype(jnp.float32)
    y = x32 * lax.rsqrt(jnp.mean(x32 * x32, axis=-1, keepdims=True) + EPS)
    return y.astype(x.dtype) * g


def _modulate(h, shift, scale):
    return h * (1.0 + scale) + shift


def _swiglu(h, w_gate, w_up, w_down):
    return (jax.nn.silu(h @ w_gate) * (h @ w_up)) @ w_down


def _half_ffn(x, gain, shift, scale, gate, w_gate, w_up, w_down):
    return x + 0.5 * gate * _swiglu(_modulate(_rms(x, gain), shift, scale), w_gate, w_up, w_down)


def _heads(x):
    b, l, _ = x.shape
    return x.reshape(b, l, NA_HEADS, HEAD_DIM).transpose(0, 2, 1, 3)


def _pool_mixer(a, w_pool, pool_scale):
    b, l, _ = a.shape
    a32 = a.astype(jnp.float32)
    csum = jnp.concatenate([jnp.zeros_like(a32[:, :1]), jnp.cumsum(a32, axis=1)], axis=1)
    t = jnp.arange(l)
    outs = []
    for gi, w in enumerate(POOL_WINDOWS):
        lo = jnp.clip(t - w // 2, 0, l)
        hi = jnp.clip(t + w // 2, 0, l)
        cg = csum[..., gi * POOL_GROUP_DIM:(gi + 1) * POOL_GROUP_DIM]
        cnt = (hi - lo).astype(jnp.float32)[None, :, None]
        outs.append((cg[:, hi] - cg[:, lo]) / cnt)
    pooled = jnp.concatenate(outs, axis=-1).astype(a.dtype) - a
    pooled = pooled.reshape(b, l, POOL_GROUPS, POOL_GROUP_DIM)
    mixed = jnp.einsum('blgc,gcd->blgd', pooled, w_pool).reshape(b, l, POOL_WIDTH)
    return mixed * pool_scale


def _chunk_mixer(u, v, vnorm_g, w_s, b_s):
    b, l, _ = u.shape
    n = l // CHUNK
    u = jax.nn.gelu(u)
    vg = _rms(jax.nn.gelu(v).reshape(b, n, CHUNK, SG_GROUPS, SG_GROUP_DIM), vnorm_g)
    sp = jnp.einsum('gpq,bnqgc->bnpgc', w_s, vg) + b_s.T[:, :, None]
    return u * sp.reshape(b, l, SG_WIDTH)


def _context_attention(q, k, v):
    b, h, lc, d = q.shape
    nb = lc // CHUNK
    scale = d ** -0.5
    qb = q.reshape(b, h, nb, CHUNK, d).transpose(2, 0, 1, 3, 4)

    def block(qi):
        s = jnp.einsum('bhqd,bhkd->bhqk', qi, k).astype(jnp.float32) * scale
        p = jax.nn.softmax(s, axis=-1).astype(v.dtype)
        return jnp.einsum('bhqk,bhkd->bhqd', p, v)

    o = lax.map(block, qb)
    return o.transpose(1, 2, 0, 3, 4).reshape(b, h, lc, d)


def _neighbourhood_attention(q, k, v, k_ctx, v_ctx, rpb):
    b, h, l, d = q.shape
    rows = l // GRID_W
    kh = min(NA_ROWS_MAX, rows)
    scale = d ** -0.5
    kg = k.reshape(b, h, rows, GRID_W, d)
    vg = v.reshape(b, h, rows, GRID_W, d)
    qg = q.reshape(b, h, rows, GRID_W, d).transpose(2, 0, 1, 3, 4)
    cols = jnp.arange(GRID_W)
    col_start = jnp.clip(cols - NA_COLS // 2, 0, GRID_W - NA_COLS)
    col_ok = (cols[None, :] >= col_start[:, None]) & (cols[None, :] < col_start[:, None] + NA_COLS)
    dc_idx = jnp.clip(cols[None, :] - cols[:, None], -(NA_COLS - 1), NA_COLS - 1) + (NA_COLS - 1)

    def row_block(args):
        r, q_r = args
        start = jnp.clip(r - kh // 2, 0, rows - kh)
        k_r = lax.dynamic_slice_in_dim(kg, start, kh, axis=2)
        v_r = lax.dynamic_slice_in_dim(vg, start, kh, axis=2)
        dr_idx = start + jnp.arange(kh) - r + (NA_ROWS_MAX - 1)
        bias = rpb[:, dr_idx][:, :, dc_idx].transpose(0, 2, 1, 3)
        s_loc = jnp.einsum('bhqd,bhkwd->bhqkw', q_r, k_r).astype(jnp.float32) * scale
        s_loc = jnp.where(col_ok[:, None, :], s_loc + bias[None].astype(jnp.float32), -jnp.inf)
        s_ctx = jnp.einsum('bhqd,bhcd->bhqc', q_r, k_ctx).astype(jnp.float32) * scale
        s = jnp.concatenate([s_loc.reshape(b, h, GRID_W, kh * GRID_W), s_ctx], axis=-1)
        p = jax.nn.softmax(s, axis=-1).astype(v.dtype)
        p_loc = p[..., :kh * GRID_W].reshape(b, h, GRID_W, kh, GRID_W)
        p_ctx = p[..., kh * GRID_W:]
        return (jnp.einsum('bhqkw,bhkwd->bhqd', p_loc, v_r)
                + jnp.einsum('bhqc,bhcd->bhqd', p_ctx, v_ctx))

    o = lax.map(row_block, (jnp.arange(rows), qg))
    return o.transpose(1, 2, 0, 3, 4).reshape(b, h, l, d)


def _mixer_inputs(h, w_in, q_norm_g, k_norm_g):
    z = h @ w_in
    za, zq, zk, zv, zu, zsv = jnp.split(z, IN_SPLITS, axis=-1)
    q = _rms(_heads(zq), q_norm_g)
    k = _rms(_heads(zk), k_norm_g)
    return za, q, k, _heads(zv), zu, zsv


def _mixer_output(a_out, attn_out, sg_out, out_norm_g, w_out):
    b, h, l, d = attn_out.shape
    o_b = attn_out.transpose(0, 2, 1, 3).reshape(b, l, NA_WIDTH)
    g_a, g_b, g_c = jnp.split(out_norm_g, [POOL_WIDTH, POOL_WIDTH + NA_WIDTH])
    o = jnp.concatenate([_rms(a_out, g_a), _rms(o_b, g_b), _rms(sg_out, g_c)], axis=-1)
    return o @ w_out


def setup_inputs(seed: int = 0) -> dict:
    key = jax.random.key(seed)
    ks = jax.random.split(key, 24)

    def nrm(k, shape, scale):
        return jax.random.normal(k, shape, jnp.float32) * scale

    return {
        "x_prompt": nrm(ks[0], (BATCH, SEQ, D_MODEL), 1.0),
        "x_sample": nrm(ks[1], (DEC_BATCH, DEC_SEQ, D_MODEL), 1.0),
        "cache_k": nrm(ks[2], (DEC_BATCH, DEPTH, NA_HEADS, PAST_LEN, HEAD_DIM), 1.0),
        "cache_v": nrm(ks[3], (DEC_BATCH, DEPTH, NA_HEADS, PAST_LEN, HEAD_DIM), 1.0),
        "c": nrm(ks[4], (DEC_BATCH, D_MODEL), 1.0),
        "c_ctx": nrm(ks[5], (D_MODEL,), 1.0),
        "ada_w": nrm(ks[6], (DEPTH, D_MODEL, N_MOD * D_MODEL), 0.5 * D_MODEL ** -0.5),
        "ada_b": nrm(ks[7], (DEPTH, N_MOD * D_MODEL), 0.02),
        "norm_g": 1.0 + nrm(ks[8], (DEPTH, 3, D_MODEL), 0.05),
        "ffn_w_gate": nrm(ks[9], (DEPTH, 2, D_MODEL, D_FF), D_MODEL ** -0.5),
        "ffn_w_up": nrm(ks[10], (DEPTH, 2, D_MODEL, D_FF), D_MODEL ** -0.5),
        "ffn_w_down": nrm(ks[11], (DEPTH, 2, D_FF, D_MODEL), D_FF ** -0.5),
        "w_in": nrm(ks[12], (DEPTH, D_MODEL, IN_WIDTH), D_MODEL ** -0.5),
        "pool_w": nrm(ks[13], (DEPTH, POOL_GROUPS, POOL_GROUP_DIM, POOL_GROUP_DIM), POOL_GROUP_DIM ** -0.5),
        "pool_scale": 1.0 + nrm(ks[14], (DEPTH, POOL_WIDTH), 0.05),
        "q_norm_g": 1.0 + nrm(ks[15], (DEPTH, HEAD_DIM), 0.05),
        "k_norm_g": 1.0 + nrm(ks[16], (DEPTH, HEAD_DIM), 0.05),
        "na_rpb": nrm(ks[17], (DEPTH, NA_HEADS, 2 * NA_ROWS_MAX - 1, 2 * NA_COLS - 1), 0.1),
        "sg_vnorm_g": 1.0 + nrm(ks[18], (DEPTH, SG_GROUPS, SG_GROUP_DIM), 0.05),
        "sg_w": nrm(ks[19], (DEPTH, SG_GROUPS, CHUNK, CHUNK), CHUNK ** -0.5),
        "sg_b": 1.0 + nrm(ks[20], (DEPTH, SG_GROUPS, CHUNK), 0.05),
        "out_norm_g": 1.0 + nrm(ks[21], (DEPTH, MIX_WIDTH), 0.05),
        "w_out": nrm(ks[22], (DEPTH, MIX_WIDTH, D_MODEL), MIX_WIDTH ** -0.5),
    }


def reference(x_prompt, x_sample, cache_k, cache_v, c, c_ctx, ada_w, ada_b, norm_g,
              ffn_w_gate, ffn_w_up, ffn_w_down, w_in, pool_w, pool_scale, q_norm_g,
              k_norm_g, na_rpb, sg_vnorm_g, sg_w, sg_b, out_norm_g, w_out):
    xp = x_prompt
    xs = x_sample
    ctx_k = []
    ctx_v = []
    for l in range(DEPTH):
        mc = (jax.nn.silu(c_ctx) @ ada_w[l] + ada_b[l]).reshape(N_MOD, D_MODEL)
        ml_all = (jax.nn.silu(c) @ ada_w[l] + ada_b[l]).reshape(c.shape[0], 1, N_MOD, D_MODEL)
        ml = [ml_all[:, :, i] for i in range(N_MOD)]

        xp = _half_ffn(xp, norm_g[l, 0], mc[0], mc[1], mc[2],
                       ffn_w_gate[l, 0], ffn_w_up[l, 0], ffn_w_down[l, 0])
        h = _modulate(_rms(xp, norm_g[l, 1]), mc[3], mc[4])
        za, q, k, v, zu, zsv = _mixer_inputs(h, w_in[l], q_norm_g[l], k_norm_g[l])
        ctx_k.append(k)
        ctx_v.append(v)
        o = _mixer_output(_pool_mixer(za, pool_w[l], pool_scale[l]),
                          _context_attention(q, k, v),
                          _chunk_mixer(zu, zsv, sg_vnorm_g[l], sg_w[l], sg_b[l]),
                          out_norm_g[l], w_out[l])
        xp = xp + mc[5] * o
        xp = _half_ffn(xp, norm_g[l, 2], mc[6], mc[7], mc[8],
                       ffn_w_gate[l, 1], ffn_w_up[l, 1], ffn_w_down[l, 1])

        xs = _half_ffn(xs, norm_g[l, 0], ml[0], ml[1], ml[2],
                       ffn_w_gate[l, 0], ffn_w_up[l, 0], ffn_w_down[l, 0])
        h = _modulate(_rms(xs, norm_g[l, 1]), ml[3], ml[4])
        za, q, k, v, zu, zsv = _mixer_inputs(h, w_in[l], q_norm_g[l], k_norm_g[l])
        o = _mixer_output(_pool_mixer(za, pool_w[l], pool_scale[l]),
                          _neighbourhood_attention(q, k, v, cache_k[:, l], cache_v[:, l], na_rpb[l]),
                          _chunk_mixer(zu, zsv, sg_vnorm_g[l], sg_w[l], sg_b[l]),
                          out_norm_g[l], w_out[l])
        xs = xs + ml[5] * o
        xs = _half_ffn(xs, norm_g[l, 2], ml[6], ml[7], ml[8],
                       ffn_w_gate[l, 1], ffn_w_up[l, 1], ffn_w_down[l, 1])

    new_k = jnp.stack(ctx_k, axis=1)
    new_v = jnp.stack(ctx_v, axis=1)
    return (xp, xs, new_k, new_v)
```

```python
import numpy as np
from contextlib import ExitStack
import concourse.bass as bass
import concourse.mybir as mybir
from concourse.bass_utils import run_bass_kernel_spmd

F32 = mybir.dt.float32
BF16 = mybir.dt.bfloat16
AF = mybir.ActivationFunctionType
ALU = mybir.AluOpType
AX = mybir.AxisListType

D = 1024
DFF = 2816
L = 2
NH = 8
EPS = 1e-6
NCORES = 8
WINS = (2, 4, 8, 16)


class Tl:
    __slots__ = ("name", "w", "r")

    def __init__(self, name=""):
        self.name = name
        self.w = None
        self.r = {}


class Op:
    __slots__ = ("eng", "fn", "deps", "marked", "count", "dkey", "dcount", "is_dma", "uid")


class Prog:
    ENGS = ("pe", "act", "dve", "pool", "sp")

    def __init__(self, nc):
        self.nc = nc
        self.ops = {e: [] for e in self.ENGS}
        self.latest_dma = {}
        self.dma_counts = {}
        self.uid = 0
        self.out_keys = set()

    def add(self, eng, fn, reads=(), writes=(), dma=None, is_out=False):
        op = Op()
        op.eng = eng
        op.fn = fn
        op.marked = False
        op.count = None
        op.is_dma = dma is not None
        op.dkey = dma
        op.uid = self.uid
        self.uid += 1
        deps = {}
        for t in reads:
            if t.w is not None:
                deps[t.w.uid] = t.w
        for t in writes:
            if t.w is not None:
                deps[t.w.uid] = t.w
            for d in t.r.values():
                deps[d.uid] = d
        final = {}
        for d in deps.values():
            if d.is_dma:
                d = self.latest_dma[d.dkey]
            elif d.eng == "pe" and eng == "pe" and not op.is_dma:
                continue
            else:
                d.marked = True
            final[d.uid] = d
        op.deps = list(final.values())
        if op.is_dma:
            c = self.dma_counts.get(dma, 0) + 16
            self.dma_counts[dma] = c
            op.dcount = c
            self.latest_dma[dma] = op
            if is_out:
                self.out_keys.add(dma)
        rkey = ("dma", op.uid) if op.is_dma else eng
        for t in reads:
            t.r[rkey] = op
        for t in writes:
            t.w = op
            t.r = {}
        self.ops[eng].append(op)
        return op

    def dma(self, eng, key, out, in_, reads=(), writes=(), is_out=False):
        return self.add(eng, lambda e: e.dma_start(out=out, in_=in_), reads, writes,
                        dma=key, is_out=is_out)

    def mm(self, out, lhsT, rhs, start, stop, reads, writes, tp=None):
        if tp is None:
            return self.add("pe", lambda e: e.matmul(out, lhsT, rhs, start=start, stop=stop), reads, writes)
        return self.add("pe", lambda e: e.matmul(out, lhsT, rhs, start=start, stop=stop, tile_position=tp),
                        reads, writes)

    def tr(self, out, in_, ident, reads, writes):
        return self.add("pe", lambda e: e.transpose(out, in_, ident), reads, writes)

    def actv(self, out, in_, func, reads, writes, scale=None, bias=None):
        kw = {}
        if scale is not None:
            kw["scale"] = scale
        if bias is not None:
            kw["bias"] = bias
        return self.add("act", lambda e: e.activation(out, in_, func, **kw), reads, writes)

    def tt(self, eng, out, in0, in1, op, reads, writes):
        return self.add(eng, lambda e: e.tensor_tensor(out, in0, in1, op=op), reads, writes)

    def ts(self, eng, out, in0, s1, s2, op0, op1, reads, writes):
        if s2 is None:
            return self.add(eng, lambda e: e.tensor_scalar(out, in0, s1, None, op0=op0), reads, writes)
        return self.add(eng, lambda e: e.tensor_scalar(out, in0, s1, s2, op0=op0, op1=op1), reads, writes)

    def stt(self, out, in0, scalar, in1, op0, op1, reads, writes):
        return self.add("dve", lambda e: e.scalar_tensor_tensor(out, in0, scalar, in1, op0=op0, op1=op1),
                        reads, writes)

    def cp(self, eng, out, in_, reads, writes):
        if eng == "act":
            return self.add("act", lambda e: e.copy(out, in_), reads, writes)
        return self.add(eng, lambda e: e.tensor_copy(out, in_), reads, writes)

    def red(self, out, in_, reads, writes):
        return self.add("dve", lambda e: e.tensor_reduce(out, in_, axis=AX.X, op=ALU.add), reads, writes)

    def memset(self, eng, ap, val, writes):
        return self.add(eng, lambda e: e.memset(ap, val), (), writes)

    def emit(self, stack):
        nc = self.nc
        for e in self.ENGS:
            c = 0
            for op in self.ops[e]:
                if op.marked and not op.is_dma:
                    c += 1
                    op.count = c
        esem = {e: stack.enter_context(nc.semaphore("s_" + e)) for e in self.ENGS}
        dsem = {k: stack.enter_context(nc.semaphore("d_%d" % i)) for i, k in enumerate(self.dma_counts)}
        block = stack.enter_context(nc.Block())
        ops = self.ops
        out_final = [(dsem[k], self.dma_counts[k]) for k in sorted(self.out_keys, key=str)]

        def run(ename, e):
            known = {}
            for op in ops[ename]:
                for d in op.deps:
                    if d.is_dma:
                        s, v = dsem[d.dkey], d.dcount
                    else:
                        s, v = esem[d.eng], d.count
                    kid = id(s)
                    if known.get(kid, 0) >= v:
                        continue
                    known[kid] = v
                    e.wait_ge(s, v)
                ins = op.fn(e)
                if op.is_dma:
                    ins.then_inc(dsem[op.dkey], 16)
                elif op.marked:
                    ins.then_inc(esem[ename], 1)
            if ename == "sp":
                for s, v in out_final:
                    e.wait_ge(s, v)

        @block.sync
        def _(e):
            run("sp", e)

        @block.tensor
        def _(e):
            run("pe", e)

        @block.scalar
        def _(e):
            run("act", e)

        @block.vector
        def _(e):
            run("dve", e)

        @block.gpsimd
        def _(e):
            run("pool", e)


class Stream:
    NS = 6
    LOOK = 5

    def __init__(self, P, slots, t_slots):
        self.P = P
        self.slots = slots
        self.t_slots = t_slots
        self.pieces = []
        self.issued = 0
        self.freed = 0
        self.want = 0

    def plan(self, parts):
        self.pieces.append(parts)
        return len(self.pieces) - 1

    def pump(self):
        while self.issued <= self.want and self.issued < len(self.pieces) and self.issued < self.freed + self.NS:
            i = self.issued
            s = i % self.NS
            for dst_fn, src in self.pieces[i]:
                self.P.dma("pool", ("w", s), dst_fn(self.slots[s]), src, writes=[self.t_slots[s]])
            self.issued += 1

    def need(self, p):
        self.want = max(self.want, p + self.LOOK)
        self.pump()
        assert self.issued > p, (self.issued, p, self.freed)
        return p % self.NS

    def done(self, p):
        self.freed = max(self.freed, p + 1)
        self.pump()


def _v3(t, k):
    return t[:].rearrange("p (k n) -> p k n", k=k)


def build_program(stop_after=None):
    nc = bass.Bass("TRN2", target_bir_lowering=False)

    def din(name, shape):
        return nc.dram_tensor(name, shape, F32, kind="ExternalInput").ap()

    def dout(name, shape):
        return nc.dram_tensor(name, shape, F32, kind="ExternalOutput").ap()

    x_d = din("x", [2, 1024, D])
    ck_d = din("ck", [L, NH, 256, 64])
    cv_d = din("cv", [L, NH, 256, 64])
    cvec_d = din("cvec", [128, 128])
    vt_d = din("vt", [L, 128, 128])
    gt_d = din("gt", [L, 384])
    poolw_d = din("poolw", [L, 4, 64, 64])
    sgw_d = din("sgw", [L, 4, 128, 128])
    sgb_d = din("sgb", [L, 4, 128])
    rpb_d = din("rpb", [L, 128, 31])
    adaw_d = din("adaw", [L, D, 9 * D])
    wg_d = din("wg", [L, 2, D, DFF])
    wu_d = din("wu", [L, 2, D, DFF])
    wd_d = din("wd", [L, 2, DFF, D])
    win_d = din("win", [L, D, 2304])
    wout_d = din("wout", [L, D, D])
    ident_d = din("ident", [128, 128])
    bands_d = din("bands", [128, 20, 128])
    invc_d = din("invc", [128, 6, 128])
    dw_d = din("dw", [31, 127])
    colok_d = din("colok", [128, 64])
    y_d = dout("y", [2, 1024, D])
    nk_d = dout("nk", [4, L, NH, 256, 64])
    nv_d = dout("nv", [4, L, NH, 256, 64])

    P = Prog(nc)
    st = ExitStack()

    def sb(name, shape, dt=F32):
        return st.enter_context(nc.sbuf_tensor("s_" + name, shape, dt))

    with st:
        bank = [st.enter_context(nc.psum_tensor("bank%d" % i, [128, 512], F32)) for i in range(7)]
        bankb = st.enter_context(nc.psum_tensor("bankb", [128, 1024], BF16))
        t_bank = [Tl("bank%d" % i) for i in range(7)]
        t_bankb = Tl("bankb")

        ident = sb("ident", [128, 128]); t_ident = Tl()
        identb = sb("identb", [128, 128], BF16); t_identb = Tl()
        ones = sb("ones", [128, 128], BF16); t_ones = Tl()
        chalf1 = sb("chalf", [128, 1]); t_chalf = Tl()
        chalf = chalf1[:, 0:1].to_broadcast([128, 512])
        bands = sb("bands", [128, 20, 128], BF16); t_bands = Tl()
        invc = sb("invc", [128, 6, 128]); t_invc = Tl()
        dw = sb("dw", [31, 127]); t_dw = Tl()
        colok = sb("colok", [128, 64]); t_colok = Tl()
        P.dma("sp", "c0", ident[:], ident_d, writes=[t_ident])
        P.dma("sp", "c1", invc[:], invc_d, writes=[t_invc])
        P.dma("sp", "c2", dw[:], dw_d, writes=[t_dw])
        P.dma("sp", "c3", colok[:], colok_d, writes=[t_colok])
        P.dma("pool", "c4", bands[:], bands_d, writes=[t_bands])
        P.cp("dve", identb[:], ident[:], [t_ident], [t_identb])
        P.memset("pool", ones[:], 1.0, [t_ones])
        P.memset("pool", chalf1[:], -0.5, [t_chalf])

        xT = sb("xT", [128, 8, 1024]); t_x = [[Tl() for _ in range(2)] for _ in range(8)]
        hT = sb("hT", [128, 8, 1024], BF16); t_h = [[Tl() for _ in range(2)] for _ in range(8)]
        wslot = [sb("wslot%d" % i, [128, 4096], BF16) for i in range(Stream.NS)]
        t_slot = [Tl("slot%d" % i) for i in range(Stream.NS)]
        S = Stream(P, wslot, t_slot)
        aT = [sb("aT%d" % i, [128, 4, 512], BF16) for i in range(2)]
        t_aT = [[Tl() for _ in range(4)] for _ in range(2)]
        stmp = [sb("stmp%d" % i, [128, 512]) for i in range(2)]; t_stmp = [Tl(), Tl()]
        xio = sb("xio", [128, 2048]); t_or = [Tl() for _ in range(8)]
        oraw = xio[:].rearrange("p (c n) -> p c n", c=8)
        sq = [sb("sq%d" % i, [128, 512], BF16) for i in range(2)]; t_sq = [Tl(), Tl()]
        ms = sb("ms", [128, 512]); t_ms = Tl()
        rstd = sb("rstd", [128, 512]); t_rstd = Tl()
        ntmp = [sb("ntmp%d" % i, [128, 512]) for i in range(2)]; t_ntmp = [Tl(), Tl()]
        mods = sb("mods", [128, L, 72, 2]); t_mods = Tl()
        Am = sb("Am", [128, L, 3, 8, 2])
        Gm = sb("Gm", [128, L, 3, 8, 2])
        vtT = sb("vtT", [128, L, 128]); t_vtT = Tl()
        scT = sb("scT", [128, 16], BF16); t_scT = Tl()
        vtr = sb("vtr", [128, 128]); t_vtr = Tl()
        za_tok = sb("za_tok", [128, 8, 256], BF16); t_za = [Tl() for _ in range(8)]
        vg_tok = sb("vg_tok", [128, 8, 256], BF16); t_vg = [Tl() for _ in range(8)]
        v_tok = sb("v_tok", [128, 8, 512], BF16); t_v = [Tl() for _ in range(8)]
        qT = sb("qT", [128, 4, 1024], BF16); t_q = [[Tl() for _ in range(8)] for _ in range(4)]
        kT = sb("kT", [128, 4, 1024], BF16); t_k = [[Tl() for _ in range(8)] for _ in range(4)]
        uT = sb("uT", [128, 2, 1024], BF16); t_u = [[Tl() for _ in range(2)] for _ in range(2)]
        kcT = sb("kcT", [128, 4, 256], BF16); t_kc = Tl()
        kctok = sb("kctok", [128, 2, 512], BF16); t_kctok = Tl()
        vc = sb("vc", [128, 2, 512], BF16); t_vc = Tl()
        pooledT = sb("pooledT", [128, 2, 256], BF16); t_pl = [Tl(), Tl()]
        EB = sb("EB", [128, 8, 14, 64], BF16); t_EB = Tl()
        wsT = sb("wsT", [128, 4, 128], BF16); t_wsT = Tl()
        pwf = sb("pwf", [128, 2, 128]); t_pwf = Tl()
        pwb = sb("pwb", [128, 2, 128], BF16); t_pwb = Tl()
        bsb = sb("bsb", [128, 2, 128]); t_bsb = Tl()
        gtb = sb("gtb", [128, 384]); t_gtb = Tl()
        RT = sb("RT", [31, 128]); t_RT = Tl()
        RT01 = sb("RT01", [31, 2, 112]); t_RT01 = Tl()
        rden = sb("rden", [128, 256]); t_rden = Tl()
        st8 = sb("st8", [128, 16]); t_st8 = Tl()

        pid = {}
        groups = [(i * 4, 4) for i in range(5)] + [(20, 2)]

        def plan_ada(l):
            for pc in range(18):
                pid[("ada", l, pc)] = S.plan([(
                    lambda t: _v3(t, 8),
                    adaw_d[l].rearrange("(k p) n -> p k n", p=128)[:, :, pc * 512:(pc + 1) * 512])])

        def plan_ffn(g, l, j):
            for gi, (c0, nfc) in enumerate(groups):
                w = nfc * 128
                pid[("g", g, l, j, gi)] = S.plan([(
                    lambda t, w=w: _v3(t, 8)[:, :, 0:w],
                    wg_d[l, j].rearrange("(k p) n -> p k n", p=128)[:, :, c0 * 128:c0 * 128 + w])])
                pid[("u", g, l, j, gi)] = S.plan([(
                    lambda t, w=w: _v3(t, 8)[:, :, 0:w],
                    wu_d[l, j].rearrange("(k p) n -> p k n", p=128)[:, :, c0 * 128:c0 * 128 + w])])
                pid[("d", g, l, j, gi)] = S.plan([(
                    lambda t, nfc=nfc: _v3(t, 4)[:, 0:nfc, :],
                    wd_d[l, j][c0 * 128:c0 * 128 + w, :].rearrange("(f p) n -> p f n", p=128))])

        def plan_mixer(g, l):
            wv = win_d[l].rearrange("(k p) n -> p k n", p=128)
            pid[("A", g, l)] = S.plan([
                (lambda t: _v3(t, 8)[:, :, 0:256], wv[:, :, 0:256]),
                (lambda t: _v3(t, 8)[:, :, 256:512], wv[:, :, 2048:2304])])
            pid[("B", g, l)] = S.plan([(lambda t: _v3(t, 8), wv[:, :, 256:768])])
            pid[("C", g, l)] = S.plan([(lambda t: _v3(t, 8), wv[:, :, 768:1280])])
            pid[("D", g, l)] = S.plan([(lambda t: _v3(t, 8), wv[:, :, 1280:1792])])
            pid[("E", g, l)] = S.plan([(lambda t: _v3(t, 8)[:, :, 0:256], wv[:, :, 1792:2048])])
            wo = wout_d[l].rearrange("(k p) n -> p k n", p=128)
            pid[("F", g, l)] = S.plan([(lambda t: _v3(t, 8), wo[:, :, 0:512])])
            pid[("G", g, l)] = S.plan([(lambda t: _v3(t, 8), wo[:, :, 512:1024])])

        for l in range(L):
            plan_ada(l)
        for g in range(2):
            for l in range(L):
                plan_ffn(g, l, 0)
                plan_mixer(g, l)
                plan_ffn(g, l, 1)

        P.dma("sp", "v0", vtr[:], cvec_d, writes=[t_vtr])
        P.tr(bank[6][:, 0:128], vtr[:], ident[:], [t_vtr, t_ident], [t_bank[6]])
        P.actv(scT[:], bank[6][:, 0:16], AF.Silu, [t_bank[6]], [t_scT])
        for l in range(L):
            P.dma("sp", "v0", vtr[:], vt_d[l], writes=[t_vtr])
            P.tr(bank[6][:, 0:128], vtr[:], ident[:], [t_vtr, t_ident], [t_bank[6]])
            P.cp("dve", vtT[:, l, :], bank[6][:, 0:128], [t_bank[6]], [t_vtT])
        for l in range(L):
            for pc in range(18):
                p_ = pid[("ada", l, pc)]
                s = S.need(p_)
                wv = _v3(wslot[s], 8)
                for fc in range(4):
                    j = pc * 4 + fc
                    for k in range(8):
                        P.mm(bank[5][:, 2 * j:2 * j + 2], wv[:, k, fc * 128:(fc + 1) * 128],
                             scT[:, k:16:8], k == 0, k == 7, [t_slot[s], t_scT], [t_bank[5]])
                S.done(p_)
            P.tt("dve", mods[:, l, :, :], bank[5][:, 0:144].rearrange("p (j v) -> p j v", v=2),
                 vtT[:, l, 0:72].rearrange("p (j o) -> p j o", o=1).to_broadcast([128, 72, 2]),
                 ALU.add, [t_bank[5], t_vtT], [t_mods])
            for n in range(3):
                P.ts("dve", Am[:, l, n, :, :], mods[:, l, (3 * n + 1) * 8:(3 * n + 2) * 8, :], 1.0, None,
                     ALU.add, None, [t_mods], [t_mods])
                P.tt("dve", Am[:, l, n, :, :], Am[:, l, n, :, :],
                     vtT[:, l, 72 + n * 8:80 + n * 8].rearrange("p (j o) -> p j o", o=1).to_broadcast([128, 8, 2]),
                     ALU.mult, [t_mods, t_vtT], [t_mods])
                P.ts("dve", Gm[:, l, n, :, :], mods[:, l, (3 * n + 2) * 8:(3 * n + 3) * 8, :],
                     (1.0 if n == 1 else 0.5), None, ALU.mult, None, [t_mods], [t_mods])

        def norm_mod(l, n, v):
            for tt in range(2):
                cols = slice(tt * 512, (tt + 1) * 512)
                for k in range(8):
                    P.actv(sq[k % 2][:], xT[:, k, cols], AF.Square, [t_x[k][tt]], [t_sq[k % 2]])
                    P.mm(bank[6][:], ones[:], sq[k % 2][:], k == 0, k == 7, [t_sq[k % 2], t_ones], [t_bank[6]])
                P.ts("dve", ms[:], bank[6][:], 1.0 / D, EPS, ALU.mult, ALU.add, [t_bank[6]], [t_ms])
                P.tt("pool", rstd[:], ms[:], chalf, ALU.pow, [t_ms, t_chalf], [t_rstd])
                for k in range(8):
                    P.tt("dve", ntmp[k % 2][:], xT[:, k, cols], rstd[:], ALU.mult,
                         [t_x[k][tt], t_rstd], [t_ntmp[k % 2]])
                    P.actv(hT[:, k, cols], ntmp[k % 2][:], AF.Identity, [t_ntmp[k % 2], t_mods], [t_h[k][tt]],
                           scale=Am[:, l, n, k, v:v + 1], bias=mods[:, l, 3 * n * 8 + k, v:v + 1])

        cnt = {"up": 0, "dn": 0, "ab": 0}

        def ffn(g, l, j):
            n = 0 if j == 0 else 2
            v = g
            norm_mod(l, n, v)
            pending = []

            def flush():
                while pending:
                    pending.pop(0)()

            for gi, (c0, nfc) in enumerate(groups):
                pg_, pu_, pd_ = pid[("g", g, l, j, gi)], pid[("u", g, l, j, gi)], pid[("d", g, l, j, gi)]
                sg, su, sd = S.need(pg_), S.need(pu_), S.need(pd_)
                vg_, vu_, vd_ = _v3(wslot[sg], 8), _v3(wslot[su], 8), _v3(wslot[sd], 4)
                for tt in range(2):
                    cols = slice(tt * 512, (tt + 1) * 512)
                    ab = cnt["ab"] % 2
                    cnt["ab"] += 1
                    for f in range(nfc):
                        ib = cnt["up"] % 2
                        cnt["up"] += 1
                        for k in range(8):
                            P.mm(bank[ib][:], vg_[:, k, f * 128:(f + 1) * 128], hT[:, k, cols], k == 0, k == 7,
                                 [t_slot[sg], t_h[k][tt]], [t_bank[ib]])
                        for k in range(8):
                            P.mm(bank[2 + ib][:], vu_[:, k, f * 128:(f + 1) * 128], hT[:, k, cols], k == 0, k == 7,
                                 [t_slot[su], t_h[k][tt]], [t_bank[2 + ib]])
                        P.actv(stmp[ib][:], bank[ib][:], AF.Silu, [t_bank[ib]], [t_stmp[ib]])
                        P.tt("dve", aT[ab][:, f, :], stmp[ib][:], bank[2 + ib][:], ALU.mult,
                             [t_stmp[ib], t_bank[2 + ib]], [t_aT[ab][f]])

                    def down(ab=ab, nfc=nfc, sd=sd, vd_=vd_, cols=cols, tt=tt):
                        for dc in range(8):
                            ib2 = 4 + cnt["dn"] % 2
                            cnt["dn"] += 1
                            for f in range(nfc):
                                P.mm(bank[ib2][:], vd_[:, f, dc * 128:(dc + 1) * 128], aT[ab][:, f, :],
                                     f == 0, f == nfc - 1, [t_slot[sd], t_aT[ab][f]], [t_bank[ib2]])
                            P.stt(xT[:, dc, cols], bank[ib2][:], Gm[:, l, n, dc, v:v + 1], xT[:, dc, cols],
                                  ALU.mult, ALU.add, [t_bank[ib2], t_x[dc][tt], t_mods], [t_x[dc][tt]])

                    flush()
                    pending.append(down)
                S.done(pg_)
                S.done(pu_)
                if gi > 0:
                    S.done(pid[("d", g, l, j, gi - 1)])
            flush()
            S.done(pid[("d", g, l, j, len(groups) - 1)])

        def load_x(g):
            for i in range(8):
                xb = i % 2
                xs_ = xio[:, xb * 1024:(xb + 1) * 1024]
                tl_ = t_or[xb * 4:xb * 4 + 4]
                P.dma("sp", ("xio", xb), xs_, x_d[g, i * 128:(i + 1) * 128, :], writes=tl_)
                for hb in range(2):
                    for kk in range(4):
                        k = hb * 4 + kk
                        P.tr(bank[hb][:, kk * 128:(kk + 1) * 128], xs_[:, k * 128:(k + 1) * 128], ident[:],
                             tl_ + [t_ident], [t_bank[hb]])
                    P.cp("dve" if hb == 0 else "act", xT[:, hb * 4:hb * 4 + 4, i * 128:(i + 1) * 128],
                         bank[hb][:].rearrange("p (k n) -> p k n", k=4), [t_bank[hb]],
                         [t_x[k][i // 4] for k in range(hb * 4, hb * 4 + 4)])

        def store_y(g):
            for i in range(8):
                xb = i % 2
                xs_ = xio[:, xb * 1024:(xb + 1) * 1024]
                tl_ = t_or[xb * 4:xb * 4 + 4]
                for hb in range(2):
                    for kk in range(4):
                        k = hb * 4 + kk
                        P.tr(bank[hb][:, kk * 128:(kk + 1) * 128], xT[:, k, i * 128:(i + 1) * 128], ident[:],
                             [t_x[k][i // 4], t_ident], [t_bank[hb]])
                    P.cp("dve" if hb == 0 else "act", xs_[:, hb * 512:(hb + 1) * 512], bank[hb][:],
                         [t_bank[hb]], tl_)
                P.dma("sp", ("y", xb), y_d[g, i * 128:(i + 1) * 128, :], xs_, reads=tl_, is_out=True)

        def grp_rms(src, t_src, nh):
            n = nh * 64
            s3 = src.rearrange("p (h d) -> p h d", d=64)
            sqs, t_sqs = sq[0], t_sq[0]
            P.tt("dve", sqs[:, 0:n], src, src, ALU.mult, [t_src], [t_sqs])
            P.red(st8[:, 0:nh], sqs[:, 0:n].rearrange("p (h d) -> p h d", d=64), [t_sqs], [t_st8])
            P.ts("dve", st8[:, 0:nh], st8[:, 0:nh], 1.0 / 64, EPS, ALU.mult, ALU.add, [t_st8], [t_st8])
            P.tt("pool", st8[:, 8:8 + nh], st8[:, 0:nh], chalf1[:, 0:1].to_broadcast([128, nh]), ALU.pow,
                 [t_st8, t_chalf], [t_st8])
            P.tt("dve", s3, s3, st8[:, 8:8 + nh].rearrange("p (h o) -> p h o", o=1).to_broadcast([128, nh, 64]),
                 ALU.mult, [t_src, t_st8], [t_src])

        def mixer_setup(g, l):
            P.dma("sp", "ms0", gtb[:], gt_d[l:l + 1, :].to_broadcast([128, 384]), writes=[t_gtb])
            P.ts("dve", gtb[:, 0:64], gtb[:, 0:64], 0.125, None, ALU.mult, None, [t_gtb], [t_gtb])
            P.memset("pool", pwf[:], 0.0, [t_pwf])
            for c in range(2):
                for gh in range(2):
                    P.dma("sp", "ms1", pwf[gh * 64:(gh + 1) * 64, c, gh * 64:(gh + 1) * 64], poolw_d[l, 2 * c + gh],
                          writes=[t_pwf])
            P.cp("dve", pwb[:], pwf[:], [t_pwf], [t_pwb])
            for g4 in range(4):
                P.dma("sp", "ms2", stmp[0][:, 0:128], sgw_d[l, g4], writes=[t_stmp[0]])
                P.tr(bank[6][:, 0:128], stmp[0][:, 0:128], ident[:], [t_stmp[0], t_ident], [t_bank[6]])
                P.cp("dve", wsT[:, g4, :], bank[6][:, 0:128], [t_bank[6]], [t_wsT])
            for g4 in range(4):
                P.dma("sp", "ms3", bsb[(g4 % 2) * 64:(g4 % 2 + 1) * 64, g4 // 2, :],
                      sgb_d[l, g4:g4 + 1, :].to_broadcast([64, 128]), writes=[t_bsb])
            if g == 1:
                for t in range(2):
                    P.dma("pool", "ms4", kctok[:, t, :].rearrange("p (h d) -> p h d", h=8),
                          ck_d[l][:, t * 128:(t + 1) * 128, :].rearrange("h p d -> p h d"), writes=[t_kctok])
                    P.dma("pool", "ms5", vc[:, t, :].rearrange("p (h d) -> p h d", h=8),
                          cv_d[l][:, t * 128:(t + 1) * 128, :].rearrange("h p d -> p h d"), writes=[t_vc])
                for t in range(2):
                    for c in range(4):
                        P.tr(bankb[:, c * 128:(c + 1) * 128], kctok[:, t, c * 128:(c + 1) * 128], identb[:],
                             [t_kctok, t_identb], [t_bankb])
                    P.cp("dve", kcT[:, :, t * 128:(t + 1) * 128],
                         bankb[:, 0:512].rearrange("p (c n) -> p c n", c=4), [t_bankb], [t_kc])
                P.dma("sp", "ms6", stmp[1][:, 0:31], rpb_d[l], writes=[t_stmp[1]])
                P.tr(bank[6][0:31, 0:128], stmp[1][:, 0:31], ident[:], [t_stmp[1], t_ident], [t_bank[6]])
                P.cp("dve", RT[:], bank[6][0:31, 0:128], [t_bank[6]], [t_RT])
                rt3 = RT[:, 0:120].rearrange("p (h r) -> p h r", h=8)
                for half in range(2):
                    P.cp("dve", RT01[:, half, :].rearrange("p (h r) -> p h r", h=8), rt3[:, :, half:half + 14],
                         [t_RT], [t_RT01])
                for b in range(16):
                    bk = bank[b % 2]
                    for qi in range(4):
                        qc = b * 4 + qi
                        for half in range(2):
                            P.mm(bk[half * 64:(half + 1) * 64, qi * 112:(qi + 1) * 112],
                                 dw[0:31, 63 - qc:127 - qc], RT01[:, half, :], True, True,
                                 [t_dw, t_RT01], [t_bank[b % 2]], tp=(0, half * 64))
                    P.actv(EB[:, :, :, b * 4:(b + 1) * 4].rearrange("p h r q -> p q (h r)"),
                           bk[:, 0:448].rearrange("p (q n) -> p q n", q=4), AF.Exp, [t_bank[b % 2]], [t_EB])
                ebv = EB[:].rearrange("p h r q -> p (h r) q")
                P.tt("dve", ebv, ebv, colok[:, :].rearrange("p (o q) -> p o q", o=1).to_broadcast([128, 112, 64]),
                     ALU.mult, [t_EB, t_colok], [t_EB])

        def mixer(g, l):
            v = g
            is_p = (g == 0)
            mixer_setup(g, l)
            norm_mod(l, 1, v)
            pA, pB, pC, pD, pE, pF, pG = [pid[(nm, g, l)] for nm in "ABCDEFG"]
            sA, sB, sC, sD, sE = [S.need(p_) for p_ in (pA, pB, pC, pD, pE)]
            vA, vB, vC, vD, vE = [_v3(wslot[s_], 8) for s_ in (sA, sB, sC, sD, sE)]
            qf, kf, vf, gv = stmp[0], stmp[1], ntmp[0], ntmp[1]
            t_qf, t_kf, t_vf, t_gv = t_stmp[0], t_stmp[1], t_ntmp[0], t_ntmp[1]
            qb, t_qb = sq[1], t_sq[1]
            kb, t_kb = aT[0][:, 0, :], t_aT[0][0]
            h3 = lambda ap: ap.rearrange("p (h d) -> p h d", d=64)
            for i in range(8):
                tt = i // 4
                tok = slice(i * 128, (i + 1) * 128)
                for bi, (vw, sl) in enumerate(((vA, sA), (vB, sB), (vC, sC), (vD, sD))):
                    for k in range(8):
                        P.mm(bank[bi][:], hT[:, k, tok], vw[:, k, :], k == 0, k == 7,
                             [t_h[k][tt], t_slot[sl]], [t_bank[bi]])
                P.cp("act", za_tok[:, i, :], bank[0][:, 0:256], [t_bank[0]], [t_za[i]])
                P.actv(gv[:, 0:256], bank[0][:, 256:512], AF.Gelu_apprx_tanh, [t_bank[0]], [t_gv])
                grp_rms(gv[:, 0:256], t_gv, 4)
                P.tt("pool", vg_tok[:, i, :], gv[:, 0:256], gtb[:, 128:384], ALU.mult, [t_gv, t_gtb], [t_vg[i]])
                P.cp("act", qf[:], bank[1][:], [t_bank[1]], [t_qf])
                grp_rms(qf[:], t_qf, 8)
                P.tt("pool", h3(qb[:]), h3(qf[:]),
                     gtb[:, 0:64].rearrange("p (o d) -> p o d", o=1).to_broadcast([128, 8, 64]),
                     ALU.mult, [t_qf, t_gtb], [t_qb])
                for c in range(4):
                    P.tr(bankb[:, c * 128:(c + 1) * 128], qb[:, c * 128:(c + 1) * 128], identb[:],
                         [t_qb, t_identb], [t_bankb])
                P.cp("dve", qT[:, :, tok], bankb[:, 0:512].rearrange("p (c n) -> p c n", c=4), [t_bankb],
                     [t_q[c][i] for c in range(4)])
                P.cp("act", kf[:], bank[2][:], [t_bank[2]], [t_kf])
                grp_rms(kf[:], t_kf, 8)
                P.tt("pool", h3(kf[:]), h3(kf[:]),
                     gtb[:, 64:128].rearrange("p (o d) -> p o d", o=1).to_broadcast([128, 8, 64]),
                     ALU.mult, [t_kf, t_gtb], [t_kf])
                if is_p:
                    b_, s0_ = i // 2, (i % 2) * 128
                    P.dma("sp", "nk", nk_d[b_, l].rearrange("h s d -> s h d")[s0_:s0_ + 128], h3(kf[:]),
                          reads=[t_kf], is_out=True)
                P.cp("act", kb, kf[:], [t_kf], [t_kb])
                for c in range(4):
                    P.tr(bankb[:, 512 + c * 128:512 + (c + 1) * 128], kb[:, c * 128:(c + 1) * 128], identb[:],
                         [t_kb, t_identb], [t_bankb])
                P.cp("dve", kT[:, :, tok], bankb[:, 512:1024].rearrange("p (c n) -> p c n", c=4), [t_bankb],
                     [t_k[c][i] for c in range(4)])
                P.cp("act", v_tok[:, i, :], bank[3][:], [t_bank[3]], [t_v[i]])
                if is_p:
                    P.cp("dve", vf[:], bank[3][:], [t_bank[3]], [t_vf])
                    P.dma("sp", "nv", nv_d[b_, l].rearrange("h s d -> s h d")[s0_:s0_ + 128], h3(vf[:]),
                          reads=[t_vf], is_out=True)
            for tt in range(2):
                cols = slice(tt * 512, (tt + 1) * 512)
                for c in range(2):
                    for k in range(8):
                        P.mm(bank[4 + c][:], vE[:, k, c * 128:(c + 1) * 128], hT[:, k, cols], k == 0, k == 7,
                             [t_slot[sE], t_h[k][tt]], [t_bank[4 + c]])
                    P.actv(uT[:, c, cols], bank[4 + c][:], AF.Gelu_apprx_tanh, [t_bank[4 + c]], [t_u[c][tt]])
            for p_ in (pA, pB, pC, pD, pE):
                S.done(p_)
            sF, sG = S.need(pF), S.need(pG)

            def variant(j):
                if is_p:
                    return 0 if j % 2 == 0 else 2
                return 0 if j == 0 else (2 if j == 7 else 1)

            def seq_range(j):
                if is_p:
                    return (j // 2 * 2, j // 2 * 2 + 1)
                return (0, 7)

            sqo, t_sqo = aT[0][:, 3, :], t_aT[0][3]
            rs_ap = (rstd[:, 0:256], rstd[:, 256:512], ms[:, 0:256])
            rs_t = (t_rstd, t_rstd, t_ms)
            oTn = aT[1][:].rearrange("p f n -> p (f n)").rearrange("p (c n) -> p c n", c=8)
            for u in range(4):
                tok0 = u * 256
                ucols = slice(tok0, tok0 + 256)
                tt = u // 2
                if is_p:
                    for c in range(4):
                        bo, t_bo = bank[2 + c % 2], t_bank[2 + c % 2]
                        for hp in range(2):
                            h = 2 * c + hp
                            bk = bank[hp]
                            ps_ = slice(hp * 64, (hp + 1) * 64)
                            pt, t_pt = aT[0][:, 1 + hp, :], t_aT[0][1 + hp]
                            for kt in range(2):
                                P.mm(bk[:, kt * 256:(kt + 1) * 256],
                                     kT[ps_, c, tok0 + kt * 128:tok0 + (kt + 1) * 128], qT[ps_, c, ucols],
                                     True, True, [t_k[c][2 * u + kt], t_q[c][2 * u], t_q[c][2 * u + 1]],
                                     [t_bank[hp]])
                            P.actv(pt, bk[:], AF.Exp, [t_bank[hp]], [t_pt])
                            for kt in range(2):
                                P.mm(bo[ps_, 0:256], v_tok[:, 2 * u + kt, h * 64:(h + 1) * 64],
                                     pt[:, kt * 256:(kt + 1) * 256], kt == 0, kt == 1,
                                     [t_v[2 * u + kt], t_pt], [t_bo], tp=(0, hp * 64))
                            for kt in range(2):
                                P.mm(bo[ps_, 256:512], ones[:, 0:64], pt[:, kt * 256:(kt + 1) * 256],
                                     kt == 0, kt == 1, [t_ones, t_pt], [t_bo], tp=(0, hp * 64))
                        P.add("dve", lambda e, bo=bo: e.reciprocal(rden[:], bo[:, 256:512]), [t_bo], [t_rden])
                        P.tt("dve", oraw[:, 2 + c, :], bo[:, 0:256], rden[:], ALU.mult, [t_bo, t_rden],
                             [t_or[2 + c]])
                else:
                    for c in range(4):
                        for r in range(4 * u, 4 * u + 4):
                            s0 = min(max(r - 4, 0), 8)
                            kts = list(range(s0 // 2, (s0 + 7) // 2 + 1))
                            nl = len(kts)
                            bo, t_bo = bank[2 + r % 2], t_bank[2 + r % 2]
                            qcols = slice(r * 64, (r + 1) * 64)
                            for hp in range(2):
                                h = 2 * c + hp
                                ps_ = slice(hp * 64, (hp + 1) * 64)
                                bk = bank[hp]
                                et, t_et = (ms, rstd)[hp], (t_ms, t_rstd)[hp]
                                pt, t_pt = aT[0][:, 1 + hp, :], t_aT[0][1 + hp]
                                for j_, kt in enumerate(kts):
                                    P.mm(bk[:, j_ * 64:(j_ + 1) * 64], kT[ps_, c, kt * 128:(kt + 1) * 128],
                                         qT[ps_, c, qcols], True, True, [t_k[c][kt], t_q[c][r // 2]],
                                         [t_bank[hp]])
                                for t in range(2):
                                    P.mm(bk[:, (nl + t) * 64:(nl + t + 1) * 64], kcT[ps_, c, t * 128:(t + 1) * 128],
                                         qT[ps_, c, qcols], True, True, [t_kc, t_q[c][r // 2]], [t_bank[hp]])
                                P.actv(et[:, 0:nl * 64], bk[:, 0:nl * 64], AF.Exp, [t_bank[hp]], [t_et])
                                P.actv(pt[:, nl * 64:(nl + 2) * 64], bk[:, nl * 64:(nl + 2) * 64], AF.Exp,
                                       [t_bank[hp]], [t_pt])
                                i0 = 2 * kts[0] - r + 7
                                P.tt("dve", pt[:, 0:nl * 64].rearrange("p (j q) -> p j q", q=64),
                                     et[:, 0:nl * 64].rearrange("p (j q) -> p j q", q=64),
                                     EB[:, h, i0:i0 + 2 * nl - 1:2, :], ALU.mult, [t_et, t_EB], [t_pt])
                                mml = []
                                for j_, kt in enumerate(kts):
                                    if s0 % 2 == 1 and j_ == 0:
                                        p0, p1 = 64, 128
                                    elif s0 % 2 == 1 and j_ == nl - 1:
                                        p0, p1 = 0, 64
                                    else:
                                        p0, p1 = 0, 128
                                    mml.append((v_tok[p0:p1, kt, h * 64:(h + 1) * 64], ones[p0:p1, 0:64],
                                                pt[p0:p1, j_ * 64:(j_ + 1) * 64], p0, [t_v[kt]]))
                                for t in range(2):
                                    mml.append((vc[:, t, h * 64:(h + 1) * 64], ones[:, 0:64],
                                                pt[:, (nl + t) * 64:(nl + t + 1) * 64], 0, [t_vc]))
                                for n_, (lv, lo_, rh, p0, rd) in enumerate(mml):
                                    P.mm(bo[ps_, 0:64], lv, rh, n_ == 0, n_ == len(mml) - 1, rd + [t_pt], [t_bo],
                                         tp=(p0, hp * 64))
                                for n_, (lv, lo_, rh, p0, rd) in enumerate(mml):
                                    P.mm(bo[ps_, 64:128], lo_, rh, n_ == 0, n_ == len(mml) - 1, [t_ones, t_pt],
                                         [t_bo], tp=(p0, hp * 64))
                            P.add("dve", lambda e, bo=bo: e.reciprocal(rden[:, 0:64], bo[:, 64:128]), [t_bo],
                                  [t_rden])
                            P.tt("dve", oraw[:, 2 + c, (r % 4) * 64:(r % 4 + 1) * 64], bo[:, 0:64], rden[:, 0:64],
                                 ALU.mult, [t_bo, t_rden], [t_or[2 + c]])
                for c in range(2):
                    for gh in range(2):
                        g4 = 2 * c + gh
                        for jj in range(2):
                            j = 2 * u + jj
                            lo, hi = seq_range(j)
                            ins = [i_ for i_ in (j - 1, j, j + 1) if lo <= i_ <= hi]
                            for n_, i_ in enumerate(ins):
                                var = 3 if i_ == j - 1 else (4 if i_ == j + 1 else variant(j))
                                P.mm(bank[4][gh * 64:(gh + 1) * 64, c * 256 + jj * 128:c * 256 + (jj + 1) * 128],
                                     za_tok[:, i_, g4 * 64:(g4 + 1) * 64], bands[:, g4 * 5 + var, :],
                                     n_ == 0, n_ == len(ins) - 1, [t_za[i_], t_bands], [t_bank[4]],
                                     tp=(0, gh * 64))
                    for jj in range(2):
                        P.tt("dve", pooledT[:, c, jj * 128:(jj + 1) * 128],
                             bank[4][:, c * 256 + jj * 128:c * 256 + (jj + 1) * 128],
                             invc[:, c * 3 + variant(2 * u + jj), :], ALU.mult, [t_bank[4], t_invc], [t_pl[c]])
                    P.mm(bank[5][:, c * 256:(c + 1) * 256], pwb[:, c, :], pooledT[:, c, :], True, True,
                         [t_pwb, t_pl[c]], [t_bank[5]])
                    P.actv(oraw[:, c, :], bank[5][:, c * 256:(c + 1) * 256], AF.Identity, [t_bank[5], t_vtT],
                           [t_or[c]], scale=vtT[:, l, 104 + c:105 + c])
                for c in range(2):
                    for gh in range(2):
                        g4 = 2 * c + gh
                        for jj in range(2):
                            P.mm(bank[4][gh * 64:(gh + 1) * 64, c * 256 + jj * 128:c * 256 + (jj + 1) * 128],
                                 vg_tok[:, 2 * u + jj, g4 * 64:(g4 + 1) * 64], wsT[:, g4, :], True, True,
                                 [t_vg[2 * u + jj], t_wsT], [t_bank[4]], tp=(0, gh * 64))
                    P.tt("dve", gv[:, 0:256].rearrange("p (j n) -> p j n", j=2),
                         bank[4][:, c * 256:(c + 1) * 256].rearrange("p (j n) -> p j n", j=2),
                         bsb[:, c:c + 1, :].to_broadcast([128, 2, 128]), ALU.add, [t_bank[4], t_bsb], [t_gv])
                    P.tt("pool", oraw[:, 6 + c, :], gv[:, 0:256], uT[:, c, ucols], ALU.mult,
                         [t_gv, t_u[c][tt]], [t_or[6 + c]])
                segs = ((0, 2, 256.0), (2, 6, 512.0), (6, 8, 256.0))
                for si, (o0, o1, wd_) in enumerate(segs):
                    stb, t_stb = (bank[6], t_bank[6]) if si < 2 else (bank[5], t_bank[5])
                    scol = slice((si % 2) * 256, (si % 2) * 256 + 256)
                    for oc in range(o0, o1):
                        P.actv(sqo[:, 0:256], oraw[:, oc, :], AF.Square, [t_or[oc]], [t_sqo])
                        P.mm(stb[:, scol], ones[:], sqo[:, 0:256], oc == o0, oc == o1 - 1, [t_ones, t_sqo],
                             [t_stb])
                    P.ts("dve", ms[:, 256:512], stb[:, scol], 1.0 / wd_, EPS, ALU.mult, ALU.add, [t_stb], [t_ms])
                    P.tt("pool", rs_ap[si], ms[:, 256:512], chalf1[:, 0:1].to_broadcast([128, 256]), ALU.pow,
                         [t_ms, t_chalf], [rs_t[si]])
                for oc in range(8):
                    si = 0 if oc < 2 else (1 if oc < 6 else 2)
                    P.stt(oTn[:, oc, :], oraw[:, oc, :], vtT[:, l, 96 + oc:97 + oc], rs_ap[si], ALU.mult, ALU.mult,
                          [t_or[oc], t_vtT, rs_t[si]], [t_aT[1][oc // 2]])
                for dc in range(8):
                    sl = sF if dc < 4 else sG
                    vw = _v3(wslot[sl], 8)
                    bw, t_bw = bank[2 + dc % 2], t_bank[2 + dc % 2]
                    for oc in range(8):
                        P.mm(bw[:, 0:256], vw[:, oc, (dc % 4) * 128:(dc % 4 + 1) * 128], oTn[:, oc, :],
                             oc == 0, oc == 7, [t_slot[sl], t_aT[1][oc // 2]], [t_bw])
                    P.stt(xT[:, dc, ucols], bw[:, 0:256], Gm[:, l, 1, dc, v:v + 1], xT[:, dc, ucols],
                          ALU.mult, ALU.add, [t_bw, t_x[dc][tt], t_mods], [t_x[dc][tt]])
            S.done(pF)
            S.done(pG)

        for g in range(2):
            load_x(g)
            for l in range(L):
                ffn(g, l, 0)
                mixer(g, l)
                ffn(g, l, 1)
            store_y(g)
        P.emit(st)
    return nc


def _consts():
    ident = np.eye(128, dtype=np.float32)
    bands = np.zeros((128, 20, 128), np.float32)
    invc = np.zeros((128, 6, 128), np.float32)
    for gi, w in enumerate(WINS):
        h = w // 2
        tp = np.arange(128)[:, None]
        t = np.arange(128)[None, :]
        inb = ((tp >= t - h) & (tp < t + h)).astype(np.float32)
        cnt_first = (np.minimum(t + h, 10 ** 6) - np.maximum(t - h, 0)).astype(np.float32)
        cnt_mid = np.full((1, 128), float(w), np.float32)
        cnt_last = ((128 - t) + h - np.maximum(0, 0)).astype(np.float32)
        cnt_last = np.minimum(cnt_last, w).astype(np.float32)
        eye = np.eye(128, dtype=np.float32)
        bands[:, gi * 5 + 0, :] = inb - eye * cnt_first
        bands[:, gi * 5 + 1, :] = inb - eye * cnt_mid
        bands[:, gi * 5 + 2, :] = inb - eye * cnt_last
        bands[:, gi * 5 + 3, :] = ((tp - 128) >= (t - h)).astype(np.float32)
        bands[:, gi * 5 + 4, :] = ((tp + 128) < (t + h)).astype(np.float32)
        c, half = gi // 2, gi % 2
        for vi, cn in enumerate((cnt_first, cnt_mid, cnt_last)):
            invc[half * 64:(half + 1) * 64, c * 3 + vi, :] = 1.0 / cn
    dw = np.zeros((31, 127), np.float32)
    for j in range(31):
        dw[j, j + 48] = 1.0
    cols = np.arange(64)
    cs = np.clip(cols - 8, 0, 48)
    ok = (cols[None, :] >= cs[:, None]) & (cols[None, :] < cs[:, None] + 16)
    colok = np.concatenate([ok.T, ok.T], 0).astype(np.float32)
    return ident, bands, invc, dw, colok


def make_in_maps(inp):
    f = lambda a: np.ascontiguousarray(np.asarray(a, dtype=np.float32))
    ident, bands, invc, dw, colok = _consts()
    vt = np.zeros((L, 128, 128), np.float32)
    gt = np.zeros((L, 384), np.float32)
    rpb = np.zeros((L, 128, 31), np.float32)
    for l in range(L):
        vt[l, 0:72] = f(inp["ada_b"])[l].reshape(72, 128)
        vt[l, 72:96] = f(inp["norm_g"])[l].reshape(24, 128)
        vt[l, 96:104] = f(inp["out_norm_g"])[l].reshape(8, 128)
        vt[l, 104:106] = f(inp["pool_scale"])[l].reshape(2, 128)
        gt[l, 0:64] = f(inp["q_norm_g"])[l]
        gt[l, 64:128] = f(inp["k_norm_g"])[l]
        gt[l, 128:384] = f(inp["sg_vnorm_g"])[l].reshape(256)
        rpb[l, 0:120] = f(inp["na_rpb"])[l].reshape(120, 31)
    shared = {
        "vt": vt, "gt": gt, "rpb": rpb,
        "poolw": f(inp["pool_w"]), "sgw": f(inp["sg_w"]), "sgb": f(inp["sg_b"]),
        "adaw": f(inp["ada_w"]), "wg": f(inp["ffn_w_gate"]), "wu": f(inp["ffn_w_up"]),
        "wd": f(inp["ffn_w_down"]), "win": f(inp["w_in"]), "wout": f(inp["w_out"]),
        "ident": ident, "bands": bands, "invc": invc, "dw": dw, "colok": colok,
    }
    xp = f(inp["x_prompt"]); xs = f(inp["x_sample"])
    ck = f(inp["cache_k"]); cv = f(inp["cache_v"]); c = f(inp["c"]); cc = f(inp["c_ctx"])
    maps = []
    for i in range(NCORES):
        x = np.stack([xp[4 * i:4 * i + 4].reshape(1024, D), xs[i]], 0)
        cvec = np.zeros((128, 128), np.float32)
        cvec[0:8] = cc.reshape(8, 128)
        cvec[8:16] = c[i].reshape(8, 128)
        m = dict(shared)
        m.update({"x": np.ascontiguousarray(x), "ck": np.ascontiguousarray(ck[i]),
                  "cv": np.ascontiguousarray(cv[i]), "cvec": cvec})
        maps.append(m)
    return maps


_NC_CACHE = {}


def kernel(**inputs):
    if "nc" not in _NC_CACHE:
        _NC_CACHE["nc"] = build_program()
    nc = _NC_CACHE["nc"]
    maps = make_in_maps(inputs)
    res = run_bass_kernel_spmd(nc, maps, core_ids=list(range(NCORES)))
    r = res.results
    yp = np.concatenate([r[i]["y"][0].reshape(4, 256, D) for i in range(NCORES)], 0)
    ys = np.stack([r[i]["y"][1] for i in range(NCORES)], 0)
    nk = np.concatenate([r[i]["nk"] for i in range(NCORES)], 0)
    nv = np.concatenate([r[i]["nv"] for i in range(NCORES)], 0)
    return (yp.astype(np.float32), ys.astype(np.float32), nk.astype(np.float32), nv.astype(np.float32))
```

```python
import numpy as np
from contextlib import ExitStack
import concourse.bass as bass
import concourse.mybir as mybir
from concourse.bass_utils import run_bass_kernel_spmd

F32 = mybir.dt.float32
BF16 = mybir.dt.bfloat16
AF = mybir.ActivationFunctionType
ALU = mybir.AluOpType
AX = mybir.AxisListType

D = 1024
DFF = 2816
L = 2
NH = 8
EPS = 1e-6
NCORES = 8
WINS = (2, 4, 8, 16)


class Tl:
    __slots__ = ("name", "w", "r")

    def __init__(self, name=""):
        self.name = name
        self.w = None
        self.r = {}


class Op:
    __slots__ = ("eng", "fn", "deps", "marked", "count", "dkey", "dcount", "is_dma", "uid")


class Prog:
    ENGS = ("pe", "act", "dve", "pool", "sp")

    def __init__(self, nc):
        self.nc = nc
        self.ops = {e: [] for e in self.ENGS}
        self.latest_dma = {}
        self.dma_counts = {}
        self.uid = 0
        self.out_keys = set()

    def add(self, eng, fn, reads=(), writes=(), dma=None, is_out=False):
        op = Op()
        op.eng = eng
        op.fn = fn
        op.marked = False
        op.count = None
        op.is_dma = dma is not None
        op.dkey = dma
        op.uid = self.uid
        self.uid += 1
        deps = {}
        for t in reads:
            if t.w is not None:
                deps[t.w.uid] = t.w
        for t in writes:
            if t.w is not None:
                deps[t.w.uid] = t.w
            for d in t.r.values():
                deps[d.uid] = d
        final = {}
        for d in deps.values():
            if d.is_dma:
                d = self.latest_dma[d.dkey]
            elif d.eng == "pe" and eng == "pe" and not op.is_dma:
                continue
            else:
                d.marked = True
            final[d.uid] = d
        op.deps = list(final.values())
        if op.is_dma:
            c = self.dma_counts.get(dma, 0) + 16
            self.dma_counts[dma] = c
            op.dcount = c
            self.latest_dma[dma] = op
            if is_out:
                self.out_keys.add(dma)
        rkey = ("dma", op.uid) if op.is_dma else eng
        for t in reads:
            t.r[rkey] = op
        for t in writes:
            t.w = op
            t.r = {}
        self.ops[eng].append(op)
        return op

    def dma(self, eng, key, out, in_, reads=(), writes=(), is_out=False):
        return self.add(eng, lambda e: e.dma_start(out=out, in_=in_), reads, writes,
                        dma=key, is_out=is_out)

    def mm(self, out, lhsT, rhs, start, stop, reads, writes, tp=None):
        if tp is None:
            return self.add("pe", lambda e: e.matmul(out, lhsT, rhs, start=start, stop=stop), reads, writes)
        return self.add("pe", lambda e: e.matmul(out, lhsT, rhs, start=start, stop=stop, tile_position=tp),
                        reads, writes)

    def tr(self, out, in_, ident, reads, writes):
        return self.add("pe", lambda e: e.transpose(out, in_, ident), reads, writes)

    def actv(self, out, in_, func, reads, writes, scale=None, bias=None):
        kw = {}
        if scale is not None:
            kw["scale"] = scale
        if bias is not None:
            kw["bias"] = bias
        return self.add("act", lambda e: e.activation(out, in_, func, **kw), reads, writes)

    def tt(self, eng, out, in0, in1, op, reads, writes):
        return self.add(eng, lambda e: e.tensor_tensor(out, in0, in1, op=op), reads, writes)

    def ts(self, eng, out, in0, s1, s2, op0, op1, reads, writes):
        if s2 is None:
            return self.add(eng, lambda e: e.tensor_scalar(out, in0, s1, None, op0=op0), reads, writes)
        return self.add(eng, lambda e: e.tensor_scalar(out, in0, s1, s2, op0=op0, op1=op1), reads, writes)

    def stt(self, out, in0, scalar, in1, op0, op1, reads, writes):
        return self.add("dve", lambda e: e.scalar_tensor_tensor(out, in0, scalar, in1, op0=op0, op1=op1),
                        reads, writes)

    def cp(self, eng, out, in_, reads, writes):
        if eng == "act":
            return self.add("act", lambda e: e.copy(out, in_), reads, writes)
        return self.add(eng, lambda e: e.tensor_copy(out, in_), reads, writes)

    def red(self, out, in_, reads, writes):
        return self.add("dve", lambda e: e.tensor_reduce(out, in_, axis=AX.X, op=ALU.add), reads, writes)

    def memset(self, eng, ap, val, writes):
        return self.add(eng, lambda e: e.memset(ap, val), (), writes)

    def emit(self, stack):
        nc = self.nc
        for e in self.ENGS:
            c = 0
            for op in self.ops[e]:
                if op.marked and not op.is_dma:
                    c += 1
                    op.count = c
        esem = {e: stack.enter_context(nc.semaphore("s_" + e)) for e in self.ENGS}
        dsem = {k: stack.enter_context(nc.semaphore("d_%d" % i)) for i, k in enumerate(self.dma_counts)}
        block = stack.enter_context(nc.Block())
        ops = self.ops
        out_final = [(dsem[k], self.dma_counts[k]) for k in sorted(self.out_keys, key=str)]

        def run(ename, e):
            known = {}
            for op in ops[ename]:
                for d in op.deps:
                    if d.is_dma:
                        s, v = dsem[d.dkey], d.dcount
                    else:
                        s, v = esem[d.eng], d.count
                    kid = id(s)
                    if known.get(kid, 0) >= v:
                        continue
                    known[kid] = v
                    e.wait_ge(s, v)
                ins = op.fn(e)
                if op.is_dma:
                    ins.then_inc(dsem[op.dkey], 16)
                elif op.marked:
                    ins.then_inc(esem[ename], 1)
            if ename == "sp":
                for s, v in out_final:
                    e.wait_ge(s, v)

        @block.sync
        def _(e):
            run("sp", e)

        @block.tensor
        def _(e):
            run("pe", e)

        @block.scalar
        def _(e):
            run("act", e)

        @block.vector
        def _(e):
            run("dve", e)

        @block.gpsimd
        def _(e):
            run("pool", e)


class Stream:
    NS = 6
    LOOK = 5

    def __init__(self, P, slots, t_slots):
        self.P = P
        self.slots = slots
        self.t_slots = t_slots
        self.pieces = []
        self.issued = 0
        self.freed = 0
        self.want = 0

    def plan(self, parts):
        self.pieces.append(parts)
        return len(self.pieces) - 1

    def pump(self):
        while self.issued <= self.want and self.issued < len(self.pieces) and self.issued < self.freed + self.NS:
            i = self.issued
            s = i % self.NS
            for dst_fn, src in self.pieces[i]:
                self.P.dma("pool", ("w", s), dst_fn(self.slots[s]), src, writes=[self.t_slots[s]])
            self.issued += 1

    def need(self, p):
        self.want = max(self.want, p + self.LOOK)
        self.pump()
        assert self.issued > p, (self.issued, p, self.freed)
        return p % self.NS

    def done(self, p):
        self.freed = max(self.freed, p + 1)
        self.pump()


def _v3(t, k):
    return t[:].rearrange("p (k n) -> p k n", k=k)


def build_program(stop_after=None):
    nc = bass.Bass("TRN2", target_bir_lowering=False)

    def din(name, shape):
        return nc.dram_tensor(name, shape, F32, kind="ExternalInput").ap()

    def dout(name, shape):
        return nc.dram_tensor(name, shape, F32, kind="ExternalOutput").ap()

    x_d = din("x", [2, 1024, D])
    ck_d = din("ck", [L, NH, 256, 64])
    cv_d = din("cv", [L, NH, 256, 64])
    cvec_d = din("cvec", [128, 128])
    vt_d = din("vt", [L, 128, 128])
    gt_d = din("gt", [L, 384])
    poolw_d = din("poolw", [L, 4, 64, 64])
    sgw_d = din("sgw", [L, 4, 128, 128])
    sgb_d = din("sgb", [L, 4, 128])
    rpb_d = din("rpb", [L, 128, 31])
    adaw_d = din("adaw", [L, D, 9 * D])
    wg_d = din("wg", [L, 2, D, DFF])
    wu_d = din("wu", [L, 2, D, DFF])
    wd_d = din("wd", [L, 2, DFF, D])
    win_d = din("win", [L, D, 2304])
    wout_d = din("wout", [L, D, D])
    ident_d = din("ident", [128, 128])
    bands_d = din("bands", [128, 20, 128])
    invc_d = din("invc", [128, 6, 128])
    dw_d = din("dw", [31, 127])
    colok_d = din("colok", [128, 64])
    y_d = dout("y", [2, 1024, D])
    nk_d = dout("nk", [4, L, NH, 256, 64])
    nv_d = dout("nv", [4, L, NH, 256, 64])

    P = Prog(nc)
    st = ExitStack()

    def sb(name, shape, dt=F32):
        return st.enter_context(nc.sbuf_tensor("s_" + name, shape, dt))

    with st:
        bank = [st.enter_context(nc.psum_tensor("bank%d" % i, [128, 512], F32)) for i in range(7)]
        bankb = st.enter_context(nc.psum_tensor("bankb", [128, 1024], BF16))
        t_bank = [Tl("bank%d" % i) for i in range(7)]
        t_bankb = Tl("bankb")

        ident = sb("ident", [128, 128]); t_ident = Tl()
        identb = sb("identb", [128, 128], BF16); t_identb = Tl()
        ones = sb("ones", [128, 128], BF16); t_ones = Tl()
        epsc = sb("epsc", [128, 1]); t_chalf = Tl()
        bands = sb("bands", [128, 20, 128], BF16); t_bands = Tl()
        invc = sb("invc", [128, 6, 128]); t_invc = Tl()
        dw = sb("dw", [31, 127]); t_dw = Tl()
        colok = sb("colok", [128, 64]); t_colok = Tl()
        P.dma("sp", "c0", ident[:], ident_d, writes=[t_ident])
        P.dma("sp", "c1", invc[:], invc_d, writes=[t_invc])
        P.dma("sp", "c2", dw[:], dw_d, writes=[t_dw])
        P.dma("sp", "c3", colok[:], colok_d, writes=[t_colok])
        P.dma("pool", "c4", bands[:], bands_d, writes=[t_bands])
        P.cp("dve", identb[:], ident[:], [t_ident], [t_identb])
        P.memset("pool", ones[:], 1.0, [t_ones])
        P.memset("pool", epsc[:], EPS, [t_chalf])

        xT = sb("xT", [128, 8, 1024]); t_x = [[Tl() for _ in range(2)] for _ in range(8)]
        hT = sb("hT", [128, 8, 1024], BF16); t_h = [[Tl() for _ in range(2)] for _ in range(8)]
        wslot = [sb("wslot%d" % i, [128, 4096], BF16) for i in range(Stream.NS)]
        t_slot = [Tl("slot%d" % i) for i in range(Stream.NS)]
        S = Stream(P, wslot, t_slot)
        aT = [sb("aT%d" % i, [128, 4, 512], BF16) for i in range(2)]
        t_aT = [[Tl() for _ in range(4)] for _ in range(2)]
        stmp = [sb("stmp%d" % i, [128, 512]) for i in range(2)]; t_stmp = [Tl(), Tl()]
        xio = sb("xio", [128, 2048]); t_or = [Tl() for _ in range(8)]
        oraw = xio[:].rearrange("p (c n) -> p c n", c=8)
        sq = [sb("sq%d" % i, [128, 512], BF16) for i in range(2)]; t_sq = [Tl(), Tl()]
        ms = sb("ms", [128, 512]); t_ms = Tl()
        rstd = sb("rstd", [128, 512]); t_rstd = Tl()
        ntmp = [sb("ntmp%d" % i, [128, 512]) for i in range(2)]; t_ntmp = [Tl(), Tl()]
        mods = sb("mods", [128, L, 72, 2]); t_mods = Tl()
        Am = sb("Am", [128, L, 3, 8, 2])
        Gm = sb("Gm", [128, L, 3, 8, 2])
        vtT = sb("vtT", [128, L, 128]); t_vtT = Tl()
        scT = sb("scT", [128, 16], BF16); t_scT = Tl()
        vtr = sb("vtr", [128, 128]); t_vtr = Tl()
        za_tok = sb("za_tok", [128, 8, 256], BF16); t_za = [Tl() for _ in range(8)]
        vg_tok = sb("vg_tok", [128, 8, 256], BF16); t_vg = [Tl() for _ in range(8)]
        v_tok = sb("v_tok", [128, 8, 512], BF16); t_v = [Tl() for _ in range(8)]
        qT = sb("qT", [128, 4, 1024], BF16); t_q = [[Tl() for _ in range(8)] for _ in range(4)]
        kT = sb("kT", [128, 4, 1024], BF16); t_k = [[Tl() for _ in range(8)] for _ in range(4)]
        uT = sb("uT", [128, 2, 1024], BF16); t_u = [[Tl() for _ in range(2)] for _ in range(2)]
        kcT = sb("kcT", [128, 4, 256], BF16); t_kc = Tl()
        kctok = sb("kctok", [128, 2, 512], BF16); t_kctok = Tl()
        vc = sb("vc", [128, 2, 512], BF16); t_vc = Tl()
        pooledT = sb("pooledT", [128, 2, 256], BF16); t_pl = [Tl(), Tl()]
        EB = sb("EB", [128, 8, 14, 64], BF16); t_EB = Tl()
        wsT = sb("wsT", [128, 4, 128], BF16); t_wsT = Tl()
        pwf = sb("pwf", [128, 2, 128]); t_pwf = Tl()
        pwb = sb("pwb", [128, 2, 128], BF16); t_pwb = Tl()
        bsb = sb("bsb", [128, 2, 128]); t_bsb = Tl()
        gtb = sb("gtb", [128, 384]); t_gtb = Tl()
        RT = sb("RT", [31, 128]); t_RT = Tl()
        RT01 = sb("RT01", [31, 2, 112]); t_RT01 = Tl()
        rden = sb("rden", [128, 256]); t_rden = Tl()
        st8 = sb("st8", [128, 16]); t_st8 = Tl()

        pid = {}
        groups = [(i * 4, 4) for i in range(5)] + [(20, 2)]

        def plan_ada(l):
            for pc in range(18):
                pid[("ada", l, pc)] = S.plan([(
                    lambda t: _v3(t, 8),
                    adaw_d[l].rearrange("(k p) n -> p k n", p=128)[:, :, pc * 512:(pc + 1) * 512])])

        def plan_ffn(g, l, j):
            for gi, (c0, nfc) in enumerate(groups):
                w = nfc * 128
                pid[("g", g, l, j, gi)] = S.plan([(
                    lambda t, w=w: _v3(t, 8)[:, :, 0:w],
                    wg_d[l, j].rearrange("(k p) n -> p k n", p=128)[:, :, c0 * 128:c0 * 128 + w])])
                pid[("u", g, l, j, gi)] = S.plan([(
                    lambda t, w=w: _v3(t, 8)[:, :, 0:w],
                    wu_d[l, j].rearrange("(k p) n -> p k n", p=128)[:, :, c0 * 128:c0 * 128 + w])])
                pid[("d", g, l, j, gi)] = S.plan([(
                    lambda t, nfc=nfc: _v3(t, 4)[:, 0:nfc, :],
                    wd_d[l, j][c0 * 128:c0 * 128 + w, :].rearrange("(f p) n -> p f n", p=128))])

        def plan_mixer(g, l):
            wv = win_d[l].rearrange("(k p) n -> p k n", p=128)
            pid[("A", g, l)] = S.plan([
                (lambda t: _v3(t, 8)[:, :, 0:256], wv[:, :, 0:256]),
                (lambda t: _v3(t, 8)[:, :, 256:512], wv[:, :, 2048:2304])])
            pid[("B", g, l)] = S.plan([(lambda t: _v3(t, 8), wv[:, :, 256:768])])
            pid[("C", g, l)] = S.plan([(lambda t: _v3(t, 8), wv[:, :, 768:1280])])
            pid[("D", g, l)] = S.plan([(lambda t: _v3(t, 8), wv[:, :, 1280:1792])])
            pid[("E", g, l)] = S.plan([(lambda t: _v3(t, 8)[:, :, 0:256], wv[:, :, 1792:2048])])
            wo = wout_d[l].rearrange("(k p) n -> p k n", p=128)
            pid[("F", g, l)] = S.plan([(lambda t: _v3(t, 8), wo[:, :, 0:512])])
            pid[("G", g, l)] = S.plan([(lambda t: _v3(t, 8), wo[:, :, 512:1024])])

        for l in range(L):
            plan_ada(l)
        for g in range(2):
            for l in range(L):
                plan_ffn(g, l, 0)
                plan_mixer(g, l)
                plan_ffn(g, l, 1)

        P.dma("sp", "v0", vtr[:], cvec_d, writes=[t_vtr])
        P.tr(bank[6][:, 0:128], vtr[:], ident[:], [t_vtr, t_ident], [t_bank[6]])
        P.actv(scT[:], bank[6][:, 0:16], AF.Silu, [t_bank[6]], [t_scT])
        for l in range(L):
            P.dma("sp", "v0", vtr[:], vt_d[l], writes=[t_vtr])
            P.tr(bank[6][:, 0:128], vtr[:], ident[:], [t_vtr, t_ident], [t_bank[6]])
            P.cp("dve", vtT[:, l, :], bank[6][:, 0:128], [t_bank[6]], [t_vtT])
        for l in range(L):
            for pc in range(18):
                p_ = pid[("ada", l, pc)]
                s = S.need(p_)
                wv = _v3(wslot[s], 8)
                for fc in range(4):
                    j = pc * 4 + fc
                    for k in range(8):
                        P.mm(bank[5][:, 2 * j:2 * j + 2], wv[:, k, fc * 128:(fc + 1) * 128],
                             scT[:, k:16:8], k == 0, k == 7, [t_slot[s], t_scT], [t_bank[5]])
                S.done(p_)
            P.tt("dve", mods[:, l, :, :], bank[5][:, 0:144].rearrange("p (j v) -> p j v", v=2),
                 vtT[:, l, 0:72].rearrange("p (j o) -> p j o", o=1).to_broadcast([128, 72, 2]),
                 ALU.add, [t_bank[5], t_vtT], [t_mods])
            for n in range(3):
                P.ts("dve", Am[:, l, n, :, :], mods[:, l, (3 * n + 1) * 8:(3 * n + 2) * 8, :], 1.0, None,
                     ALU.add, None, [t_mods], [t_mods])
                P.tt("dve", Am[:, l, n, :, :], Am[:, l, n, :, :],
                     vtT[:, l, 72 + n * 8:80 + n * 8].rearrange("p (j o) -> p j o", o=1).to_broadcast([128, 8, 2]),
                     ALU.mult, [t_mods, t_vtT], [t_mods])
                P.ts("dve", Gm[:, l, n, :, :], mods[:, l, (3 * n + 2) * 8:(3 * n + 3) * 8, :],
                     (1.0 if n == 1 else 0.5), None, ALU.mult, None, [t_mods], [t_mods])

        def norm_mod(l, n, v):
            for tt in range(2):
                cols = slice(tt * 512, (tt + 1) * 512)
                for k in range(8):
                    P.actv(sq[k % 2][:], xT[:, k, cols], AF.Square, [t_x[k][tt]], [t_sq[k % 2]])
                    P.mm(bank[6][:], ones[:], sq[k % 2][:], k == 0, k == 7, [t_sq[k % 2], t_ones], [t_bank[6]])
                P.actv(ms[:], bank[6][:], AF.Sqrt, [t_bank[6], t_chalf], [t_ms], scale=1.0 / D, bias=epsc[:, 0:1])
                P.add("dve", lambda e: e.reciprocal(rstd[:], ms[:]), [t_ms], [t_rstd])
                for k in range(8):
                    P.tt("dve", ntmp[k % 2][:], xT[:, k, cols], rstd[:], ALU.mult,
                         [t_x[k][tt], t_rstd], [t_ntmp[k % 2]])
                    P.actv(hT[:, k, cols], ntmp[k % 2][:], AF.Identity, [t_ntmp[k % 2], t_mods], [t_h[k][tt]],
                           scale=Am[:, l, n, k, v:v + 1], bias=mods[:, l, 3 * n * 8 + k, v:v + 1])

        cnt = {"up": 0, "dn": 0, "ab": 0}

        def ffn(g, l, j):
            n = 0 if j == 0 else 2
            v = g
            norm_mod(l, n, v)
            pending = []

            def flush():
                while pending:
                    pending.pop(0)()

            for gi, (c0, nfc) in enumerate(groups):
                pg_, pu_, pd_ = pid[("g", g, l, j, gi)], pid[("u", g, l, j, gi)], pid[("d", g, l, j, gi)]
                sg, su, sd = S.need(pg_), S.need(pu_), S.need(pd_)
                vg_, vu_, vd_ = _v3(wslot[sg], 8), _v3(wslot[su], 8), _v3(wslot[sd], 4)
                for tt in range(2):
                    cols = slice(tt * 512, (tt + 1) * 512)
                    ab = cnt["ab"] % 2
                    cnt["ab"] += 1
                    for f in range(nfc):
                        ib = cnt["up"] % 2
                        cnt["up"] += 1
                        for k in range(8):
                            P.mm(bank[ib][:], vg_[:, k, f * 128:(f + 1) * 128], hT[:, k, cols], k == 0, k == 7,
                                 [t_slot[sg], t_h[k][tt]], [t_bank[ib]])
                        for k in range(8):
                            P.mm(bank[2 + ib][:], vu_[:, k, f * 128:(f + 1) * 128], hT[:, k, cols], k == 0, k == 7,
                                 [t_slot[su], t_h[k][tt]], [t_bank[2 + ib]])
                        P.actv(stmp[ib][:], bank[ib][:], AF.Silu, [t_bank[ib]], [t_stmp[ib]])
                        P.tt("dve", aT[ab][:, f, :], stmp[ib][:], bank[2 + ib][:], ALU.mult,
                             [t_stmp[ib], t_bank[2 + ib]], [t_aT[ab][f]])

                    def down(ab=ab, nfc=nfc, sd=sd, vd_=vd_, cols=cols, tt=tt):
                        for dc in range(8):
                            ib2 = 4 + cnt["dn"] % 2
                            cnt["dn"] += 1
                            for f in range(nfc):
                                P.mm(bank[ib2][:], vd_[:, f, dc * 128:(dc + 1) * 128], aT[ab][:, f, :],
                                     f == 0, f == nfc - 1, [t_slot[sd], t_aT[ab][f]], [t_bank[ib2]])
                            P.stt(xT[:, dc, cols], bank[ib2][:], Gm[:, l, n, dc, v:v + 1], xT[:, dc, cols],
                                  ALU.mult, ALU.add, [t_bank[ib2], t_x[dc][tt], t_mods], [t_x[dc][tt]])

                    flush()
                    pending.append(down)
                S.done(pg_)
                S.done(pu_)
                if gi > 0:
                    S.done(pid[("d", g, l, j, gi - 1)])
            flush()
            S.done(pid[("d", g, l, j, len(groups) - 1)])

        def load_x(g):
            for i in range(8):
                xb = i % 2
                xs_ = xio[:, xb * 1024:(xb + 1) * 1024]
                tl_ = t_or[xb * 4:xb * 4 + 4]
                P.dma("sp", ("xio", xb), xs_, x_d[g, i * 128:(i + 1) * 128, :], writes=tl_)
                for hb in range(2):
                    for kk in range(4):
                        k = hb * 4 + kk
                        P.tr(bank[hb][:, kk * 128:(kk + 1) * 128], xs_[:, k * 128:(k + 1) * 128], ident[:],
                             tl_ + [t_ident], [t_bank[hb]])
                    P.cp("dve" if hb == 0 else "act", xT[:, hb * 4:hb * 4 + 4, i * 128:(i + 1) * 128],
                         bank[hb][:].rearrange("p (k n) -> p k n", k=4), [t_bank[hb]],
                         [t_x[k][i // 4] for k in range(hb * 4, hb * 4 + 4)])

        def store_y(g):
            for i in range(8):
                xb = i % 2
                xs_ = xio[:, xb * 1024:(xb + 1) * 1024]
                tl_ = t_or[xb * 4:xb * 4 + 4]
                for hb in range(2):
                    for kk in range(4):
                        k = hb * 4 + kk
                        P.tr(bank[hb][:, kk * 128:(kk + 1) * 128], xT[:, k, i * 128:(i + 1) * 128], ident[:],
                             [t_x[k][i // 4], t_ident], [t_bank[hb]])
                    P.cp("dve" if hb == 0 else "act", xs_[:, hb * 512:(hb + 1) * 512], bank[hb][:],
                         [t_bank[hb]], tl_)
                P.dma("sp", ("y", xb), y_d[g, i * 128:(i + 1) * 128, :], xs_, reads=tl_, is_out=True)

        def grp_rms(src, t_src, nh):
            n = nh * 64
            s3 = src.rearrange("p (h d) -> p h d", d=64)
            sqs, t_sqs = sq[0], t_sq[0]
            P.tt("dve", sqs[:, 0:n], src, src, ALU.mult, [t_src], [t_sqs])
            P.red(st8[:, 0:nh], sqs[:, 0:n].rearrange("p (h d) -> p h d", d=64), [t_sqs], [t_st8])
            P.actv(st8[:, 0:nh], st8[:, 0:nh], AF.Sqrt, [t_st8, t_chalf], [t_st8], scale=1.0 / 64, bias=epsc[:, 0:1])
            P.add("dve", lambda e: e.reciprocal(st8[:, 8:8 + nh], st8[:, 0:nh]), [t_st8], [t_st8])
            P.tt("dve", s3, s3, st8[:, 8:8 + nh].rearrange("p (h o) -> p h o", o=1).to_broadcast([128, nh, 64]),
                 ALU.mult, [t_src, t_st8], [t_src])

        def mixer_setup(g, l):
            P.dma("sp", "ms0", gtb[:], gt_d[l:l + 1, :].to_broadcast([128, 384]), writes=[t_gtb])
            P.ts("dve", gtb[:, 0:64], gtb[:, 0:64], 0.125, None, ALU.mult, None, [t_gtb], [t_gtb])
            P.memset("pool", pwf[:], 0.0, [t_pwf])
            for c in range(2):
                for gh in range(2):
                    P.dma("sp", "ms1", pwf[gh * 64:(gh + 1) * 64, c, gh * 64:(gh + 1) * 64], poolw_d[l, 2 * c + gh],
                          writes=[t_pwf])
            P.cp("dve", pwb[:], pwf[:], [t_pwf], [t_pwb])
            for g4 in range(4):
                P.dma("sp", "ms2", stmp[0][:, 0:128], sgw_d[l, g4], writes=[t_stmp[0]])
                P.tr(bank[6][:, 0:128], stmp[0][:, 0:128], ident[:], [t_stmp[0], t_ident], [t_bank[6]])
                P.cp("dve", wsT[:, g4, :], bank[6][:, 0:128], [t_bank[6]], [t_wsT])
            for g4 in range(4):
                P.dma("sp", "ms3", bsb[(g4 % 2) * 64:(g4 % 2 + 1) * 64, g4 // 2, :],
                      sgb_d[l, g4:g4 + 1, :].to_broadcast([64, 128]), writes=[t_bsb])
            if g == 1:
                for t in range(2):
                    P.dma("pool", "ms4", kctok[:, t, :].rearrange("p (h d) -> p h d", h=8),
                          ck_d[l][:, t * 128:(t + 1) * 128, :].rearrange("h p d -> p h d"), writes=[t_kctok])
                    P.dma("pool", "ms5", vc[:, t, :].rearrange("p (h d) -> p h d", h=8),
                          cv_d[l][:, t * 128:(t + 1) * 128, :].rearrange("h p d -> p h d"), writes=[t_vc])
                for t in range(2):
                    for c in range(4):
                        P.tr(bankb[:, c * 128:(c + 1) * 128], kctok[:, t, c * 128:(c + 1) * 128], identb[:],
                             [t_kctok, t_identb], [t_bankb])
                    P.cp("dve", kcT[:, :, t * 128:(t + 1) * 128],
                         bankb[:, 0:512].rearrange("p (c n) -> p c n", c=4), [t_bankb], [t_kc])
                P.dma("sp", "ms6", stmp[1][:, 0:31], rpb_d[l], writes=[t_stmp[1]])
                P.tr(bank[6][0:31, 0:128], stmp[1][:, 0:31], ident[:], [t_stmp[1], t_ident], [t_bank[6]])
                P.cp("dve", RT[:], bank[6][0:31, 0:128], [t_bank[6]], [t_RT])
                rt3 = RT[:, 0:120].rearrange("p (h r) -> p h r", h=8)
                for half in range(2):
                    P.cp("dve", RT01[:, half, :].rearrange("p (h r) -> p h r", h=8), rt3[:, :, half:half + 14],
                         [t_RT], [t_RT01])
                for b in range(16):
                    bk = bank[b % 2]
                    for qi in range(4):
                        qc = b * 4 + qi
                        for half in range(2):
                            P.mm(bk[half * 64:(half + 1) * 64, qi * 112:(qi + 1) * 112],
                                 dw[0:31, 63 - qc:127 - qc], RT01[:, half, :], True, True,
                                 [t_dw, t_RT01], [t_bank[b % 2]], tp=(0, half * 64))
                    P.actv(EB[:, :, :, b * 4:(b + 1) * 4].rearrange("p h r q -> p q (h r)"),
                           bk[:, 0:448].rearrange("p (q n) -> p q n", q=4), AF.Exp, [t_bank[b % 2]], [t_EB])
                ebv = EB[:].rearrange("p h r q -> p (h r) q")
                P.tt("dve", ebv, ebv, colok[:, :].rearrange("p (o q) -> p o q", o=1).to_broadcast([128, 112, 64]),
                     ALU.mult, [t_EB, t_colok], [t_EB])

        def mixer(g, l):
            v = g
            is_p = (g == 0)
            mixer_setup(g, l)
            norm_mod(l, 1, v)
            pA, pB, pC, pD, pE, pF, pG = [pid[(nm, g, l)] for nm in "ABCDEFG"]
            sA, sB, sC, sD, sE = [S.need(p_) for p_ in (pA, pB, pC, pD, pE)]
            vA, vB, vC, vD, vE = [_v3(wslot[s_], 8) for s_ in (sA, sB, sC, sD, sE)]
            qf, kf, vf, gv = stmp[0], stmp[1], ntmp[0], ntmp[1]
            t_qf, t_kf, t_vf, t_gv = t_stmp[0], t_stmp[1], t_ntmp[0], t_ntmp[1]
            qb, t_qb = sq[1], t_sq[1]
            kb, t_kb = aT[0][:, 0, :], t_aT[0][0]
            h3 = lambda ap: ap.rearrange("p (h d) -> p h d", d=64)
            for i in range(8):
                tt = i // 4
                tok = slice(i * 128, (i + 1) * 128)
                for bi, (vw, sl) in enumerate(((vA, sA), (vB, sB), (vC, sC), (vD, sD))):
                    for k in range(8):
                        P.mm(bank[bi][:], hT[:, k, tok], vw[:, k, :], k == 0, k == 7,
                             [t_h[k][tt], t_slot[sl]], [t_bank[bi]])
                P.cp("act", za_tok[:, i, :], bank[0][:, 0:256], [t_bank[0]], [t_za[i]])
                P.actv(gv[:, 0:256], bank[0][:, 256:512], AF.Gelu_apprx_tanh, [t_bank[0]], [t_gv])
                grp_rms(gv[:, 0:256], t_gv, 4)
                P.tt("pool", vg_tok[:, i, :], gv[:, 0:256], gtb[:, 128:384], ALU.mult, [t_gv, t_gtb], [t_vg[i]])
                P.cp("act", qf[:], bank[1][:], [t_bank[1]], [t_qf])
                grp_rms(qf[:], t_qf, 8)
                P.tt("pool", h3(qb[:]), h3(qf[:]),
                     gtb[:, 0:64].rearrange("p (o d) -> p o d", o=1).to_broadcast([128, 8, 64]),
                     ALU.mult, [t_qf, t_gtb], [t_qb])
                for c in range(4):
                    P.tr(bankb[:, c * 128:(c + 1) * 128], qb[:, c * 128:(c + 1) * 128], identb[:],
                         [t_qb, t_identb], [t_bankb])
                P.cp("dve", qT[:, :, tok], bankb[:, 0:512].rearrange("p (c n) -> p c n", c=4), [t_bankb],
                     [t_q[c][i] for c in range(4)])
                P.cp("act", kf[:], bank[2][:], [t_bank[2]], [t_kf])
                grp_rms(kf[:], t_kf, 8)
                P.tt("pool", h3(kf[:]), h3(kf[:]),
                     gtb[:, 64:128].rearrange("p (o d) -> p o d", o=1).to_broadcast([128, 8, 64]),
                     ALU.mult, [t_kf, t_gtb], [t_kf])
                if is_p:
                    b_, s0_ = i // 2, (i % 2) * 128
                    P.dma("sp", "nk", nk_d[b_, l].rearrange("h s d -> s h d")[s0_:s0_ + 128], h3(kf[:]),
                          reads=[t_kf], is_out=True)
                P.cp("act", kb, kf[:], [t_kf], [t_kb])
                for c in range(4):
                    P.tr(bankb[:, 512 + c * 128:512 + (c + 1) * 128], kb[:, c * 128:(c + 1) * 128], identb[:],
                         [t_kb, t_identb], [t_bankb])
                P.cp("dve", kT[:, :, tok], bankb[:, 512:1024].rearrange("p (c n) -> p c n", c=4), [t_bankb],
                     [t_k[c][i] for c in range(4)])
                P.cp("act", v_tok[:, i, :], bank[3][:], [t_bank[3]], [t_v[i]])
                if is_p:
                    P.cp("dve", vf[:], bank[3][:], [t_bank[3]], [t_vf])
                    P.dma("sp", "nv", nv_d[b_, l].rearrange("h s d -> s h d")[s0_:s0_ + 128], h3(vf[:]),
                          reads=[t_vf], is_out=True)
            for tt in range(2):
                cols = slice(tt * 512, (tt + 1) * 512)
                for c in range(2):
                    for k in range(8):
                        P.mm(bank[4 + c][:], vE[:, k, c * 128:(c + 1) * 128], hT[:, k, cols], k == 0, k == 7,
                             [t_slot[sE], t_h[k][tt]], [t_bank[4 + c]])
                    P.actv(uT[:, c, cols], bank[4 + c][:], AF.Gelu_apprx_tanh, [t_bank[4 + c]], [t_u[c][tt]])
            for p_ in (pA, pB, pC, pD, pE):
                S.done(p_)
            sF, sG = S.need(pF), S.need(pG)

            def variant(j):
                if is_p:
                    return 0 if j % 2 == 0 else 2
                return 0 if j == 0 else (2 if j == 7 else 1)

            def seq_range(j):
                if is_p:
                    return (j // 2 * 2, j // 2 * 2 + 1)
                return (0, 7)

            sqo, t_sqo = aT[0][:, 3, :], t_aT[0][3]
            rs_ap = (rstd[:, 0:256], rstd[:, 256:512], ms[:, 0:256])
            rs_t = (t_rstd, t_rstd, t_ms)
            oTn = aT[1][:].rearrange("p f n -> p (f n)").rearrange("p (c n) -> p c n", c=8)
            for u in range(4):
                tok0 = u * 256
                ucols = slice(tok0, tok0 + 256)
                tt = u // 2
                if is_p:
                    for c in range(4):
                        bo, t_bo = bank[2 + c % 2], t_bank[2 + c % 2]
                        for hp in range(2):
                            h = 2 * c + hp
                            bk = bank[hp]
                            ps_ = slice(hp * 64, (hp + 1) * 64)
                            pt, t_pt = aT[0][:, 1 + hp, :], t_aT[0][1 + hp]
                            for kt in range(2):
                                P.mm(bk[:, kt * 256:(kt + 1) * 256],
                                     kT[ps_, c, tok0 + kt * 128:tok0 + (kt + 1) * 128], qT[ps_, c, ucols],
                                     True, True, [t_k[c][2 * u + kt], t_q[c][2 * u], t_q[c][2 * u + 1]],
                                     [t_bank[hp]])
                            P.actv(pt, bk[:], AF.Exp, [t_bank[hp]], [t_pt])
                            for kt in range(2):
                                P.mm(bo[ps_, 0:256], v_tok[:, 2 * u + kt, h * 64:(h + 1) * 64],
                                     pt[:, kt * 256:(kt + 1) * 256], kt == 0, kt == 1,
                                     [t_v[2 * u + kt], t_pt], [t_bo], tp=(0, hp * 64))
                            for kt in range(2):
                                P.mm(bo[ps_, 256:512], ones[:, 0:64], pt[:, kt * 256:(kt + 1) * 256],
                                     kt == 0, kt == 1, [t_ones, t_pt], [t_bo], tp=(0, hp * 64))
                        P.add("dve", lambda e, bo=bo: e.reciprocal(rden[:], bo[:, 256:512]), [t_bo], [t_rden])
                        P.tt("dve", oraw[:, 2 + c, :], bo[:, 0:256], rden[:], ALU.mult, [t_bo, t_rden],
                             [t_or[2 + c]])
                else:
                    for c in range(4):
                        for r in range(4 * u, 4 * u + 4):
                            s0 = min(max(r - 4, 0), 8)
                            kts = list(range(s0 // 2, (s0 + 7) // 2 + 1))
                            nl = len(kts)
                            bo, t_bo = bank[2 + r % 2], t_bank[2 + r % 2]
                            qcols = slice(r * 64, (r + 1) * 64)
                            for hp in range(2):
                                h = 2 * c + hp
                                ps_ = slice(hp * 64, (hp + 1) * 64)
                                bk = bank[hp]
                                et, t_et = (ms, rstd)[hp], (t_ms, t_rstd)[hp]
                                pt, t_pt = aT[0][:, 1 + hp, :], t_aT[0][1 + hp]
                                for j_, kt in enumerate(kts):
                                    P.mm(bk[:, j_ * 64:(j_ + 1) * 64], kT[ps_, c, kt * 128:(kt + 1) * 128],
                                         qT[ps_, c, qcols], True, True, [t_k[c][kt], t_q[c][r // 2]],
                                         [t_bank[hp]])
                                for t in range(2):
                                    P.mm(bk[:, (nl + t) * 64:(nl + t + 1) * 64], kcT[ps_, c, t * 128:(t + 1) * 128],
                                         qT[ps_, c, qcols], True, True, [t_kc, t_q[c][r // 2]], [t_bank[hp]])
                                P.actv(et[:, 0:nl * 64], bk[:, 0:nl * 64], AF.Exp, [t_bank[hp]], [t_et])
                                P.actv(pt[:, nl * 64:(nl + 2) * 64], bk[:, nl * 64:(nl + 2) * 64], AF.Exp,
                                       [t_bank[hp]], [t_pt])
                                i0 = 2 * kts[0] - r + 7
                                P.tt("dve", pt[:, 0:nl * 64].rearrange("p (j q) -> p j q", q=64),
                                     et[:, 0:nl * 64].rearrange("p (j q) -> p j q", q=64),
                                     EB[:, h, i0:i0 + 2 * nl - 1:2, :], ALU.mult, [t_et, t_EB], [t_pt])
                                mml = []
                                for j_, kt in enumerate(kts):
                                    if s0 % 2 == 1 and j_ == 0:
                                        p0, p1 = 64, 128
                                    elif s0 % 2 == 1 and j_ == nl - 1:
                                        p0, p1 = 0, 64
                                    else:
                                        p0, p1 = 0, 128
                                    mml.append((v_tok[p0:p1, kt, h * 64:(h + 1) * 64], ones[p0:p1, 0:64],
                                                pt[p0:p1, j_ * 64:(j_ + 1) * 64], p0, [t_v[kt]]))
                                for t in range(2):
                                    mml.append((vc[:, t, h * 64:(h + 1) * 64], ones[:, 0:64],
                                                pt[:, (nl + t) * 64:(nl + t + 1) * 64], 0, [t_vc]))
                                for n_, (lv, lo_, rh, p0, rd) in enumerate(mml):
                                    P.mm(bo[ps_, 0:64], lv, rh, n_ == 0, n_ == len(mml) - 1, rd + [t_pt], [t_bo],
                                         tp=(p0, hp * 64))
                                for n_, (lv, lo_, rh, p0, rd) in enumerate(mml):
                                    P.mm(bo[ps_, 64:128], lo_, rh, n_ == 0, n_ == len(mml) - 1, [t_ones, t_pt],
                                         [t_bo], tp=(p0, hp * 64))
                            P.add("dve", lambda e, bo=bo: e.reciprocal(rden[:, 0:64], bo[:, 64:128]), [t_bo],
                                  [t_rden])
                            P.tt("dve", oraw[:, 2 + c, (r % 4) * 64:(r % 4 + 1) * 64], bo[:, 0:64], rden[:, 0:64],
                                 ALU.mult, [t_bo, t_rden], [t_or[2 + c]])
                for c in range(2):
                    for gh in range(2):
                        g4 = 2 * c + gh
                        for jj in range(2):
                            j = 2 * u + jj
                            lo, hi = seq_range(j)
                            ins = [i_ for i_ in (j - 1, j, j + 1) if lo <= i_ <= hi]
                            for n_, i_ in enumerate(ins):
                                var = 3 if i_ == j - 1 else (4 if i_ == j + 1 else variant(j))
                                P.mm(bank[4][gh * 64:(gh + 1) * 64, c * 256 + jj * 128:c * 256 + (jj + 1) * 128],
                                     za_tok[:, i_, g4 * 64:(g4 + 1) * 64], bands[:, g4 * 5 + var, :],
                                     n_ == 0, n_ == len(ins) - 1, [t_za[i_], t_bands], [t_bank[4]],
                                     tp=(0, gh * 64))
                    for jj in range(2):
                        P.tt("dve", pooledT[:, c, jj * 128:(jj + 1) * 128],
                             bank[4][:, c * 256 + jj * 128:c * 256 + (jj + 1) * 128],
                             invc[:, c * 3 + variant(2 * u + jj), :], ALU.mult, [t_bank[4], t_invc], [t_pl[c]])
                    P.mm(bank[5][:, c * 256:(c + 1) * 256], pwb[:, c, :], pooledT[:, c, :], True, True,
                         [t_pwb, t_pl[c]], [t_bank[5]])
                    P.actv(oraw[:, c, :], bank[5][:, c * 256:(c + 1) * 256], AF.Identity, [t_bank[5], t_vtT],
                           [t_or[c]], scale=vtT[:, l, 104 + c:105 + c])
                for c in range(2):
                    for gh in range(2):
                        g4 = 2 * c + gh
                        for jj in range(2):
                            P.mm(bank[4][gh * 64:(gh + 1) * 64, c * 256 + jj * 128:c * 256 + (jj + 1) * 128],
                                 vg_tok[:, 2 * u + jj, g4 * 64:(g4 + 1) * 64], wsT[:, g4, :], True, True,
                                 [t_vg[2 * u + jj], t_wsT], [t_bank[4]], tp=(0, gh * 64))
                    P.tt("dve", gv[:, 0:256].rearrange("p (j n) -> p j n", j=2),
                         bank[4][:, c * 256:(c + 1) * 256].rearrange("p (j n) -> p j n", j=2),
                         bsb[:, c:c + 1, :].to_broadcast([128, 2, 128]), ALU.add, [t_bank[4], t_bsb], [t_gv])
                    P.tt("pool", oraw[:, 6 + c, :], gv[:, 0:256], uT[:, c, ucols], ALU.mult,
                         [t_gv, t_u[c][tt]], [t_or[6 + c]])
                segs = ((0, 2, 256.0), (2, 6, 512.0), (6, 8, 256.0))
                for si, (o0, o1, wd_) in enumerate(segs):
                    stb, t_stb = (bank[6], t_bank[6]) if si < 2 else (bank[5], t_bank[5])
                    scol = slice((si % 2) * 256, (si % 2) * 256 + 256)
                    for oc in range(o0, o1):
                        P.actv(sqo[:, 0:256], oraw[:, oc, :], AF.Square, [t_or[oc]], [t_sqo])
                        P.mm(stb[:, scol], ones[:], sqo[:, 0:256], oc == o0, oc == o1 - 1, [t_ones, t_sqo],
                             [t_stb])
                    P.actv(ms[:, 256:512], stb[:, scol], AF.Sqrt, [t_stb, t_chalf], [t_ms], scale=1.0 / wd_,
                           bias=epsc[:, 0:1])
                    P.add("dve", lambda e, si=si: e.reciprocal(rs_ap[si], ms[:, 256:512]), [t_ms], [rs_t[si]])
                for oc in range(8):
                    si = 0 if oc < 2 else (1 if oc < 6 else 2)
                    P.stt(oTn[:, oc, :], oraw[:, oc, :], vtT[:, l, 96 + oc:97 + oc], rs_ap[si], ALU.mult, ALU.mult,
                          [t_or[oc], t_vtT, rs_t[si]], [t_aT[1][oc // 2]])
                for dc in range(8):
                    sl = sF if dc < 4 else sG
                    vw = _v3(wslot[sl], 8)
                    bw, t_bw = bank[2 + dc % 2], t_bank[2 + dc % 2]
                    for oc in range(8):
                        P.mm(bw[:, 0:256], vw[:, oc, (dc % 4) * 128:(dc % 4 + 1) * 128], oTn[:, oc, :],
                             oc == 0, oc == 7, [t_slot[sl], t_aT[1][oc // 2]], [t_bw])
                    P.stt(xT[:, dc, ucols], bw[:, 0:256], Gm[:, l, 1, dc, v:v + 1], xT[:, dc, ucols],
                          ALU.mult, ALU.add, [t_bw, t_x[dc][tt], t_mods], [t_x[dc][tt]])
            S.done(pF)
            S.done(pG)

        for g in range(2):
            load_x(g)
            for l in range(L):
                ffn(g, l, 0)
                mixer(g, l)
                ffn(g, l, 1)
            store_y(g)
        P.emit(st)
    return nc


def _consts():
    ident = np.eye(128, dtype=np.float32)
    bands = np.zeros((128, 20, 128), np.float32)
    invc = np.zeros((128, 6, 128), np.float32)
    for gi, w in enumerate(WINS):
        h = w // 2
        tp = np.arange(128)[:, None]
        t = np.arange(128)[None, :]
        inb = ((tp >= t - h) & (tp < t + h)).astype(np.float32)
        cnt_first = (np.minimum(t + h, 10 ** 6) - np.maximum(t - h, 0)).astype(np.float32)
        cnt_mid = np.full((1, 128), float(w), np.float32)
        cnt_last = ((128 - t) + h - np.maximum(0, 0)).astype(np.float32)
        cnt_last = np.minimum(cnt_last, w).astype(np.float32)
        eye = np.eye(128, dtype=np.float32)
        bands[:, gi * 5 + 0, :] = inb - eye * cnt_first
        bands[:, gi * 5 + 1, :] = inb - eye * cnt_mid
        bands[:, gi * 5 + 2, :] = inb - eye * cnt_last
        bands[:, gi * 5 + 3, :] = ((tp - 128) >= (t - h)).astype(np.float32)
        bands[:, gi * 5 + 4, :] = ((tp + 128) < (t + h)).astype(np.float32)
        c, half = gi // 2, gi % 2
        for vi, cn in enumerate((cnt_first, cnt_mid, cnt_last)):
            invc[half * 64:(half + 1) * 64, c * 3 + vi, :] = 1.0 / cn
    dw = np.zeros((31, 127), np.float32)
    for j in range(31):
        dw[j, j + 48] = 1.0
    cols = np.arange(64)
    cs = np.clip(cols - 8, 0, 48)
    ok = (cols[None, :] >= cs[:, None]) & (cols[None, :] < cs[:, None] + 16)
    colok = np.concatenate([ok.T, ok.T], 0).astype(np.float32)
    return ident, bands, invc, dw, colok


def make_in_maps(inp):
    f = lambda a: np.ascontiguousarray(np.asarray(a, dtype=np.float32))
    ident, bands, invc, dw, colok = _consts()
    vt = np.zeros((L, 128, 128), np.float32)
    gt = np.zeros((L, 384), np.float32)
    rpb = np.zeros((L, 128, 31), np.float32)
    for l in range(L):
        vt[l, 0:72] = f(inp["ada_b"])[l].reshape(72, 128)
        vt[l, 72:96] = f(inp["norm_g"])[l].reshape(24, 128)
        vt[l, 96:104] = f(inp["out_norm_g"])[l].reshape(8, 128)
        vt[l, 104:106] = f(inp["pool_scale"])[l].reshape(2, 128)
        gt[l, 0:64] = f(inp["q_norm_g"])[l]
        gt[l, 64:128] = f(inp["k_norm_g"])[l]
        gt[l, 128:384] = f(inp["sg_vnorm_g"])[l].reshape(256)
        rpb[l, 0:120] = f(inp["na_rpb"])[l].reshape(120, 31)
    shared = {
        "vt": vt, "gt": gt, "rpb": rpb,
        "poolw": f(inp["pool_w"]), "sgw": f(inp["sg_w"]), "sgb": f(inp["sg_b"]),
        "adaw": f(inp["ada_w"]), "wg": f(inp["ffn_w_gate"]), "wu": f(inp["ffn_w_up"]),
        "wd": f(inp["ffn_w_down"]), "win": f(inp["w_in"]), "wout": f(inp["w_out"]),
        "ident": ident, "bands": bands, "invc": invc, "dw": dw, "colok": colok,
    }
    xp = f(inp["x_prompt"]); xs = f(inp["x_sample"])
    ck = f(inp["cache_k"]); cv = f(inp["cache_v"]); c = f(inp["c"]); cc = f(inp["c_ctx"])
    maps = []
    for i in range(NCORES):
        x = np.stack([xp[4 * i:4 * i + 4].reshape(1024, D), xs[i]], 0)
        cvec = np.zeros((128, 128), np.float32)
        cvec[0:8] = cc.reshape(8, 128)
        cvec[8:16] = c[i].reshape(8, 128)
        m = dict(shared)
        m.update({"x": np.ascontiguousarray(x), "ck": np.ascontiguousarray(ck[i]),
                  "cv": np.ascontiguousarray(cv[i]), "cvec": cvec})
        maps.append(m)
    return maps


_NC_CACHE = {}


def kernel(**inputs):
    if "nc" not in _NC_CACHE:
        _NC_CACHE["nc"] = build_program()
    nc = _NC_CACHE["nc"]
    maps = make_in_maps(inputs)
    res = run_bass_kernel_spmd(nc, maps, core_ids=list(range(NCORES)))
    r = res.results
    yp = np.concatenate([r[i]["y"][0].reshape(4, 256, D) for i in range(NCORES)], 0)
    ys = np.stack([r[i]["y"][1] for i in range(NCORES)], 0)
    nk = np.concatenate([r[i]["nk"] for i in range(NCORES)], 0)
    nv = np.concatenate([r[i]["nv"] for i in range(NCORES)], 0)
    return (yp.astype(np.float32), ys.astype(np.float32), nk.astype(np.float32), nv.astype(np.float32))
```

```python
import numpy as np
from contextlib import ExitStack
import concourse.bass as bass
import concourse.mybir as mybir
from concourse.bass_utils import run_bass_kernel_spmd

F32 = mybir.dt.float32
BF16 = mybir.dt.bfloat16
AF = mybir.ActivationFunctionType
ALU = mybir.AluOpType
AX = mybir.AxisListType

D = 1024
DFF = 2816
L = 2
NH = 8
EPS = 1e-6
NCORES = 8
WINS = (2, 4, 8, 16)
EARLY_NORM = True


class Tl:
    __slots__ = ("name", "w", "r")

    def __init__(self, name=""):
        self.name = name
        self.w = None
        self.r = {}


class Op:
    __slots__ = ("eng", "fn", "deps", "marked", "count", "dkey", "dcount", "is_dma", "uid")


class Prog:
    ENGS = ("pe", "act", "dve", "pool", "sp")

    def __init__(self, nc):
        self.nc = nc
        self.ops = {e: [] for e in self.ENGS}
        self.latest_dma = {}
        self.dma_counts = {}
        self.uid = 0
        self.out_keys = set()

    def add(self, eng, fn, reads=(), writes=(), dma=None, is_out=False):
        op = Op()
        op.eng = eng
        op.fn = fn
        op.marked = False
        op.count = None
        op.is_dma = dma is not None
        op.dkey = dma
        op.uid = self.uid
        self.uid += 1
        deps = {}
        for t in reads:
            if t.w is not None:
                deps[t.w.uid] = t.w
        for t in writes:
            if t.w is not None:
                deps[t.w.uid] = t.w
            for d in t.r.values():
                deps[d.uid] = d
        final = {}
        for d in deps.values():
            if d.is_dma:
                d = self.latest_dma[d.dkey]
            elif d.eng == "pe" and eng == "pe" and not op.is_dma:
                continue
            else:
                d.marked = True
            final[d.uid] = d
        op.deps = list(final.values())
        if op.is_dma:
            c = self.dma_counts.get(dma, 0) + 16
            self.dma_counts[dma] = c
            op.dcount = c
            self.latest_dma[dma] = op
            if is_out:
                self.out_keys.add(dma)
        rkey = ("dma", op.uid) if op.is_dma else eng
        for t in reads:
            t.r[rkey] = op
        for t in writes:
            t.w = op
            t.r = {}
        self.ops[eng].append(op)
        return op

    def dma(self, eng, key, out, in_, reads=(), writes=(), is_out=False):
        return self.add(eng, lambda e: e.dma_start(out=out, in_=in_), reads, writes,
                        dma=key, is_out=is_out)

    def mm(self, out, lhsT, rhs, start, stop, reads, writes, tp=None):
        if tp is None:
            return self.add("pe", lambda e: e.matmul(out, lhsT, rhs, start=start, stop=stop), reads, writes)
        return self.add("pe", lambda e: e.matmul(out, lhsT, rhs, start=start, stop=stop, tile_position=tp),
                        reads, writes)

    def tr(self, out, in_, ident, reads, writes):
        return self.add("pe", lambda e: e.transpose(out, in_, ident), reads, writes)

    def actv(self, out, in_, func, reads, writes, scale=None, bias=None):
        kw = {}
        if scale is not None:
            kw["scale"] = scale
        if bias is not None:
            kw["bias"] = bias
        return self.add("act", lambda e: e.activation(out, in_, func, **kw), reads, writes)

    def tt(self, eng, out, in0, in1, op, reads, writes):
        return self.add(eng, lambda e: e.tensor_tensor(out, in0, in1, op=op), reads, writes)

    def ts(self, eng, out, in0, s1, s2, op0, op1, reads, writes):
        if s2 is None:
            return self.add(eng, lambda e: e.tensor_scalar(out, in0, s1, None, op0=op0), reads, writes)
        return self.add(eng, lambda e: e.tensor_scalar(out, in0, s1, s2, op0=op0, op1=op1), reads, writes)

    def stt(self, out, in0, scalar, in1, op0, op1, reads, writes):
        return self.add("dve", lambda e: e.scalar_tensor_tensor(out, in0, scalar, in1, op0=op0, op1=op1),
                        reads, writes)

    def cp(self, eng, out, in_, reads, writes):
        if eng == "act":
            return self.add("act", lambda e: e.copy(out, in_), reads, writes)
        return self.add(eng, lambda e: e.tensor_copy(out, in_), reads, writes)

    def red(self, out, in_, reads, writes):
        return self.add("dve", lambda e: e.tensor_reduce(out, in_, axis=AX.X, op=ALU.add), reads, writes)

    def memset(self, eng, ap, val, writes):
        return self.add(eng, lambda e: e.memset(ap, val), (), writes)

    def emit(self, stack):
        nc = self.nc
        for e in self.ENGS:
            c = 0
            for op in self.ops[e]:
                if op.marked and not op.is_dma:
                    c += 1
                    op.count = c
        esem = {e: stack.enter_context(nc.semaphore("s_" + e)) for e in self.ENGS}
        dsem = {k: stack.enter_context(nc.semaphore("d_%d" % i)) for i, k in enumerate(self.dma_counts)}
        block = stack.enter_context(nc.Block())
        ops = self.ops
        out_final = [(dsem[k], self.dma_counts[k]) for k in sorted(self.out_keys, key=str)]

        def run(ename, e):
            known = {}
            for op in ops[ename]:
                for d in op.deps:
                    if d.is_dma:
                        s, v = dsem[d.dkey], d.dcount
                    else:
                        s, v = esem[d.eng], d.count
                    kid = id(s)
                    if known.get(kid, 0) >= v:
                        continue
                    known[kid] = v
                    e.wait_ge(s, v)
                ins = op.fn(e)
                if op.is_dma:
                    ins.then_inc(dsem[op.dkey], 16)
                elif op.marked:
                    ins.then_inc(esem[ename], 1)
            if ename == "sp":
                for s, v in out_final:
                    e.wait_ge(s, v)

        @block.sync
        def _(e):
            run("sp", e)

        @block.tensor
        def _(e):
            run("pe", e)

        @block.scalar
        def _(e):
            run("act", e)

        @block.vector
        def _(e):
            run("dve", e)

        @block.gpsimd
        def _(e):
            run("pool", e)


class Stream:
    NS = 6
    LOOK = 5

    def __init__(self, P, slots, t_slots):
        self.P = P
        self.slots = slots
        self.t_slots = t_slots
        self.pieces = []
        self.issued = 0
        self.freed = 0
        self.want = 0

    def plan(self, parts):
        self.pieces.append(parts)
        return len(self.pieces) - 1

    def pump(self):
        while self.issued <= self.want and self.issued < len(self.pieces) and self.issued < self.freed + self.NS:
            i = self.issued
            s = i % self.NS
            for dst_fn, src in self.pieces[i]:
                self.P.dma("pool", ("w", s), dst_fn(self.slots[s]), src, writes=[self.t_slots[s]])
            self.issued += 1

    def need(self, p):
        self.want = max(self.want, p + self.LOOK)
        self.pump()
        assert self.issued > p, (self.issued, p, self.freed)
        return p % self.NS

    def done(self, p):
        self.freed = max(self.freed, p + 1)
        self.pump()


def _v3(t, k):
    return t[:].rearrange("p (k n) -> p k n", k=k)


def build_program(stop_after=None):
    nc = bass.Bass("TRN2", target_bir_lowering=False)

    def din(name, shape):
        return nc.dram_tensor(name, shape, F32, kind="ExternalInput").ap()

    def dout(name, shape):
        return nc.dram_tensor(name, shape, F32, kind="ExternalOutput").ap()

    x_d = din("x", [2, 1024, D])
    ck_d = din("ck", [L, NH, 256, 64])
    cv_d = din("cv", [L, NH, 256, 64])
    cvec_d = din("cvec", [128, 128])
    vt_d = din("vt", [L, 128, 128])
    gt_d = din("gt", [L, 384])
    poolw_d = din("poolw", [L, 4, 64, 64])
    sgw_d = din("sgw", [L, 4, 128, 128])
    sgb_d = din("sgb", [L, 4, 128])
    rpb_d = din("rpb", [L, 128, 31])
    adaw_d = din("adaw", [L, D, 9 * D])
    wg_d = din("wg", [L, 2, D, DFF])
    wu_d = din("wu", [L, 2, D, DFF])
    wd_d = din("wd", [L, 2, DFF, D])
    win_d = din("win", [L, D, 2304])
    wout_d = din("wout", [L, D, D])
    ident_d = din("ident", [128, 128])
    bands_d = din("bands", [128, 20, 128])
    invc_d = din("invc", [128, 6, 128])
    dw_d = din("dw", [31, 127])
    colok_d = din("colok", [128, 64])
    y_d = dout("y", [2, 1024, D])
    nk_d = dout("nk", [4, L, NH, 256, 64])
    nv_d = dout("nv", [4, L, NH, 256, 64])

    P = Prog(nc)
    st = ExitStack()

    def sb(name, shape, dt=F32):
        return st.enter_context(nc.sbuf_tensor("s_" + name, shape, dt))

    with st:
        bank = [st.enter_context(nc.psum_tensor("bank%d" % i, [128, 512], F32)) for i in range(7)]
        bankb = st.enter_context(nc.psum_tensor("bankb", [128, 1024], BF16))
        t_bank = [Tl("bank%d" % i) for i in range(7)]
        t_bankb = Tl("bankb")

        ident = sb("ident", [128, 128]); t_ident = Tl()
        identb = sb("identb", [128, 128], BF16); t_identb = Tl()
        ones = sb("ones", [128, 128], BF16); t_ones = Tl()
        epsc = sb("epsc", [128, 1]); t_chalf = Tl()
        bands = sb("bands", [128, 20, 128], BF16); t_bands = Tl()
        invc = sb("invc", [128, 6, 128]); t_invc = Tl()
        dw = sb("dw", [31, 127]); t_dw = Tl()
        colok = sb("colok", [128, 64]); t_colok = Tl()
        P.dma("sp", "c0", ident[:], ident_d, writes=[t_ident])
        P.dma("sp", "c1", invc[:], invc_d, writes=[t_invc])
        P.dma("sp", "c2", dw[:], dw_d, writes=[t_dw])
        P.dma("sp", "c3", colok[:], colok_d, writes=[t_colok])
        P.dma("pool", "c4", bands[:], bands_d, writes=[t_bands])
        P.cp("dve", identb[:], ident[:], [t_ident], [t_identb])
        P.memset("pool", ones[:], 1.0, [t_ones])
        P.memset("pool", epsc[:], EPS, [t_chalf])

        xT = sb("xT", [128, 8, 1024]); t_x = [[Tl() for _ in range(2)] for _ in range(8)]
        hT = sb("hT", [128, 8, 1024], BF16); t_h = [[Tl() for _ in range(2)] for _ in range(8)]
        wslot = [sb("wslot%d" % i, [128, 4096], BF16) for i in range(Stream.NS)]
        t_slot = [Tl("slot%d" % i) for i in range(Stream.NS)]
        S = Stream(P, wslot, t_slot)
        aT = [sb("aT%d" % i, [128, 4, 512], BF16) for i in range(2)]
        t_aT = [[Tl() for _ in range(4)] for _ in range(2)]
        stmp = [sb("stmp%d" % i, [128, 512]) for i in range(2)]; t_stmp = [Tl(), Tl()]
        xio = sb("xio", [128, 2048]); t_or = [Tl() for _ in range(8)]
        oraw = xio[:].rearrange("p (c n) -> p c n", c=8)
        sq = [sb("sq%d" % i, [128, 512], BF16) for i in range(2)]; t_sq = [Tl(), Tl()]
        ms = sb("ms", [128, 512]); t_ms = Tl()
        rstd, t_rstd = ms, t_ms
        ntmp = [sb("ntmp%d" % i, [128, 512]) for i in range(2)]; t_ntmp = [Tl(), Tl()]
        mods = sb("mods", [128, L, 72, 2]); t_mods = Tl()
        Am = sb("Am", [128, L, 3, 8, 2])
        Gm = sb("Gm", [128, L, 3, 8, 2])
        vtT = sb("vtT", [128, L, 128]); t_vtT = Tl()
        scT = sb("scT", [128, 16], BF16); t_scT = Tl()
        vtr_t = sb("vtr", [128, 128]); t_vtr = Tl()
        vtr = vtr_t[:]
        za_tok = sb("za_tok", [128, 8, 256], BF16); t_za = [Tl() for _ in range(8)]
        vg_tok = sb("vg_tok", [128, 8, 256], BF16); t_vg = [Tl() for _ in range(8)]
        v_tok = sb("v_tok", [128, 8, 512], BF16); t_v = [Tl() for _ in range(8)]
        qT = sb("qT", [128, 4, 1024], BF16); t_q = [[Tl() for _ in range(8)] for _ in range(4)]
        kT = sb("kT", [128, 4, 1024], BF16); t_k = [[Tl() for _ in range(8)] for _ in range(4)]
        uT = sb("uT", [128, 2, 1024], BF16); t_u = [[Tl() for _ in range(2)] for _ in range(2)]
        kcT = sb("kcT", [128, 4, 256], BF16); t_kc = Tl()
        qkz = sb("qkz", [128, 1280]); t_qkz = Tl()
        vc = sb("vc", [128, 2, 512], BF16); t_vc = Tl()
        pooledT = sb("pooledT", [128, 2, 256], BF16); t_pl = [Tl(), Tl()]
        EB = sb("EB", [128, 8, 14, 64], BF16); t_EB = Tl()
        wsT = sb("wsT", [128, 4, 128], BF16); t_wsT = Tl()
        pwb = sb("pwb", [128, 2, 128], BF16); t_pwb = Tl()
        bsb = sb("bsb", [128, 2, 128]); t_bsb = Tl()
        gtb = sb("gtb", [128, 384]); t_gtb = Tl()
        rden = sb("rden", [128, 256]); t_rden = Tl()
        st8 = sb("st8", [128, 40]); t_st8 = Tl()

        pid = {}
        groups = [(i * 4, 4) for i in range(5)] + [(20, 2)]

        def plan_ada(l):
            for pc in range(18):
                pid[("ada", l, pc)] = S.plan([(
                    lambda t: _v3(t, 8),
                    adaw_d[l].rearrange("(k p) n -> p k n", p=128)[:, :, pc * 512:(pc + 1) * 512])])

        def plan_ffn(g, l, j):
            for gi, (c0, nfc) in enumerate(groups):
                w = nfc * 128
                pid[("g", g, l, j, gi)] = S.plan([(
                    lambda t, w=w: _v3(t, 8)[:, :, 0:w],
                    wg_d[l, j].rearrange("(k p) n -> p k n", p=128)[:, :, c0 * 128:c0 * 128 + w])])
                pid[("u", g, l, j, gi)] = S.plan([(
                    lambda t, w=w: _v3(t, 8)[:, :, 0:w],
                    wu_d[l, j].rearrange("(k p) n -> p k n", p=128)[:, :, c0 * 128:c0 * 128 + w])])
                pid[("d", g, l, j, gi)] = S.plan([(
                    lambda t, nfc=nfc: _v3(t, 4)[:, 0:nfc, :],
                    wd_d[l, j][c0 * 128:c0 * 128 + w, :].rearrange("(f p) n -> p f n", p=128))])

        def plan_mixer(g, l):
            wv = win_d[l].rearrange("(k p) n -> p k n", p=128)
            pid[("A", g, l)] = S.plan([
                (lambda t: _v3(t, 8)[:, :, 0:256], wv[:, :, 0:256]),
                (lambda t: _v3(t, 8)[:, :, 256:512], wv[:, :, 2048:2304])])
            pid[("B", g, l)] = S.plan([(lambda t: _v3(t, 8), wv[:, :, 256:768])])
            pid[("C", g, l)] = S.plan([(lambda t: _v3(t, 8), wv[:, :, 768:1280])])
            pid[("D", g, l)] = S.plan([(lambda t: _v3(t, 8), wv[:, :, 1280:1792])])
            pid[("E", g, l)] = S.plan([(lambda t: _v3(t, 8)[:, :, 0:256], wv[:, :, 1792:2048])])
            wo = wout_d[l].rearrange("(k p) n -> p k n", p=128)
            pid[("F", g, l)] = S.plan([(lambda t: _v3(t, 8), wo[:, :, 0:512])])
            pid[("G", g, l)] = S.plan([(lambda t: _v3(t, 8), wo[:, :, 512:1024])])

        for l in range(L):
            plan_ada(l)
        for g in range(2):
            for l in range(L):
                plan_ffn(g, l, 0)
                plan_mixer(g, l)
                plan_ffn(g, l, 1)

        P.dma("sp", "v0", vtr, cvec_d, writes=[t_vtr])
        P.tr(bank[6][:, 0:128], vtr, ident[:], [t_vtr, t_ident], [t_bank[6]])
        P.actv(scT[:], bank[6][:, 0:16], AF.Silu, [t_bank[6]], [t_scT])
        for l in range(L):
            P.dma("sp", "v0", vtr, vt_d[l], writes=[t_vtr])
            P.tr(bank[6][:, 0:128], vtr, ident[:], [t_vtr, t_ident], [t_bank[6]])
            P.cp("dve", vtT[:, l, :], bank[6][:, 0:128], [t_bank[6]], [t_vtT])
        for l in range(L):
            for pc in range(18):
                p_ = pid[("ada", l, pc)]
                s = S.need(p_)
                wv = _v3(wslot[s], 8)
                for fc in range(4):
                    j = pc * 4 + fc
                    for k in range(8):
                        P.mm(bank[5][:, 2 * j:2 * j + 2], wv[:, k, fc * 128:(fc + 1) * 128],
                             scT[:, k:16:8], k == 0, k == 7, [t_slot[s], t_scT], [t_bank[5]])
                S.done(p_)
            P.tt("dve", mods[:, l, :, :], bank[5][:, 0:144].rearrange("p (j v) -> p j v", v=2),
                 vtT[:, l, 0:72].rearrange("p (j o) -> p j o", o=1).to_broadcast([128, 72, 2]),
                 ALU.add, [t_bank[5], t_vtT], [t_mods])
            for n in range(3):
                P.ts("dve", Am[:, l, n, :, :], mods[:, l, (3 * n + 1) * 8:(3 * n + 2) * 8, :], 1.0, None,
                     ALU.add, None, [t_mods], [t_mods])
                P.tt("dve", Am[:, l, n, :, :], Am[:, l, n, :, :],
                     vtT[:, l, 72 + n * 8:80 + n * 8].rearrange("p (j o) -> p j o", o=1).to_broadcast([128, 8, 2]),
                     ALU.mult, [t_mods, t_vtT], [t_mods])
                P.ts("dve", Gm[:, l, n, :, :], mods[:, l, (3 * n + 2) * 8:(3 * n + 3) * 8, :],
                     (1.0 if n == 1 else 0.5), None, ALU.mult, None, [t_mods], [t_mods])

        PH = []
        _mark = lambda nm: PH.append((nm, len(P.ops["pe"])))
        def norm_tt(l, n, v, tt):
            cols = slice(tt * 512, (tt + 1) * 512)
            for k in range(8):
                P.actv(sq[k % 2][:], xT[:, k, cols], AF.Square, [t_x[k][tt]], [t_sq[k % 2]])
                P.mm(bank[6][:], ones[:], sq[k % 2][:], k == 0, k == 7, [t_sq[k % 2], t_ones], [t_bank[6]])
            P.actv(ms[:], bank[6][:], AF.Sqrt, [t_bank[6], t_chalf], [t_ms], scale=1.0 / D, bias=epsc[:, 0:1])
            P.add("dve", lambda e: e.reciprocal(rstd[:], ms[:]), [t_ms], [t_rstd])
            for k in range(8):
                P.tt("dve", ntmp[k % 2][:], xT[:, k, cols], rstd[:], ALU.mult,
                     [t_x[k][tt], t_rstd], [t_ntmp[k % 2]])
                P.actv(hT[:, k, cols], ntmp[k % 2][:], AF.Identity, [t_ntmp[k % 2], t_mods], [t_h[k][tt]],
                       scale=Am[:, l, n, k, v:v + 1], bias=mods[:, l, 3 * n * 8 + k, v:v + 1])

        cnt = {"up": 0, "dn": 0, "ab": 0}

        def ffn(g, l, j, after_tt):
            n = 0 if j == 0 else 2
            v = g
            pending = []
            last_gi = len(groups) - 1

            def flush():
                while pending:
                    pending.pop(0)()

            for gi, (c0, nfc) in enumerate(groups):
                pg_, pu_, pd_ = pid[("g", g, l, j, gi)], pid[("u", g, l, j, gi)], pid[("d", g, l, j, gi)]
                sg, su, sd = S.need(pg_), S.need(pu_), S.need(pd_)
                vg_, vu_, vd_ = _v3(wslot[sg], 8), _v3(wslot[su], 8), _v3(wslot[sd], 4)
                for tt in range(2):
                    cols = slice(tt * 512, (tt + 1) * 512)
                    ab = cnt["ab"] % 2
                    cnt["ab"] += 1
                    for f in range(nfc):
                        ib = cnt["up"] % 2
                        cnt["up"] += 1
                        for k in range(8):
                            P.mm(bank[ib][:], vg_[:, k, f * 128:(f + 1) * 128], hT[:, k, cols], k == 0, k == 7,
                                 [t_slot[sg], t_h[k][tt]], [t_bank[ib]])
                        for k in range(8):
                            P.mm(bank[2 + ib][:], vu_[:, k, f * 128:(f + 1) * 128], hT[:, k, cols], k == 0, k == 7,
                                 [t_slot[su], t_h[k][tt]], [t_bank[2 + ib]])
                        P.actv(stmp[ib][:], bank[ib][:], AF.Silu, [t_bank[ib]], [t_stmp[ib]])
                        P.tt("dve", aT[ab][:, f, :], stmp[ib][:], bank[2 + ib][:], ALU.mult,
                             [t_stmp[ib], t_bank[2 + ib]], [t_aT[ab][f]])

                    def down(ab=ab, nfc=nfc, sd=sd, vd_=vd_, cols=cols, tt=tt, gi=gi):
                        for dc in range(8):
                            ib2 = 4 + cnt["dn"] % 2
                            cnt["dn"] += 1
                            for f in range(nfc):
                                P.mm(bank[ib2][:], vd_[:, f, dc * 128:(dc + 1) * 128], aT[ab][:, f, :],
                                     f == 0, f == nfc - 1, [t_slot[sd], t_aT[ab][f]], [t_bank[ib2]])
                            P.stt(xT[:, dc, cols], bank[ib2][:], Gm[:, l, n, dc, v:v + 1], xT[:, dc, cols],
                                  ALU.mult, ALU.add, [t_bank[ib2], t_x[dc][tt], t_mods], [t_x[dc][tt]])
                        if gi == last_gi:
                            after_tt(tt)

                    flush()
                    pending.append(down)
                S.done(pg_)
                S.done(pu_)
                if gi > 0:
                    S.done(pid[("d", g, l, j, gi - 1)])
            flush()
            S.done(pid[("d", g, l, j, len(groups) - 1)])

        def load_x(g, after_tt):
            for i in range(8):
                xb = i % 2
                xs_ = xio[:, xb * 1024:(xb + 1) * 1024]
                tl_ = t_or[xb * 4:xb * 4 + 4]
                P.dma("sp", ("xio", xb), xs_, x_d[g, i * 128:(i + 1) * 128, :], writes=tl_)
                for hb in range(2):
                    for kk in range(4):
                        k = hb * 4 + kk
                        P.tr(bank[hb][:, kk * 128:(kk + 1) * 128], xs_[:, k * 128:(k + 1) * 128], ident[:],
                             tl_ + [t_ident], [t_bank[hb]])
                    P.cp("dve" if hb == 0 else "act", xT[:, hb * 4:hb * 4 + 4, i * 128:(i + 1) * 128],
                         bank[hb][:].rearrange("p (k n) -> p k n", k=4), [t_bank[hb]],
                         [t_x[k][i // 4] for k in range(hb * 4, hb * 4 + 4)])
                if i % 4 == 3:
                    after_tt(i // 4)

        def store_y(g):
            for i in range(8):
                xb = i % 2
                xs_ = xio[:, xb * 1024:(xb + 1) * 1024]
                tl_ = t_or[xb * 4:xb * 4 + 4]
                for hb in range(2):
                    for kk in range(4):
                        k = hb * 4 + kk
                        P.tr(bank[hb][:, kk * 128:(kk + 1) * 128], xT[:, k, i * 128:(i + 1) * 128], ident[:],
                             [t_x[k][i // 4], t_ident], [t_bank[hb]])
                    P.cp("dve" if hb == 0 else "act", xs_[:, hb * 512:(hb + 1) * 512], bank[hb][:],
                         [t_bank[hb]], tl_)
                P.dma("sp", ("y", xb), y_d[g, i * 128:(i + 1) * 128, :], xs_, reads=tl_, is_out=True)

        def grp_rms(src, t_src, nh):
            n = nh * 64
            s3 = src.rearrange("p (h d) -> p h d", d=64)
            sqs, t_sqs = sq[0], t_sq[0]
            P.tt("dve", sqs[:, 0:n], src, src, ALU.mult, [t_src], [t_sqs])
            P.red(st8[:, 0:nh], sqs[:, 0:n].rearrange("p (h d) -> p h d", d=64), [t_sqs], [t_st8])
            P.actv(st8[:, 0:nh], st8[:, 0:nh], AF.Sqrt, [t_st8, t_chalf], [t_st8], scale=1.0 / 64, bias=epsc[:, 0:1])
            P.add("dve", lambda e: e.reciprocal(st8[:, 8:8 + nh], st8[:, 0:nh]), [t_st8], [t_st8])
            P.tt("dve", s3, s3, st8[:, 8:8 + nh].rearrange("p (h o) -> p h o", o=1).to_broadcast([128, nh, 64]),
                 ALU.mult, [t_src, t_st8], [t_src])

        def mixer_setup(g, l):
            pwf = stmp[1][:, 0:256].rearrange("p (c n) -> p c n", c=2)
            t_pwf = t_stmp[1]
            RT = ntmp[0][0:31, 0:128]
            RT01 = ntmp[0][0:31, 128:352].rearrange("p (a n) -> p a n", a=2)
            t_RT = t_RT01 = t_ntmp[0]
            P.dma("sp", "ms0", gtb[:], gt_d[l:l + 1, :].to_broadcast([128, 384]), writes=[t_gtb])
            P.ts("dve", gtb[:, 0:64], gtb[:, 0:64], 0.125, None, ALU.mult, None, [t_gtb], [t_gtb])
            P.memset("pool", pwf, 0.0, [t_pwf])
            for c in range(2):
                for gh in range(2):
                    P.dma("sp", "ms1", pwf[gh * 64:(gh + 1) * 64, c, gh * 64:(gh + 1) * 64], poolw_d[l, 2 * c + gh],
                          writes=[t_pwf])
            P.cp("dve", pwb[:], pwf, [t_pwf], [t_pwb])
            for g4 in range(4):
                P.dma("sp", "ms2", stmp[0][:, 0:128], sgw_d[l, g4], writes=[t_stmp[0]])
                P.tr(bank[6][:, 0:128], stmp[0][:, 0:128], ident[:], [t_stmp[0], t_ident], [t_bank[6]])
                P.cp("dve", wsT[:, g4, :], bank[6][:, 0:128], [t_bank[6]], [t_wsT])
            for g4 in range(4):
                P.dma("sp", "ms3", bsb[(g4 % 2) * 64:(g4 % 2 + 1) * 64, g4 // 2, :],
                      sgb_d[l, g4:g4 + 1, :].to_broadcast([64, 128]), writes=[t_bsb])
            if g == 1:
                kctok = aT[1][:, 0:2, :]
                t_kctok = Tl()
                for t in range(2):
                    P.dma("pool", "ms4", kctok[:, t, :].rearrange("p (h d) -> p h d", h=8),
                          ck_d[l][:, t * 128:(t + 1) * 128, :].rearrange("h p d -> p h d"),
                          writes=[t_aT[1][0], t_aT[1][1]])
                    P.dma("pool", "ms5", vc[:, t, :].rearrange("p (h d) -> p h d", h=8),
                          cv_d[l][:, t * 128:(t + 1) * 128, :].rearrange("h p d -> p h d"), writes=[t_vc])
                for t in range(2):
                    for c in range(4):
                        P.tr(bankb[:, c * 128:(c + 1) * 128], kctok[:, t, c * 128:(c + 1) * 128], identb[:],
                             [t_aT[1][0], t_aT[1][1], t_identb], [t_bankb])
                    P.cp("dve", kcT[:, :, t * 128:(t + 1) * 128],
                         bankb[:, 0:512].rearrange("p (c n) -> p c n", c=4), [t_bankb], [t_kc])
                P.dma("sp", "ms6", stmp[1][:, 0:31], rpb_d[l], writes=[t_stmp[1]])
                P.tr(bank[6][0:31, 0:128], stmp[1][:, 0:31], ident[:], [t_stmp[1], t_ident], [t_bank[6]])
                P.cp("dve", RT, bank[6][0:31, 0:128], [t_bank[6]], [t_RT])
                rt3 = RT[:, 0:120].rearrange("p (h r) -> p h r", h=8)
                for half in range(2):
                    P.cp("dve", RT01[:, half, :].rearrange("p (h r) -> p h r", h=8), rt3[:, :, half:half + 14],
                         [t_RT], [t_RT01])
                for b in range(16):
                    bk = bank[b % 2]
                    for qi in range(4):
                        qc = b * 4 + qi
                        for half in range(2):
                            P.mm(bk[half * 64:(half + 1) * 64, qi * 112:(qi + 1) * 112],
                                 dw[0:31, 63 - qc:127 - qc], RT01[:, half, :], True, True,
                                 [t_dw, t_RT01], [t_bank[b % 2]], tp=(0, half * 64))
                    P.actv(EB[:, :, :, b * 4:(b + 1) * 4].rearrange("p h r q -> p q (h r)"),
                           bk[:, 0:448].rearrange("p (q n) -> p q n", q=4), AF.Exp, [t_bank[b % 2]], [t_EB])
                ebv = EB[:].rearrange("p h r q -> p (h r) q")
                P.tt("dve", ebv, ebv, colok[:, :].rearrange("p (o q) -> p o q", o=1).to_broadcast([128, 112, 64]),
                     ALU.mult, [t_EB, t_colok], [t_EB])

        acnt = {"n": 0, "bo": 0}
        SB_ = (0, 1, 4, 5)
        DEPTH = 2

        def mixer(g, l, after_tt):
            v = g
            is_p = (g == 0)
            if stop_after == "premix":
                raise build_program.Stop()
            pA, pB, pC, pD, pE, pF, pG = [pid[(nm, g, l)] for nm in "ABCDEFG"]
            sA, sB, sC, sD, sE = [S.need(p_) for p_ in (pA, pB, pC, pD, pE)]
            vA, vB, vC, vD, vE = [_v3(wslot[s_], 8) for s_ in (sA, sB, sC, sD, sE)]
            qf, kf, vf, gv = stmp[0], stmp[1], ntmp[0], ntmp[1]
            t_qf, t_kf, t_vf, t_gv = t_stmp[0], t_stmp[1], t_ntmp[0], t_ntmp[1]
            qb, t_qb = sq[1], t_sq[1]
            kb, t_kb = aT[0][:, 0, :], t_aT[0][0]
            h3 = lambda ap: ap.rearrange("p (h d) -> p h d", d=64)
            for i in range(8):
                tt = i // 4
                tok = slice(i * 128, (i + 1) * 128)
                for bi, (vw, sl) in enumerate(((vA, sA), (vB, sB), (vC, sC), (vD, sD))):
                    for k in range(8):
                        P.mm(bank[bi][:], hT[:, k, tok], vw[:, k, :], k == 0, k == 7,
                             [t_h[k][tt], t_slot[sl]], [t_bank[bi]])
                P.cp("act", za_tok[:, i, :], bank[0][:, 0:256], [t_bank[0]], [t_za[i]])
                P.actv(gv[:, 0:256], bank[0][:, 256:512], AF.Gelu_apprx_tanh, [t_bank[0]], [t_gv])
                grp_rms(gv[:, 0:256], t_gv, 4)
                P.tt("pool", vg_tok[:, i, :], gv[:, 0:256], gtb[:, 128:384], ALU.mult, [t_gv, t_gtb], [t_vg[i]])
                P.cp("act", qf[:], bank[1][:], [t_bank[1]], [t_qf])
                grp_rms(qf[:], t_qf, 8)
                P.tt("pool", h3(qb[:]), h3(qf[:]),
                     gtb[:, 0:64].rearrange("p (o d) -> p o d", o=1).to_broadcast([128, 8, 64]),
                     ALU.mult, [t_qf, t_gtb], [t_qb])
                for c in range(4):
                    P.tr(bankb[:, c * 128:(c + 1) * 128], qb[:, c * 128:(c + 1) * 128], identb[:],
                         [t_qb, t_identb], [t_bankb])
                P.cp("dve", qT[:, :, tok], bankb[:, 0:512].rearrange("p (c n) -> p c n", c=4), [t_bankb],
                     [t_q[c][i] for c in range(4)])
                P.cp("act", kf[:], bank[2][:], [t_bank[2]], [t_kf])
                grp_rms(kf[:], t_kf, 8)
                P.tt("pool", h3(kf[:]), h3(kf[:]),
                     gtb[:, 64:128].rearrange("p (o d) -> p o d", o=1).to_broadcast([128, 8, 64]),
                     ALU.mult, [t_kf, t_gtb], [t_kf])
                if is_p:
                    b_, s0_ = i // 2, (i % 2) * 128
                    P.dma("sp", "nk", nk_d[b_, l].rearrange("h s d -> s h d")[s0_:s0_ + 128], h3(kf[:]),
                          reads=[t_kf], is_out=True)
                P.cp("act", kb, kf[:], [t_kf], [t_kb])
                for c in range(4):
                    P.tr(bankb[:, 512 + c * 128:512 + (c + 1) * 128], kb[:, c * 128:(c + 1) * 128], identb[:],
                         [t_kb, t_identb], [t_bankb])
                P.cp("dve", kT[:, :, tok], bankb[:, 512:1024].rearrange("p (c n) -> p c n", c=4), [t_bankb],
                     [t_k[c][i] for c in range(4)])
                P.cp("act", v_tok[:, i, :], bank[3][:], [t_bank[3]], [t_v[i]])
                if is_p:
                    P.cp("dve", vf[:], bank[3][:], [t_bank[3]], [t_vf])
                    P.dma("sp", "nv", nv_d[b_, l].rearrange("h s d -> s h d")[s0_:s0_ + 128], h3(vf[:]),
                          reads=[t_vf], is_out=True)
            gv, t_gv = stmp[1], t_stmp[1]
            _mark(" zu")
            for tt in range(2):
                cols = slice(tt * 512, (tt + 1) * 512)
                for c in range(2):
                    for k in range(8):
                        P.mm(bank[4 + c][:], vE[:, k, c * 128:(c + 1) * 128], hT[:, k, cols], k == 0, k == 7,
                             [t_slot[sE], t_h[k][tt]], [t_bank[4 + c]])
                    P.actv(uT[:, c, cols], bank[4 + c][:], AF.Gelu_apprx_tanh, [t_bank[4 + c]], [t_u[c][tt]])
            for p_ in (pA, pB, pC, pD, pE):
                S.done(p_)
            sF, sG = S.need(pF), S.need(pG)
            if stop_after == "proj":
                raise build_program.Stop()

            def variant(j):
                if is_p:
                    return 0 if j % 2 == 0 else 2
                return 0 if j == 0 else (2 if j == 7 else 1)

            def seq_range(j):
                if is_p:
                    return (j // 2 * 2, j // 2 * 2 + 1)
                return (0, 7)

            sqo, t_sqo = sq[0], t_sq[0]
            rs_ap = (ms[:, 0:256], ms[:, 256:512], ntmp[0][:, 0:256])
            rs_t = (t_ms, t_ms, t_ntmp[0])
            ms2, t_ms2 = ntmp[0][:, 256:512], t_ntmp[0]
            oTn = aT[1][:].rearrange("p f n -> p (f n)").rearrange("p (c n) -> p c n", c=8)
            ETB = ((ms, t_ms), (ntmp[0], t_ntmp[0]), (ntmp[1], t_ntmp[1]))

            def attn_S(it):
                n = it["n"]
                c, hp, h = it["c"], it["hp"], it["h"]
                ps_ = slice(hp * 64, (hp + 1) * 64)
                bk, t_bk = bank[SB_[n % 4]], t_bank[SB_[n % 4]]
                pt, t_pt = aT[0][:, n % 4, :], t_aT[0][n % 4]
                if is_p:
                    u_, tok0_ = it["u"], it["u"] * 256
                    for kt in range(2):
                        P.mm(bk[:, kt * 256:(kt + 1) * 256],
                             kT[ps_, c, tok0_ + kt * 128:tok0_ + (kt + 1) * 128], qT[ps_, c, tok0_:tok0_ + 256],
                             True, True, [t_k[c][2 * u_ + kt], t_q[c][2 * u_], t_q[c][2 * u_ + 1]], [t_bk])
                    P.actv(pt, bk[:], AF.Exp, [t_bk], [t_pt])
                else:
                    r, kts, nl = it["r"], it["kts"], it["nl"]
                    et, t_et = ETB[n % 3]
                    qcols = slice(r * 64, (r + 1) * 64)
                    for j_, kt in enumerate(kts):
                        P.mm(bk[:, j_ * 64:(j_ + 1) * 64], kT[ps_, c, kt * 128:(kt + 1) * 128],
                             qT[ps_, c, qcols], True, True, [t_k[c][kt], t_q[c][r // 2]], [t_bk])
                    for t in range(2):
                        P.mm(bk[:, (nl + t) * 64:(nl + t + 1) * 64], kcT[ps_, c, t * 128:(t + 1) * 128],
                             qT[ps_, c, qcols], True, True, [t_kc, t_q[c][r // 2]], [t_bk])
                    P.actv(et[:, 0:nl * 64], bk[:, 0:nl * 64], AF.Exp, [t_bk], [t_et])
                    P.actv(pt[:, nl * 64:(nl + 2) * 64], bk[:, nl * 64:(nl + 2) * 64], AF.Exp, [t_bk], [t_pt])
                    i0 = 2 * kts[0] - r + 7
                    P.tt("dve", pt[:, 0:nl * 64].rearrange("p (j q) -> p j q", q=64),
                         et[:, 0:nl * 64].rearrange("p (j q) -> p j q", q=64),
                         EB[:, h, i0:i0 + 2 * nl - 1:2, :], ALU.mult, [t_et, t_EB], [t_pt])

            def attn_PV(it):
                n = it["n"]
                c, hp, h = it["c"], it["hp"], it["h"]
                ps_ = slice(hp * 64, (hp + 1) * 64)
                pt, t_pt = aT[0][:, n % 4, :], t_aT[0][n % 4]
                if hp == 0:
                    acnt["bo"] += 1
                bo, t_bo = bank[2 + acnt["bo"] % 2], t_bank[2 + acnt["bo"] % 2]
                if is_p:
                    u_ = it["u"]
                    for kt in range(2):
                        P.mm(bo[ps_, 0:256], v_tok[:, 2 * u_ + kt, h * 64:(h + 1) * 64],
                             pt[:, kt * 256:(kt + 1) * 256], kt == 0, kt == 1,
                             [t_v[2 * u_ + kt], t_pt], [t_bo], tp=(0, hp * 64))
                    for kt in range(2):
                        P.mm(bo[ps_, 256:512], ones[:, 0:64], pt[:, kt * 256:(kt + 1) * 256],
                             kt == 0, kt == 1, [t_ones, t_pt], [t_bo], tp=(0, hp * 64))
                    if hp == 1:
                        P.add("dve", lambda e, bo=bo: e.reciprocal(rden[:], bo[:, 256:512]), [t_bo], [t_rden])
                        P.tt("dve", oraw[:, 2 + c, :], bo[:, 0:256], rden[:], ALU.mult, [t_bo, t_rden],
                             [t_or[2 + c]])
                else:
                    r, kts, nl, s0 = it["r"], it["kts"], it["nl"], it["s0"]
                    mml = []
                    for j_, kt in enumerate(kts):
                        if s0 % 2 == 1 and j_ == 0:
                            p0, p1 = 64, 128
                        elif s0 % 2 == 1 and j_ == nl - 1:
                            p0, p1 = 0, 64
                        else:
                            p0, p1 = 0, 128
                        mml.append((v_tok[p0:p1, kt, h * 64:(h + 1) * 64], ones[p0:p1, 0:64],
                                    pt[p0:p1, j_ * 64:(j_ + 1) * 64], p0, [t_v[kt]]))
                    for t in range(2):
                        mml.append((vc[:, t, h * 64:(h + 1) * 64], ones[:, 0:64],
                                    pt[:, (nl + t) * 64:(nl + t + 1) * 64], 0, [t_vc]))
                    for n_, (lv, lo_, rh, p0, rd) in enumerate(mml):
                        P.mm(bo[ps_, 0:64], lv, rh, n_ == 0, n_ == len(mml) - 1, rd + [t_pt], [t_bo],
                             tp=(p0, hp * 64))
                    for n_, (lv, lo_, rh, p0, rd) in enumerate(mml):
                        P.mm(bo[ps_, 64:128], lo_, rh, n_ == 0, n_ == len(mml) - 1, [t_ones, t_pt],
                             [t_bo], tp=(p0, hp * 64))
                    if hp == 1:
                        P.add("dve", lambda e, bo=bo: e.reciprocal(rden[:, 0:64], bo[:, 64:128]), [t_bo],
                              [t_rden])
                        P.tt("dve", oraw[:, 2 + c, (r % 4) * 64:(r % 4 + 1) * 64], bo[:, 0:64], rden[:, 0:64],
                             ALU.mult, [t_bo, t_rden], [t_or[2 + c]])

            for u in range(4):
                tok0 = u * 256
                ucols = slice(tok0, tok0 + 256)
                tt = u // 2
                _mark(" u%d-attn" % u)
                items = []
                for c in range(4):
                    if is_p:
                        for hp in range(2):
                            items.append({"c": c, "hp": hp, "h": 2 * c + hp, "u": u})
                    else:
                        for r in range(4 * u, 4 * u + 4):
                            s0 = min(max(r - 4, 0), 8)
                            kts = list(range(s0 // 2, (s0 + 7) // 2 + 1))
                            for hp in range(2):
                                items.append({"c": c, "hp": hp, "h": 2 * c + hp, "r": r, "s0": s0, "kts": kts,
                                              "nl": len(kts)})
                for it in items:
                    it["n"] = acnt["n"]
                    acnt["n"] += 1
                for idx in range(len(items) + DEPTH):
                    if idx < len(items):
                        attn_S(items[idx])
                    if idx - DEPTH >= 0:
                        attn_PV(items[idx - DEPTH])
                _mark(" u%d-pool" % u)
                for c in range(2):
                    for gh in range(2):
                        g4 = 2 * c + gh
                        for jj in range(2):
                            j = 2 * u + jj
                            lo, hi = seq_range(j)
                            ins = [i_ for i_ in (j - 1, j, j + 1) if lo <= i_ <= hi]
                            for n_, i_ in enumerate(ins):
                                var = 3 if i_ == j - 1 else (4 if i_ == j + 1 else variant(j))
                                P.mm(bank[4][gh * 64:(gh + 1) * 64, c * 256 + jj * 128:c * 256 + (jj + 1) * 128],
                                     za_tok[:, i_, g4 * 64:(g4 + 1) * 64], bands[:, g4 * 5 + var, :],
                                     n_ == 0, n_ == len(ins) - 1, [t_za[i_], t_bands], [t_bank[4]],
                                     tp=(0, gh * 64))
                    for jj in range(2):
                        P.tt("dve", pooledT[:, c, jj * 128:(jj + 1) * 128],
                             bank[4][:, c * 256 + jj * 128:c * 256 + (jj + 1) * 128],
                             invc[:, c * 3 + variant(2 * u + jj), :], ALU.mult, [t_bank[4], t_invc], [t_pl[c]])
                    P.mm(bank[5][:, c * 256:(c + 1) * 256], pwb[:, c, :], pooledT[:, c, :], True, True,
                         [t_pwb, t_pl[c]], [t_bank[5]])
                    P.actv(oraw[:, c, :], bank[5][:, c * 256:(c + 1) * 256], AF.Identity, [t_bank[5], t_vtT],
                           [t_or[c]], scale=vtT[:, l, 104 + c:105 + c])
                for c in range(2):
                    for gh in range(2):
                        g4 = 2 * c + gh
                        for jj in range(2):
                            P.mm(bank[4][gh * 64:(gh + 1) * 64, c * 256 + jj * 128:c * 256 + (jj + 1) * 128],
                                 vg_tok[:, 2 * u + jj, g4 * 64:(g4 + 1) * 64], wsT[:, g4, :], True, True,
                                 [t_vg[2 * u + jj], t_wsT], [t_bank[4]], tp=(0, gh * 64))
                    P.tt("dve", gv[:, 0:256].rearrange("p (j n) -> p j n", j=2),
                         bank[4][:, c * 256:(c + 1) * 256].rearrange("p (j n) -> p j n", j=2),
                         bsb[:, c:c + 1, :].to_broadcast([128, 2, 128]), ALU.add, [t_bank[4], t_bsb], [t_gv])
                    P.tt("pool", oraw[:, 6 + c, :], gv[:, 0:256], uT[:, c, ucols], ALU.mult,
                         [t_gv, t_u[c][tt]], [t_or[6 + c]])
                _mark(" u%d-out" % u)
                segs = ((0, 2, 256.0), (2, 6, 512.0), (6, 8, 256.0))
                for si, (o0, o1, wd_) in enumerate(segs):
                    stb, t_stb = (bank[6], t_bank[6]) if si < 2 else (bank[5], t_bank[5])
                    scol = slice((si % 2) * 256, (si % 2) * 256 + 256)
                    for oc in range(o0, o1):
                        P.actv(sqo[:, 0:256], oraw[:, oc, :], AF.Square, [t_or[oc]], [t_sqo])
                        P.mm(stb[:, scol], ones[:], sqo[:, 0:256], oc == o0, oc == o1 - 1, [t_ones, t_sqo],
                             [t_stb])
                    P.actv(ms2, stb[:, scol], AF.Sqrt, [t_stb, t_chalf], [t_ms2], scale=1.0 / wd_,
                           bias=epsc[:, 0:1])
                    P.add("dve", lambda e, si=si: e.reciprocal(rs_ap[si], ms2), [t_ms2], [rs_t[si]])
                for oc in range(8):
                    si = 0 if oc < 2 else (1 if oc < 6 else 2)
                    P.stt(oTn[:, oc, :], oraw[:, oc, :], vtT[:, l, 96 + oc:97 + oc], rs_ap[si], ALU.mult, ALU.mult,
                          [t_or[oc], t_vtT, rs_t[si]], [t_aT[1][oc // 2]])
                for dc in range(8):
                    sl = sF if dc < 4 else sG
                    vw = _v3(wslot[sl], 8)
                    bw, t_bw = bank[2 + dc % 2], t_bank[2 + dc % 2]
                    for oc in range(8):
                        P.mm(bw[:, 0:256], vw[:, oc, (dc % 4) * 128:(dc % 4 + 1) * 128], oTn[:, oc, :],
                             oc == 0, oc == 7, [t_slot[sl], t_aT[1][oc // 2]], [t_bw])
                    P.stt(xT[:, dc, ucols], bw[:, 0:256], Gm[:, l, 1, dc, v:v + 1], xT[:, dc, ucols],
                          ALU.mult, ALU.add, [t_bw, t_x[dc][tt], t_mods], [t_x[dc][tt]])
                if u % 2 == 1:
                    after_tt(u // 2)
            S.done(pF)
            S.done(pG)

        class _Stop(Exception):
            pass
        build_program.Stop = _Stop
        try:
            for g in range(2):
                _mark("load%d" % g)
                if not EARLY_NORM:
                    nop = lambda tt: None
                    load_x(g, nop)
                    for l in range(L):
                        _mark("ffn%d%d0" % (g, l))
                        norm_tt(l, 0, g, 0); norm_tt(l, 0, g, 1)
                        ffn(g, l, 0, nop)
                        _mark("mix%d%d" % (g, l))
                        mixer_setup(g, l)
                        norm_tt(l, 1, g, 0); norm_tt(l, 1, g, 1)
                        mixer(g, l, nop)
                        _mark("ffn%d%d1" % (g, l))
                        norm_tt(l, 2, g, 0); norm_tt(l, 2, g, 1)
                        ffn(g, l, 1, nop)
                    _mark("store%d" % g)
                    store_y(g)
                    continue
                load_x(g, lambda tt, g=g: norm_tt(0, 0, g, tt))
                for l in range(L):
                    _mark("ffn%d%d0" % (g, l))
                    mixer_setup(g, l)
                    ffn(g, l, 0, lambda tt, g=g, l=l: norm_tt(l, 1, g, tt))
                    _mark("mix%d%d" % (g, l))
                    mixer(g, l, lambda tt, g=g, l=l: norm_tt(l, 2, g, tt))
                    _mark("ffn%d%d1" % (g, l))
                    if l + 1 < L:
                        ffn(g, l, 1, lambda tt, g=g, l=l: norm_tt(l + 1, 0, g, tt))
                    else:
                        ffn(g, l, 1, lambda tt: None)
                _mark("store%d" % g)
                store_y(g)
        except _Stop:
            pass
        _mark("end")
        build_program.phases = PH
        P.emit(st)
    return nc


def _consts():
    ident = np.eye(128, dtype=np.float32)
    bands = np.zeros((128, 20, 128), np.float32)
    invc = np.zeros((128, 6, 128), np.float32)
    for gi, w in enumerate(WINS):
        h = w // 2
        tp = np.arange(128)[:, None]
        t = np.arange(128)[None, :]
        inb = ((tp >= t - h) & (tp < t + h)).astype(np.float32)
        cnt_first = (np.minimum(t + h, 10 ** 6) - np.maximum(t - h, 0)).astype(np.float32)
        cnt_mid = np.full((1, 128), float(w), np.float32)
        cnt_last = ((128 - t) + h - np.maximum(0, 0)).astype(np.float32)
        cnt_last = np.minimum(cnt_last, w).astype(np.float32)
        eye = np.eye(128, dtype=np.float32)
        bands[:, gi * 5 + 0, :] = inb - eye * cnt_first
        bands[:, gi * 5 + 1, :] = inb - eye * cnt_mid
        bands[:, gi * 5 + 2, :] = inb - eye * cnt_last
        bands[:, gi * 5 + 3, :] = ((tp - 128) >= (t - h)).astype(np.float32)
        bands[:, gi * 5 + 4, :] = ((tp + 128) < (t + h)).astype(np.float32)
        c, half = gi // 2, gi % 2
        for vi, cn in enumerate((cnt_first, cnt_mid, cnt_last)):
            invc[half * 64:(half + 1) * 64, c * 3 + vi, :] = 1.0 / cn
    dw = np.zeros((31, 127), np.float32)
    for j in range(31):
        dw[j, j + 48] = 1.0
    cols = np.arange(64)
    cs = np.clip(cols - 8, 0, 48)
    ok = (cols[None, :] >= cs[:, None]) & (cols[None, :] < cs[:, None] + 16)
    colok = np.concatenate([ok.T, ok.T], 0).astype(np.float32)
    return ident, bands, invc, dw, colok


def make_in_maps(inp):
    f = lambda a: np.ascontiguousarray(np.asarray(a, dtype=np.float32))
    ident, bands, invc, dw, colok = _consts()
    vt = np.zeros((L, 128, 128), np.float32)
    gt = np.zeros((L, 384), np.float32)
    rpb = np.zeros((L, 128, 31), np.float32)
    for l in range(L):
        vt[l, 0:72] = f(inp["ada_b"])[l].reshape(72, 128)
        vt[l, 72:96] = f(inp["norm_g"])[l].reshape(24, 128)
        vt[l, 96:104] = f(inp["out_norm_g"])[l].reshape(8, 128)
        vt[l, 104:106] = f(inp["pool_scale"])[l].reshape(2, 128)
        gt[l, 0:64] = f(inp["q_norm_g"])[l]
        gt[l, 64:128] = f(inp["k_norm_g"])[l]
        gt[l, 128:384] = f(inp["sg_vnorm_g"])[l].reshape(256)
        rpb[l, 0:120] = f(inp["na_rpb"])[l].reshape(120, 31)
    shared = {
        "vt": vt, "gt": gt, "rpb": rpb,
        "poolw": f(inp["pool_w"]), "sgw": f(inp["sg_w"]), "sgb": f(inp["sg_b"]),
        "adaw": f(inp["ada_w"]), "wg": f(inp["ffn_w_gate"]), "wu": f(inp["ffn_w_up"]),
        "wd": f(inp["ffn_w_down"]), "win": f(inp["w_in"]), "wout": f(inp["w_out"]),
        "ident": ident, "bands": bands, "invc": invc, "dw": dw, "colok": colok,
    }
    xp = f(inp["x_prompt"]); xs = f(inp["x_sample"])
    ck = f(inp["cache_k"]); cv = f(inp["cache_v"]); c = f(inp["c"]); cc = f(inp["c_ctx"])
    maps = []
    for i in range(NCORES):
        x = np.stack([xp[4 * i:4 * i + 4].reshape(1024, D), xs[i]], 0)
        cvec = np.zeros((128, 128), np.float32)
        cvec[0:8] = cc.reshape(8, 128)
        cvec[8:16] = c[i].reshape(8, 128)
        m = dict(shared)
        m.update({"x": np.ascontiguousarray(x), "ck": np.ascontiguousarray(ck[i]),
                  "cv": np.ascontiguousarray(cv[i]), "cvec": cvec})
        maps.append(m)
    return maps


_NC_CACHE = {}


def kernel(**inputs):
    if "nc" not in _NC_CACHE:
        _NC_CACHE["nc"] = build_program()
    nc = _NC_CACHE["nc"]
    maps = make_in_maps(inputs)
    res = run_bass_kernel_spmd(nc, maps, core_ids=list(range(NCORES)))
    r = res.results
    yp = np.concatenate([r[i]["y"][0].reshape(4, 256, D) for i in range(NCORES)], 0)
    ys = np.stack([r[i]["y"][1] for i in range(NCORES)], 0)
    nk = np.concatenate([r[i]["nk"] for i in range(NCORES)], 0)
    nv = np.concatenate([r[i]["nv"] for i in range(NCORES)], 0)
    return (yp.astype(np.float32), ys.astype(np.float32), nk.astype(np.float32), nv.astype(np.float32))
```

```python
import numpy as np
from contextlib import ExitStack
import concourse.bass as bass
import concourse.mybir as mybir
from concourse.bass_utils import run_bass_kernel_spmd

F32 = mybir.dt.float32
BF16 = mybir.dt.bfloat16
AF = mybir.ActivationFunctionType
ALU = mybir.AluOpType
AX = mybir.AxisListType

D = 1024
DFF = 2816
L = 2
NH = 8
EPS = 1e-6
NCORES = 8
WINS = (2, 4, 8, 16)
EARLY_NORM = True


class Tl:
    __slots__ = ("name", "w", "r")

    def __init__(self, name=""):
        self.name = name
        self.w = None
        self.r = {}


class Op:
    __slots__ = ("eng", "fn", "deps", "marked", "count", "dkey", "dcount", "is_dma", "uid")


class Prog:
    ENGS = ("pe", "act", "dve", "pool", "sp")

    def __init__(self, nc):
        self.nc = nc
        self.ops = {e: [] for e in self.ENGS}
        self.latest_dma = {}
        self.dma_counts = {}
        self.uid = 0
        self.out_keys = set()

    def add(self, eng, fn, reads=(), writes=(), dma=None, is_out=False):
        op = Op()
        op.eng = eng
        op.fn = fn
        op.marked = False
        op.count = None
        op.is_dma = dma is not None
        op.dkey = dma
        op.uid = self.uid
        self.uid += 1
        deps = {}
        for t in reads:
            if t.w is not None:
                deps[t.w.uid] = t.w
        for t in writes:
            if t.w is not None:
                deps[t.w.uid] = t.w
            for d in t.r.values():
                deps[d.uid] = d
        final = {}
        for d in deps.values():
            if d.is_dma:
                d = self.latest_dma[d.dkey]
            elif d.eng == "pe" and eng == "pe" and not op.is_dma:
                continue
            else:
                d.marked = True
            final[d.uid] = d
        op.deps = list(final.values())
        if op.is_dma:
            c = self.dma_counts.get(dma, 0) + 16
            self.dma_counts[dma] = c
            op.dcount = c
            self.latest_dma[dma] = op
            if is_out:
                self.out_keys.add(dma)
        rkey = ("dma", op.uid) if op.is_dma else eng
        for t in reads:
            t.r[rkey] = op
        for t in writes:
            t.w = op
            t.r = {}
        self.ops[eng].append(op)
        return op

    def dma(self, eng, key, out, in_, reads=(), writes=(), is_out=False):
        return self.add(eng, lambda e: e.dma_start(out=out, in_=in_), reads, writes,
                        dma=key, is_out=is_out)

    def mm(self, out, lhsT, rhs, start, stop, reads, writes, tp=None):
        if tp is None:
            return self.add("pe", lambda e: e.matmul(out, lhsT, rhs, start=start, stop=stop), reads, writes)
        return self.add("pe", lambda e: e.matmul(out, lhsT, rhs, start=start, stop=stop, tile_position=tp),
                        reads, writes)

    def tr(self, out, in_, ident, reads, writes):
        return self.add("pe", lambda e: e.transpose(out, in_, ident), reads, writes)

    def actv(self, out, in_, func, reads, writes, scale=None, bias=None):
        kw = {}
        if scale is not None:
            kw["scale"] = scale
        if bias is not None:
            kw["bias"] = bias
        return self.add("act", lambda e: e.activation(out, in_, func, **kw), reads, writes)

    def tt(self, eng, out, in0, in1, op, reads, writes):
        return self.add(eng, lambda e: e.tensor_tensor(out, in0, in1, op=op), reads, writes)

    def ts(self, eng, out, in0, s1, s2, op0, op1, reads, writes):
        if s2 is None:
            return self.add(eng, lambda e: e.tensor_scalar(out, in0, s1, None, op0=op0), reads, writes)
        return self.add(eng, lambda e: e.tensor_scalar(out, in0, s1, s2, op0=op0, op1=op1), reads, writes)

    def stt(self, out, in0, scalar, in1, op0, op1, reads, writes):
        return self.add("dve", lambda e: e.scalar_tensor_tensor(out, in0, scalar, in1, op0=op0, op1=op1),
                        reads, writes)

    def cp(self, eng, out, in_, reads, writes):
        if eng == "act":
            return self.add("act", lambda e: e.copy(out, in_), reads, writes)
        return self.add(eng, lambda e: e.tensor_copy(out, in_), reads, writes)

    def red(self, out, in_, reads, writes):
        return self.add("dve", lambda e: e.tensor_reduce(out, in_, axis=AX.X, op=ALU.add), reads, writes)

    def memset(self, eng, ap, val, writes):
        return self.add(eng, lambda e: e.memset(ap, val), (), writes)

    def emit(self, stack):
        nc = self.nc
        for e in self.ENGS:
            c = 0
            for op in self.ops[e]:
                if op.marked and not op.is_dma:
                    c += 1
                    op.count = c
        esem = {e: stack.enter_context(nc.semaphore("s_" + e)) for e in self.ENGS}
        dsem = {k: stack.enter_context(nc.semaphore("d_%d" % i)) for i, k in enumerate(self.dma_counts)}
        block = stack.enter_context(nc.Block())
        ops = self.ops
        out_final = [(dsem[k], self.dma_counts[k]) for k in sorted(self.out_keys, key=str)]

        def run(ename, e):
            known = {}
            for op in ops[ename]:
                for d in op.deps:
                    if d.is_dma:
                        s, v = dsem[d.dkey], d.dcount
                    else:
                        s, v = esem[d.eng], d.count
                    kid = id(s)
                    if known.get(kid, 0) >= v:
                        continue
                    known[kid] = v
                    e.wait_ge(s, v)
                ins = op.fn(e)
                if op.is_dma:
                    ins.then_inc(dsem[op.dkey], 16)
                elif op.marked:
                    ins.then_inc(esem[ename], 1)
            if ename == "sp":
                for s, v in out_final:
                    e.wait_ge(s, v)

        @block.sync
        def _(e):
            run("sp", e)

        @block.tensor
        def _(e):
            run("pe", e)

        @block.scalar
        def _(e):
            run("act", e)

        @block.vector
        def _(e):
            run("dve", e)

        @block.gpsimd
        def _(e):
            run("pool", e)


class Stream:
    NS = 6
    LOOK = 5

    def __init__(self, P, slots, t_slots):
        self.P = P
        self.slots = slots
        self.t_slots = t_slots
        self.pieces = []
        self.issued = 0
        self.freed = 0
        self.want = 0

    def plan(self, parts):
        self.pieces.append(parts)
        return len(self.pieces) - 1

    def pump(self):
        while self.issued <= self.want and self.issued < len(self.pieces) and self.issued < self.freed + self.NS:
            i = self.issued
            s = i % self.NS
            for dst_fn, src in self.pieces[i]:
                self.P.dma("pool", ("w", s), dst_fn(self.slots[s]), src, writes=[self.t_slots[s]])
            self.issued += 1

    def need(self, p):
        self.want = max(self.want, p + self.LOOK)
        self.pump()
        assert self.issued > p, (self.issued, p, self.freed)
        return p % self.NS

    def done(self, p):
        self.freed = max(self.freed, p + 1)
        self.pump()


def _v3(t, k):
    return t[:].rearrange("p (k n) -> p k n", k=k)


def build_program(stop_after=None):
    nc = bass.Bass("TRN2", target_bir_lowering=False)

    def din(name, shape):
        return nc.dram_tensor(name, shape, F32, kind="ExternalInput").ap()

    def dout(name, shape):
        return nc.dram_tensor(name, shape, F32, kind="ExternalOutput").ap()

    x_d = din("x", [2, 1024, D])
    ck_d = din("ck", [L, NH, 256, 64])
    cv_d = din("cv", [L, NH, 256, 64])
    cvec_d = din("cvec", [128, 128])
    vt_d = din("vt", [L, 128, 128])
    gt_d = din("gt", [L, 384])
    poolw_d = din("poolw", [L, 4, 64, 64])
    sgw_d = din("sgw", [L, 4, 128, 128])
    sgb_d = din("sgb", [L, 4, 128])
    rpb_d = din("rpb", [L, 128, 31])
    adaw_d = din("adaw", [L, D, 9 * D])
    wg_d = din("wg", [L, 2, D, DFF])
    wu_d = din("wu", [L, 2, D, DFF])
    wd_d = din("wd", [L, 2, DFF, D])
    win_d = din("win", [L, D, 2304])
    wout_d = din("wout", [L, D, D])
    ident_d = din("ident", [128, 128])
    bands_d = din("bands", [128, 20, 128])
    invc_d = din("invc", [128, 6, 128])
    dw_d = din("dw", [31, 127])
    colok_d = din("colok", [128, 64])
    y_d = dout("y", [2, 1024, D])
    nk_d = dout("nk", [4, L, NH, 256, 64])
    nv_d = dout("nv", [4, L, NH, 256, 64])

    P = Prog(nc)
    st = ExitStack()

    def sb(name, shape, dt=F32):
        return st.enter_context(nc.sbuf_tensor("s_" + name, shape, dt))

    with st:
        bank = [st.enter_context(nc.psum_tensor("bank%d" % i, [128, 512], F32)) for i in range(7)]
        bankb = st.enter_context(nc.psum_tensor("bankb", [128, 1024], BF16))
        t_bank = [Tl("bank%d" % i) for i in range(7)]
        t_bankb = Tl("bankb")

        ident = sb("ident", [128, 128]); t_ident = Tl()
        identb = sb("identb", [128, 128], BF16); t_identb = Tl()
        ones = sb("ones", [128, 128], BF16); t_ones = Tl()
        epsc = sb("epsc", [128, 1]); t_chalf = Tl()
        bands = sb("bands", [128, 20, 128], BF16); t_bands = Tl()
        invc = sb("invc", [128, 6, 128]); t_invc = Tl()
        dw = sb("dw", [31, 127]); t_dw = Tl()
        colok = sb("colok", [128, 64]); t_colok = Tl()
        P.dma("sp", "c0", ident[:], ident_d, writes=[t_ident])
        P.dma("sp", "c1", invc[:], invc_d, writes=[t_invc])
        P.dma("sp", "c2", dw[:], dw_d, writes=[t_dw])
        P.dma("sp", "c3", colok[:], colok_d, writes=[t_colok])
        P.dma("pool", "c4", bands[:], bands_d, writes=[t_bands])
        P.cp("dve", identb[:], ident[:], [t_ident], [t_identb])
        P.memset("pool", ones[:], 1.0, [t_ones])
        P.memset("pool", epsc[:], EPS, [t_chalf])

        xT = sb("xT", [128, 8, 1024]); t_x = [[Tl() for _ in range(2)] for _ in range(8)]
        hT = sb("hT", [128, 8, 1024], BF16); t_h = [[Tl() for _ in range(2)] for _ in range(8)]
        wslot = [sb("wslot%d" % i, [128, 4096], BF16) for i in range(Stream.NS)]
        t_slot = [Tl("slot%d" % i) for i in range(Stream.NS)]
        S = Stream(P, wslot, t_slot)
        aT = [sb("aT%d" % i, [128, 4, 512], BF16) for i in range(2)]
        t_aT = [[Tl() for _ in range(4)] for _ in range(2)]
        stmp = [sb("stmp%d" % i, [128, 512]) for i in range(2)]; t_stmp = [Tl(), Tl()]
        xio = sb("xio", [128, 2048]); t_or = [Tl() for _ in range(8)]
        oraw = xio[:].rearrange("p (c n) -> p c n", c=8)
        sq = [sb("sq%d" % i, [128, 512], BF16) for i in range(2)]; t_sq = [Tl(), Tl()]
        ms = sb("ms", [128, 512]); t_ms = Tl()
        rstd, t_rstd = ms, t_ms
        ntmp = [sb("ntmp%d" % i, [128, 512]) for i in range(2)]; t_ntmp = [Tl(), Tl()]
        mods = sb("mods", [128, L, 72, 2]); t_mods = Tl()
        Am = sb("Am", [128, L, 3, 8, 2])
        Gm = sb("Gm", [128, L, 3, 8, 2])
        vtT = sb("vtT", [128, L, 128]); t_vtT = Tl()
        scT = sb("scT", [128, 16], BF16); t_scT = Tl()
        vtr_t = sb("vtr", [128, 128]); t_vtr = Tl()
        vtr = vtr_t[:]
        za_tok = sb("za_tok", [128, 8, 256], BF16); t_za = [Tl() for _ in range(8)]
        vg_tok = sb("vg_tok", [128, 8, 256], BF16); t_vg = [Tl() for _ in range(8)]
        v_tok = sb("v_tok", [128, 8, 512], BF16); t_v = [Tl() for _ in range(8)]
        qT = sb("qT", [128, 4, 1024], BF16); t_q = [[Tl() for _ in range(8)] for _ in range(4)]
        kT = sb("kT", [128, 4, 1024], BF16); t_k = [[Tl() for _ in range(8)] for _ in range(4)]
        uT = sb("uT", [128, 2, 1024], BF16); t_u = [[Tl() for _ in range(2)] for _ in range(2)]
        kcT = sb("kcT", [128, 4, 256], BF16); t_kc = Tl()
        qkz = sb("qkz", [128, 1280]); t_qkz = Tl()
        vc = sb("vc", [128, 2, 512], BF16); t_vc = Tl()
        pooledT = sb("pooledT", [128, 2, 256], BF16); t_pl = [Tl(), Tl()]
        EB = sb("EB", [128, 8, 14, 64], BF16); t_EB = Tl()
        wsT = sb("wsT", [128, 4, 128], BF16); t_wsT = Tl()
        pwb = sb("pwb", [128, 2, 128], BF16); t_pwb = Tl()
        bsb = sb("bsb", [128, 2, 128]); t_bsb = Tl()
        gtb = sb("gtb", [128, 384]); t_gtb = Tl()
        rden = sb("rden", [128, 256]); t_rden = Tl()
        st8 = sb("st8", [128, 40]); t_st8 = Tl()

        pid = {}
        groups = [(i * 4, 4) for i in range(5)] + [(20, 2)]

        def plan_ada(l):
            for pc in range(18):
                pid[("ada", l, pc)] = S.plan([(
                    lambda t: _v3(t, 8),
                    adaw_d[l].rearrange("(k p) n -> p k n", p=128)[:, :, pc * 512:(pc + 1) * 512])])

        def plan_ffn(g, l, j):
            for gi, (c0, nfc) in enumerate(groups):
                w = nfc * 128
                pid[("g", g, l, j, gi)] = S.plan([(
                    lambda t, w=w: _v3(t, 8)[:, :, 0:w],
                    wg_d[l, j].rearrange("(k p) n -> p k n", p=128)[:, :, c0 * 128:c0 * 128 + w])])
                pid[("u", g, l, j, gi)] = S.plan([(
                    lambda t, w=w: _v3(t, 8)[:, :, 0:w],
                    wu_d[l, j].rearrange("(k p) n -> p k n", p=128)[:, :, c0 * 128:c0 * 128 + w])])
                pid[("d", g, l, j, gi)] = S.plan([(
                    lambda t, nfc=nfc: _v3(t, 4)[:, 0:nfc, :],
                    wd_d[l, j][c0 * 128:c0 * 128 + w, :].rearrange("(f p) n -> p f n", p=128))])

        def plan_mixer(g, l):
            wv = win_d[l].rearrange("(k p) n -> p k n", p=128)
            pid[("A", g, l)] = S.plan([
                (lambda t: _v3(t, 8)[:, :, 0:256], wv[:, :, 0:256]),
                (lambda t: _v3(t, 8)[:, :, 256:512], wv[:, :, 2048:2304])])
            pid[("B", g, l)] = S.plan([(lambda t: _v3(t, 8), wv[:, :, 256:768])])
            pid[("C", g, l)] = S.plan([(lambda t: _v3(t, 8), wv[:, :, 768:1280])])
            pid[("D", g, l)] = S.plan([(lambda t: _v3(t, 8), wv[:, :, 1280:1792])])
            pid[("E", g, l)] = S.plan([(lambda t: _v3(t, 8)[:, :, 0:256], wv[:, :, 1792:2048])])
            wo = wout_d[l].rearrange("(k p) n -> p k n", p=128)
            pid[("F", g, l)] = S.plan([(lambda t: _v3(t, 8), wo[:, :, 0:512])])
            pid[("G", g, l)] = S.plan([(lambda t: _v3(t, 8), wo[:, :, 512:1024])])

        plan_ada(0)
        for g in range(2):
            for l in range(L):
                plan_ffn(g, l, 0)
                if g == 0 and l == 0:
                    plan_ada(1)
                plan_mixer(g, l)
                plan_ffn(g, l, 1)

        P.dma("sp", "v0", vtr, cvec_d, writes=[t_vtr])
        P.tr(bank[6][:, 0:128], vtr, ident[:], [t_vtr, t_ident], [t_bank[6]])
        P.actv(scT[:], bank[6][:, 0:16], AF.Silu, [t_bank[6]], [t_scT])
        for l in range(L):
            P.dma("sp", "v0", vtr, vt_d[l], writes=[t_vtr])
            P.tr(bank[6][:, 0:128], vtr, ident[:], [t_vtr, t_ident], [t_bank[6]])
            P.cp("dve", vtT[:, l, :], bank[6][:, 0:128], [t_bank[6]], [t_vtT])
        def ada_compute(l):
            for pc in range(18):
                p_ = pid[("ada", l, pc)]
                s = S.need(p_)
                wv = _v3(wslot[s], 8)
                for fc in range(4):
                    j = pc * 4 + fc
                    for k in range(8):
                        P.mm(bank[5][:, 2 * j:2 * j + 2], wv[:, k, fc * 128:(fc + 1) * 128],
                             scT[:, k:16:8], k == 0, k == 7, [t_slot[s], t_scT], [t_bank[5]])
                S.done(p_)
            P.tt("dve", mods[:, l, :, :], bank[5][:, 0:144].rearrange("p (j v) -> p j v", v=2),
                 vtT[:, l, 0:72].rearrange("p (j o) -> p j o", o=1).to_broadcast([128, 72, 2]),
                 ALU.add, [t_bank[5], t_vtT], [t_mods])
            for n in range(3):
                P.ts("dve", Am[:, l, n, :, :], mods[:, l, (3 * n + 1) * 8:(3 * n + 2) * 8, :], 1.0, None,
                     ALU.add, None, [t_mods], [t_mods])
                P.tt("dve", Am[:, l, n, :, :], Am[:, l, n, :, :],
                     vtT[:, l, 72 + n * 8:80 + n * 8].rearrange("p (j o) -> p j o", o=1).to_broadcast([128, 8, 2]),
                     ALU.mult, [t_mods, t_vtT], [t_mods])
                P.ts("dve", Gm[:, l, n, :, :], mods[:, l, (3 * n + 2) * 8:(3 * n + 3) * 8, :],
                     (1.0 if n == 1 else 0.5), None, ALU.mult, None, [t_mods], [t_mods])


        ada_compute(0)

        PH = []
        _mark = lambda nm: PH.append((nm, len(P.ops["pe"])))
        def norm_tt(l, n, v, tt):
            cols = slice(tt * 512, (tt + 1) * 512)
            for k in range(8):
                P.actv(sq[k % 2][:], xT[:, k, cols], AF.Square, [t_x[k][tt]], [t_sq[k % 2]])
                P.mm(bank[6][:], ones[:], sq[k % 2][:], k == 0, k == 7, [t_sq[k % 2], t_ones], [t_bank[6]])
            P.actv(ms[:], bank[6][:], AF.Sqrt, [t_bank[6], t_chalf], [t_ms], scale=1.0 / D, bias=epsc[:, 0:1])
            P.add("dve", lambda e: e.reciprocal(rstd[:], ms[:]), [t_ms], [t_rstd])
            for k in range(8):
                P.tt("dve", ntmp[k % 2][:], xT[:, k, cols], rstd[:], ALU.mult,
                     [t_x[k][tt], t_rstd], [t_ntmp[k % 2]])
                P.actv(hT[:, k, cols], ntmp[k % 2][:], AF.Identity, [t_ntmp[k % 2], t_mods], [t_h[k][tt]],
                       scale=Am[:, l, n, k, v:v + 1], bias=mods[:, l, 3 * n * 8 + k, v:v + 1])

        cnt = {"up": 0, "dn": 0, "ab": 0}

        def ffn(g, l, j, after_tt):
            n = 0 if j == 0 else 2
            v = g
            pending = []
            last_gi = len(groups) - 1

            def flush():
                while pending:
                    pending.pop(0)()

            for gi, (c0, nfc) in enumerate(groups):
                pg_, pu_, pd_ = pid[("g", g, l, j, gi)], pid[("u", g, l, j, gi)], pid[("d", g, l, j, gi)]
                sg, su, sd = S.need(pg_), S.need(pu_), S.need(pd_)
                vg_, vu_, vd_ = _v3(wslot[sg], 8), _v3(wslot[su], 8), _v3(wslot[sd], 4)
                for tt in range(2):
                    cols = slice(tt * 512, (tt + 1) * 512)
                    ab = cnt["ab"] % 2
                    cnt["ab"] += 1
                    for f in range(nfc):
                        ib = cnt["up"] % 2
                        cnt["up"] += 1
                        for k in range(8):
                            P.mm(bank[ib][:], vg_[:, k, f * 128:(f + 1) * 128], hT[:, k, cols], k == 0, k == 7,
                                 [t_slot[sg], t_h[k][tt]], [t_bank[ib]])
                        for k in range(8):
                            P.mm(bank[2 + ib][:], vu_[:, k, f * 128:(f + 1) * 128], hT[:, k, cols], k == 0, k == 7,
                                 [t_slot[su], t_h[k][tt]], [t_bank[2 + ib]])
                        P.actv(stmp[ib][:], bank[ib][:], AF.Silu, [t_bank[ib]], [t_stmp[ib]])
                        P.tt("dve", aT[ab][:, f, :], stmp[ib][:], bank[2 + ib][:], ALU.mult,
                             [t_stmp[ib], t_bank[2 + ib]], [t_aT[ab][f]])

                    def down(ab=ab, nfc=nfc, sd=sd, vd_=vd_, cols=cols, tt=tt, gi=gi):
                        for dc in range(8):
                            ib2 = 4 + cnt["dn"] % 2
                            cnt["dn"] += 1
                            for f in range(nfc):
                                P.mm(bank[ib2][:], vd_[:, f, dc * 128:(dc + 1) * 128], aT[ab][:, f, :],
                                     f == 0, f == nfc - 1, [t_slot[sd], t_aT[ab][f]], [t_bank[ib2]])
                            P.stt(xT[:, dc, cols], bank[ib2][:], Gm[:, l, n, dc, v:v + 1], xT[:, dc, cols],
                                  ALU.mult, ALU.add, [t_bank[ib2], t_x[dc][tt], t_mods], [t_x[dc][tt]])
                        if gi == last_gi:
                            after_tt(tt)

                    flush()
                    pending.append(down)
                S.done(pg_)
                S.done(pu_)
                if gi > 0:
                    S.done(pid[("d", g, l, j, gi - 1)])
            flush()
            S.done(pid[("d", g, l, j, len(groups) - 1)])

        def load_x(g, after_tt):
            for i in range(8):
                xb = i % 2
                xs_ = xio[:, xb * 1024:(xb + 1) * 1024]
                tl_ = t_or[xb * 4:xb * 4 + 4]
                P.dma("sp", ("xio", xb), xs_, x_d[g, i * 128:(i + 1) * 128, :], writes=tl_)
                for hb in range(2):
                    for kk in range(4):
                        k = hb * 4 + kk
                        P.tr(bank[hb][:, kk * 128:(kk + 1) * 128], xs_[:, k * 128:(k + 1) * 128], ident[:],
                             tl_ + [t_ident], [t_bank[hb]])
                    P.cp("dve" if hb == 0 else "act", xT[:, hb * 4:hb * 4 + 4, i * 128:(i + 1) * 128],
                         bank[hb][:].rearrange("p (k n) -> p k n", k=4), [t_bank[hb]],
                         [t_x[k][i // 4] for k in range(hb * 4, hb * 4 + 4)])
                if i % 4 == 3:
                    after_tt(i // 4)

        def store_y(g):
            for i in range(8):
                xb = i % 2
                xs_ = xio[:, xb * 1024:(xb + 1) * 1024]
                tl_ = t_or[xb * 4:xb * 4 + 4]
                for hb in range(2):
                    for kk in range(4):
                        k = hb * 4 + kk
                        P.tr(bank[hb][:, kk * 128:(kk + 1) * 128], xT[:, k, i * 128:(i + 1) * 128], ident[:],
                             [t_x[k][i // 4], t_ident], [t_bank[hb]])
                    P.cp("dve" if hb == 0 else "act", xs_[:, hb * 512:(hb + 1) * 512], bank[hb][:],
                         [t_bank[hb]], tl_)
                P.dma("sp", ("y", xb), y_d[g, i * 128:(i + 1) * 128, :], xs_, reads=tl_, is_out=True)

        def grp_rms(src, t_src, nh):
            n = nh * 64
            s3 = src.rearrange("p (h d) -> p h d", d=64)
            sqs, t_sqs = sq[0], t_sq[0]
            P.tt("dve", sqs[:, 0:n], src, src, ALU.mult, [t_src], [t_sqs])
            P.red(st8[:, 0:nh], sqs[:, 0:n].rearrange("p (h d) -> p h d", d=64), [t_sqs], [t_st8])
            P.actv(st8[:, 0:nh], st8[:, 0:nh], AF.Sqrt, [t_st8, t_chalf], [t_st8], scale=1.0 / 64, bias=epsc[:, 0:1])
            P.add("dve", lambda e: e.reciprocal(st8[:, 8:8 + nh], st8[:, 0:nh]), [t_st8], [t_st8])
            P.tt("dve", s3, s3, st8[:, 8:8 + nh].rearrange("p (h o) -> p h o", o=1).to_broadcast([128, nh, 64]),
                 ALU.mult, [t_src, t_st8], [t_src])

        def mixer_setup(g, l):
            pwf = stmp[1][:, 0:256].rearrange("p (c n) -> p c n", c=2)
            t_pwf = t_stmp[1]
            RT = ntmp[0][0:31, 0:128]
            RT01 = ntmp[0][0:31, 128:352].rearrange("p (a n) -> p a n", a=2)
            t_RT = t_RT01 = t_ntmp[0]
            P.dma("sp", "ms0", gtb[:], gt_d[l:l + 1, :].to_broadcast([128, 384]), writes=[t_gtb])
            P.ts("dve", gtb[:, 0:64], gtb[:, 0:64], 0.125, None, ALU.mult, None, [t_gtb], [t_gtb])
            P.memset("pool", pwf, 0.0, [t_pwf])
            for c in range(2):
                for gh in range(2):
                    P.dma("sp", "ms1", pwf[gh * 64:(gh + 1) * 64, c, gh * 64:(gh + 1) * 64], poolw_d[l, 2 * c + gh],
                          writes=[t_pwf])
            P.cp("dve", pwb[:], pwf, [t_pwf], [t_pwb])
            for g4 in range(4):
                P.dma("sp", "ms2", stmp[0][:, 0:128], sgw_d[l, g4], writes=[t_stmp[0]])
                P.tr(bank[6][:, 0:128], stmp[0][:, 0:128], ident[:], [t_stmp[0], t_ident], [t_bank[6]])
                P.cp("dve", wsT[:, g4, :], bank[6][:, 0:128], [t_bank[6]], [t_wsT])
            for g4 in range(4):
                P.dma("sp", "ms3", bsb[(g4 % 2) * 64:(g4 % 2 + 1) * 64, g4 // 2, :],
                      sgb_d[l, g4:g4 + 1, :].to_broadcast([64, 128]), writes=[t_bsb])
            if g == 1:
                kctok = aT[1][:, 0:2, :]
                t_kctok = Tl()
                for t in range(2):
                    P.dma("pool", "ms4", kctok[:, t, :].rearrange("p (h d) -> p h d", h=8),
                          ck_d[l][:, t * 128:(t + 1) * 128, :].rearrange("h p d -> p h d"),
                          writes=[t_aT[1][0], t_aT[1][1]])
                    P.dma("pool", "ms5", vc[:, t, :].rearrange("p (h d) -> p h d", h=8),
                          cv_d[l][:, t * 128:(t + 1) * 128, :].rearrange("h p d -> p h d"), writes=[t_vc])
                for t in range(2):
                    for c in range(4):
                        P.tr(bankb[:, c * 128:(c + 1) * 128], kctok[:, t, c * 128:(c + 1) * 128], identb[:],
                             [t_aT[1][0], t_aT[1][1], t_identb], [t_bankb])
                    P.cp("dve", kcT[:, :, t * 128:(t + 1) * 128],
                         bankb[:, 0:512].rearrange("p (c n) -> p c n", c=4), [t_bankb], [t_kc])
                P.dma("sp", "ms6", stmp[1][:, 0:31], rpb_d[l], writes=[t_stmp[1]])
                P.tr(bank[6][0:31, 0:128], stmp[1][:, 0:31], ident[:], [t_stmp[1], t_ident], [t_bank[6]])
                P.cp("dve", RT, bank[6][0:31, 0:128], [t_bank[6]], [t_RT])
                rt3 = RT[:, 0:120].rearrange("p (h r) -> p h r", h=8)
                for half in range(2):
                    P.cp("dve", RT01[:, half, :].rearrange("p (h r) -> p h r", h=8), rt3[:, :, half:half + 14],
                         [t_RT], [t_RT01])
                for b in range(16):
                    bk = bank[b % 2]
                    for qi in range(4):
                        qc = b * 4 + qi
                        for half in range(2):
                            P.mm(bk[half * 64:(half + 1) * 64, qi * 112:(qi + 1) * 112],
                                 dw[0:31, 63 - qc:127 - qc], RT01[:, half, :], True, True,
                                 [t_dw, t_RT01], [t_bank[b % 2]], tp=(0, half * 64))
                    P.actv(EB[:, :, :, b * 4:(b + 1) * 4].rearrange("p h r q -> p q (h r)"),
                           bk[:, 0:448].rearrange("p (q n) -> p q n", q=4), AF.Exp, [t_bank[b % 2]], [t_EB])
                ebv = EB[:].rearrange("p h r q -> p (h r) q")
                P.tt("dve", ebv, ebv, colok[:, :].rearrange("p (o q) -> p o q", o=1).to_broadcast([128, 112, 64]),
                     ALU.mult, [t_EB, t_colok], [t_EB])

        acnt = {"n": 0, "bo": 0}
        SB_ = (0, 1, 4, 5)
        DEPTH = 2

        def mixer(g, l, after_tt):
            v = g
            is_p = (g == 0)
            if stop_after == "premix":
                raise build_program.Stop()
            pA, pB, pC, pD, pE, pF, pG = [pid[(nm, g, l)] for nm in "ABCDEFG"]
            sA, sB, sC, sD, sE = [S.need(p_) for p_ in (pA, pB, pC, pD, pE)]
            vA, vB, vC, vD, vE = [_v3(wslot[s_], 8) for s_ in (sA, sB, sC, sD, sE)]
            qf, kf, vf, gv = stmp[0], stmp[1], ntmp[0], ntmp[1]
            t_qf, t_kf, t_vf, t_gv = t_stmp[0], t_stmp[1], t_ntmp[0], t_ntmp[1]
            qb, t_qb = sq[1], t_sq[1]
            kb, t_kb = aT[0][:, 0, :], t_aT[0][0]
            h3 = lambda ap: ap.rearrange("p (h d) -> p h d", d=64)
            for i in range(8):
                tt = i // 4
                tok = slice(i * 128, (i + 1) * 128)
                for bi, (vw, sl) in enumerate(((vA, sA), (vB, sB), (vC, sC), (vD, sD))):
                    for k in range(8):
                        P.mm(bank[bi][:], hT[:, k, tok], vw[:, k, :], k == 0, k == 7,
                             [t_h[k][tt], t_slot[sl]], [t_bank[bi]])
                P.cp("act", za_tok[:, i, :], bank[0][:, 0:256], [t_bank[0]], [t_za[i]])
                P.actv(gv[:, 0:256], bank[0][:, 256:512], AF.Gelu_apprx_tanh, [t_bank[0]], [t_gv])
                grp_rms(gv[:, 0:256], t_gv, 4)
                P.tt("pool", vg_tok[:, i, :], gv[:, 0:256], gtb[:, 128:384], ALU.mult, [t_gv, t_gtb], [t_vg[i]])
                P.cp("act", qf[:], bank[1][:], [t_bank[1]], [t_qf])
                grp_rms(qf[:], t_qf, 8)
                P.tt("pool", h3(qb[:]), h3(qf[:]),
                     gtb[:, 0:64].rearrange("p (o d) -> p o d", o=1).to_broadcast([128, 8, 64]),
                     ALU.mult, [t_qf, t_gtb], [t_qb])
                for c in range(4):
                    P.tr(bankb[:, c * 128:(c + 1) * 128], qb[:, c * 128:(c + 1) * 128], identb[:],
                         [t_qb, t_identb], [t_bankb])
                P.cp("dve", qT[:, :, tok], bankb[:, 0:512].rearrange("p (c n) -> p c n", c=4), [t_bankb],
                     [t_q[c][i] for c in range(4)])
                P.cp("act", kf[:], bank[2][:], [t_bank[2]], [t_kf])
                grp_rms(kf[:], t_kf, 8)
                P.tt("pool", h3(kf[:]), h3(kf[:]),
                     gtb[:, 64:128].rearrange("p (o d) -> p o d", o=1).to_broadcast([128, 8, 64]),
                     ALU.mult, [t_kf, t_gtb], [t_kf])
                if is_p:
                    b_, s0_ = i // 2, (i % 2) * 128
                    P.dma("sp", "nk", nk_d[b_, l].rearrange("h s d -> s h d")[s0_:s0_ + 128], h3(kf[:]),
                          reads=[t_kf], is_out=True)
                P.cp("act", kb, kf[:], [t_kf], [t_kb])
                for c in range(4):
                    P.tr(bankb[:, 512 + c * 128:512 + (c + 1) * 128], kb[:, c * 128:(c + 1) * 128], identb[:],
                         [t_kb, t_identb], [t_bankb])
                P.cp("dve", kT[:, :, tok], bankb[:, 512:1024].rearrange("p (c n) -> p c n", c=4), [t_bankb],
                     [t_k[c][i] for c in range(4)])
                P.cp("act", v_tok[:, i, :], bank[3][:], [t_bank[3]], [t_v[i]])
                if is_p:
                    P.cp("dve", vf[:], bank[3][:], [t_bank[3]], [t_vf])
                    P.dma("sp", "nv", nv_d[b_, l].rearrange("h s d -> s h d")[s0_:s0_ + 128], h3(vf[:]),
                          reads=[t_vf], is_out=True)
            gv, t_gv = stmp[1], t_stmp[1]
            _mark(" zu")
            for tt in range(2):
                cols = slice(tt * 512, (tt + 1) * 512)
                for c in range(2):
                    for k in range(8):
                        P.mm(bank[4 + c][:], vE[:, k, c * 128:(c + 1) * 128], hT[:, k, cols], k == 0, k == 7,
                             [t_slot[sE], t_h[k][tt]], [t_bank[4 + c]])
                    P.actv(uT[:, c, cols], bank[4 + c][:], AF.Gelu_apprx_tanh, [t_bank[4 + c]], [t_u[c][tt]])
            for p_ in (pA, pB, pC, pD, pE):
                S.done(p_)
            sF, sG = S.need(pF), S.need(pG)
            if stop_after == "proj":
                raise build_program.Stop()

            def variant(j):
                if is_p:
                    return 0 if j % 2 == 0 else 2
                return 0 if j == 0 else (2 if j == 7 else 1)

            def seq_range(j):
                if is_p:
                    return (j // 2 * 2, j // 2 * 2 + 1)
                return (0, 7)

            sqo, t_sqo = sq[0], t_sq[0]
            rs_ap = (ms[:, 0:256], ms[:, 256:512], ntmp[0][:, 0:256])
            rs_t = (t_ms, t_ms, t_ntmp[0])
            ms2, t_ms2 = ntmp[0][:, 256:512], t_ntmp[0]
            oTn = aT[1][:].rearrange("p f n -> p (f n)").rearrange("p (c n) -> p c n", c=8)
            ETB = ((ms, t_ms), (ntmp[0], t_ntmp[0]), (ntmp[1], t_ntmp[1]))

            def attn_S(it):
                n = it["n"]
                c, hp, h = it["c"], it["hp"], it["h"]
                ps_ = slice(hp * 64, (hp + 1) * 64)
                bk, t_bk = bank[SB_[n % 4]], t_bank[SB_[n % 4]]
                pt, t_pt = aT[0][:, n % 4, :], t_aT[0][n % 4]
                if is_p:
                    u_, tok0_ = it["u"], it["u"] * 256
                    for kt in range(2):
                        P.mm(bk[:, kt * 256:(kt + 1) * 256],
                             kT[ps_, c, tok0_ + kt * 128:tok0_ + (kt + 1) * 128], qT[ps_, c, tok0_:tok0_ + 256],
                             True, True, [t_k[c][2 * u_ + kt], t_q[c][2 * u_], t_q[c][2 * u_ + 1]], [t_bk])
                    P.actv(pt, bk[:], AF.Exp, [t_bk], [t_pt])
                else:
                    r, kts, nl = it["r"], it["kts"], it["nl"]
                    et, t_et = ETB[n % 3]
                    qcols = slice(r * 64, (r + 1) * 64)
                    for j_, kt in enumerate(kts):
                        P.mm(bk[:, j_ * 64:(j_ + 1) * 64], kT[ps_, c, kt * 128:(kt + 1) * 128],
                             qT[ps_, c, qcols], True, True, [t_k[c][kt], t_q[c][r // 2]], [t_bk])
                    for t in range(2):
                        P.mm(bk[:, (nl + t) * 64:(nl + t + 1) * 64], kcT[ps_, c, t * 128:(t + 1) * 128],
                             qT[ps_, c, qcols], True, True, [t_kc, t_q[c][r // 2]], [t_bk])
                    P.actv(et[:, 0:nl * 64], bk[:, 0:nl * 64], AF.Exp, [t_bk], [t_et])
                    P.actv(pt[:, nl * 64:(nl + 2) * 64], bk[:, nl * 64:(nl + 2) * 64], AF.Exp, [t_bk], [t_pt])
                    i0 = 2 * kts[0] - r + 7
                    P.tt("dve", pt[:, 0:nl * 64].rearrange("p (j q) -> p j q", q=64),
                         et[:, 0:nl * 64].rearrange("p (j q) -> p j q", q=64),
                         EB[:, h, i0:i0 + 2 * nl - 1:2, :], ALU.mult, [t_et, t_EB], [t_pt])

            def attn_PV(it):
                n = it["n"]
                c, hp, h = it["c"], it["hp"], it["h"]
                ps_ = slice(hp * 64, (hp + 1) * 64)
                pt, t_pt = aT[0][:, n % 4, :], t_aT[0][n % 4]
                if hp == 0:
                    acnt["bo"] += 1
                bo, t_bo = bank[2 + acnt["bo"] % 2], t_bank[2 + acnt["bo"] % 2]
                if is_p:
                    u_ = it["u"]
                    for kt in range(2):
                        P.mm(bo[ps_, 0:256], v_tok[:, 2 * u_ + kt, h * 64:(h + 1) * 64],
                             pt[:, kt * 256:(kt + 1) * 256], kt == 0, kt == 1,
                             [t_v[2 * u_ + kt], t_pt], [t_bo], tp=(0, hp * 64))
                    for kt in range(2):
                        P.mm(bo[ps_, 256:512], ones[:, 0:64], pt[:, kt * 256:(kt + 1) * 256],
                             kt == 0, kt == 1, [t_ones, t_pt], [t_bo], tp=(0, hp * 64))
                    if hp == 1:
                        P.add("dve", lambda e, bo=bo: e.reciprocal(rden[:], bo[:, 256:512]), [t_bo], [t_rden])
                        P.tt("dve", oraw[:, 2 + c, :], bo[:, 0:256], rden[:], ALU.mult, [t_bo, t_rden],
                             [t_or[2 + c]])
                else:
                    r, kts, nl, s0 = it["r"], it["kts"], it["nl"], it["s0"]
                    mml = []
                    for j_, kt in enumerate(kts):
                        if s0 % 2 == 1 and j_ == 0:
                            p0, p1 = 64, 128
                        elif s0 % 2 == 1 and j_ == nl - 1:
                            p0, p1 = 0, 64
                        else:
                            p0, p1 = 0, 128
                        mml.append((v_tok[p0:p1, kt, h * 64:(h + 1) * 64], ones[p0:p1, 0:64],
                                    pt[p0:p1, j_ * 64:(j_ + 1) * 64], p0, [t_v[kt]]))
                    for t in range(2):
                        mml.append((vc[:, t, h * 64:(h + 1) * 64], ones[:, 0:64],
                                    pt[:, (nl + t) * 64:(nl + t + 1) * 64], 0, [t_vc]))
                    for n_, (lv, lo_, rh, p0, rd) in enumerate(mml):
                        P.mm(bo[ps_, 0:64], lv, rh, n_ == 0, n_ == len(mml) - 1, rd + [t_pt], [t_bo],
                             tp=(p0, hp * 64))
                    for n_, (lv, lo_, rh, p0, rd) in enumerate(mml):
                        P.mm(bo[ps_, 64:128], lo_, rh, n_ == 0, n_ == len(mml) - 1, [t_ones, t_pt],
                             [t_bo], tp=(p0, hp * 64))
                    if hp == 1:
                        P.add("dve", lambda e, bo=bo: e.reciprocal(rden[:, 0:64], bo[:, 64:128]), [t_bo],
                              [t_rden])
                        P.tt("dve", oraw[:, 2 + c, (r % 4) * 64:(r % 4 + 1) * 64], bo[:, 0:64], rden[:, 0:64],
                             ALU.mult, [t_bo, t_rden], [t_or[2 + c]])

            for u in range(4):
                tok0 = u * 256
                ucols = slice(tok0, tok0 + 256)
                tt = u // 2
                _mark(" u%d-attn" % u)
                items = []
                for c in range(4):
                    if is_p:
                        for hp in range(2):
                            items.append({"c": c, "hp": hp, "h": 2 * c + hp, "u": u})
                    else:
                        for r in range(4 * u, 4 * u + 4):
                            s0 = min(max(r - 4, 0), 8)
                            kts = list(range(s0 // 2, (s0 + 7) // 2 + 1))
                            for hp in range(2):
                                items.append({"c": c, "hp": hp, "h": 2 * c + hp, "r": r, "s0": s0, "kts": kts,
                                              "nl": len(kts)})
                for it in items:
                    it["n"] = acnt["n"]
                    acnt["n"] += 1
                for idx in range(len(items) + DEPTH):
                    if idx < len(items):
                        attn_S(items[idx])
                    if idx - DEPTH >= 0:
                        attn_PV(items[idx - DEPTH])
                _mark(" u%d-pool" % u)
                for c in range(2):
                    for gh in range(2):
                        g4 = 2 * c + gh
                        for jj in range(2):
                            j = 2 * u + jj
                            lo, hi = seq_range(j)
                            ins = [i_ for i_ in (j - 1, j, j + 1) if lo <= i_ <= hi]
                            for n_, i_ in enumerate(ins):
                                var = 3 if i_ == j - 1 else (4 if i_ == j + 1 else variant(j))
                                P.mm(bank[4][gh * 64:(gh + 1) * 64, c * 256 + jj * 128:c * 256 + (jj + 1) * 128],
                                     za_tok[:, i_, g4 * 64:(g4 + 1) * 64], bands[:, g4 * 5 + var, :],
                                     n_ == 0, n_ == len(ins) - 1, [t_za[i_], t_bands], [t_bank[4]],
                                     tp=(0, gh * 64))
                    for jj in range(2):
                        P.tt("dve", pooledT[:, c, jj * 128:(jj + 1) * 128],
                             bank[4][:, c * 256 + jj * 128:c * 256 + (jj + 1) * 128],
                             invc[:, c * 3 + variant(2 * u + jj), :], ALU.mult, [t_bank[4], t_invc], [t_pl[c]])
                    P.mm(bank[5][:, c * 256:(c + 1) * 256], pwb[:, c, :], pooledT[:, c, :], True, True,
                         [t_pwb, t_pl[c]], [t_bank[5]])
                    P.actv(oraw[:, c, :], bank[5][:, c * 256:(c + 1) * 256], AF.Identity, [t_bank[5], t_vtT],
                           [t_or[c]], scale=vtT[:, l, 104 + c:105 + c])
                for c in range(2):
                    for gh in range(2):
                        g4 = 2 * c + gh
                        for jj in range(2):
                            P.mm(bank[4][gh * 64:(gh + 1) * 64, c * 256 + jj * 128:c * 256 + (jj + 1) * 128],
                                 vg_tok[:, 2 * u + jj, g4 * 64:(g4 + 1) * 64], wsT[:, g4, :], True, True,
                                 [t_vg[2 * u + jj], t_wsT], [t_bank[4]], tp=(0, gh * 64))
                    P.tt("dve", gv[:, 0:256].rearrange("p (j n) -> p j n", j=2),
                         bank[4][:, c * 256:(c + 1) * 256].rearrange("p (j n) -> p j n", j=2),
                         bsb[:, c:c + 1, :].to_broadcast([128, 2, 128]), ALU.add, [t_bank[4], t_bsb], [t_gv])
                    P.tt("pool", oraw[:, 6 + c, :], gv[:, 0:256], uT[:, c, ucols], ALU.mult,
                         [t_gv, t_u[c][tt]], [t_or[6 + c]])
                _mark(" u%d-out" % u)
                segs = ((0, 2, 256.0), (2, 6, 512.0), (6, 8, 256.0))
                for si, (o0, o1, wd_) in enumerate(segs):
                    stb, t_stb = (bank[6], t_bank[6]) if si < 2 else (bank[5], t_bank[5])
                    scol = slice((si % 2) * 256, (si % 2) * 256 + 256)
                    for oc in range(o0, o1):
                        P.actv(sqo[:, 0:256], oraw[:, oc, :], AF.Square, [t_or[oc]], [t_sqo])
                        P.mm(stb[:, scol], ones[:], sqo[:, 0:256], oc == o0, oc == o1 - 1, [t_ones, t_sqo],
                             [t_stb])
                    P.actv(ms2, stb[:, scol], AF.Sqrt, [t_stb, t_chalf], [t_ms2], scale=1.0 / wd_,
                           bias=epsc[:, 0:1])
                    P.add("dve", lambda e, si=si: e.reciprocal(rs_ap[si], ms2), [t_ms2], [rs_t[si]])
                for oc in range(8):
                    si = 0 if oc < 2 else (1 if oc < 6 else 2)
                    P.stt(oTn[:, oc, :], oraw[:, oc, :], vtT[:, l, 96 + oc:97 + oc], rs_ap[si], ALU.mult, ALU.mult,
                          [t_or[oc], t_vtT, rs_t[si]], [t_aT[1][oc // 2]])
                for dc in range(8):
                    sl = sF if dc < 4 else sG
                    vw = _v3(wslot[sl], 8)
                    bw, t_bw = bank[2 + dc % 2], t_bank[2 + dc % 2]
                    for oc in range(8):
                        P.mm(bw[:, 0:256], vw[:, oc, (dc % 4) * 128:(dc % 4 + 1) * 128], oTn[:, oc, :],
                             oc == 0, oc == 7, [t_slot[sl], t_aT[1][oc // 2]], [t_bw])
                    P.stt(xT[:, dc, ucols], bw[:, 0:256], Gm[:, l, 1, dc, v:v + 1], xT[:, dc, ucols],
                          ALU.mult, ALU.add, [t_bw, t_x[dc][tt], t_mods], [t_x[dc][tt]])
                if u % 2 == 1:
                    after_tt(u // 2)
            S.done(pF)
            S.done(pG)

        class _Stop(Exception):
            pass
        build_program.Stop = _Stop
        try:
            for g in range(2):
                _mark("load%d" % g)
                if not EARLY_NORM:
                    nop = lambda tt: None
                    load_x(g, nop)
                    for l in range(L):
                        _mark("ffn%d%d0" % (g, l))
                        norm_tt(l, 0, g, 0); norm_tt(l, 0, g, 1)
                        ffn(g, l, 0, nop)
                        if g == 0 and l == 0:
                            ada_compute(1)
                        _mark("mix%d%d" % (g, l))
                        mixer_setup(g, l)
                        norm_tt(l, 1, g, 0); norm_tt(l, 1, g, 1)
                        mixer(g, l, nop)
                        _mark("ffn%d%d1" % (g, l))
                        norm_tt(l, 2, g, 0); norm_tt(l, 2, g, 1)
                        ffn(g, l, 1, nop)
                    _mark("store%d" % g)
                    store_y(g)
                    continue
                load_x(g, lambda tt, g=g: norm_tt(0, 0, g, tt))
                for l in range(L):
                    _mark("ffn%d%d0" % (g, l))
                    mixer_setup(g, l)
                    ffn(g, l, 0, lambda tt, g=g, l=l: norm_tt(l, 1, g, tt))
                    if g == 0 and l == 0:
                        ada_compute(1)
                    _mark("mix%d%d" % (g, l))
                    mixer(g, l, lambda tt, g=g, l=l: norm_tt(l, 2, g, tt))
                    _mark("ffn%d%d1" % (g, l))
                    if l + 1 < L:
                        ffn(g, l, 1, lambda tt, g=g, l=l: norm_tt(l + 1, 0, g, tt))
                    else:
                        ffn(g, l, 1, lambda tt: None)
                _mark("store%d" % g)
                store_y(g)
        except _Stop:
            pass
        _mark("end")
        build_program.phases = PH
        P.emit(st)
    return nc


def _consts():
    ident = np.eye(128, dtype=np.float32)
    bands = np.zeros((128, 20, 128), np.float32)
    invc = np.zeros((128, 6, 128), np.float32)
    for gi, w in enumerate(WINS):
        h = w // 2
        tp = np.arange(128)[:, None]
        t = np.arange(128)[None, :]
        inb = ((tp >= t - h) & (tp < t + h)).astype(np.float32)
        cnt_first = (np.minimum(t + h, 10 ** 6) - np.maximum(t - h, 0)).astype(np.float32)
        cnt_mid = np.full((1, 128), float(w), np.float32)
        cnt_last = ((128 - t) + h - np.maximum(0, 0)).astype(np.float32)
        cnt_last = np.minimum(cnt_last, w).astype(np.float32)
        eye = np.eye(128, dtype=np.float32)
        bands[:, gi * 5 + 0, :] = inb - eye * cnt_first
        bands[:, gi * 5 + 1, :] = inb - eye * cnt_mid
        bands[:, gi * 5 + 2, :] = inb - eye * cnt_last
        bands[:, gi * 5 + 3, :] = ((tp - 128) >= (t - h)).astype(np.float32)
        bands[:, gi * 5 + 4, :] = ((tp + 128) < (t + h)).astype(np.float32)
        c, half = gi // 2, gi % 2
        for vi, cn in enumerate((cnt_first, cnt_mid, cnt_last)):
            invc[half * 64:(half + 1) * 64, c * 3 + vi, :] = 1.0 / cn
    dw = np.zeros((31, 127), np.float32)
    for j in range(31):
        dw[j, j + 48] = 1.0
    cols = np.arange(64)
    cs = np.clip(cols - 8, 0, 48)
    ok = (cols[None, :] >= cs[:, None]) & (cols[None, :] < cs[:, None] + 16)
    colok = np.concatenate([ok.T, ok.T], 0).astype(np.float32)
    return ident, bands, invc, dw, colok


def make_in_maps(inp):
    f = lambda a: np.ascontiguousarray(np.asarray(a, dtype=np.float32))
    ident, bands, invc, dw, colok = _consts()
    vt = np.zeros((L, 128, 128), np.float32)
    gt = np.zeros((L, 384), np.float32)
    rpb = np.zeros((L, 128, 31), np.float32)
    for l in range(L):
        vt[l, 0:72] = f(inp["ada_b"])[l].reshape(72, 128)
        vt[l, 72:96] = f(inp["norm_g"])[l].reshape(24, 128)
        vt[l, 96:104] = f(inp["out_norm_g"])[l].reshape(8, 128)
        vt[l, 104:106] = f(inp["pool_scale"])[l].reshape(2, 128)
        gt[l, 0:64] = f(inp["q_norm_g"])[l]
        gt[l, 64:128] = f(inp["k_norm_g"])[l]
        gt[l, 128:384] = f(inp["sg_vnorm_g"])[l].reshape(256)
        rpb[l, 0:120] = f(inp["na_rpb"])[l].reshape(120, 31)
    shared = {
        "vt": vt, "gt": gt, "rpb": rpb,
        "poolw": f(inp["pool_w"]), "sgw": f(inp["sg_w"]), "sgb": f(inp["sg_b"]),
        "adaw": f(inp["ada_w"]), "wg": f(inp["ffn_w_gate"]), "wu": f(inp["ffn_w_up"]),
        "wd": f(inp["ffn_w_down"]), "win": f(inp["w_in"]), "wout": f(inp["w_out"]),
        "ident": ident, "bands": bands, "invc": invc, "dw": dw, "colok": colok,
    }
    xp = f(inp["x_prompt"]); xs = f(inp["x_sample"])
    ck = f(inp["cache_k"]); cv = f(inp["cache_v"]); c = f(inp["c"]); cc = f(inp["c_ctx"])
    maps = []
    for i in range(NCORES):
        x = np.stack([xp[4 * i:4 * i + 4].reshape(1024, D), xs[i]], 0)
        cvec = np.zeros((128, 128), np.float32)
        cvec[0:8] = cc.reshape(8, 128)
        cvec[8:16] = c[i].reshape(8, 128)
        m = dict(shared)
        m.update({"x": np.ascontiguousarray(x), "ck": np.ascontiguousarray(ck[i]),
                  "cv": np.ascontiguousarray(cv[i]), "cvec": cvec})
        maps.append(m)
    return maps


_NC_CACHE = {}


def kernel(**inputs):
    if "nc" not in _NC_CACHE:
        _NC_CACHE["nc"] = build_program()
    nc = _NC_CACHE["nc"]
    maps = make_in_maps(inputs)
    res = run_bass_kernel_spmd(nc, maps, core_ids=list(range(NCORES)))
    r = res.results
    yp = np.concatenate([r[i]["y"][0].reshape(4, 256, D) for i in range(NCORES)], 0)
    ys = np.stack([r[i]["y"][1] for i in range(NCORES)], 0)
    nk = np.concatenate([r[i]["nk"] for i in range(NCORES)], 0)
    nv = np.concatenate([r[i]["nv"] for i in range(NCORES)], 0)
    return (yp.astype(np.float32), ys.astype(np.float32), nk.astype(np.float32), nv.astype(np.float32))
```

```python
import numpy as np
from contextlib import ExitStack
import concourse.bass as bass
import concourse.mybir as mybir
from concourse.bass_utils import run_bass_kernel_spmd

F32 = mybir.dt.float32
BF16 = mybir.dt.bfloat16
AF = mybir.ActivationFunctionType
ALU = mybir.AluOpType
AX = mybir.AxisListType

D = 1024
DFF = 2816
L = 2
NH = 8
EPS = 1e-6
NCORES = 8
WINS = (2, 4, 8, 16)
EARLY_NORM = True


class Tl:
    __slots__ = ("name", "w", "r")

    def __init__(self, name=""):
        self.name = name
        self.w = None
        self.r = {}


class Op:
    __slots__ = ("eng", "fn", "deps", "marked", "count", "dkey", "dcount", "is_dma", "uid")


class Prog:
    ENGS = ("pe", "act", "dve", "pool", "sp")

    def __init__(self, nc):
        self.nc = nc
        self.ops = {e: [] for e in self.ENGS}
        self.latest_dma = {}
        self.dma_counts = {}
        self.uid = 0
        self.out_keys = set()

    def add(self, eng, fn, reads=(), writes=(), dma=None, is_out=False):
        op = Op()
        op.eng = eng
        op.fn = fn
        op.marked = False
        op.count = None
        op.is_dma = dma is not None
        op.dkey = dma
        op.uid = self.uid
        self.uid += 1
        deps = {}
        for t in reads:
            if t.w is not None:
                deps[t.w.uid] = t.w
        for t in writes:
            if t.w is not None:
                deps[t.w.uid] = t.w
            for d in t.r.values():
                deps[d.uid] = d
        final = {}
        for d in deps.values():
            if d.is_dma:
                d = self.latest_dma[d.dkey]
            elif d.eng == "pe" and eng == "pe" and not op.is_dma:
                continue
            else:
                d.marked = True
            final[d.uid] = d
        op.deps = list(final.values())
        if op.is_dma:
            c = self.dma_counts.get(dma, 0) + 16
            self.dma_counts[dma] = c
            op.dcount = c
            self.latest_dma[dma] = op
            if is_out:
                self.out_keys.add(dma)
        rkey = ("dma", op.uid) if op.is_dma else eng
        for t in reads:
            t.r[rkey] = op
        for t in writes:
            t.w = op
            t.r = {}
        self.ops[eng].append(op)
        return op

    def dma(self, eng, key, out, in_, reads=(), writes=(), is_out=False):
        return self.add(eng, lambda e: e.dma_start(out=out, in_=in_), reads, writes,
                        dma=key, is_out=is_out)

    def mm(self, out, lhsT, rhs, start, stop, reads, writes, tp=None):
        if tp is None:
            return self.add("pe", lambda e: e.matmul(out, lhsT, rhs, start=start, stop=stop), reads, writes)
        return self.add("pe", lambda e: e.matmul(out, lhsT, rhs, start=start, stop=stop, tile_position=tp),
                        reads, writes)

    def tr(self, out, in_, ident, reads, writes):
        return self.add("pe", lambda e: e.transpose(out, in_, ident), reads, writes)

    def actv(self, out, in_, func, reads, writes, scale=None, bias=None):
        kw = {}
        if scale is not None:
            kw["scale"] = scale
        if bias is not None:
            kw["bias"] = bias
        return self.add("act", lambda e: e.activation(out, in_, func, **kw), reads, writes)

    def tt(self, eng, out, in0, in1, op, reads, writes):
        return self.add(eng, lambda e: e.tensor_tensor(out, in0, in1, op=op), reads, writes)

    def ts(self, eng, out, in0, s1, s2, op0, op1, reads, writes):
        if s2 is None:
            return self.add(eng, lambda e: e.tensor_scalar(out, in0, s1, None, op0=op0), reads, writes)
        return self.add(eng, lambda e: e.tensor_scalar(out, in0, s1, s2, op0=op0, op1=op1), reads, writes)

    def stt(self, out, in0, scalar, in1, op0, op1, reads, writes):
        return self.add("dve", lambda e: e.scalar_tensor_tensor(out, in0, scalar, in1, op0=op0, op1=op1),
                        reads, writes)

    def cp(self, eng, out, in_, reads, writes):
        if eng == "act":
            return self.add("act", lambda e: e.copy(out, in_), reads, writes)
        return self.add(eng, lambda e: e.tensor_copy(out, in_), reads, writes)

    def red(self, out, in_, reads, writes):
        return self.add("dve", lambda e: e.tensor_reduce(out, in_, axis=AX.X, op=ALU.add), reads, writes)

    def memset(self, eng, ap, val, writes):
        return self.add(eng, lambda e: e.memset(ap, val), (), writes)

    def emit(self, stack):
        nc = self.nc
        for e in self.ENGS:
            c = 0
            for op in self.ops[e]:
                if op.marked and not op.is_dma:
                    c += 1
                    op.count = c
        esem = {e: stack.enter_context(nc.semaphore("s_" + e)) for e in self.ENGS}
        dsem = {k: stack.enter_context(nc.semaphore("d_%d" % i)) for i, k in enumerate(self.dma_counts)}
        block = stack.enter_context(nc.Block())
        ops = self.ops
        out_final = [(dsem[k], self.dma_counts[k]) for k in sorted(self.out_keys, key=str)]

        def run(ename, e):
            known = {}
            for op in ops[ename]:
                for d in op.deps:
                    if d.is_dma:
                        s, v = dsem[d.dkey], d.dcount
                    else:
                        s, v = esem[d.eng], d.count
                    kid = id(s)
                    if known.get(kid, 0) >= v:
                        continue
                    known[kid] = v
                    e.wait_ge(s, v)
                ins = op.fn(e)
                if op.is_dma:
                    ins.then_inc(dsem[op.dkey], 16)
                elif op.marked:
                    ins.then_inc(esem[ename], 1)
            if ename == "sp":
                for s, v in out_final:
                    e.wait_ge(s, v)

        @block.sync
        def _(e):
            run("sp", e)

        @block.tensor
        def _(e):
            run("pe", e)

        @block.scalar
        def _(e):
            run("act", e)

        @block.vector
        def _(e):
            run("dve", e)

        @block.gpsimd
        def _(e):
            run("pool", e)


class Stream:
    NS = 6
    LOOK = 5

    def __init__(self, P, slots, t_slots):
        self.P = P
        self.slots = slots
        self.t_slots = t_slots
        self.pieces = []
        self.issued = 0
        self.freed = 0
        self.want = 0

    def plan(self, parts):
        self.pieces.append(parts)
        return len(self.pieces) - 1

    def pump(self):
        while self.issued <= self.want and self.issued < len(self.pieces) and self.issued < self.freed + self.NS:
            i = self.issued
            s = i % self.NS
            for dst_fn, src in self.pieces[i]:
                self.P.dma("pool", ("w", s), dst_fn(self.slots[s]), src, writes=[self.t_slots[s]])
            self.issued += 1

    def need(self, p):
        self.want = max(self.want, p + self.LOOK)
        self.pump()
        assert self.issued > p, (self.issued, p, self.freed)
        return p % self.NS

    def done(self, p):
        self.freed = max(self.freed, p + 1)
        self.pump()


def _v3(t, k):
    return t[:].rearrange("p (k n) -> p k n", k=k)


def build_program(stop_after=None):
    nc = bass.Bass("TRN2", target_bir_lowering=False)

    def din(name, shape):
        return nc.dram_tensor(name, shape, F32, kind="ExternalInput").ap()

    def dout(name, shape):
        return nc.dram_tensor(name, shape, F32, kind="ExternalOutput").ap()

    x_d = din("x", [2, 1024, D])
    ck_d = din("ck", [L, NH, 256, 64])
    cv_d = din("cv", [L, NH, 256, 64])
    cvec_d = din("cvec", [128, 128])
    vt_d = din("vt", [L, 128, 128])
    gt_d = din("gt", [L, 384])
    poolw_d = din("poolw", [L, 4, 64, 64])
    sgw_d = din("sgw", [L, 4, 128, 128])
    sgb_d = din("sgb", [L, 4, 128])
    rpb_d = din("rpb", [L, 128, 31])
    adaw_d = din("adaw", [L, D, 9 * D])
    wg_d = din("wg", [L, 2, D, DFF])
    wu_d = din("wu", [L, 2, D, DFF])
    wd_d = din("wd", [L, 2, DFF, D])
    win_d = din("win", [L, D, 2304])
    wout_d = din("wout", [L, D, D])
    ident_d = din("ident", [128, 128])
    bands_d = din("bands", [128, 20, 128])
    invc_d = din("invc", [128, 6, 128])
    dw_d = din("dw", [31, 127])
    colok_d = din("colok", [128, 64])
    y_d = dout("y", [2, 1024, D])
    nk_d = dout("nk", [4, L, NH, 256, 64])
    nv_d = dout("nv", [4, L, NH, 256, 64])

    P = Prog(nc)
    st = ExitStack()

    def sb(name, shape, dt=F32):
        return st.enter_context(nc.sbuf_tensor("s_" + name, shape, dt))

    with st:
        bank = [st.enter_context(nc.psum_tensor("bank%d" % i, [128, 512], F32)) for i in range(7)]
        bankb = st.enter_context(nc.psum_tensor("bankb", [128, 1024], BF16))
        t_bank = [Tl("bank%d" % i) for i in range(7)]
        t_bankb = Tl("bankb")

        ident = sb("ident", [128, 128]); t_ident = Tl()
        identb = sb("identb", [128, 128], BF16); t_identb = Tl()
        ones = sb("ones", [128, 128], BF16); t_ones = Tl()
        epsc = sb("epsc", [128, 1]); t_chalf = Tl()
        bands = sb("bands", [128, 20, 128], BF16); t_bands = Tl()
        invc = sb("invc", [128, 6, 128]); t_invc = Tl()
        dw = sb("dw", [31, 127]); t_dw = Tl()
        colok = sb("colok", [128, 64]); t_colok = Tl()
        P.dma("sp", "c0", ident[:], ident_d, writes=[t_ident])
        P.dma("sp", "c1", invc[:], invc_d, writes=[t_invc])
        P.dma("sp", "c2", dw[:], dw_d, writes=[t_dw])
        P.dma("sp", "c3", colok[:], colok_d, writes=[t_colok])
        P.dma("pool", "c4", bands[:], bands_d, writes=[t_bands])
        P.cp("dve", identb[:], ident[:], [t_ident], [t_identb])
        P.memset("pool", ones[:], 1.0, [t_ones])
        P.memset("pool", epsc[:], EPS, [t_chalf])

        xT = sb("xT", [128, 8, 1024]); t_x = [[Tl() for _ in range(2)] for _ in range(8)]
        hT = sb("hT", [128, 8, 1024], BF16); t_h = [[Tl() for _ in range(2)] for _ in range(8)]
        wslot = [sb("wslot%d" % i, [128, 4096], BF16) for i in range(Stream.NS)]
        t_slot = [Tl("slot%d" % i) for i in range(Stream.NS)]
        S = Stream(P, wslot, t_slot)
        aT = [sb("aT%d" % i, [128, 4, 512], BF16) for i in range(2)]
        t_aT = [[Tl() for _ in range(4)] for _ in range(2)]
        stmp = [sb("stmp%d" % i, [128, 512]) for i in range(2)]; t_stmp = [Tl(), Tl()]
        xio = sb("xio", [128, 2048]); t_or = [Tl() for _ in range(8)]
        oraw = xio[:].rearrange("p (c n) -> p c n", c=8)
        sq = [sb("sq%d" % i, [128, 512], BF16) for i in range(2)]; t_sq = [Tl(), Tl()]
        ms = sb("ms", [128, 512]); t_ms = Tl()
        rstd, t_rstd = ms, t_ms
        ntmp = [sb("ntmp%d" % i, [128, 512]) for i in range(2)]; t_ntmp = [Tl(), Tl()]
        mods = sb("mods", [128, L, 72, 2]); t_mods = Tl()
        Am = sb("Am", [128, L, 3, 8, 2])
        Gm = sb("Gm", [128, L, 3, 8, 2])
        vtT = sb("vtT", [128, L, 128]); t_vtT = Tl()
        scT = sb("scT", [128, 16], BF16); t_scT = Tl()
        vtr_t = sb("vtr", [128, 128]); t_vtr = Tl()
        vtr = vtr_t[:]
        za_tok = sb("za_tok", [128, 8, 256], BF16); t_za = [Tl() for _ in range(8)]
        vg_tok = sb("vg_tok", [128, 8, 256], BF16); t_vg = [Tl() for _ in range(8)]
        v_tok = sb("v_tok", [128, 8, 512], BF16); t_v = [Tl() for _ in range(8)]
        qT = sb("qT", [128, 4, 1024], BF16); t_q = [[Tl() for _ in range(8)] for _ in range(4)]
        kT = sb("kT", [128, 4, 1024], BF16); t_k = [[Tl() for _ in range(8)] for _ in range(4)]
        uT = sb("uT", [128, 2, 1024], BF16); t_u = [[Tl() for _ in range(2)] for _ in range(2)]
        kcT = sb("kcT", [128, 4, 256], BF16); t_kc = Tl()
        qkz = sb("qkz", [128, 1280]); t_qkz = Tl()
        vc = sb("vc", [128, 2, 512], BF16); t_vc = Tl()
        pooledT = sb("pooledT", [128, 2, 256], BF16); t_pl = [Tl(), Tl()]
        EB = sb("EB", [128, 8, 14, 64], BF16); t_EB = Tl()
        wsT = sb("wsT", [128, 4, 128], BF16); t_wsT = Tl()
        pwb = sb("pwb", [128, 2, 128], BF16); t_pwb = Tl()
        bsb = sb("bsb", [128, 2, 128]); t_bsb = Tl()
        gtb = sb("gtb", [128, 384]); t_gtb = Tl()
        rden = sb("rden", [128, 256]); t_rden = Tl()
        st8 = sb("st8", [128, 40]); t_st8 = Tl()

        pid = {}
        groups = [(0, 4), (4, 4), (8, 2), (10, 4), (14, 4), (18, 4)]

        def plan_ada(l):
            for pc in range(18):
                pid[("ada", l, pc)] = S.plan([(
                    lambda t: _v3(t, 8),
                    adaw_d[l].rearrange("(k p) n -> p k n", p=128)[:, :, pc * 512:(pc + 1) * 512])])

        def plan_ffn(g, l, j):
            for gi, (c0, nfc) in enumerate(groups):
                w = nfc * 128
                pid[("g", g, l, j, gi)] = S.plan([(
                    lambda t, w=w: _v3(t, 8)[:, :, 0:w],
                    wg_d[l, j].rearrange("(k p) n -> p k n", p=128)[:, :, c0 * 128:c0 * 128 + w])])
                pid[("u", g, l, j, gi)] = S.plan([(
                    lambda t, w=w: _v3(t, 8)[:, :, 0:w],
                    wu_d[l, j].rearrange("(k p) n -> p k n", p=128)[:, :, c0 * 128:c0 * 128 + w])])
                pid[("d", g, l, j, gi)] = S.plan([(
                    lambda t, nfc=nfc: _v3(t, 4)[:, 0:nfc, :],
                    wd_d[l, j][c0 * 128:c0 * 128 + w, :].rearrange("(f p) n -> p f n", p=128))])

        def plan_mixer(g, l):
            wv = win_d[l].rearrange("(k p) n -> p k n", p=128)
            pid[("A", g, l)] = S.plan([
                (lambda t: _v3(t, 8)[:, :, 0:256], wv[:, :, 0:256]),
                (lambda t: _v3(t, 8)[:, :, 256:512], wv[:, :, 2048:2304])])
            pid[("B", g, l)] = S.plan([(lambda t: _v3(t, 8), wv[:, :, 256:768])])
            pid[("C", g, l)] = S.plan([(lambda t: _v3(t, 8), wv[:, :, 768:1280])])
            pid[("D", g, l)] = S.plan([(lambda t: _v3(t, 8), wv[:, :, 1280:1792])])
            pid[("E", g, l)] = S.plan([(lambda t: _v3(t, 8)[:, :, 0:256], wv[:, :, 1792:2048])])
            wo = wout_d[l].rearrange("(k p) n -> p k n", p=128)
            pid[("F", g, l)] = S.plan([(lambda t: _v3(t, 8), wo[:, :, 0:512])])
            pid[("G", g, l)] = S.plan([(lambda t: _v3(t, 8), wo[:, :, 512:1024])])

        plan_ada(0)
        for g in range(2):
            for l in range(L):
                plan_ffn(g, l, 0)
                if g == 0 and l == 0:
                    plan_ada(1)
                plan_mixer(g, l)
                plan_ffn(g, l, 1)

        P.dma("sp", "v0", vtr, cvec_d, writes=[t_vtr])
        P.tr(bank[6][:, 0:128], vtr, ident[:], [t_vtr, t_ident], [t_bank[6]])
        P.actv(scT[:], bank[6][:, 0:16], AF.Silu, [t_bank[6]], [t_scT])
        for l in range(L):
            P.dma("sp", "v0", vtr, vt_d[l], writes=[t_vtr])
            P.tr(bank[6][:, 0:128], vtr, ident[:], [t_vtr, t_ident], [t_bank[6]])
            P.cp("dve", vtT[:, l, :], bank[6][:, 0:128], [t_bank[6]], [t_vtT])
        def ada_compute(l):
            for pc in range(18):
                p_ = pid[("ada", l, pc)]
                s = S.need(p_)
                wv = _v3(wslot[s], 8)
                for fc in range(4):
                    j = pc * 4 + fc
                    for k in range(8):
                        P.mm(bank[5][:, 2 * j:2 * j + 2], wv[:, k, fc * 128:(fc + 1) * 128],
                             scT[:, k:16:8], k == 0, k == 7, [t_slot[s], t_scT], [t_bank[5]])
                S.done(p_)
            P.tt("dve", mods[:, l, :, :], bank[5][:, 0:144].rearrange("p (j v) -> p j v", v=2),
                 vtT[:, l, 0:72].rearrange("p (j o) -> p j o", o=1).to_broadcast([128, 72, 2]),
                 ALU.add, [t_bank[5], t_vtT], [t_mods])
            for n in range(3):
                P.ts("dve", Am[:, l, n, :, :], mods[:, l, (3 * n + 1) * 8:(3 * n + 2) * 8, :], 1.0, None,
                     ALU.add, None, [t_mods], [t_mods])
                P.tt("dve", Am[:, l, n, :, :], Am[:, l, n, :, :],
                     vtT[:, l, 72 + n * 8:80 + n * 8].rearrange("p (j o) -> p j o", o=1).to_broadcast([128, 8, 2]),
                     ALU.mult, [t_mods, t_vtT], [t_mods])
                P.ts("dve", Gm[:, l, n, :, :], mods[:, l, (3 * n + 2) * 8:(3 * n + 3) * 8, :],
                     (1.0 if n == 1 else 0.5), None, ALU.mult, None, [t_mods], [t_mods])


        ada_compute(0)

        PH = []
        _mark = lambda nm: PH.append((nm, len(P.ops["pe"])))
        def norm_tt(l, n, v, tt):
            cols = slice(tt * 512, (tt + 1) * 512)
            for k in range(8):
                P.actv(sq[k % 2][:], xT[:, k, cols], AF.Square, [t_x[k][tt]], [t_sq[k % 2]])
                P.mm(bank[6][:], ones[:], sq[k % 2][:], k == 0, k == 7, [t_sq[k % 2], t_ones], [t_bank[6]])
            P.actv(ms[:], bank[6][:], AF.Sqrt, [t_bank[6], t_chalf], [t_ms], scale=1.0 / D, bias=epsc[:, 0:1])
            P.add("dve", lambda e: e.reciprocal(rstd[:], ms[:]), [t_ms], [t_rstd])
            for k in range(8):
                P.tt("dve", ntmp[k % 2][:], xT[:, k, cols], rstd[:], ALU.mult,
                     [t_x[k][tt], t_rstd], [t_ntmp[k % 2]])
                P.actv(hT[:, k, cols], ntmp[k % 2][:], AF.Identity, [t_ntmp[k % 2], t_mods], [t_h[k][tt]],
                       scale=Am[:, l, n, k, v:v + 1], bias=mods[:, l, 3 * n * 8 + k, v:v + 1])

        cnt = {"up": 0, "dn": 0, "ab": 0}

        def ffn(g, l, j, after_tt):
            n = 0 if j == 0 else 2
            v = g
            pending = []
            last_gi = len(groups) - 1

            def flush():
                while pending:
                    pending.pop(0)()

            for gi, (c0, nfc) in enumerate(groups):
                pg_, pu_, pd_ = pid[("g", g, l, j, gi)], pid[("u", g, l, j, gi)], pid[("d", g, l, j, gi)]
                sg, su, sd = S.need(pg_), S.need(pu_), S.need(pd_)
                vg_, vu_, vd_ = _v3(wslot[sg], 8), _v3(wslot[su], 8), _v3(wslot[sd], 4)
                for tt in range(2):
                    cols = slice(tt * 512, (tt + 1) * 512)
                    ab = cnt["ab"] % 2
                    cnt["ab"] += 1
                    for f in range(nfc):
                        ib = cnt["up"] % 2
                        cnt["up"] += 1
                        for k in range(8):
                            P.mm(bank[ib][:], vg_[:, k, f * 128:(f + 1) * 128], hT[:, k, cols], k == 0, k == 7,
                                 [t_slot[sg], t_h[k][tt]], [t_bank[ib]])
                        for k in range(8):
                            P.mm(bank[2 + ib][:], vu_[:, k, f * 128:(f + 1) * 128], hT[:, k, cols], k == 0, k == 7,
                                 [t_slot[su], t_h[k][tt]], [t_bank[2 + ib]])
                        P.actv(stmp[ib][:], bank[ib][:], AF.Silu, [t_bank[ib]], [t_stmp[ib]])
                        P.tt("dve", aT[ab][:, f, :], stmp[ib][:], bank[2 + ib][:], ALU.mult,
                             [t_stmp[ib], t_bank[2 + ib]], [t_aT[ab][f]])

                    def down(ab=ab, nfc=nfc, sd=sd, vd_=vd_, cols=cols, tt=tt, gi=gi):
                        for dc in range(8):
                            ib2 = 4 + cnt["dn"] % 2
                            cnt["dn"] += 1
                            for f in range(nfc):
                                P.mm(bank[ib2][:], vd_[:, f, dc * 128:(dc + 1) * 128], aT[ab][:, f, :],
                                     f == 0, f == nfc - 1, [t_slot[sd], t_aT[ab][f]], [t_bank[ib2]])
                            P.stt(xT[:, dc, cols], bank[ib2][:], Gm[:, l, n, dc, v:v + 1], xT[:, dc, cols],
                                  ALU.mult, ALU.add, [t_bank[ib2], t_x[dc][tt], t_mods], [t_x[dc][tt]])
                        if gi == last_gi:
                            after_tt(tt)

                    flush()
                    if gi == last_gi:
                        down()
                    else:
                        pending.append(down)
                S.done(pg_)
                S.done(pu_)
                if gi > 0:
                    S.done(pid[("d", g, l, j, gi - 1)])
            flush()
            S.done(pid[("d", g, l, j, len(groups) - 1)])

        def load_x(g, after_tt):
            for i in range(8):
                xb = i % 2
                xs_ = xio[:, xb * 1024:(xb + 1) * 1024]
                tl_ = t_or[xb * 4:xb * 4 + 4]
                P.dma("sp", ("xio", xb), xs_, x_d[g, i * 128:(i + 1) * 128, :], writes=tl_)
                for hb in range(2):
                    for kk in range(4):
                        k = hb * 4 + kk
                        P.tr(bank[hb][:, kk * 128:(kk + 1) * 128], xs_[:, k * 128:(k + 1) * 128], ident[:],
                             tl_ + [t_ident], [t_bank[hb]])
                    P.cp("dve" if hb == 0 else "act", xT[:, hb * 4:hb * 4 + 4, i * 128:(i + 1) * 128],
                         bank[hb][:].rearrange("p (k n) -> p k n", k=4), [t_bank[hb]],
                         [t_x[k][i // 4] for k in range(hb * 4, hb * 4 + 4)])
                if i % 4 == 3:
                    after_tt(i // 4)

        def store_y(g):
            for i in range(8):
                xb = i % 2
                xs_ = xio[:, xb * 1024:(xb + 1) * 1024]
                tl_ = t_or[xb * 4:xb * 4 + 4]
                for hb in range(2):
                    for kk in range(4):
                        k = hb * 4 + kk
                        P.tr(bank[hb][:, kk * 128:(kk + 1) * 128], xT[:, k, i * 128:(i + 1) * 128], ident[:],
                             [t_x[k][i // 4], t_ident], [t_bank[hb]])
                    P.cp("dve" if hb == 0 else "act", xs_[:, hb * 512:(hb + 1) * 512], bank[hb][:],
                         [t_bank[hb]], tl_)
                P.dma("sp", ("y", xb), y_d[g, i * 128:(i + 1) * 128, :], xs_, reads=tl_, is_out=True)

        def grp_rms(src, t_src, nh):
            n = nh * 64
            s3 = src.rearrange("p (h d) -> p h d", d=64)
            sqs, t_sqs = sq[0], t_sq[0]
            P.tt("dve", sqs[:, 0:n], src, src, ALU.mult, [t_src], [t_sqs])
            P.red(st8[:, 0:nh], sqs[:, 0:n].rearrange("p (h d) -> p h d", d=64), [t_sqs], [t_st8])
            P.actv(st8[:, 0:nh], st8[:, 0:nh], AF.Sqrt, [t_st8, t_chalf], [t_st8], scale=1.0 / 64, bias=epsc[:, 0:1])
            P.add("dve", lambda e: e.reciprocal(st8[:, 8:8 + nh], st8[:, 0:nh]), [t_st8], [t_st8])
            P.tt("dve", s3, s3, st8[:, 8:8 + nh].rearrange("p (h o) -> p h o", o=1).to_broadcast([128, nh, 64]),
                 ALU.mult, [t_src, t_st8], [t_src])

        def mixer_setup(g, l):
            pwf = stmp[1][:, 0:256].rearrange("p (c n) -> p c n", c=2)
            t_pwf = t_stmp[1]
            RT = ntmp[0][0:31, 0:128]
            RT01 = ntmp[0][0:31, 128:352].rearrange("p (a n) -> p a n", a=2)
            t_RT = t_RT01 = t_ntmp[0]
            P.dma("sp", "ms0", gtb[:], gt_d[l:l + 1, :].to_broadcast([128, 384]), writes=[t_gtb])
            P.ts("dve", gtb[:, 0:64], gtb[:, 0:64], 0.125, None, ALU.mult, None, [t_gtb], [t_gtb])
            P.memset("pool", pwf, 0.0, [t_pwf])
            for c in range(2):
                for gh in range(2):
                    P.dma("sp", "ms1", pwf[gh * 64:(gh + 1) * 64, c, gh * 64:(gh + 1) * 64], poolw_d[l, 2 * c + gh],
                          writes=[t_pwf])
            P.cp("dve", pwb[:], pwf, [t_pwf], [t_pwb])
            for g4 in range(4):
                P.dma("sp", "ms2", stmp[0][:, 0:128], sgw_d[l, g4], writes=[t_stmp[0]])
                P.tr(bank[6][:, 0:128], stmp[0][:, 0:128], ident[:], [t_stmp[0], t_ident], [t_bank[6]])
                P.cp("dve", wsT[:, g4, :], bank[6][:, 0:128], [t_bank[6]], [t_wsT])
            for g4 in range(4):
                P.dma("sp", "ms3", bsb[(g4 % 2) * 64:(g4 % 2 + 1) * 64, g4 // 2, :],
                      sgb_d[l, g4:g4 + 1, :].to_broadcast([64, 128]), writes=[t_bsb])
            if g == 1:
                kctok = aT[1][:, 0:2, :]
                t_kctok = Tl()
                for t in range(2):
                    P.dma("pool", "ms4", kctok[:, t, :].rearrange("p (h d) -> p h d", h=8),
                          ck_d[l][:, t * 128:(t + 1) * 128, :].rearrange("h p d -> p h d"),
                          writes=[t_aT[1][0], t_aT[1][1]])
                    P.dma("pool", "ms5", vc[:, t, :].rearrange("p (h d) -> p h d", h=8),
                          cv_d[l][:, t * 128:(t + 1) * 128, :].rearrange("h p d -> p h d"), writes=[t_vc])
                for t in range(2):
                    for c in range(4):
                        P.tr(bankb[:, c * 128:(c + 1) * 128], kctok[:, t, c * 128:(c + 1) * 128], identb[:],
                             [t_aT[1][0], t_aT[1][1], t_identb], [t_bankb])
                    P.cp("dve", kcT[:, :, t * 128:(t + 1) * 128],
                         bankb[:, 0:512].rearrange("p (c n) -> p c n", c=4), [t_bankb], [t_kc])
                P.dma("sp", "ms6", stmp[1][:, 0:31], rpb_d[l], writes=[t_stmp[1]])
                P.tr(bank[6][0:31, 0:128], stmp[1][:, 0:31], ident[:], [t_stmp[1], t_ident], [t_bank[6]])
                P.cp("dve", RT, bank[6][0:31, 0:128], [t_bank[6]], [t_RT])
                rt3 = RT[:, 0:120].rearrange("p (h r) -> p h r", h=8)
                for half in range(2):
                    P.cp("dve", RT01[:, half, :].rearrange("p (h r) -> p h r", h=8), rt3[:, :, half:half + 14],
                         [t_RT], [t_RT01])
                for b in range(16):
                    bk = bank[b % 2]
                    for qi in range(4):
                        qc = b * 4 + qi
                        for half in range(2):
                            P.mm(bk[half * 64:(half + 1) * 64, qi * 112:(qi + 1) * 112],
                                 dw[0:31, 63 - qc:127 - qc], RT01[:, half, :], True, True,
                                 [t_dw, t_RT01], [t_bank[b % 2]], tp=(0, half * 64))
                    P.actv(EB[:, :, :, b * 4:(b + 1) * 4].rearrange("p h r q -> p q (h r)"),
                           bk[:, 0:448].rearrange("p (q n) -> p q n", q=4), AF.Exp, [t_bank[b % 2]], [t_EB])
                ebv = EB[:].rearrange("p h r q -> p (h r) q")
                P.tt("dve", ebv, ebv, colok[:, :].rearrange("p (o q) -> p o q", o=1).to_broadcast([128, 112, 64]),
                     ALU.mult, [t_EB, t_colok], [t_EB])

        acnt = {"n": 0, "bo": 0}
        SB_ = (0, 1, 4, 5)
        DEPTH = 2

        def mixer(g, l, after_tt):
            v = g
            is_p = (g == 0)
            if stop_after == "premix":
                raise build_program.Stop()
            pA, pB, pC, pD, pE, pF, pG = [pid[(nm, g, l)] for nm in "ABCDEFG"]
            sA, sB, sC, sD, sE = [S.need(p_) for p_ in (pA, pB, pC, pD, pE)]
            vA, vB, vC, vD, vE = [_v3(wslot[s_], 8) for s_ in (sA, sB, sC, sD, sE)]
            qf, kf, vf, gv = stmp[0], stmp[1], ntmp[0], ntmp[1]
            t_qf, t_kf, t_vf, t_gv = t_stmp[0], t_stmp[1], t_ntmp[0], t_ntmp[1]
            qb, t_qb = sq[1], t_sq[1]
            kb, t_kb = aT[0][:, 0, :], t_aT[0][0]
            h3 = lambda ap: ap.rearrange("p (h d) -> p h d", d=64)
            for i in range(8):
                tt = i // 4
                tok = slice(i * 128, (i + 1) * 128)
                for bi, (vw, sl) in enumerate(((vA, sA), (vB, sB), (vC, sC), (vD, sD))):
                    for k in range(8):
                        P.mm(bank[bi][:], hT[:, k, tok], vw[:, k, :], k == 0, k == 7,
                             [t_h[k][tt], t_slot[sl]], [t_bank[bi]])
                P.cp("act", za_tok[:, i, :], bank[0][:, 0:256], [t_bank[0]], [t_za[i]])
                P.actv(gv[:, 0:256], bank[0][:, 256:512], AF.Gelu_apprx_tanh, [t_bank[0]], [t_gv])
                grp_rms(gv[:, 0:256], t_gv, 4)
                P.tt("pool", vg_tok[:, i, :], gv[:, 0:256], gtb[:, 128:384], ALU.mult, [t_gv, t_gtb], [t_vg[i]])
                P.cp("act", qf[:], bank[1][:], [t_bank[1]], [t_qf])
                grp_rms(qf[:], t_qf, 8)
                P.tt("pool", h3(qb[:]), h3(qf[:]),
                     gtb[:, 0:64].rearrange("p (o d) -> p o d", o=1).to_broadcast([128, 8, 64]),
                     ALU.mult, [t_qf, t_gtb], [t_qb])
                for c in range(4):
                    P.tr(bankb[:, c * 128:(c + 1) * 128], qb[:, c * 128:(c + 1) * 128], identb[:],
                         [t_qb, t_identb], [t_bankb])
                P.cp("dve", qT[:, :, tok], bankb[:, 0:512].rearrange("p (c n) -> p c n", c=4), [t_bankb],
                     [t_q[c][i] for c in range(4)])
                P.cp("act", kf[:], bank[2][:], [t_bank[2]], [t_kf])
                grp_rms(kf[:], t_kf, 8)
                P.tt("pool", h3(kf[:]), h3(kf[:]),
                     gtb[:, 64:128].rearrange("p (o d) -> p o d", o=1).to_broadcast([128, 8, 64]),
                     ALU.mult, [t_kf, t_gtb], [t_kf])
                if is_p:
                    b_, s0_ = i // 2, (i % 2) * 128
                    P.dma("sp", "nk", nk_d[b_, l].rearrange("h s d -> s h d")[s0_:s0_ + 128], h3(kf[:]),
                          reads=[t_kf], is_out=True)
                P.cp("act", kb, kf[:], [t_kf], [t_kb])
                for c in range(4):
                    P.tr(bankb[:, 512 + c * 128:512 + (c + 1) * 128], kb[:, c * 128:(c + 1) * 128], identb[:],
                         [t_kb, t_identb], [t_bankb])
                P.cp("dve", kT[:, :, tok], bankb[:, 512:1024].rearrange("p (c n) -> p c n", c=4), [t_bankb],
                     [t_k[c][i] for c in range(4)])
                P.cp("act", v_tok[:, i, :], bank[3][:], [t_bank[3]], [t_v[i]])
                if is_p:
                    P.cp("dve", vf[:], bank[3][:], [t_bank[3]], [t_vf])
                    P.dma("sp", "nv", nv_d[b_, l].rearrange("h s d -> s h d")[s0_:s0_ + 128], h3(vf[:]),
                          reads=[t_vf], is_out=True)
            gv, t_gv = stmp[1], t_stmp[1]
            _mark(" zu")
            for tt in range(2):
                cols = slice(tt * 512, (tt + 1) * 512)
                for c in range(2):
                    for k in range(8):
                        P.mm(bank[4 + c][:], vE[:, k, c * 128:(c + 1) * 128], hT[:, k, cols], k == 0, k == 7,
                             [t_slot[sE], t_h[k][tt]], [t_bank[4 + c]])
                    P.actv(uT[:, c, cols], bank[4 + c][:], AF.Gelu_apprx_tanh, [t_bank[4 + c]], [t_u[c][tt]])
            for p_ in (pA, pB, pC, pD, pE):
                S.done(p_)
            sF, sG = S.need(pF), S.need(pG)
            if stop_after == "proj":
                raise build_program.Stop()

            def variant(j):
                if is_p:
                    return 0 if j % 2 == 0 else 2
                return 0 if j == 0 else (2 if j == 7 else 1)

            def seq_range(j):
                if is_p:
                    return (j // 2 * 2, j // 2 * 2 + 1)
                return (0, 7)

            sqo, t_sqo = sq[0], t_sq[0]
            rs_ap = (ms[:, 0:256], ms[:, 256:512], ntmp[0][:, 0:256])
            rs_t = (t_ms, t_ms, t_ntmp[0])
            ms2, t_ms2 = ntmp[0][:, 256:512], t_ntmp[0]
            oTn = aT[1][:].rearrange("p f n -> p (f n)").rearrange("p (c n) -> p c n", c=8)
            ETB = ((ms, t_ms), (ntmp[0], t_ntmp[0]), (ntmp[1], t_ntmp[1]))

            def attn_S(it):
                n = it["n"]
                c, hp, h = it["c"], it["hp"], it["h"]
                ps_ = slice(hp * 64, (hp + 1) * 64)
                bk, t_bk = bank[SB_[n % 4]], t_bank[SB_[n % 4]]
                pt, t_pt = aT[0][:, n % 4, :], t_aT[0][n % 4]
                if is_p:
                    u_, tok0_ = it["u"], it["u"] * 256
                    for kt in range(2):
                        P.mm(bk[:, kt * 256:(kt + 1) * 256],
                             kT[ps_, c, tok0_ + kt * 128:tok0_ + (kt + 1) * 128], qT[ps_, c, tok0_:tok0_ + 256],
                             True, True, [t_k[c][2 * u_ + kt], t_q[c][2 * u_], t_q[c][2 * u_ + 1]], [t_bk])
                    P.actv(pt, bk[:], AF.Exp, [t_bk], [t_pt])
                else:
                    r, kts, nl = it["r"], it["kts"], it["nl"]
                    et, t_et = ETB[n % 3]
                    qcols = slice(r * 64, (r + 1) * 64)
                    for j_, kt in enumerate(kts):
                        P.mm(bk[:, j_ * 64:(j_ + 1) * 64], kT[ps_, c, kt * 128:(kt + 1) * 128],
                             qT[ps_, c, qcols], True, True, [t_k[c][kt], t_q[c][r // 2]], [t_bk])
                    for t in range(2):
                        P.mm(bk[:, (nl + t) * 64:(nl + t + 1) * 64], kcT[ps_, c, t * 128:(t + 1) * 128],
                             qT[ps_, c, qcols], True, True, [t_kc, t_q[c][r // 2]], [t_bk])
                    P.actv(et[:, 0:nl * 64], bk[:, 0:nl * 64], AF.Exp, [t_bk], [t_et])
                    P.actv(pt[:, nl * 64:(nl + 2) * 64], bk[:, nl * 64:(nl + 2) * 64], AF.Exp, [t_bk], [t_pt])
                    i0 = 2 * kts[0] - r + 7
                    P.tt("dve", pt[:, 0:nl * 64].rearrange("p (j q) -> p j q", q=64),
                         et[:, 0:nl * 64].rearrange("p (j q) -> p j q", q=64),
                         EB[:, h, i0:i0 + 2 * nl - 1:2, :], ALU.mult, [t_et, t_EB], [t_pt])

            def attn_PV(it):
                n = it["n"]
                c, hp, h = it["c"], it["hp"], it["h"]
                ps_ = slice(hp * 64, (hp + 1) * 64)
                pt, t_pt = aT[0][:, n % 4, :], t_aT[0][n % 4]
                if hp == 0:
                    acnt["bo"] += 1
                bo, t_bo = bank[2 + acnt["bo"] % 2], t_bank[2 + acnt["bo"] % 2]
                if is_p:
                    u_ = it["u"]
                    for kt in range(2):
                        P.mm(bo[ps_, 0:256], v_tok[:, 2 * u_ + kt, h * 64:(h + 1) * 64],
                             pt[:, kt * 256:(kt + 1) * 256], kt == 0, kt == 1,
                             [t_v[2 * u_ + kt], t_pt], [t_bo], tp=(0, hp * 64))
                    for kt in range(2):
                        P.mm(bo[ps_, 256:512], ones[:, 0:64], pt[:, kt * 256:(kt + 1) * 256],
                             kt == 0, kt == 1, [t_ones, t_pt], [t_bo], tp=(0, hp * 64))
                    if hp == 1:
                        P.add("dve", lambda e, bo=bo: e.reciprocal(rden[:], bo[:, 256:512]), [t_bo], [t_rden])
                        P.tt("dve", oraw[:, 2 + c, :], bo[:, 0:256], rden[:], ALU.mult, [t_bo, t_rden],
                             [t_or[2 + c]])
                else:
                    r, kts, nl, s0 = it["r"], it["kts"], it["nl"], it["s0"]
                    mml = []
                    for j_, kt in enumerate(kts):
                        if s0 % 2 == 1 and j_ == 0:
                            p0, p1 = 64, 128
                        elif s0 % 2 == 1 and j_ == nl - 1:
                            p0, p1 = 0, 64
                        else:
                            p0, p1 = 0, 128
                        mml.append((v_tok[p0:p1, kt, h * 64:(h + 1) * 64], ones[p0:p1, 0:64],
                                    pt[p0:p1, j_ * 64:(j_ + 1) * 64], p0, [t_v[kt]]))
                    for t in range(2):
                        mml.append((vc[:, t, h * 64:(h + 1) * 64], ones[:, 0:64],
                                    pt[:, (nl + t) * 64:(nl + t + 1) * 64], 0, [t_vc]))
                    for n_, (lv, lo_, rh, p0, rd) in enumerate(mml):
                        P.mm(bo[ps_, 0:64], lv, rh, n_ == 0, n_ == len(mml) - 1, rd + [t_pt], [t_bo],
                             tp=(p0, hp * 64))
                    for n_, (lv, lo_, rh, p0, rd) in enumerate(mml):
                        P.mm(bo[ps_, 64:128], lo_, rh, n_ == 0, n_ == len(mml) - 1, [t_ones, t_pt],
                             [t_bo], tp=(p0, hp * 64))
                    if hp == 1:
                        P.add("dve", lambda e, bo=bo: e.reciprocal(rden[:, 0:64], bo[:, 64:128]), [t_bo],
                              [t_rden])
                        P.tt("dve", oraw[:, 2 + c, (r % 4) * 64:(r % 4 + 1) * 64], bo[:, 0:64], rden[:, 0:64],
                             ALU.mult, [t_bo, t_rden], [t_or[2 + c]])

            for u in range(4):
                tok0 = u * 256
                ucols = slice(tok0, tok0 + 256)
                tt = u // 2
                _mark(" u%d-attn" % u)
                items = []
                for c in range(4):
                    if is_p:
                        for hp in range(2):
                            items.append({"c": c, "hp": hp, "h": 2 * c + hp, "u": u})
                    else:
                        for r in range(4 * u, 4 * u + 4):
                            s0 = min(max(r - 4, 0), 8)
                            kts = list(range(s0 // 2, (s0 + 7) // 2 + 1))
                            for hp in range(2):
                                items.append({"c": c, "hp": hp, "h": 2 * c + hp, "r": r, "s0": s0, "kts": kts,
                                              "nl": len(kts)})
                for it in items:
                    it["n"] = acnt["n"]
                    acnt["n"] += 1
                def pool_chunk():
                    _mark(" u%d-pool" % u)
                    for c in range(2):
                        for gh in range(2):
                            g4 = 2 * c + gh
                            for jj in range(2):
                                j = 2 * u + jj
                                lo, hi = seq_range(j)
                                ins = [i_ for i_ in (j - 1, j, j + 1) if lo <= i_ <= hi]
                                for n_, i_ in enumerate(ins):
                                    var = 3 if i_ == j - 1 else (4 if i_ == j + 1 else variant(j))
                                    P.mm(bank[4][gh * 64:(gh + 1) * 64, c * 256 + jj * 128:c * 256 + (jj + 1) * 128],
                                         za_tok[:, i_, g4 * 64:(g4 + 1) * 64], bands[:, g4 * 5 + var, :],
                                         n_ == 0, n_ == len(ins) - 1, [t_za[i_], t_bands], [t_bank[4]],
                                         tp=(0, gh * 64))
                        for jj in range(2):
                            P.tt("dve", pooledT[:, c, jj * 128:(jj + 1) * 128],
                                 bank[4][:, c * 256 + jj * 128:c * 256 + (jj + 1) * 128],
                                 invc[:, c * 3 + variant(2 * u + jj), :], ALU.mult, [t_bank[4], t_invc], [t_pl[c]])
                        P.mm(bank[5][:, c * 256:(c + 1) * 256], pwb[:, c, :], pooledT[:, c, :], True, True,
                             [t_pwb, t_pl[c]], [t_bank[5]])
                        P.actv(oraw[:, c, :], bank[5][:, c * 256:(c + 1) * 256], AF.Identity, [t_bank[5], t_vtT],
                               [t_or[c]], scale=vtT[:, l, 104 + c:105 + c])
                    for c in range(2):
                        for gh in range(2):
                            g4 = 2 * c + gh
                            for jj in range(2):
                                P.mm(bank[4][gh * 64:(gh + 1) * 64, c * 256 + jj * 128:c * 256 + (jj + 1) * 128],
                                     vg_tok[:, 2 * u + jj, g4 * 64:(g4 + 1) * 64], wsT[:, g4, :], True, True,
                                     [t_vg[2 * u + jj], t_wsT], [t_bank[4]], tp=(0, gh * 64))
                        P.tt("dve", gv[:, 0:256].rearrange("p (j n) -> p j n", j=2),
                             bank[4][:, c * 256:(c + 1) * 256].rearrange("p (j n) -> p j n", j=2),
                             bsb[:, c:c + 1, :].to_broadcast([128, 2, 128]), ALU.add, [t_bank[4], t_bsb], [t_gv])
                        P.tt("pool", oraw[:, 6 + c, :], gv[:, 0:256], uT[:, c, ucols], ALU.mult,
                             [t_gv, t_u[c][tt]], [t_or[6 + c]])

                for idx in range(len(items) + DEPTH):
                    if idx == len(items) // 2:
                        pool_chunk()
                    if idx < len(items):
                        attn_S(items[idx])
                    if idx - DEPTH >= 0:
                        attn_PV(items[idx - DEPTH])
                _mark(" u%d-out" % u)
                segs = ((0, 2, 256.0), (2, 6, 512.0), (6, 8, 256.0))
                for si, (o0, o1, wd_) in enumerate(segs):
                    stb, t_stb = (bank[6], t_bank[6]) if si < 2 else (bank[5], t_bank[5])
                    scol = slice((si % 2) * 256, (si % 2) * 256 + 256)
                    for oc in range(o0, o1):
                        P.actv(sqo[:, 0:256], oraw[:, oc, :], AF.Square, [t_or[oc]], [t_sqo])
                        P.mm(stb[:, scol], ones[:], sqo[:, 0:256], oc == o0, oc == o1 - 1, [t_ones, t_sqo],
                             [t_stb])
                    P.actv(ms2, stb[:, scol], AF.Sqrt, [t_stb, t_chalf], [t_ms2], scale=1.0 / wd_,
                           bias=epsc[:, 0:1])
                    P.add("dve", lambda e, si=si: e.reciprocal(rs_ap[si], ms2), [t_ms2], [rs_t[si]])
                for oc in range(8):
                    si = 0 if oc < 2 else (1 if oc < 6 else 2)
                    P.stt(oTn[:, oc, :], oraw[:, oc, :], vtT[:, l, 96 + oc:97 + oc], rs_ap[si], ALU.mult, ALU.mult,
                          [t_or[oc], t_vtT, rs_t[si]], [t_aT[1][oc // 2]])
                for dc in range(8):
                    sl = sF if dc < 4 else sG
                    vw = _v3(wslot[sl], 8)
                    bw, t_bw = bank[2 + dc % 2], t_bank[2 + dc % 2]
                    for oc in range(8):
                        P.mm(bw[:, 0:256], vw[:, oc, (dc % 4) * 128:(dc % 4 + 1) * 128], oTn[:, oc, :],
                             oc == 0, oc == 7, [t_slot[sl], t_aT[1][oc // 2]], [t_bw])
                    P.stt(xT[:, dc, ucols], bw[:, 0:256], Gm[:, l, 1, dc, v:v + 1], xT[:, dc, ucols],
                          ALU.mult, ALU.add, [t_bw, t_x[dc][tt], t_mods], [t_x[dc][tt]])
                if u % 2 == 1:
                    after_tt(u // 2)
            S.done(pF)
            S.done(pG)

        class _Stop(Exception):
            pass
        build_program.Stop = _Stop
        try:
            for g in range(2):
                _mark("load%d" % g)
                if not EARLY_NORM:
                    nop = lambda tt: None
                    load_x(g, nop)
                    for l in range(L):
                        _mark("ffn%d%d0" % (g, l))
                        norm_tt(l, 0, g, 0); norm_tt(l, 0, g, 1)
                        ffn(g, l, 0, nop)
                        if g == 0 and l == 0:
                            ada_compute(1)
                        _mark("mix%d%d" % (g, l))
                        mixer_setup(g, l)
                        norm_tt(l, 1, g, 0); norm_tt(l, 1, g, 1)
                        mixer(g, l, nop)
                        _mark("ffn%d%d1" % (g, l))
                        norm_tt(l, 2, g, 0); norm_tt(l, 2, g, 1)
                        ffn(g, l, 1, nop)
                    _mark("store%d" % g)
                    store_y(g)
                    continue
                load_x(g, lambda tt, g=g: norm_tt(0, 0, g, tt))
                for l in range(L):
                    _mark("ffn%d%d0" % (g, l))
                    mixer_setup(g, l)
                    ffn(g, l, 0, lambda tt, g=g, l=l: norm_tt(l, 1, g, tt))
                    if g == 0 and l == 0:
                        ada_compute(1)
                    _mark("mix%d%d" % (g, l))
                    mixer(g, l, lambda tt, g=g, l=l: norm_tt(l, 2, g, tt))
                    _mark("ffn%d%d1" % (g, l))
                    if l + 1 < L:
                        ffn(g, l, 1, lambda tt, g=g, l=l: norm_tt(l + 1, 0, g, tt))
                    else:
                        ffn(g, l, 1, lambda tt: None)
                _mark("store%d" % g)
                store_y(g)
        except _Stop:
            pass
        _mark("end")
        build_program.phases = PH
        P.emit(st)
    return nc


def _consts():
    ident = np.eye(128, dtype=np.float32)
    bands = np.zeros((128, 20, 128), np.float32)
    invc = np.zeros((128, 6, 128), np.float32)
    for gi, w in enumerate(WINS):
        h = w // 2
        tp = np.arange(128)[:, None]
        t = np.arange(128)[None, :]
        inb = ((tp >= t - h) & (tp < t + h)).astype(np.float32)
        cnt_first = (np.minimum(t + h, 10 ** 6) - np.maximum(t - h, 0)).astype(np.float32)
        cnt_mid = np.full((1, 128), float(w), np.float32)
        cnt_last = ((128 - t) + h - np.maximum(0, 0)).astype(np.float32)
        cnt_last = np.minimum(cnt_last, w).astype(np.float32)
        eye = np.eye(128, dtype=np.float32)
        bands[:, gi * 5 + 0, :] = inb - eye * cnt_first
        bands[:, gi * 5 + 1, :] = inb - eye * cnt_mid
        bands[:, gi * 5 + 2, :] = inb - eye * cnt_last
        bands[:, gi * 5 + 3, :] = ((tp - 128) >= (t - h)).astype(np.float32)
        bands[:, gi * 5 + 4, :] = ((tp + 128) < (t + h)).astype(np.float32)
        c, half = gi // 2, gi % 2
        for vi, cn in enumerate((cnt_first, cnt_mid, cnt_last)):
            invc[half * 64:(half + 1) * 64, c * 3 + vi, :] = 1.0 / cn
    dw = np.zeros((31, 127), np.float32)
    for j in range(31):
        dw[j, j + 48] = 1.0
    cols = np.arange(64)
    cs = np.clip(cols - 8, 0, 48)
    ok = (cols[None, :] >= cs[:, None]) & (cols[None, :] < cs[:, None] + 16)
    colok = np.concatenate([ok.T, ok.T], 0).astype(np.float32)
    return ident, bands, invc, dw, colok


def make_in_maps(inp):
    f = lambda a: np.ascontiguousarray(np.asarray(a, dtype=np.float32))
    ident, bands, invc, dw, colok = _consts()
    vt = np.zeros((L, 128, 128), np.float32)
    gt = np.zeros((L, 384), np.float32)
    rpb = np.zeros((L, 128, 31), np.float32)
    for l in range(L):
        vt[l, 0:72] = f(inp["ada_b"])[l].reshape(72, 128)
        vt[l, 72:96] = f(inp["norm_g"])[l].reshape(24, 128)
        vt[l, 96:104] = f(inp["out_norm_g"])[l].reshape(8, 128)
        vt[l, 104:106] = f(inp["pool_scale"])[l].reshape(2, 128)
        gt[l, 0:64] = f(inp["q_norm_g"])[l]
        gt[l, 64:128] = f(inp["k_norm_g"])[l]
        gt[l, 128:384] = f(inp["sg_vnorm_g"])[l].reshape(256)
        rpb[l, 0:120] = f(inp["na_rpb"])[l].reshape(120, 31)
    shared = {
        "vt": vt, "gt": gt, "rpb": rpb,
        "poolw": f(inp["pool_w"]), "sgw": f(inp["sg_w"]), "sgb": f(inp["sg_b"]),
        "adaw": f(inp["ada_w"]), "wg": f(inp["ffn_w_gate"]), "wu": f(inp["ffn_w_up"]),
        "wd": f(inp["ffn_w_down"]), "win": f(inp["w_in"]), "wout": f(inp["w_out"]),
        "ident": ident, "bands": bands, "invc": invc, "dw": dw, "colok": colok,
    }
    xp = f(inp["x_prompt"]); xs = f(inp["x_sample"])
    ck = f(inp["cache_k"]); cv = f(inp["cache_v"]); c = f(inp["c"]); cc = f(inp["c_ctx"])
    maps = []
    for i in range(NCORES):
        x = np.stack([xp[4 * i:4 * i + 4].reshape(1024, D), xs[i]], 0)
        cvec = np.zeros((128, 128), np.float32)
        cvec[0:8] = cc.reshape(8, 128)
        cvec[8:16] = c[i].reshape(8, 128)
        m = dict(shared)
        m.update({"x": np.ascontiguousarray(x), "ck": np.ascontiguousarray(ck[i]),
                  "cv": np.ascontiguousarray(cv[i]), "cvec": cvec})
        maps.append(m)
    return maps


_NC_CACHE = {}


def kernel(**inputs):
    if "nc" not in _NC_CACHE:
        _NC_CACHE["nc"] = build_program()
    nc = _NC_CACHE["nc"]
    maps = make_in_maps(inputs)
    res = run_bass_kernel_spmd(nc, maps, core_ids=list(range(NCORES)))
    r = res.results
    yp = np.concatenate([r[i]["y"][0].reshape(4, 256, D) for i in range(NCORES)], 0)
    ys = np.stack([r[i]["y"][1] for i in range(NCORES)], 0)
    nk = np.concatenate([r[i]["nk"] for i in range(NCORES)], 0)
    nv = np.concatenate([r[i]["nv"] for i in range(NCORES)], 0)
    return (yp.astype(np.float32), ys.astype(np.float32), nk.astype(np.float32), nv.astype(np.float32))
```

```python
import numpy as np
from contextlib import ExitStack
import concourse.bass as bass
import concourse.mybir as mybir
from concourse.bass_utils import run_bass_kernel_spmd

F32 = mybir.dt.float32
BF16 = mybir.dt.bfloat16
AF = mybir.ActivationFunctionType
ALU = mybir.AluOpType
AX = mybir.AxisListType

D = 1024
DFF = 2816
L = 2
NH = 8
EPS = 1e-6
NCORES = 8
WINS = (2, 4, 8, 16)
EARLY_NORM = True


class Tl:
    __slots__ = ("name", "w", "r", "excl")

    def __init__(self, name="", excl=False):
        self.name = name
        self.w = None
        self.r = {}
        self.excl = excl


class Op:
    __slots__ = ("eng", "fn", "deps", "marked", "count", "dkey", "dcount", "is_dma", "uid")


class Prog:
    ENGS = ("pe", "act", "dve", "pool", "sp")

    def __init__(self, nc):
        self.nc = nc
        self.ops = {e: [] for e in self.ENGS}
        self.latest_dma = {}
        self.dma_counts = {}
        self.uid = 0
        self.out_keys = set()

    def add(self, eng, fn, reads=(), writes=(), dma=None, is_out=False):
        op = Op()
        op.eng = eng
        op.fn = fn
        op.marked = False
        op.count = None
        op.is_dma = dma is not None
        op.dkey = dma
        op.uid = self.uid
        self.uid += 1
        deps = {}
        for t in reads:
            if t.w is not None:
                deps[t.w.uid] = t.w
            if t.excl:
                for d in t.r.values():
                    if d.eng != eng:
                        deps[d.uid] = d
        for t in writes:
            if t.w is not None:
                deps[t.w.uid] = t.w
            for d in t.r.values():
                deps[d.uid] = d
        final = {}
        for d in deps.values():
            if d.is_dma:
                d = self.latest_dma[d.dkey]
            elif d.eng == "pe" and eng == "pe" and not op.is_dma:
                continue
            else:
                d.marked = True
            final[d.uid] = d
        op.deps = list(final.values())
        if op.is_dma:
            c = self.dma_counts.get(dma, 0) + 16
            self.dma_counts[dma] = c
            op.dcount = c
            self.latest_dma[dma] = op
            if is_out:
                self.out_keys.add(dma)
        rkey = ("dma", op.uid) if op.is_dma else eng
        for t in reads:
            t.r[rkey] = op
        for t in writes:
            t.w = op
            t.r = {}
        self.ops[eng].append(op)
        return op

    def dma(self, eng, key, out, in_, reads=(), writes=(), is_out=False):
        return self.add(eng, lambda e: e.dma_start(out=out, in_=in_), reads, writes,
                        dma=key, is_out=is_out)

    def mm(self, out, lhsT, rhs, start, stop, reads, writes, tp=None):
        if tp is None:
            return self.add("pe", lambda e: e.matmul(out, lhsT, rhs, start=start, stop=stop), reads, writes)
        return self.add("pe", lambda e: e.matmul(out, lhsT, rhs, start=start, stop=stop, tile_position=tp),
                        reads, writes)

    def tr(self, out, in_, ident, reads, writes):
        return self.add("pe", lambda e: e.transpose(out, in_, ident), reads, writes)

    def actv(self, out, in_, func, reads, writes, scale=None, bias=None):
        kw = {}
        if scale is not None:
            kw["scale"] = scale
        if bias is not None:
            kw["bias"] = bias
        return self.add("act", lambda e: e.activation(out, in_, func, **kw), reads, writes)

    def tt(self, eng, out, in0, in1, op, reads, writes):
        return self.add(eng, lambda e: e.tensor_tensor(out, in0, in1, op=op), reads, writes)

    def ts(self, eng, out, in0, s1, s2, op0, op1, reads, writes):
        if s2 is None:
            return self.add(eng, lambda e: e.tensor_scalar(out, in0, s1, None, op0=op0), reads, writes)
        return self.add(eng, lambda e: e.tensor_scalar(out, in0, s1, s2, op0=op0, op1=op1), reads, writes)

    def stt(self, out, in0, scalar, in1, op0, op1, reads, writes):
        return self.add("dve", lambda e: e.scalar_tensor_tensor(out, in0, scalar, in1, op0=op0, op1=op1),
                        reads, writes)

    def cp(self, eng, out, in_, reads, writes):
        if eng == "act":
            return self.add("act", lambda e: e.copy(out, in_), reads, writes)
        return self.add(eng, lambda e: e.tensor_copy(out, in_), reads, writes)

    def red(self, out, in_, reads, writes):
        return self.add("dve", lambda e: e.tensor_reduce(out, in_, axis=AX.X, op=ALU.add), reads, writes)

    def memset(self, eng, ap, val, writes):
        return self.add(eng, lambda e: e.memset(ap, val), (), writes)

    def emit(self, stack):
        nc = self.nc
        for e in self.ENGS:
            c = 0
            for op in self.ops[e]:
                if op.marked and not op.is_dma:
                    c += 1
                    op.count = c
        esem = {e: stack.enter_context(nc.semaphore("s_" + e)) for e in self.ENGS}
        dsem = {k: stack.enter_context(nc.semaphore("d_%d" % i)) for i, k in enumerate(self.dma_counts)}
        block = stack.enter_context(nc.Block())
        ops = self.ops
        out_final = [(dsem[k], self.dma_counts[k]) for k in sorted(self.out_keys, key=str)]

        def run(ename, e):
            known = {}
            for op in ops[ename]:
                for d in op.deps:
                    if d.is_dma:
                        s, v = dsem[d.dkey], d.dcount
                    else:
                        s, v = esem[d.eng], d.count
                    kid = id(s)
                    if known.get(kid, 0) >= v:
                        continue
                    known[kid] = v
                    e.wait_ge(s, v)
                ins = op.fn(e)
                if op.is_dma:
                    ins.then_inc(dsem[op.dkey], 16)
                elif op.marked:
                    ins.then_inc(esem[ename], 1)
            if ename == "sp":
                for s, v in out_final:
                    e.wait_ge(s, v)

        @block.sync
        def _(e):
            run("sp", e)

        @block.tensor
        def _(e):
            run("pe", e)

        @block.scalar
        def _(e):
            run("act", e)

        @block.vector
        def _(e):
            run("dve", e)

        @block.gpsimd
        def _(e):
            run("pool", e)


class Stream:
    NS = 6
    LOOK = 5

    def __init__(self, P, slots, t_slots):
        self.P = P
        self.slots = slots
        self.t_slots = t_slots
        self.pieces = []
        self.issued = 0
        self.freed = 0
        self.want = 0

    def plan(self, parts):
        self.pieces.append(parts)
        return len(self.pieces) - 1

    def pump(self):
        while self.issued <= self.want and self.issued < len(self.pieces) and self.issued < self.freed + self.NS:
            i = self.issued
            s = i % self.NS
            for dst_fn, src in self.pieces[i]:
                self.P.dma("pool", ("w", s), dst_fn(self.slots[s]), src, writes=[self.t_slots[s]])
            self.issued += 1

    def need(self, p):
        self.want = max(self.want, p + self.LOOK)
        self.pump()
        assert self.issued > p, (self.issued, p, self.freed)
        return p % self.NS

    def done(self, p):
        self.freed = max(self.freed, p + 1)
        self.pump()


def _v3(t, k):
    return t[:].rearrange("p (k n) -> p k n", k=k)


def build_program(stop_after=None):
    nc = bass.Bass("TRN2", target_bir_lowering=False)

    def din(name, shape):
        return nc.dram_tensor(name, shape, F32, kind="ExternalInput").ap()

    def dout(name, shape):
        return nc.dram_tensor(name, shape, F32, kind="ExternalOutput").ap()

    x_d = din("x", [2, 1024, D])
    ck_d = din("ck", [L, NH, 256, 64])
    cv_d = din("cv", [L, NH, 256, 64])
    cvec_d = din("cvec", [128, 128])
    vt_d = din("vt", [L, 128, 128])
    gt_d = din("gt", [L, 384])
    poolw_d = din("poolw", [L, 4, 64, 64])
    sgw_d = din("sgw", [L, 4, 128, 128])
    sgb_d = din("sgb", [L, 4, 128])
    rpb_d = din("rpb", [L, 128, 31])
    adaw_d = din("adaw", [L, D, 9 * D])
    wg_d = din("wg", [L, 2, D, DFF])
    wu_d = din("wu", [L, 2, D, DFF])
    wd_d = din("wd", [L, 2, DFF, D])
    win_d = din("win", [L, D, 2304])
    wout_d = din("wout", [L, D, D])
    ident_d = din("ident", [128, 128])
    bands_d = din("bands", [128, 20, 128])
    invc_d = din("invc", [128, 6, 128])
    dw_d = din("dw", [31, 127])
    colok_d = din("colok", [128, 64])
    y_d = dout("y", [2, 1024, D])
    nk_d = dout("nk", [4, L, NH, 256, 64])
    nv_d = dout("nv", [4, L, NH, 256, 64])

    P = Prog(nc)
    st = ExitStack()

    def sb(name, shape, dt=F32):
        return st.enter_context(nc.sbuf_tensor("s_" + name, shape, dt))

    with st:
        bank = [st.enter_context(nc.psum_tensor("bank%d" % i, [128, 512], F32)) for i in range(7)]
        bankb = st.enter_context(nc.psum_tensor("bankb", [128, 1024], BF16))
        t_bank = [Tl("bank%d" % i, excl=True) for i in range(7)]
        t_bankb = Tl("bankb", excl=True)

        ident = sb("ident", [128, 128]); t_ident = Tl()
        identb = sb("identb", [128, 128], BF16); t_identb = Tl()
        ones = sb("ones", [128, 128], BF16); t_ones = Tl()
        epsc = sb("epsc", [128, 1]); t_chalf = Tl()
        bands = sb("bands", [128, 20, 128], BF16); t_bands = Tl()
        invc = sb("invc", [128, 6, 128]); t_invc = Tl()
        dw = sb("dw", [31, 127]); t_dw = Tl()
        colok = sb("colok", [128, 64]); t_colok = Tl()
        P.dma("sp", "c0", ident[:], ident_d, writes=[t_ident])
        P.dma("sp", "c1", invc[:], invc_d, writes=[t_invc])
        P.dma("sp", "c2", dw[:], dw_d, writes=[t_dw])
        P.dma("sp", "c3", colok[:], colok_d, writes=[t_colok])
        P.dma("pool", "c4", bands[:], bands_d, writes=[t_bands])
        P.cp("dve", identb[:], ident[:], [t_ident], [t_identb])
        P.memset("pool", ones[:], 1.0, [t_ones])
        P.memset("pool", epsc[:], EPS, [t_chalf])

        xT = sb("xT", [128, 8, 1024]); t_x = [[Tl() for _ in range(2)] for _ in range(8)]
        hT = sb("hT", [128, 8, 1024], BF16); t_h = [[Tl() for _ in range(2)] for _ in range(8)]
        wslot = [sb("wslot%d" % i, [128, 4096], BF16) for i in range(Stream.NS)]
        t_slot = [Tl("slot%d" % i) for i in range(Stream.NS)]
        S = Stream(P, wslot, t_slot)
        aT = [sb("aT%d" % i, [128, 4, 512], BF16) for i in range(2)]
        t_aT = [[Tl() for _ in range(4)] for _ in range(2)]
        stmp = [sb("stmp%d" % i, [128, 512]) for i in range(2)]; t_stmp = [Tl(), Tl()]
        xio = sb("xio", [128, 2048]); t_or = [Tl() for _ in range(8)]
        oraw = xio[:].rearrange("p (c n) -> p c n", c=8)
        sq = [sb("sq%d" % i, [128, 512], BF16) for i in range(2)]; t_sq = [Tl(), Tl()]
        ms = sb("ms", [128, 512]); t_ms = Tl()
        rstd, t_rstd = ms, t_ms
        ntmp = [sb("ntmp%d" % i, [128, 512]) for i in range(2)]; t_ntmp = [Tl(), Tl()]
        mods = sb("mods", [128, L, 72, 2]); t_mods = Tl()
        Am = sb("Am", [128, L, 3, 8, 2])
        Gm = sb("Gm", [128, L, 3, 8, 2])
        vtT = sb("vtT", [128, L, 128]); t_vtT = Tl()
        scT = sb("scT", [128, 16], BF16); t_scT = Tl()
        vtr_t = sb("vtr", [128, 128]); t_vtr = Tl()
        vtr = vtr_t[:]
        za_tok = sb("za_tok", [128, 8, 256], BF16); t_za = [Tl() for _ in range(8)]
        vg_tok = sb("vg_tok", [128, 8, 256], BF16); t_vg = [Tl() for _ in range(8)]
        v_tok = sb("v_tok", [128, 8, 512], BF16); t_v = [Tl() for _ in range(8)]
        qT = sb("qT", [128, 4, 1024], BF16); t_q = [[Tl() for _ in range(8)] for _ in range(4)]
        kT = sb("kT", [128, 4, 1024], BF16); t_k = [[Tl() for _ in range(8)] for _ in range(4)]
        uT = sb("uT", [128, 2, 1024], BF16); t_u = [[Tl() for _ in range(2)] for _ in range(2)]
        kcT = sb("kcT", [128, 4, 256], BF16); t_kc = Tl()
        qkz = sb("qkz", [128, 1280]); t_qkz = Tl()
        vc = sb("vc", [128, 2, 512], BF16); t_vc = Tl()
        pooledT = sb("pooledT", [128, 2, 256], BF16); t_pl = [Tl(), Tl()]
        EB = sb("EB", [128, 8, 14, 64], BF16); t_EB = Tl()
        wsT = sb("wsT", [128, 4, 128], BF16); t_wsT = Tl()
        pwb = sb("pwb", [128, 2, 128], BF16); t_pwb = Tl()
        bsb = sb("bsb", [128, 2, 128]); t_bsb = Tl()
        gtb = sb("gtb", [128, 384]); t_gtb = Tl()
        rden = sb("rden", [128, 256]); t_rden = Tl()
        st8 = sb("st8", [128, 40]); t_st8 = Tl()

        pid = {}
        groups = [(0, 4), (4, 4), (8, 2), (10, 4), (14, 4), (18, 4)]

        def plan_ada(l):
            for pc in range(18):
                pid[("ada", l, pc)] = S.plan([(
                    lambda t: _v3(t, 8),
                    adaw_d[l].rearrange("(k p) n -> p k n", p=128)[:, :, pc * 512:(pc + 1) * 512])])

        def plan_ffn(g, l, j):
            for gi, (c0, nfc) in enumerate(groups):
                w = nfc * 128
                pid[("g", g, l, j, gi)] = S.plan([(
                    lambda t, w=w: _v3(t, 8)[:, :, 0:w],
                    wg_d[l, j].rearrange("(k p) n -> p k n", p=128)[:, :, c0 * 128:c0 * 128 + w])])
                pid[("u", g, l, j, gi)] = S.plan([(
                    lambda t, w=w: _v3(t, 8)[:, :, 0:w],
                    wu_d[l, j].rearrange("(k p) n -> p k n", p=128)[:, :, c0 * 128:c0 * 128 + w])])
                pid[("d", g, l, j, gi)] = S.plan([(
                    lambda t, nfc=nfc: _v3(t, 4)[:, 0:nfc, :],
                    wd_d[l, j][c0 * 128:c0 * 128 + w, :].rearrange("(f p) n -> p f n", p=128))])

        def plan_mixer(g, l):
            wv = win_d[l].rearrange("(k p) n -> p k n", p=128)
            pid[("A", g, l)] = S.plan([
                (lambda t: _v3(t, 8)[:, :, 0:256], wv[:, :, 0:256]),
                (lambda t: _v3(t, 8)[:, :, 256:512], wv[:, :, 2048:2304])])
            pid[("B", g, l)] = S.plan([(lambda t: _v3(t, 8), wv[:, :, 256:768])])
            pid[("C", g, l)] = S.plan([(lambda t: _v3(t, 8), wv[:, :, 768:1280])])
            pid[("D", g, l)] = S.plan([(lambda t: _v3(t, 8), wv[:, :, 1280:1792])])
            pid[("E", g, l)] = S.plan([(lambda t: _v3(t, 8)[:, :, 0:256], wv[:, :, 1792:2048])])
            wo = wout_d[l].rearrange("(k p) n -> p k n", p=128)
            pid[("F", g, l)] = S.plan([(lambda t: _v3(t, 8), wo[:, :, 0:512])])
            pid[("G", g, l)] = S.plan([(lambda t: _v3(t, 8), wo[:, :, 512:1024])])

        plan_ada(0)
        for g in range(2):
            for l in range(L):
                plan_ffn(g, l, 0)
                if g == 0 and l == 0:
                    plan_ada(1)
                plan_mixer(g, l)
                plan_ffn(g, l, 1)

        P.dma("sp", "v0", vtr, cvec_d, writes=[t_vtr])
        P.tr(bank[6][:, 0:128], vtr, ident[:], [t_vtr, t_ident], [t_bank[6]])
        P.actv(scT[:], bank[6][:, 0:16], AF.Silu, [t_bank[6]], [t_scT])
        for l in range(L):
            P.dma("sp", "v0", vtr, vt_d[l], writes=[t_vtr])
            P.tr(bank[6][:, 0:128], vtr, ident[:], [t_vtr, t_ident], [t_bank[6]])
            P.cp("dve", vtT[:, l, :], bank[6][:, 0:128], [t_bank[6]], [t_vtT])
        def ada_compute(l):
            for pc in range(18):
                p_ = pid[("ada", l, pc)]
                s = S.need(p_)
                wv = _v3(wslot[s], 8)
                for fc in range(4):
                    j = pc * 4 + fc
                    for k in range(8):
                        P.mm(bank[5][:, 2 * j:2 * j + 2], wv[:, k, fc * 128:(fc + 1) * 128],
                             scT[:, k:16:8], k == 0, k == 7, [t_slot[s], t_scT], [t_bank[5]])
                S.done(p_)
            P.tt("dve", mods[:, l, :, :], bank[5][:, 0:144].rearrange("p (j v) -> p j v", v=2),
                 vtT[:, l, 0:72].rearrange("p (j o) -> p j o", o=1).to_broadcast([128, 72, 2]),
                 ALU.add, [t_bank[5], t_vtT], [t_mods])
            for n in range(3):
                P.ts("dve", Am[:, l, n, :, :], mods[:, l, (3 * n + 1) * 8:(3 * n + 2) * 8, :], 1.0, None,
                     ALU.add, None, [t_mods], [t_mods])
                P.tt("dve", Am[:, l, n, :, :], Am[:, l, n, :, :],
                     vtT[:, l, 72 + n * 8:80 + n * 8].rearrange("p (j o) -> p j o", o=1).to_broadcast([128, 8, 2]),
                     ALU.mult, [t_mods, t_vtT], [t_mods])
                P.ts("dve", Gm[:, l, n, :, :], mods[:, l, (3 * n + 2) * 8:(3 * n + 3) * 8, :],
                     (1.0 if n == 1 else 0.5), None, ALU.mult, None, [t_mods], [t_mods])


        ada_compute(0)

        PH = []
        _mark = lambda nm: PH.append((nm, len(P.ops["pe"])))
        def norm_tt(l, n, v, tt):
            cols = slice(tt * 512, (tt + 1) * 512)
            for k in range(8):
                P.actv(sq[k % 2][:], xT[:, k, cols], AF.Square, [t_x[k][tt]], [t_sq[k % 2]])
                P.mm(bank[6][:], ones[:], sq[k % 2][:], k == 0, k == 7, [t_sq[k % 2], t_ones], [t_bank[6]])
            P.actv(ms[:], bank[6][:], AF.Sqrt, [t_bank[6], t_chalf], [t_ms], scale=1.0 / D, bias=epsc[:, 0:1])
            P.add("dve", lambda e: e.reciprocal(rstd[:], ms[:]), [t_ms], [t_rstd])
            for k in range(8):
                P.tt("dve", ntmp[k % 2][:], xT[:, k, cols], rstd[:], ALU.mult,
                     [t_x[k][tt], t_rstd], [t_ntmp[k % 2]])
                P.actv(hT[:, k, cols], ntmp[k % 2][:], AF.Identity, [t_ntmp[k % 2], t_mods], [t_h[k][tt]],
                       scale=Am[:, l, n, k, v:v + 1], bias=mods[:, l, 3 * n * 8 + k, v:v + 1])

        cnt = {"up": 0, "dn": 0, "ab": 0}

        def ffn(g, l, j, after_tt):
            n = 0 if j == 0 else 2
            v = g
            pending = []
            last_gi = len(groups) - 1

            def flush():
                while pending:
                    pending.pop(0)()

            for gi, (c0, nfc) in enumerate(groups):
                pg_, pu_, pd_ = pid[("g", g, l, j, gi)], pid[("u", g, l, j, gi)], pid[("d", g, l, j, gi)]
                sg, su, sd = S.need(pg_), S.need(pu_), S.need(pd_)
                vg_, vu_, vd_ = _v3(wslot[sg], 8), _v3(wslot[su], 8), _v3(wslot[sd], 4)
                for tt in range(2):
                    cols = slice(tt * 512, (tt + 1) * 512)
                    ab = cnt["ab"] % 2
                    cnt["ab"] += 1
                    for f in range(nfc):
                        ib = cnt["up"] % 2
                        cnt["up"] += 1
                        for k in range(8):
                            P.mm(bank[ib][:], vg_[:, k, f * 128:(f + 1) * 128], hT[:, k, cols], k == 0, k == 7,
                                 [t_slot[sg], t_h[k][tt]], [t_bank[ib]])
                        for k in range(8):
                            P.mm(bank[2 + ib][:], vu_[:, k, f * 128:(f + 1) * 128], hT[:, k, cols], k == 0, k == 7,
                                 [t_slot[su], t_h[k][tt]], [t_bank[2 + ib]])
                        P.actv(stmp[ib][:], bank[ib][:], AF.Silu, [t_bank[ib]], [t_stmp[ib]])
                        P.tt("dve", aT[ab][:, f, :], stmp[ib][:], bank[2 + ib][:], ALU.mult,
                             [t_stmp[ib], t_bank[2 + ib]], [t_aT[ab][f]])

                    def down(ab=ab, nfc=nfc, sd=sd, vd_=vd_, cols=cols, tt=tt, gi=gi):
                        for dc in range(8):
                            ib2 = 4 + cnt["dn"] % 2
                            cnt["dn"] += 1
                            for f in range(nfc):
                                P.mm(bank[ib2][:], vd_[:, f, dc * 128:(dc + 1) * 128], aT[ab][:, f, :],
                                     f == 0, f == nfc - 1, [t_slot[sd], t_aT[ab][f]], [t_bank[ib2]])
                            P.stt(xT[:, dc, cols], bank[ib2][:], Gm[:, l, n, dc, v:v + 1], xT[:, dc, cols],
                                  ALU.mult, ALU.add, [t_bank[ib2], t_x[dc][tt], t_mods], [t_x[dc][tt]])
                        if gi == last_gi:
                            after_tt(tt)

                    flush()
                    if gi == last_gi:
                        down()
                    else:
                        pending.append(down)
                S.done(pg_)
                S.done(pu_)
                if gi > 0:
                    S.done(pid[("d", g, l, j, gi - 1)])
            flush()
            S.done(pid[("d", g, l, j, len(groups) - 1)])

        def load_x(g, after_tt):
            for i in range(8):
                xb = i % 2
                xs_ = xio[:, xb * 1024:(xb + 1) * 1024]
                tl_ = t_or[xb * 4:xb * 4 + 4]
                P.dma("sp", ("xio", xb), xs_, x_d[g, i * 128:(i + 1) * 128, :], writes=tl_)
                for hb in range(2):
                    for kk in range(4):
                        k = hb * 4 + kk
                        P.tr(bank[hb][:, kk * 128:(kk + 1) * 128], xs_[:, k * 128:(k + 1) * 128], ident[:],
                             tl_ + [t_ident], [t_bank[hb]])
                    P.cp("dve" if hb == 0 else "act", xT[:, hb * 4:hb * 4 + 4, i * 128:(i + 1) * 128],
                         bank[hb][:].rearrange("p (k n) -> p k n", k=4), [t_bank[hb]],
                         [t_x[k][i // 4] for k in range(hb * 4, hb * 4 + 4)])
                if i % 4 == 3:
                    after_tt(i // 4)

        def store_y(g):
            for i in range(8):
                xb = i % 2
                xs_ = xio[:, xb * 1024:(xb + 1) * 1024]
                tl_ = t_or[xb * 4:xb * 4 + 4]
                for hb in range(2):
                    for kk in range(4):
                        k = hb * 4 + kk
                        P.tr(bank[hb][:, kk * 128:(kk + 1) * 128], xT[:, k, i * 128:(i + 1) * 128], ident[:],
                             [t_x[k][i // 4], t_ident], [t_bank[hb]])
                    P.cp("dve" if hb == 0 else "act", xs_[:, hb * 512:(hb + 1) * 512], bank[hb][:],
                         [t_bank[hb]], tl_)
                P.dma("sp", ("y", xb), y_d[g, i * 128:(i + 1) * 128, :], xs_, reads=tl_, is_out=True)

        def grp_rms(src, t_src, nh):
            n = nh * 64
            s3 = src.rearrange("p (h d) -> p h d", d=64)
            sqs, t_sqs = sq[0], t_sq[0]
            P.tt("dve", sqs[:, 0:n], src, src, ALU.mult, [t_src], [t_sqs])
            P.red(st8[:, 0:nh], sqs[:, 0:n].rearrange("p (h d) -> p h d", d=64), [t_sqs], [t_st8])
            P.actv(st8[:, 0:nh], st8[:, 0:nh], AF.Sqrt, [t_st8, t_chalf], [t_st8], scale=1.0 / 64, bias=epsc[:, 0:1])
            P.add("dve", lambda e: e.reciprocal(st8[:, 8:8 + nh], st8[:, 0:nh]), [t_st8], [t_st8])
            P.tt("dve", s3, s3, st8[:, 8:8 + nh].rearrange("p (h o) -> p h o", o=1).to_broadcast([128, nh, 64]),
                 ALU.mult, [t_src, t_st8], [t_src])

        def mixer_setup(g, l):
            pwf = stmp[1][:, 0:256].rearrange("p (c n) -> p c n", c=2)
            t_pwf = t_stmp[1]
            RT = ntmp[0][0:31, 0:128]
            RT01 = ntmp[0][0:31, 128:352].rearrange("p (a n) -> p a n", a=2)
            t_RT = t_RT01 = t_ntmp[0]
            P.dma("sp", "ms0", gtb[:], gt_d[l:l + 1, :].to_broadcast([128, 384]), writes=[t_gtb])
            P.ts("dve", gtb[:, 0:64], gtb[:, 0:64], 0.125, None, ALU.mult, None, [t_gtb], [t_gtb])
            P.memset("pool", pwf, 0.0, [t_pwf])
            for c in range(2):
                for gh in range(2):
                    P.dma("sp", "ms1", pwf[gh * 64:(gh + 1) * 64, c, gh * 64:(gh + 1) * 64], poolw_d[l, 2 * c + gh],
                          writes=[t_pwf])
            P.cp("dve", pwb[:], pwf, [t_pwf], [t_pwb])
            for g4 in range(4):
                P.dma("sp", "ms2", stmp[0][:, 0:128], sgw_d[l, g4], writes=[t_stmp[0]])
                P.tr(bank[6][:, 0:128], stmp[0][:, 0:128], ident[:], [t_stmp[0], t_ident], [t_bank[6]])
                P.cp("dve", wsT[:, g4, :], bank[6][:, 0:128], [t_bank[6]], [t_wsT])
            for g4 in range(4):
                P.dma("sp", "ms3", bsb[(g4 % 2) * 64:(g4 % 2 + 1) * 64, g4 // 2, :],
                      sgb_d[l, g4:g4 + 1, :].to_broadcast([64, 128]), writes=[t_bsb])
            if g == 1:
                kctok = aT[1][:, 0:2, :]
                t_kctok = Tl()
                for t in range(2):
                    P.dma("pool", "ms4", kctok[:, t, :].rearrange("p (h d) -> p h d", h=8),
                          ck_d[l][:, t * 128:(t + 1) * 128, :].rearrange("h p d -> p h d"),
                          writes=[t_aT[1][0], t_aT[1][1]])
                    P.dma("pool", "ms5", vc[:, t, :].rearrange("p (h d) -> p h d", h=8),
                          cv_d[l][:, t * 128:(t + 1) * 128, :].rearrange("h p d -> p h d"), writes=[t_vc])
                for t in range(2):
                    for c in range(4):
                        P.tr(bankb[:, c * 128:(c + 1) * 128], kctok[:, t, c * 128:(c + 1) * 128], identb[:],
                             [t_aT[1][0], t_aT[1][1], t_identb], [t_bankb])
                    P.cp("dve", kcT[:, :, t * 128:(t + 1) * 128],
                         bankb[:, 0:512].rearrange("p (c n) -> p c n", c=4), [t_bankb], [t_kc])
                P.dma("sp", "ms6", stmp[1][:, 0:31], rpb_d[l], writes=[t_stmp[1]])
                P.tr(bank[6][0:31, 0:128], stmp[1][:, 0:31], ident[:], [t_stmp[1], t_ident], [t_bank[6]])
                P.cp("dve", RT, bank[6][0:31, 0:128], [t_bank[6]], [t_RT])
                rt3 = RT[:, 0:120].rearrange("p (h r) -> p h r", h=8)
                for half in range(2):
                    P.cp("dve", RT01[:, half, :].rearrange("p (h r) -> p h r", h=8), rt3[:, :, half:half + 14],
                         [t_RT], [t_RT01])
                for b in range(16):
                    bk = bank[b % 2]
                    for qi in range(4):
                        qc = b * 4 + qi
                        for half in range(2):
                            P.mm(bk[half * 64:(half + 1) * 64, qi * 112:(qi + 1) * 112],
                                 dw[0:31, 63 - qc:127 - qc], RT01[:, half, :], True, True,
                                 [t_dw, t_RT01], [t_bank[b % 2]], tp=(0, half * 64))
                    P.actv(EB[:, :, :, b * 4:(b + 1) * 4].rearrange("p h r q -> p q (h r)"),
                           bk[:, 0:448].rearrange("p (q n) -> p q n", q=4), AF.Exp, [t_bank[b % 2]], [t_EB])
                ebv = EB[:].rearrange("p h r q -> p (h r) q")
                P.tt("dve", ebv, ebv, colok[:, :].rearrange("p (o q) -> p o q", o=1).to_broadcast([128, 112, 64]),
                     ALU.mult, [t_EB, t_colok], [t_EB])

        acnt = {"n": 0, "bo": 0}
        SB_ = (0, 1, 4, 5)
        DEPTH = 2

        def mixer(g, l, after_tt):
            v = g
            is_p = (g == 0)
            if stop_after == "premix":
                raise build_program.Stop()
            pA, pB, pC, pD, pE, pF, pG = [pid[(nm, g, l)] for nm in "ABCDEFG"]
            sA, sB, sC, sD, sE = [S.need(p_) for p_ in (pA, pB, pC, pD, pE)]
            vA, vB, vC, vD, vE = [_v3(wslot[s_], 8) for s_ in (sA, sB, sC, sD, sE)]
            h3 = lambda ap: ap.rearrange("p (h d) -> p h d", d=64)
            SETS = (
                dict(qf=stmp[0][:], t_qf=[t_stmp[0]], kf=stmp[1][:], t_kf=[t_stmp[1]],
                     gv=ntmp[1][:, 0:256], t_gv=[t_ntmp[1]], vf=ntmp[0][:], t_vf=[t_ntmp[0]]),
                dict(qf=xio[:, 0:512], t_qf=[t_or[0], t_or[1]], kf=xio[:, 512:1024], t_kf=[t_or[2], t_or[3]],
                     gv=xio[:, 1024:1280], t_gv=[t_or[4]], vf=xio[:, 1536:2048], t_vf=[t_or[6], t_or[7]]),
            )
            sqq, t_sqq = sq[0][:], [t_sq[0]]
            sqk, t_sqk = aT[0][:, 1, :], [t_aT[0][1]]
            sqz, t_sqz = aT[0][:, 2, 0:256], [t_aT[0][2]]
            qb, t_qb = sq[1], t_sq[1]
            kb, t_kb = aT[0][:, 0, :], t_aT[0][0]
            bc = lambda ap, n: ap.rearrange("p (h o) -> p h o", o=1).to_broadcast([128, n, 64])
            gb = lambda ap: ap.rearrange("p (o d) -> p o d", o=1).to_broadcast([128, 8, 64])

            def proj_mm(i):
                tt = i // 4
                tok = slice(i * 128, (i + 1) * 128)
                for bi, (vw, sl) in enumerate(((vA, sA), (vB, sB), (vC, sC), (vD, sD))):
                    for k in range(8):
                        P.mm(bank[bi][:], hT[:, k, tok], vw[:, k, :], k == 0, k == 7,
                             [t_h[k][tt], t_slot[sl]], [t_bank[bi]])

            def proj_evac(i):
                B = SETS[i % 2]
                P.cp("act", B["qf"], bank[1][:], [t_bank[1]], B["t_qf"])
                P.cp("act", B["kf"], bank[2][:], [t_bank[2]], B["t_kf"])
                P.actv(B["gv"], bank[0][:, 256:512], AF.Gelu_apprx_tanh, [t_bank[0]], B["t_gv"])
                P.cp("act", za_tok[:, i, :], bank[0][:, 0:256], [t_bank[0]], [t_za[i]])
                P.cp("act", v_tok[:, i, :], bank[3][:], [t_bank[3]], [t_v[i]])
                if is_p:
                    b_, s0_ = i // 2, (i % 2) * 128
                    P.cp("dve", B["vf"], bank[3][:], [t_bank[3]], B["t_vf"])
                    P.dma("sp", "nv", nv_d[b_, l].rearrange("h s d -> s h d")[s0_:s0_ + 128], h3(B["vf"]),
                          reads=B["t_vf"], is_out=True)

            def proj_chain(i):
                B = SETS[i % 2]
                tok = slice(i * 128, (i + 1) * 128)
                qf, kf, gv = B["qf"], B["kf"], B["gv"]
                t_qf, t_kf, t_gv = B["t_qf"], B["t_kf"], B["t_gv"]
                P.tt("dve", sqq, qf, qf, ALU.mult, t_qf, t_sqq)
                P.tt("dve", sqk, kf, kf, ALU.mult, t_kf, t_sqk)
                P.tt("pool", sqz, gv, gv, ALU.mult, t_gv, t_sqz)
                P.red(st8[:, 0:8], h3(sqq), t_sqq, [t_st8])
                P.red(st8[:, 8:16], h3(sqk), t_sqk, [t_st8])
                P.red(st8[:, 16:20], h3(sqz), t_sqz, [t_st8])
                P.actv(st8[:, 0:20], st8[:, 0:20], AF.Sqrt, [t_st8, t_chalf], [t_st8], scale=1.0 / 64,
                       bias=epsc[:, 0:1])
                P.add("dve", lambda e: e.reciprocal(st8[:, 20:40], st8[:, 0:20]), [t_st8], [t_st8])
                P.tt("dve", h3(qf), h3(qf), bc(st8[:, 20:28], 8), ALU.mult, t_qf + [t_st8], t_qf)
                P.tt("dve", h3(kf), h3(kf), bc(st8[:, 28:36], 8), ALU.mult, t_kf + [t_st8], t_kf)
                P.tt("dve", h3(gv), h3(gv), bc(st8[:, 36:40], 4), ALU.mult, t_gv + [t_st8], t_gv)
                P.tt("pool", h3(qb[:]), h3(qf), gb(gtb[:, 0:64]), ALU.mult, t_qf + [t_gtb], [t_qb])
                P.tt("dve", h3(kf), h3(kf), gb(gtb[:, 64:128]), ALU.mult, t_kf + [t_gtb], t_kf)
                P.tt("pool", vg_tok[:, i, :], gv, gtb[:, 128:384], ALU.mult, t_gv + [t_gtb], [t_vg[i]])
                if is_p:
                    b_, s0_ = i // 2, (i % 2) * 128
                    P.dma("sp", "nk", nk_d[b_, l].rearrange("h s d -> s h d")[s0_:s0_ + 128], h3(kf),
                          reads=t_kf, is_out=True)
                P.cp("act", kb, kf, t_kf, [t_kb])
                for c in range(4):
                    P.tr(bankb[:, c * 128:(c + 1) * 128], qb[:, c * 128:(c + 1) * 128], identb[:],
                         [t_qb, t_identb], [t_bankb])
                for c in range(4):
                    P.tr(bankb[:, 512 + c * 128:512 + (c + 1) * 128], kb[:, c * 128:(c + 1) * 128], identb[:],
                         [t_kb, t_identb], [t_bankb])
                P.cp("dve", qT[:, :, tok], bankb[:, 0:512].rearrange("p (c n) -> p c n", c=4), [t_bankb],
                     [t_q[c][i] for c in range(4)])
                P.cp("dve", kT[:, :, tok], bankb[:, 512:1024].rearrange("p (c n) -> p c n", c=4), [t_bankb],
                     [t_k[c][i] for c in range(4)])

            proj_mm(0)
            proj_evac(0)
            for i in range(8):
                if i + 1 < 8:
                    proj_mm(i + 1)
                proj_chain(i)
                if i + 1 < 8:
                    proj_evac(i + 1)
            gv, t_gv = stmp[1], t_stmp[1]
            _mark(" zu")
            for tt in range(2):
                cols = slice(tt * 512, (tt + 1) * 512)
                for c in range(2):
                    for k in range(8):
                        P.mm(bank[4 + c][:], vE[:, k, c * 128:(c + 1) * 128], hT[:, k, cols], k == 0, k == 7,
                             [t_slot[sE], t_h[k][tt]], [t_bank[4 + c]])
                    P.actv(uT[:, c, cols], bank[4 + c][:], AF.Gelu_apprx_tanh, [t_bank[4 + c]], [t_u[c][tt]])
            for p_ in (pA, pB, pC, pD, pE):
                S.done(p_)
            sF, sG = S.need(pF), S.need(pG)
            if stop_after == "proj":
                raise build_program.Stop()

            def variant(j):
                if is_p:
                    return 0 if j % 2 == 0 else 2
                return 0 if j == 0 else (2 if j == 7 else 1)

            def seq_range(j):
                if is_p:
                    return (j // 2 * 2, j // 2 * 2 + 1)
                return (0, 7)

            sqo, t_sqo = sq[0], t_sq[0]
            rs_ap = (ms[:, 0:256], ms[:, 256:512], ntmp[0][:, 0:256])
            rs_t = (t_ms, t_ms, t_ntmp[0])
            ms2, t_ms2 = ntmp[0][:, 256:512], t_ntmp[0]
            oTn = aT[1][:].rearrange("p f n -> p (f n)").rearrange("p (c n) -> p c n", c=8)
            ETB = ((ms, t_ms), (ntmp[0], t_ntmp[0]), (ntmp[1], t_ntmp[1]))

            def attn_S(it):
                n = it["n"]
                c, hp, h = it["c"], it["hp"], it["h"]
                ps_ = slice(hp * 64, (hp + 1) * 64)
                bk, t_bk = bank[SB_[n % 4]], t_bank[SB_[n % 4]]
                pt, t_pt = aT[0][:, n % 4, :], t_aT[0][n % 4]
                if is_p:
                    u_, tok0_ = it["u"], it["u"] * 256
                    for kt in range(2):
                        P.mm(bk[:, kt * 256:(kt + 1) * 256],
                             kT[ps_, c, tok0_ + kt * 128:tok0_ + (kt + 1) * 128], qT[ps_, c, tok0_:tok0_ + 256],
                             True, True, [t_k[c][2 * u_ + kt], t_q[c][2 * u_], t_q[c][2 * u_ + 1]], [t_bk])
                    P.actv(pt, bk[:], AF.Exp, [t_bk], [t_pt])
                else:
                    r, kts, nl = it["r"], it["kts"], it["nl"]
                    et, t_et = ETB[n % 3]
                    qcols = slice(r * 64, (r + 1) * 64)
                    for j_, kt in enumerate(kts):
                        P.mm(bk[:, j_ * 64:(j_ + 1) * 64], kT[ps_, c, kt * 128:(kt + 1) * 128],
                             qT[ps_, c, qcols], True, True, [t_k[c][kt], t_q[c][r // 2]], [t_bk])
                    for t in range(2):
                        P.mm(bk[:, (nl + t) * 64:(nl + t + 1) * 64], kcT[ps_, c, t * 128:(t + 1) * 128],
                             qT[ps_, c, qcols], True, True, [t_kc, t_q[c][r // 2]], [t_bk])
                    P.actv(et[:, 0:nl * 64], bk[:, 0:nl * 64], AF.Exp, [t_bk], [t_et])
                    P.actv(pt[:, nl * 64:(nl + 2) * 64], bk[:, nl * 64:(nl + 2) * 64], AF.Exp, [t_bk], [t_pt])
                    i0 = 2 * kts[0] - r + 7
                    P.tt("dve", pt[:, 0:nl * 64].rearrange("p (j q) -> p j q", q=64),
                         et[:, 0:nl * 64].rearrange("p (j q) -> p j q", q=64),
                         EB[:, h, i0:i0 + 2 * nl - 1:2, :], ALU.mult, [t_et, t_EB], [t_pt])

            def attn_PV(it):
                n = it["n"]
                c, hp, h = it["c"], it["hp"], it["h"]
                ps_ = slice(hp * 64, (hp + 1) * 64)
                pt, t_pt = aT[0][:, n % 4, :], t_aT[0][n % 4]
                if hp == 0:
                    acnt["bo"] += 1
                bo, t_bo = bank[2 + acnt["bo"] % 2], t_bank[2 + acnt["bo"] % 2]
                if is_p:
                    u_ = it["u"]
                    for kt in range(2):
                        P.mm(bo[ps_, 0:256], v_tok[:, 2 * u_ + kt, h * 64:(h + 1) * 64],
                             pt[:, kt * 256:(kt + 1) * 256], kt == 0, kt == 1,
                             [t_v[2 * u_ + kt], t_pt], [t_bo], tp=(0, hp * 64))
                    for kt in range(2):
                        P.mm(bo[ps_, 256:512], ones[:, 0:64], pt[:, kt * 256:(kt + 1) * 256],
                             kt == 0, kt == 1, [t_ones, t_pt], [t_bo], tp=(0, hp * 64))
                    if hp == 1:
                        P.add("dve", lambda e, bo=bo: e.reciprocal(rden[:], bo[:, 256:512]), [t_bo], [t_rden])
                        P.tt("dve", oraw[:, 2 + c, :], bo[:, 0:256], rden[:], ALU.mult, [t_bo, t_rden],
                             [t_or[2 + c]])
                else:
                    r, kts, nl, s0 = it["r"], it["kts"], it["nl"], it["s0"]
                    mml = []
                    for j_, kt in enumerate(kts):
                        if s0 % 2 == 1 and j_ == 0:
                            p0, p1 = 64, 128
                        elif s0 % 2 == 1 and j_ == nl - 1:
                            p0, p1 = 0, 64
                        else:
                            p0, p1 = 0, 128
                        mml.append((v_tok[p0:p1, kt, h * 64:(h + 1) * 64], ones[p0:p1, 0:64],
                                    pt[p0:p1, j_ * 64:(j_ + 1) * 64], p0, [t_v[kt]]))
                    for t in range(2):
                        mml.append((vc[:, t, h * 64:(h + 1) * 64], ones[:, 0:64],
                                    pt[:, (nl + t) * 64:(nl + t + 1) * 64], 0, [t_vc]))
                    for n_, (lv, lo_, rh, p0, rd) in enumerate(mml):
                        P.mm(bo[ps_, 0:64], lv, rh, n_ == 0, n_ == len(mml) - 1, rd + [t_pt], [t_bo],
                             tp=(p0, hp * 64))
                    for n_, (lv, lo_, rh, p0, rd) in enumerate(mml):
                        P.mm(bo[ps_, 64:128], lo_, rh, n_ == 0, n_ == len(mml) - 1, [t_ones, t_pt],
                             [t_bo], tp=(p0, hp * 64))
                    if hp == 1:
                        P.add("dve", lambda e, bo=bo: e.reciprocal(rden[:, 0:64], bo[:, 64:128]), [t_bo],
                              [t_rden])
                        P.tt("dve", oraw[:, 2 + c, (r % 4) * 64:(r % 4 + 1) * 64], bo[:, 0:64], rden[:, 0:64],
                             ALU.mult, [t_bo, t_rden], [t_or[2 + c]])

            for u in range(4):
                tok0 = u * 256
                ucols = slice(tok0, tok0 + 256)
                tt = u // 2
                _mark(" u%d-attn" % u)
                items = []
                for c in range(4):
                    if is_p:
                        for hp in range(2):
                            items.append({"c": c, "hp": hp, "h": 2 * c + hp, "u": u})
                    else:
                        for r in range(4 * u, 4 * u + 4):
                            s0 = min(max(r - 4, 0), 8)
                            kts = list(range(s0 // 2, (s0 + 7) // 2 + 1))
                            for hp in range(2):
                                items.append({"c": c, "hp": hp, "h": 2 * c + hp, "r": r, "s0": s0, "kts": kts,
                                              "nl": len(kts)})
                for it in items:
                    it["n"] = acnt["n"]
                    acnt["n"] += 1
                def pool_chunk():
                    _mark(" u%d-pool" % u)
                    for c in range(2):
                        for gh in range(2):
                            g4 = 2 * c + gh
                            for jj in range(2):
                                j = 2 * u + jj
                                lo, hi = seq_range(j)
                                ins = [i_ for i_ in (j - 1, j, j + 1) if lo <= i_ <= hi]
                                for n_, i_ in enumerate(ins):
                                    var = 3 if i_ == j - 1 else (4 if i_ == j + 1 else variant(j))
                                    P.mm(bank[4][gh * 64:(gh + 1) * 64, c * 256 + jj * 128:c * 256 + (jj + 1) * 128],
                                         za_tok[:, i_, g4 * 64:(g4 + 1) * 64], bands[:, g4 * 5 + var, :],
                                         n_ == 0, n_ == len(ins) - 1, [t_za[i_], t_bands], [t_bank[4]],
                                         tp=(0, gh * 64))
                        for jj in range(2):
                            P.tt("dve", pooledT[:, c, jj * 128:(jj + 1) * 128],
                                 bank[4][:, c * 256 + jj * 128:c * 256 + (jj + 1) * 128],
                                 invc[:, c * 3 + variant(2 * u + jj), :], ALU.mult, [t_bank[4], t_invc], [t_pl[c]])
                        P.mm(bank[5][:, c * 256:(c + 1) * 256], pwb[:, c, :], pooledT[:, c, :], True, True,
                             [t_pwb, t_pl[c]], [t_bank[5]])
                        P.actv(oraw[:, c, :], bank[5][:, c * 256:(c + 1) * 256], AF.Identity, [t_bank[5], t_vtT],
                               [t_or[c]], scale=vtT[:, l, 104 + c:105 + c])
                    for c in range(2):
                        for gh in range(2):
                            g4 = 2 * c + gh
                            for jj in range(2):
                                P.mm(bank[4][gh * 64:(gh + 1) * 64, c * 256 + jj * 128:c * 256 + (jj + 1) * 128],
                                     vg_tok[:, 2 * u + jj, g4 * 64:(g4 + 1) * 64], wsT[:, g4, :], True, True,
                                     [t_vg[2 * u + jj], t_wsT], [t_bank[4]], tp=(0, gh * 64))
                        P.tt("dve", gv[:, 0:256].rearrange("p (j n) -> p j n", j=2),
                             bank[4][:, c * 256:(c + 1) * 256].rearrange("p (j n) -> p j n", j=2),
                             bsb[:, c:c + 1, :].to_broadcast([128, 2, 128]), ALU.add, [t_bank[4], t_bsb], [t_gv])
                        P.tt("pool", oraw[:, 6 + c, :], gv[:, 0:256], uT[:, c, ucols], ALU.mult,
                             [t_gv, t_u[c][tt]], [t_or[6 + c]])

                for idx in range(len(items) + DEPTH):
                    if idx == len(items) // 2:
                        pool_chunk()
                    if idx < len(items):
                        attn_S(items[idx])
                    if idx - DEPTH >= 0:
                        attn_PV(items[idx - DEPTH])
                _mark(" u%d-out" % u)
                segs = ((0, 2, 256.0), (2, 6, 512.0), (6, 8, 256.0))
                for si, (o0, o1, wd_) in enumerate(segs):
                    stb, t_stb = (bank[6], t_bank[6]) if si < 2 else (bank[5], t_bank[5])
                    scol = slice((si % 2) * 256, (si % 2) * 256 + 256)
                    for oc in range(o0, o1):
                        P.actv(sqo[:, 0:256], oraw[:, oc, :], AF.Square, [t_or[oc]], [t_sqo])
                        P.mm(stb[:, scol], ones[:], sqo[:, 0:256], oc == o0, oc == o1 - 1, [t_ones, t_sqo],
                             [t_stb])
                    P.actv(ms2, stb[:, scol], AF.Sqrt, [t_stb, t_chalf], [t_ms2], scale=1.0 / wd_,
                           bias=epsc[:, 0:1])
                    P.add("dve", lambda e, si=si: e.reciprocal(rs_ap[si], ms2), [t_ms2], [rs_t[si]])
                for oc in range(8):
                    si = 0 if oc < 2 else (1 if oc < 6 else 2)
                    P.stt(oTn[:, oc, :], oraw[:, oc, :], vtT[:, l, 96 + oc:97 + oc], rs_ap[si], ALU.mult, ALU.mult,
                          [t_or[oc], t_vtT, rs_t[si]], [t_aT[1][oc // 2]])
                for dc in range(8):
                    sl = sF if dc < 4 else sG
                    vw = _v3(wslot[sl], 8)
                    bw, t_bw = bank[2 + dc % 2], t_bank[2 + dc % 2]
                    for oc in range(8):
                        P.mm(bw[:, 0:256], vw[:, oc, (dc % 4) * 128:(dc % 4 + 1) * 128], oTn[:, oc, :],
                             oc == 0, oc == 7, [t_slot[sl], t_aT[1][oc // 2]], [t_bw])
                    P.stt(xT[:, dc, ucols], bw[:, 0:256], Gm[:, l, 1, dc, v:v + 1], xT[:, dc, ucols],
                          ALU.mult, ALU.add, [t_bw, t_x[dc][tt], t_mods], [t_x[dc][tt]])
                if u % 2 == 1:
                    after_tt(u // 2)
            S.done(pF)
            S.done(pG)

        class _Stop(Exception):
            pass
        build_program.Stop = _Stop
        try:
            for g in range(2):
                _mark("load%d" % g)
                if not EARLY_NORM:
                    nop = lambda tt: None
                    load_x(g, nop)
                    for l in range(L):
                        _mark("ffn%d%d0" % (g, l))
                        norm_tt(l, 0, g, 0); norm_tt(l, 0, g, 1)
                        ffn(g, l, 0, nop)
                        if g == 0 and l == 0:
                            ada_compute(1)
                        _mark("mix%d%d" % (g, l))
                        mixer_setup(g, l)
                        norm_tt(l, 1, g, 0); norm_tt(l, 1, g, 1)
                        mixer(g, l, nop)
                        _mark("ffn%d%d1" % (g, l))
                        norm_tt(l, 2, g, 0); norm_tt(l, 2, g, 1)
                        ffn(g, l, 1, nop)
                    _mark("store%d" % g)
                    store_y(g)
                    continue
                load_x(g, lambda tt, g=g: norm_tt(0, 0, g, tt))
                for l in range(L):
                    _mark("ffn%d%d0" % (g, l))
                    mixer_setup(g, l)
                    ffn(g, l, 0, lambda tt, g=g, l=l: norm_tt(l, 1, g, tt))
                    if g == 0 and l == 0:
                        ada_compute(1)
                    _mark("mix%d%d" % (g, l))
                    mixer(g, l, lambda tt, g=g, l=l: norm_tt(l, 2, g, tt))
                    _mark("ffn%d%d1" % (g, l))
                    if l + 1 < L:
                        ffn(g, l, 1, lambda tt, g=g, l=l: norm_tt(l + 1, 0, g, tt))
                    else:
                        ffn(g, l, 1, lambda tt: None)
                _mark("store%d" % g)
                store_y(g)
        except _Stop:
            pass
        _mark("end")
        build_program.phases = PH
        P.emit(st)
    return nc


def _consts():
    ident = np.eye(128, dtype=np.float32)
    bands = np.zeros((128, 20, 128), np.float32)
    invc = np.zeros((128, 6, 128), np.float32)
    for gi, w in enumerate(WINS):
        h = w // 2
        tp = np.arange(128)[:, None]
        t = np.arange(128)[None, :]
        inb = ((tp >= t - h) & (tp < t + h)).astype(np.float32)
        cnt_first = (np.minimum(t + h, 10 ** 6) - np.maximum(t - h, 0)).astype(np.float32)
        cnt_mid = np.full((1, 128), float(w), np.float32)
        cnt_last = ((128 - t) + h - np.maximum(0, 0)).astype(np.float32)
        cnt_last = np.minimum(cnt_last, w).astype(np.float32)
        eye = np.eye(128, dtype=np.float32)
        bands[:, gi * 5 + 0, :] = inb - eye * cnt_first
        bands[:, gi * 5 + 1, :] = inb - eye * cnt_mid
        bands[:, gi * 5 + 2, :] = inb - eye * cnt_last
        bands[:, gi * 5 + 3, :] = ((tp - 128) >= (t - h)).astype(np.float32)
        bands[:, gi * 5 + 4, :] = ((tp + 128) < (t + h)).astype(np.float32)
        c, half = gi // 2, gi % 2
        for vi, cn in enumerate((cnt_first, cnt_mid, cnt_last)):
            invc[half * 64:(half + 1) * 64, c * 3 + vi, :] = 1.0 / cn
    dw = np.zeros((31, 127), np.float32)
    for j in range(31):
        dw[j, j + 48] = 1.0
    cols = np.arange(64)
    cs = np.clip(cols - 8, 0, 48)
    ok = (cols[None, :] >= cs[:, None]) & (cols[None, :] < cs[:, None] + 16)
    colok = np.concatenate([ok.T, ok.T], 0).astype(np.float32)
    return ident, bands, invc, dw, colok


def make_in_maps(inp):
    f = lambda a: np.ascontiguousarray(np.asarray(a, dtype=np.float32))
    ident, bands, invc, dw, colok = _consts()
    vt = np.zeros((L, 128, 128), np.float32)
    gt = np.zeros((L, 384), np.float32)
    rpb = np.zeros((L, 128, 31), np.float32)
    for l in range(L):
        vt[l, 0:72] = f(inp["ada_b"])[l].reshape(72, 128)
        vt[l, 72:96] = f(inp["norm_g"])[l].reshape(24, 128)
        vt[l, 96:104] = f(inp["out_norm_g"])[l].reshape(8, 128)
        vt[l, 104:106] = f(inp["pool_scale"])[l].reshape(2, 128)
        gt[l, 0:64] = f(inp["q_norm_g"])[l]
        gt[l, 64:128] = f(inp["k_norm_g"])[l]
        gt[l, 128:384] = f(inp["sg_vnorm_g"])[l].reshape(256)
        rpb[l, 0:120] = f(inp["na_rpb"])[l].reshape(120, 31)
    shared = {
        "vt": vt, "gt": gt, "rpb": rpb,
        "poolw": f(inp["pool_w"]), "sgw": f(inp["sg_w"]), "sgb": f(inp["sg_b"]),
        "adaw": f(inp["ada_w"]), "wg": f(inp["ffn_w_gate"]), "wu": f(inp["ffn_w_up"]),
        "wd": f(inp["ffn_w_down"]), "win": f(inp["w_in"]), "wout": f(inp["w_out"]),
        "ident": ident, "bands": bands, "invc": invc, "dw": dw, "colok": colok,
    }
    xp = f(inp["x_prompt"]); xs = f(inp["x_sample"])
    ck = f(inp["cache_k"]); cv = f(inp["cache_v"]); c = f(inp["c"]); cc = f(inp["c_ctx"])
    maps = []
    for i in range(NCORES):
        x = np.stack([xp[4 * i:4 * i + 4].reshape(1024, D), xs[i]], 0)
        cvec = np.zeros((128, 128), np.float32)
        cvec[0:8] = cc.reshape(8, 128)
        cvec[8:16] = c[i].reshape(8, 128)
        m = dict(shared)
        m.update({"x": np.ascontiguousarray(x), "ck": np.ascontiguousarray(ck[i]),
                  "cv": np.ascontiguousarray(cv[i]), "cvec": cvec})
        maps.append(m)
    return maps


_NC_CACHE = {}


def kernel(**inputs):
    if "nc" not in _NC_CACHE:
        _NC_CACHE["nc"] = build_program()
    nc = _NC_CACHE["nc"]
    maps = make_in_maps(inputs)
    res = run_bass_kernel_spmd(nc, maps, core_ids=list(range(NCORES)))
    r = res.results
    yp = np.concatenate([r[i]["y"][0].reshape(4, 256, D) for i in range(NCORES)], 0)
    ys = np.stack([r[i]["y"][1] for i in range(NCORES)], 0)
    nk = np.concatenate([r[i]["nk"] for i in range(NCORES)], 0)
    nv = np.concatenate([r[i]["nv"] for i in range(NCORES)], 0)
    return (yp.astype(np.float32), ys.astype(np.float32), nk.astype(np.float32), nv.astype(np.float32))
```

```python
import numpy as np
from contextlib import ExitStack
import concourse.bass as bass
import concourse.mybir as mybir
from concourse.bass_utils import run_bass_kernel_spmd

F32 = mybir.dt.float32
BF16 = mybir.dt.bfloat16
AF = mybir.ActivationFunctionType
ALU = mybir.AluOpType
AX = mybir.AxisListType

D = 1024
DFF = 2816
L = 2
NH = 8
EPS = 1e-6
NCORES = 8
WINS = (2, 4, 8, 16)
EARLY_NORM = True


class Tl:
    __slots__ = ("name", "w", "r", "excl")

    def __init__(self, name="", excl=False):
        self.name = name
        self.w = None
        self.r = {}
        self.excl = excl


class Op:
    __slots__ = ("eng", "fn", "deps", "marked", "count", "dkey", "dcount", "is_dma", "uid")


class Prog:
    ENGS = ("pe", "act", "dve", "pool", "sp")

    def __init__(self, nc):
        self.nc = nc
        self.ops = {e: [] for e in self.ENGS}
        self.latest_dma = {}
        self.dma_counts = {}
        self.uid = 0
        self.out_keys = set()

    def add(self, eng, fn, reads=(), writes=(), dma=None, is_out=False):
        op = Op()
        op.eng = eng
        op.fn = fn
        op.marked = False
        op.count = None
        op.is_dma = dma is not None
        op.dkey = dma
        op.uid = self.uid
        self.uid += 1
        deps = {}
        for t in reads:
            if t.w is not None:
                deps[t.w.uid] = t.w
            if t.excl:
                for d in t.r.values():
                    if d.eng != eng:
                        deps[d.uid] = d
        for t in writes:
            if t.w is not None:
                deps[t.w.uid] = t.w
            for d in t.r.values():
                deps[d.uid] = d
        final = {}
        for d in deps.values():
            if d.is_dma:
                d = self.latest_dma[d.dkey]
            elif d.eng == "pe" and eng == "pe" and not op.is_dma:
                continue
            else:
                d.marked = True
            final[d.uid] = d
        op.deps = list(final.values())
        if op.is_dma:
            c = self.dma_counts.get(dma, 0) + 16
            self.dma_counts[dma] = c
            op.dcount = c
            self.latest_dma[dma] = op
            if is_out:
                self.out_keys.add(dma)
        rkey = ("dma", op.uid) if op.is_dma else eng
        for t in reads:
            t.r[rkey] = op
        for t in writes:
            t.w = op
            t.r = {}
        self.ops[eng].append(op)
        return op

    def dma(self, eng, key, out, in_, reads=(), writes=(), is_out=False):
        return self.add(eng, lambda e: e.dma_start(out=out, in_=in_), reads, writes,
                        dma=key, is_out=is_out)

    def mm(self, out, lhsT, rhs, start, stop, reads, writes, tp=None):
        if tp is None:
            return self.add("pe", lambda e: e.matmul(out, lhsT, rhs, start=start, stop=stop), reads, writes)
        return self.add("pe", lambda e: e.matmul(out, lhsT, rhs, start=start, stop=stop, tile_position=tp),
                        reads, writes)

    def tr(self, out, in_, ident, reads, writes):
        return self.add("pe", lambda e: e.transpose(out, in_, ident), reads, writes)

    def actv(self, out, in_, func, reads, writes, scale=None, bias=None):
        kw = {}
        if scale is not None:
            kw["scale"] = scale
        if bias is not None:
            kw["bias"] = bias
        return self.add("act", lambda e: e.activation(out, in_, func, **kw), reads, writes)

    def tt(self, eng, out, in0, in1, op, reads, writes):
        return self.add(eng, lambda e: e.tensor_tensor(out, in0, in1, op=op), reads, writes)

    def ts(self, eng, out, in0, s1, s2, op0, op1, reads, writes):
        if s2 is None:
            return self.add(eng, lambda e: e.tensor_scalar(out, in0, s1, None, op0=op0), reads, writes)
        return self.add(eng, lambda e: e.tensor_scalar(out, in0, s1, s2, op0=op0, op1=op1), reads, writes)

    def stt(self, out, in0, scalar, in1, op0, op1, reads, writes):
        return self.add("dve", lambda e: e.scalar_tensor_tensor(out, in0, scalar, in1, op0=op0, op1=op1),
                        reads, writes)

    def cp(self, eng, out, in_, reads, writes):
        if eng == "act":
            return self.add("act", lambda e: e.copy(out, in_), reads, writes)
        return self.add(eng, lambda e: e.tensor_copy(out, in_), reads, writes)

    def red(self, out, in_, reads, writes):
        return self.add("dve", lambda e: e.tensor_reduce(out, in_, axis=AX.X, op=ALU.add), reads, writes)

    def memset(self, eng, ap, val, writes):
        return self.add(eng, lambda e: e.memset(ap, val), (), writes)

    def emit(self, stack):
        nc = self.nc
        for e in self.ENGS:
            c = 0
            for op in self.ops[e]:
                if op.marked and not op.is_dma:
                    c += 1
                    op.count = c
        esem = {e: stack.enter_context(nc.semaphore("s_" + e)) for e in self.ENGS}
        dsem = {k: stack.enter_context(nc.semaphore("d_%d" % i)) for i, k in enumerate(self.dma_counts)}
        block = stack.enter_context(nc.Block())
        ops = self.ops
        out_final = [(dsem[k], self.dma_counts[k]) for k in sorted(self.out_keys, key=str)]

        def run(ename, e):
            known = {}
            for op in ops[ename]:
                for d in op.deps:
                    if d.is_dma:
                        s, v = dsem[d.dkey], d.dcount
                    else:
                        s, v = esem[d.eng], d.count
                    kid = id(s)
                    if known.get(kid, 0) >= v:
                        continue
                    known[kid] = v
                    e.wait_ge(s, v)
                ins = op.fn(e)
                if op.is_dma:
                    ins.then_inc(dsem[op.dkey], 16)
                elif op.marked:
                    ins.then_inc(esem[ename], 1)
            if ename == "sp":
                for s, v in out_final:
                    e.wait_ge(s, v)

        @block.sync
        def _(e):
            run("sp", e)

        @block.tensor
        def _(e):
            run("pe", e)

        @block.scalar
        def _(e):
            run("act", e)

        @block.vector
        def _(e):
            run("dve", e)

        @block.gpsimd
        def _(e):
            run("pool", e)


class Stream:
    NS = 6
    LOOK = 5

    def __init__(self, P, slots, t_slots):
        self.P = P
        self.slots = slots
        self.t_slots = t_slots
        self.pieces = []
        self.issued = 0
        self.freed = 0
        self.want = 0

    def plan(self, parts):
        self.pieces.append(parts)
        return len(self.pieces) - 1

    def pump(self):
        while self.issued <= self.want and self.issued < len(self.pieces) and self.issued < self.freed + self.NS:
            i = self.issued
            s = i % self.NS
            for dst_fn, src in self.pieces[i]:
                self.P.dma("pool", ("w", s), dst_fn(self.slots[s]), src, writes=[self.t_slots[s]])
            self.issued += 1

    def need(self, p):
        self.want = max(self.want, p + self.LOOK)
        self.pump()
        assert self.issued > p, (self.issued, p, self.freed)
        return p % self.NS

    def done(self, p):
        self.freed = max(self.freed, p + 1)
        self.pump()


def _v3(t, k):
    return t[:].rearrange("p (k n) -> p k n", k=k)


def build_program(stop_after=None):
    nc = bass.Bass("TRN2", target_bir_lowering=False)

    def din(name, shape):
        return nc.dram_tensor(name, shape, F32, kind="ExternalInput").ap()

    def dout(name, shape):
        return nc.dram_tensor(name, shape, F32, kind="ExternalOutput").ap()

    x_d = din("x", [2, 1024, D])
    ck_d = din("ck", [L, NH, 256, 64])
    cv_d = din("cv", [L, NH, 256, 64])
    cvec_d = din("cvec", [128, 128])
    vt_d = din("vt", [L, 128, 128])
    gt_d = din("gt", [L, 384])
    poolw_d = din("poolw", [L, 4, 64, 64])
    sgw_d = din("sgw", [L, 4, 128, 128])
    sgb_d = din("sgb", [L, 4, 128])
    rpb_d = din("rpb", [L, 128, 31])
    adaw_d = din("adaw", [L, D, 9 * D])
    wg_d = din("wg", [L, 2, D, DFF])
    wu_d = din("wu", [L, 2, D, DFF])
    wd_d = din("wd", [L, 2, DFF, D])
    win_d = din("win", [L, D, 2304])
    wout_d = din("wout", [L, D, D])
    ident_d = din("ident", [128, 128])
    bands_d = din("bands", [128, 20, 128])
    invc_d = din("invc", [128, 6, 128])
    dw_d = din("dw", [31, 127])
    colok_d = din("colok", [128, 64])
    y_d = dout("y", [2, 1024, D])
    nk_d = dout("nk", [4, L, NH, 256, 64])
    nv_d = dout("nv", [4, L, NH, 256, 64])

    P = Prog(nc)
    st = ExitStack()

    def sb(name, shape, dt=F32):
        return st.enter_context(nc.sbuf_tensor("s_" + name, shape, dt))

    with st:
        bank = [st.enter_context(nc.psum_tensor("bank%d" % i, [128, 512], F32)) for i in range(7)]
        bankb = st.enter_context(nc.psum_tensor("bankb", [128, 1024], BF16))
        t_bank = [Tl("bank%d" % i, excl=True) for i in range(7)]
        t_bankb = Tl("bankb", excl=True)

        ident = sb("ident", [128, 128]); t_ident = Tl()
        identb = sb("identb", [128, 128], BF16); t_identb = Tl()
        ones = sb("ones", [128, 128], BF16); t_ones = Tl()
        epsc = sb("epsc", [128, 1]); t_chalf = Tl()
        bands = sb("bands", [128, 20, 128], BF16); t_bands = Tl()
        invc = sb("invc", [128, 6, 128]); t_invc = Tl()
        dw = sb("dw", [31, 127]); t_dw = Tl()
        colok = sb("colok", [128, 64]); t_colok = Tl()
        P.dma("sp", "c0", ident[:], ident_d, writes=[t_ident])
        P.dma("sp", "c1", invc[:], invc_d, writes=[t_invc])
        P.dma("sp", "c2", dw[:], dw_d, writes=[t_dw])
        P.dma("sp", "c3", colok[:], colok_d, writes=[t_colok])
        P.dma("pool", "c4", bands[:], bands_d, writes=[t_bands])
        P.cp("dve", identb[:], ident[:], [t_ident], [t_identb])
        P.memset("pool", ones[:], 1.0, [t_ones])
        P.memset("pool", epsc[:], EPS, [t_chalf])

        xT = sb("xT", [128, 8, 1024]); t_x = [[Tl() for _ in range(2)] for _ in range(8)]
        hT = sb("hT", [128, 8, 1024], BF16); t_h = [[Tl() for _ in range(2)] for _ in range(8)]
        wslot = [sb("wslot%d" % i, [128, 4096], BF16) for i in range(Stream.NS)]
        t_slot = [Tl("slot%d" % i) for i in range(Stream.NS)]
        S = Stream(P, wslot, t_slot)
        aT = [sb("aT%d" % i, [128, 4, 512], BF16) for i in range(2)]
        t_aT = [[Tl() for _ in range(4)] for _ in range(2)]
        stmp = [sb("stmp%d" % i, [128, 512]) for i in range(2)]; t_stmp = [Tl(), Tl()]
        xio = sb("xio", [128, 2048]); t_or = [Tl() for _ in range(8)]
        oraw = xio[:].rearrange("p (c n) -> p c n", c=8)
        sq = [sb("sq%d" % i, [128, 512], BF16) for i in range(2)]; t_sq = [Tl(), Tl()]
        ms = sb("ms", [128, 512]); t_ms = Tl()
        rstd, t_rstd = ms, t_ms
        ntmp = [sb("ntmp%d" % i, [128, 512]) for i in range(2)]; t_ntmp = [Tl(), Tl()]
        mods = sb("mods", [128, L, 72, 2]); t_mods = Tl()
        Am = sb("Am", [128, L, 3, 8, 2])
        Gm = sb("Gm", [128, L, 3, 8, 2])
        vtT = sb("vtT", [128, L, 128]); t_vtT = Tl()
        scT = sb("scT", [128, 16], BF16); t_scT = Tl()
        vtr_t = sb("vtr", [128, 128]); t_vtr = Tl()
        vtr = vtr_t[:]
        za_tok = sb("za_tok", [128, 8, 256], BF16); t_za = [Tl() for _ in range(8)]
        vg_tok = sb("vg_tok", [128, 8, 256], BF16); t_vg = [Tl() for _ in range(8)]
        v_tok = sb("v_tok", [128, 8, 512], BF16); t_v = [Tl() for _ in range(8)]
        qT = sb("qT", [128, 4, 1024], BF16); t_q = [[Tl() for _ in range(8)] for _ in range(4)]
        kT = sb("kT", [128, 4, 1024], BF16); t_k = [[Tl() for _ in range(8)] for _ in range(4)]
        uT = sb("uT", [128, 2, 1024], BF16); t_u = [[Tl() for _ in range(2)] for _ in range(2)]
        kcT = sb("kcT", [128, 4, 256], BF16); t_kc = Tl()
        qkz = sb("qkz", [128, 1280]); t_qkz = Tl()
        vc = sb("vc", [128, 2, 512], BF16); t_vc = Tl()
        pooledT = sb("pooledT", [128, 2, 256], BF16); t_pl = [Tl(), Tl()]
        EB = sb("EB", [128, 8, 14, 64], BF16); t_EB = Tl()
        wsT = sb("wsT", [128, 4, 128], BF16); t_wsT = Tl()
        pwb = sb("pwb", [128, 2, 128], BF16); t_pwb = Tl()
        bsb = sb("bsb", [128, 2, 128]); t_bsb = Tl()
        gtb = sb("gtb", [128, 384]); t_gtb = Tl()
        rden = sb("rden", [128, 256]); t_rden = Tl()
        st8 = sb("st8", [128, 40]); t_st8 = Tl()

        pid = {}
        groups = [(0, 4), (4, 4), (8, 2), (10, 4), (14, 4), (18, 4)]

        def plan_ada(l):
            for pc in range(18):
                pid[("ada", l, pc)] = S.plan([(
                    lambda t: _v3(t, 8),
                    adaw_d[l].rearrange("(k p) n -> p k n", p=128)[:, :, pc * 512:(pc + 1) * 512])])

        def plan_ffn(g, l, j):
            for gi, (c0, nfc) in enumerate(groups):
                w = nfc * 128
                pid[("g", g, l, j, gi)] = S.plan([(
                    lambda t, w=w: _v3(t, 8)[:, :, 0:w],
                    wg_d[l, j].rearrange("(k p) n -> p k n", p=128)[:, :, c0 * 128:c0 * 128 + w])])
                pid[("u", g, l, j, gi)] = S.plan([(
                    lambda t, w=w: _v3(t, 8)[:, :, 0:w],
                    wu_d[l, j].rearrange("(k p) n -> p k n", p=128)[:, :, c0 * 128:c0 * 128 + w])])
                pid[("d", g, l, j, gi)] = S.plan([(
                    lambda t, nfc=nfc: _v3(t, 4)[:, 0:nfc, :],
                    wd_d[l, j][c0 * 128:c0 * 128 + w, :].rearrange("(f p) n -> p f n", p=128))])

        def plan_mixer(g, l):
            wv = win_d[l].rearrange("(k p) n -> p k n", p=128)
            pid[("A", g, l)] = S.plan([
                (lambda t: _v3(t, 8)[:, :, 0:256], wv[:, :, 0:256]),
                (lambda t: _v3(t, 8)[:, :, 256:512], wv[:, :, 2048:2304])])
            pid[("B", g, l)] = S.plan([(lambda t: _v3(t, 8), wv[:, :, 256:768])])
            pid[("C", g, l)] = S.plan([(lambda t: _v3(t, 8), wv[:, :, 768:1280])])
            pid[("D", g, l)] = S.plan([(lambda t: _v3(t, 8), wv[:, :, 1280:1792])])
            pid[("E", g, l)] = S.plan([(lambda t: _v3(t, 8)[:, :, 0:256], wv[:, :, 1792:2048])])
            wo = wout_d[l].rearrange("(k p) n -> p k n", p=128)
            pid[("F", g, l)] = S.plan([(lambda t: _v3(t, 8), wo[:, :, 0:512])])
            pid[("G", g, l)] = S.plan([(lambda t: _v3(t, 8), wo[:, :, 512:1024])])

        plan_ada(0)
        for g in range(2):
            for l in range(L):
                plan_ffn(g, l, 0)
                if g == 0 and l == 0:
                    plan_ada(1)
                plan_mixer(g, l)
                plan_ffn(g, l, 1)

        P.dma("sp", "v0", vtr, cvec_d, writes=[t_vtr])
        P.tr(bank[6][:, 0:128], vtr, ident[:], [t_vtr, t_ident], [t_bank[6]])
        P.actv(scT[:], bank[6][:, 0:16], AF.Silu, [t_bank[6]], [t_scT])
        for l in range(L):
            P.dma("sp", "v0", vtr, vt_d[l], writes=[t_vtr])
            P.tr(bank[6][:, 0:128], vtr, ident[:], [t_vtr, t_ident], [t_bank[6]])
            P.cp("dve", vtT[:, l, :], bank[6][:, 0:128], [t_bank[6]], [t_vtT])
        def ada_compute(l):
            for pc in range(18):
                p_ = pid[("ada", l, pc)]
                s = S.need(p_)
                wv = _v3(wslot[s], 8)
                for fc in range(4):
                    j = pc * 4 + fc
                    for k in range(8):
                        P.mm(bank[5][:, 2 * j:2 * j + 2], wv[:, k, fc * 128:(fc + 1) * 128],
                             scT[:, k:16:8], k == 0, k == 7, [t_slot[s], t_scT], [t_bank[5]])
                S.done(p_)
            P.tt("dve", mods[:, l, :, :], bank[5][:, 0:144].rearrange("p (j v) -> p j v", v=2),
                 vtT[:, l, 0:72].rearrange("p (j o) -> p j o", o=1).to_broadcast([128, 72, 2]),
                 ALU.add, [t_bank[5], t_vtT], [t_mods])
            for n in range(3):
                P.ts("dve", Am[:, l, n, :, :], mods[:, l, (3 * n + 1) * 8:(3 * n + 2) * 8, :], 1.0, None,
                     ALU.add, None, [t_mods], [t_mods])
                P.tt("dve", Am[:, l, n, :, :], Am[:, l, n, :, :],
                     vtT[:, l, 72 + n * 8:80 + n * 8].rearrange("p (j o) -> p j o", o=1).to_broadcast([128, 8, 2]),
                     ALU.mult, [t_mods, t_vtT], [t_mods])
                P.ts("dve", Gm[:, l, n, :, :], mods[:, l, (3 * n + 2) * 8:(3 * n + 3) * 8, :],
                     (1.0 if n == 1 else 0.5), None, ALU.mult, None, [t_mods], [t_mods])


        ada_compute(0)

        PH = []
        _mark = lambda nm: PH.append((nm, len(P.ops["pe"])))
        def norm_tt(l, n, v, tt):
            cols = slice(tt * 512, (tt + 1) * 512)
            for k in range(8):
                P.actv(sq[k % 2][:], xT[:, k, cols], AF.Square, [t_x[k][tt]], [t_sq[k % 2]])
                P.mm(bank[6][:], ones[:], sq[k % 2][:], k == 0, k == 7, [t_sq[k % 2], t_ones], [t_bank[6]])
            P.actv(ms[:], bank[6][:], AF.Sqrt, [t_bank[6], t_chalf], [t_ms], scale=1.0 / D, bias=epsc[:, 0:1])
            P.add("dve", lambda e: e.reciprocal(rstd[:], ms[:]), [t_ms], [t_rstd])
            for k in range(8):
                P.tt("dve", ntmp[k % 2][:], xT[:, k, cols], rstd[:], ALU.mult,
                     [t_x[k][tt], t_rstd], [t_ntmp[k % 2]])
                P.actv(hT[:, k, cols], ntmp[k % 2][:], AF.Identity, [t_ntmp[k % 2], t_mods], [t_h[k][tt]],
                       scale=Am[:, l, n, k, v:v + 1], bias=mods[:, l, 3 * n * 8 + k, v:v + 1])

        cnt = {"up": 0, "dn": 0, "ab": 0}

        def ffn(g, l, j, after_tt):
            n = 0 if j == 0 else 2
            v = g
            pending = []
            last_gi = len(groups) - 1

            def flush():
                while pending:
                    pending.pop(0)()

            for gi, (c0, nfc) in enumerate(groups):
                pg_, pu_, pd_ = pid[("g", g, l, j, gi)], pid[("u", g, l, j, gi)], pid[("d", g, l, j, gi)]
                sg, su, sd = S.need(pg_), S.need(pu_), S.need(pd_)
                vg_, vu_, vd_ = _v3(wslot[sg], 8), _v3(wslot[su], 8), _v3(wslot[sd], 4)
                for tt in range(2):
                    cols = slice(tt * 512, (tt + 1) * 512)
                    ab = cnt["ab"] % 2
                    cnt["ab"] += 1
                    for f in range(nfc):
                        ib = cnt["up"] % 2
                        cnt["up"] += 1
                        for k in range(8):
                            P.mm(bank[ib][:], vg_[:, k, f * 128:(f + 1) * 128], hT[:, k, cols], k == 0, k == 7,
                                 [t_slot[sg], t_h[k][tt]], [t_bank[ib]])
                        for k in range(8):
                            P.mm(bank[2 + ib][:], vu_[:, k, f * 128:(f + 1) * 128], hT[:, k, cols], k == 0, k == 7,
                                 [t_slot[su], t_h[k][tt]], [t_bank[2 + ib]])
                        P.actv(stmp[ib][:], bank[ib][:], AF.Silu, [t_bank[ib]], [t_stmp[ib]])
                        P.tt("dve", aT[ab][:, f, :], stmp[ib][:], bank[2 + ib][:], ALU.mult,
                             [t_stmp[ib], t_bank[2 + ib]], [t_aT[ab][f]])

                    def down(ab=ab, nfc=nfc, sd=sd, vd_=vd_, cols=cols, tt=tt, gi=gi):
                        for dc in range(8):
                            ib2 = 4 + cnt["dn"] % 2
                            cnt["dn"] += 1
                            for f in range(nfc):
                                P.mm(bank[ib2][:], vd_[:, f, dc * 128:(dc + 1) * 128], aT[ab][:, f, :],
                                     f == 0, f == nfc - 1, [t_slot[sd], t_aT[ab][f]], [t_bank[ib2]])
                            P.stt(xT[:, dc, cols], bank[ib2][:], Gm[:, l, n, dc, v:v + 1], xT[:, dc, cols],
                                  ALU.mult, ALU.add, [t_bank[ib2], t_x[dc][tt], t_mods], [t_x[dc][tt]])
                        if gi == last_gi:
                            after_tt(tt)

                    flush()
                    if gi == last_gi:
                        down()
                    else:
                        pending.append(down)
                S.done(pg_)
                S.done(pu_)
                if gi > 0:
                    S.done(pid[("d", g, l, j, gi - 1)])
            flush()
            S.done(pid[("d", g, l, j, len(groups) - 1)])

        def load_x(g, after_tt):
            for i in range(8):
                xb = i % 2
                xs_ = xio[:, xb * 1024:(xb + 1) * 1024]
                tl_ = t_or[xb * 4:xb * 4 + 4]
                P.dma("sp", ("xio", xb), xs_, x_d[g, i * 128:(i + 1) * 128, :], writes=tl_)
                for hb in range(2):
                    for kk in range(4):
                        k = hb * 4 + kk
                        P.tr(bank[hb][:, kk * 128:(kk + 1) * 128], xs_[:, k * 128:(k + 1) * 128], ident[:],
                             tl_ + [t_ident], [t_bank[hb]])
                    P.cp("dve" if hb == 0 else "act", xT[:, hb * 4:hb * 4 + 4, i * 128:(i + 1) * 128],
                         bank[hb][:].rearrange("p (k n) -> p k n", k=4), [t_bank[hb]],
                         [t_x[k][i // 4] for k in range(hb * 4, hb * 4 + 4)])
                if i % 4 == 3:
                    after_tt(i // 4)

        def store_y(g):
            for i in range(8):
                xb = i % 2
                xs_ = xio[:, xb * 1024:(xb + 1) * 1024]
                tl_ = t_or[xb * 4:xb * 4 + 4]
                for hb in range(2):
                    for kk in range(4):
                        k = hb * 4 + kk
                        P.tr(bank[hb][:, kk * 128:(kk + 1) * 128], xT[:, k, i * 128:(i + 1) * 128], ident[:],
                             [t_x[k][i // 4], t_ident], [t_bank[hb]])
                    P.cp("dve" if hb == 0 else "act", xs_[:, hb * 512:(hb + 1) * 512], bank[hb][:],
                         [t_bank[hb]], tl_)
                P.dma("sp", ("y", xb), y_d[g, i * 128:(i + 1) * 128, :], xs_, reads=tl_, is_out=True)

        def grp_rms(src, t_src, nh):
            n = nh * 64
            s3 = src.rearrange("p (h d) -> p h d", d=64)
            sqs, t_sqs = sq[0], t_sq[0]
            P.tt("dve", sqs[:, 0:n], src, src, ALU.mult, [t_src], [t_sqs])
            P.red(st8[:, 0:nh], sqs[:, 0:n].rearrange("p (h d) -> p h d", d=64), [t_sqs], [t_st8])
            P.actv(st8[:, 0:nh], st8[:, 0:nh], AF.Sqrt, [t_st8, t_chalf], [t_st8], scale=1.0 / 64, bias=epsc[:, 0:1])
            P.add("dve", lambda e: e.reciprocal(st8[:, 8:8 + nh], st8[:, 0:nh]), [t_st8], [t_st8])
            P.tt("dve", s3, s3, st8[:, 8:8 + nh].rearrange("p (h o) -> p h o", o=1).to_broadcast([128, nh, 64]),
                 ALU.mult, [t_src, t_st8], [t_src])

        def mixer_setup(g, l):
            pwf = stmp[1][:, 0:256].rearrange("p (c n) -> p c n", c=2)
            t_pwf = t_stmp[1]
            RT = ntmp[0][0:31, 0:128]
            RT01 = ntmp[0][0:31, 128:352].rearrange("p (a n) -> p a n", a=2)
            t_RT = t_RT01 = t_ntmp[0]
            P.dma("sp", "ms0", gtb[:], gt_d[l:l + 1, :].to_broadcast([128, 384]), writes=[t_gtb])
            P.ts("dve", gtb[:, 0:64], gtb[:, 0:64], 0.125, None, ALU.mult, None, [t_gtb], [t_gtb])
            P.memset("pool", pwf, 0.0, [t_pwf])
            for c in range(2):
                for gh in range(2):
                    P.dma("sp", "ms1", pwf[gh * 64:(gh + 1) * 64, c, gh * 64:(gh + 1) * 64], poolw_d[l, 2 * c + gh],
                          writes=[t_pwf])
            P.cp("dve", pwb[:], pwf, [t_pwf], [t_pwb])
            for g4 in range(4):
                P.dma("sp", "ms2", stmp[0][:, 0:128], sgw_d[l, g4], writes=[t_stmp[0]])
                P.tr(bank[6][:, 0:128], stmp[0][:, 0:128], ident[:], [t_stmp[0], t_ident], [t_bank[6]])
                P.cp("dve", wsT[:, g4, :], bank[6][:, 0:128], [t_bank[6]], [t_wsT])
            for g4 in range(4):
                P.dma("sp", "ms3", bsb[(g4 % 2) * 64:(g4 % 2 + 1) * 64, g4 // 2, :],
                      sgb_d[l, g4:g4 + 1, :].to_broadcast([64, 128]), writes=[t_bsb])
            if g == 1:
                kctok = aT[1][:, 0:2, :]
                t_kctok = Tl()
                for t in range(2):
                    P.dma("pool", "ms4", kctok[:, t, :].rearrange("p (h d) -> p h d", h=8),
                          ck_d[l][:, t * 128:(t + 1) * 128, :].rearrange("h p d -> p h d"),
                          writes=[t_aT[1][0], t_aT[1][1]])
                    P.dma("pool", "ms5", vc[:, t, :].rearrange("p (h d) -> p h d", h=8),
                          cv_d[l][:, t * 128:(t + 1) * 128, :].rearrange("h p d -> p h d"), writes=[t_vc])
                for t in range(2):
                    for c in range(4):
                        P.tr(bankb[:, c * 128:(c + 1) * 128], kctok[:, t, c * 128:(c + 1) * 128], identb[:],
                             [t_aT[1][0], t_aT[1][1], t_identb], [t_bankb])
                    P.cp("dve", kcT[:, :, t * 128:(t + 1) * 128],
                         bankb[:, 0:512].rearrange("p (c n) -> p c n", c=4), [t_bankb], [t_kc])
                P.dma("sp", "ms6", stmp[1][:, 0:31], rpb_d[l], writes=[t_stmp[1]])
                P.tr(bank[6][0:31, 0:128], stmp[1][:, 0:31], ident[:], [t_stmp[1], t_ident], [t_bank[6]])
                P.cp("dve", RT, bank[6][0:31, 0:128], [t_bank[6]], [t_RT])
                rt3 = RT[:, 0:120].rearrange("p (h r) -> p h r", h=8)
                for half in range(2):
                    P.cp("dve", RT01[:, half, :].rearrange("p (h r) -> p h r", h=8), rt3[:, :, half:half + 14],
                         [t_RT], [t_RT01])
                for b in range(16):
                    bk = bank[b % 2]
                    for qi in range(4):
                        qc = b * 4 + qi
                        for half in range(2):
                            P.mm(bk[half * 64:(half + 1) * 64, qi * 112:(qi + 1) * 112],
                                 dw[0:31, 63 - qc:127 - qc], RT01[:, half, :], True, True,
                                 [t_dw, t_RT01], [t_bank[b % 2]], tp=(0, half * 64))
                    P.actv(EB[:, :, :, b * 4:(b + 1) * 4].rearrange("p h r q -> p q (h r)"),
                           bk[:, 0:448].rearrange("p (q n) -> p q n", q=4), AF.Exp, [t_bank[b % 2]], [t_EB])
                ebv = EB[:].rearrange("p h r q -> p (h r) q")
                P.tt("dve", ebv, ebv, colok[:, :].rearrange("p (o q) -> p o q", o=1).to_broadcast([128, 112, 64]),
                     ALU.mult, [t_EB, t_colok], [t_EB])

        acnt = {"n": 0, "bo": 0}
        SB_ = (0, 1, 4, 5)
        DEPTH = 3

        def mixer(g, l, after_tt):
            v = g
            is_p = (g == 0)
            if stop_after == "premix":
                raise build_program.Stop()
            pA, pB, pC, pD, pE, pF, pG = [pid[(nm, g, l)] for nm in "ABCDEFG"]
            sA, sB, sC, sD, sE = [S.need(p_) for p_ in (pA, pB, pC, pD, pE)]
            vA, vB, vC, vD, vE = [_v3(wslot[s_], 8) for s_ in (sA, sB, sC, sD, sE)]
            h3 = lambda ap: ap.rearrange("p (h d) -> p h d", d=64)
            SETS = (
                dict(qf=stmp[0][:], t_qf=[t_stmp[0]], kf=stmp[1][:], t_kf=[t_stmp[1]],
                     gv=ntmp[1][:, 0:256], t_gv=[t_ntmp[1]], vf=ntmp[0][:], t_vf=[t_ntmp[0]]),
                dict(qf=xio[:, 0:512], t_qf=[t_or[0], t_or[1]], kf=xio[:, 512:1024], t_kf=[t_or[2], t_or[3]],
                     gv=xio[:, 1024:1280], t_gv=[t_or[4]], vf=xio[:, 1536:2048], t_vf=[t_or[6], t_or[7]]),
            )
            sqq, t_sqq = sq[0][:], [t_sq[0]]
            sqk, t_sqk = aT[0][:, 1, :], [t_aT[0][1]]
            sqz, t_sqz = aT[0][:, 2, 0:256], [t_aT[0][2]]
            qb, t_qb = sq[1], t_sq[1]
            kb, t_kb = aT[0][:, 0, :], t_aT[0][0]
            bc = lambda ap, n: ap.rearrange("p (h o) -> p h o", o=1).to_broadcast([128, n, 64])
            gb = lambda ap: ap.rearrange("p (o d) -> p o d", o=1).to_broadcast([128, 8, 64])

            def proj_mm(i):
                tt = i // 4
                tok = slice(i * 128, (i + 1) * 128)
                for bi, (vw, sl) in enumerate(((vA, sA), (vB, sB), (vC, sC), (vD, sD))):
                    for k in range(8):
                        P.mm(bank[bi][:], hT[:, k, tok], vw[:, k, :], k == 0, k == 7,
                             [t_h[k][tt], t_slot[sl]], [t_bank[bi]])

            def proj_evac(i):
                B = SETS[i % 2]
                P.cp("act", B["qf"], bank[1][:], [t_bank[1]], B["t_qf"])
                P.cp("act", B["kf"], bank[2][:], [t_bank[2]], B["t_kf"])
                P.actv(B["gv"], bank[0][:, 256:512], AF.Gelu_apprx_tanh, [t_bank[0]], B["t_gv"])
                P.cp("act", za_tok[:, i, :], bank[0][:, 0:256], [t_bank[0]], [t_za[i]])
                P.cp("act", v_tok[:, i, :], bank[3][:], [t_bank[3]], [t_v[i]])
                if is_p:
                    b_, s0_ = i // 2, (i % 2) * 128
                    P.cp("dve", B["vf"], bank[3][:], [t_bank[3]], B["t_vf"])
                    P.dma("sp", "nv", nv_d[b_, l].rearrange("h s d -> s h d")[s0_:s0_ + 128], h3(B["vf"]),
                          reads=B["t_vf"], is_out=True)

            def proj_chain(i):
                B = SETS[i % 2]
                tok = slice(i * 128, (i + 1) * 128)
                qf, kf, gv = B["qf"], B["kf"], B["gv"]
                t_qf, t_kf, t_gv = B["t_qf"], B["t_kf"], B["t_gv"]
                P.tt("dve", sqq, qf, qf, ALU.mult, t_qf, t_sqq)
                P.tt("dve", sqk, kf, kf, ALU.mult, t_kf, t_sqk)
                P.tt("pool", sqz, gv, gv, ALU.mult, t_gv, t_sqz)
                P.red(st8[:, 0:8], h3(sqq), t_sqq, [t_st8])
                P.red(st8[:, 8:16], h3(sqk), t_sqk, [t_st8])
                P.red(st8[:, 16:20], h3(sqz), t_sqz, [t_st8])
                P.actv(st8[:, 0:20], st8[:, 0:20], AF.Sqrt, [t_st8, t_chalf], [t_st8], scale=1.0 / 64,
                       bias=epsc[:, 0:1])
                P.add("dve", lambda e: e.reciprocal(st8[:, 20:40], st8[:, 0:20]), [t_st8], [t_st8])
                P.tt("dve", h3(qf), h3(qf), bc(st8[:, 20:28], 8), ALU.mult, t_qf + [t_st8], t_qf)
                P.tt("dve", h3(kf), h3(kf), bc(st8[:, 28:36], 8), ALU.mult, t_kf + [t_st8], t_kf)
                P.tt("dve", h3(gv), h3(gv), bc(st8[:, 36:40], 4), ALU.mult, t_gv + [t_st8], t_gv)
                P.tt("pool", h3(qb[:]), h3(qf), gb(gtb[:, 0:64]), ALU.mult, t_qf + [t_gtb], [t_qb])
                P.tt("dve", h3(kf), h3(kf), gb(gtb[:, 64:128]), ALU.mult, t_kf + [t_gtb], t_kf)
                P.tt("pool", vg_tok[:, i, :], gv, gtb[:, 128:384], ALU.mult, t_gv + [t_gtb], [t_vg[i]])
                if is_p:
                    b_, s0_ = i // 2, (i % 2) * 128
                    P.dma("sp", "nk", nk_d[b_, l].rearrange("h s d -> s h d")[s0_:s0_ + 128], h3(kf),
                          reads=t_kf, is_out=True)
                P.cp("act", kb, kf, t_kf, [t_kb])
                for c in range(4):
                    P.tr(bankb[:, c * 128:(c + 1) * 128], qb[:, c * 128:(c + 1) * 128], identb[:],
                         [t_qb, t_identb], [t_bankb])
                for c in range(4):
                    P.tr(bankb[:, 512 + c * 128:512 + (c + 1) * 128], kb[:, c * 128:(c + 1) * 128], identb[:],
                         [t_kb, t_identb], [t_bankb])
                P.cp("dve", qT[:, :, tok], bankb[:, 0:512].rearrange("p (c n) -> p c n", c=4), [t_bankb],
                     [t_q[c][i] for c in range(4)])
                P.cp("dve", kT[:, :, tok], bankb[:, 512:1024].rearrange("p (c n) -> p c n", c=4), [t_bankb],
                     [t_k[c][i] for c in range(4)])

            proj_mm(0)
            proj_evac(0)
            for i in range(8):
                if i + 1 < 8:
                    proj_mm(i + 1)
                proj_chain(i)
                if i + 1 < 8:
                    proj_evac(i + 1)
            gv, t_gv = stmp[1], t_stmp[1]
            _mark(" zu")
            for tt in range(2):
                cols = slice(tt * 512, (tt + 1) * 512)
                for c in range(2):
                    for k in range(8):
                        P.mm(bank[4 + c][:], vE[:, k, c * 128:(c + 1) * 128], hT[:, k, cols], k == 0, k == 7,
                             [t_slot[sE], t_h[k][tt]], [t_bank[4 + c]])
                    P.actv(uT[:, c, cols], bank[4 + c][:], AF.Gelu_apprx_tanh, [t_bank[4 + c]], [t_u[c][tt]])
            for p_ in (pA, pB, pC, pD, pE):
                S.done(p_)
            sF, sG = S.need(pF), S.need(pG)
            if stop_after == "proj":
                raise build_program.Stop()

            def variant(j):
                if is_p:
                    return 0 if j % 2 == 0 else 2
                return 0 if j == 0 else (2 if j == 7 else 1)

            def seq_range(j):
                if is_p:
                    return (j // 2 * 2, j // 2 * 2 + 1)
                return (0, 7)

            sqo, t_sqo = sq[0], t_sq[0]
            rs_ap = (ms[:, 0:256], ms[:, 256:512], ntmp[0][:, 0:256])
            rs_t = (t_ms, t_ms, t_ntmp[0])
            ms2, t_ms2 = ntmp[0][:, 256:512], t_ntmp[0]
            oTn = aT[1][:].rearrange("p f n -> p (f n)").rearrange("p (c n) -> p c n", c=8)
            ETB = ((ms, t_ms), (ntmp[0], t_ntmp[0]), (ntmp[1], t_ntmp[1]))

            def attn_S(it):
                n = it["n"]
                c, hp, h = it["c"], it["hp"], it["h"]
                ps_ = slice(hp * 64, (hp + 1) * 64)
                bk, t_bk = bank[SB_[n % 4]], t_bank[SB_[n % 4]]
                pt, t_pt = aT[0][:, n % 4, :], t_aT[0][n % 4]
                if is_p:
                    u_, tok0_ = it["u"], it["u"] * 256
                    for kt in range(2):
                        P.mm(bk[:, kt * 256:(kt + 1) * 256],
                             kT[ps_, c, tok0_ + kt * 128:tok0_ + (kt + 1) * 128], qT[ps_, c, tok0_:tok0_ + 256],
                             True, True, [t_k[c][2 * u_ + kt], t_q[c][2 * u_], t_q[c][2 * u_ + 1]], [t_bk])
                    P.actv(pt, bk[:], AF.Exp, [t_bk], [t_pt])
                else:
                    r, kts, nl = it["r"], it["kts"], it["nl"]
                    et, t_et = ETB[n % 3]
                    qcols = slice(r * 64, (r + 1) * 64)
                    for j_, kt in enumerate(kts):
                        P.mm(bk[:, j_ * 64:(j_ + 1) * 64], kT[ps_, c, kt * 128:(kt + 1) * 128],
                             qT[ps_, c, qcols], True, True, [t_k[c][kt], t_q[c][r // 2]], [t_bk])
                    for t in range(2):
                        P.mm(bk[:, (nl + t) * 64:(nl + t + 1) * 64], kcT[ps_, c, t * 128:(t + 1) * 128],
                             qT[ps_, c, qcols], True, True, [t_kc, t_q[c][r // 2]], [t_bk])
                    P.actv(et[:, 0:nl * 64], bk[:, 0:nl * 64], AF.Exp, [t_bk], [t_et])
                    P.actv(pt[:, nl * 64:(nl + 2) * 64], bk[:, nl * 64:(nl + 2) * 64], AF.Exp, [t_bk], [t_pt])
                    i0 = 2 * kts[0] - r + 7
                    P.tt("dve", pt[:, 0:nl * 64].rearrange("p (j q) -> p j q", q=64),
                         et[:, 0:nl * 64].rearrange("p (j q) -> p j q", q=64),
                         EB[:, h, i0:i0 + 2 * nl - 1:2, :], ALU.mult, [t_et, t_EB], [t_pt])

            def attn_PV(it):
                n = it["n"]
                c, hp, h = it["c"], it["hp"], it["h"]
                ps_ = slice(hp * 64, (hp + 1) * 64)
                pt, t_pt = aT[0][:, n % 4, :], t_aT[0][n % 4]
                if hp == 0:
                    acnt["bo"] += 1
                bo, t_bo = bank[2 + acnt["bo"] % 2], t_bank[2 + acnt["bo"] % 2]
                if is_p:
                    u_ = it["u"]
                    for kt in range(2):
                        P.mm(bo[ps_, 0:256], v_tok[:, 2 * u_ + kt, h * 64:(h + 1) * 64],
                             pt[:, kt * 256:(kt + 1) * 256], kt == 0, kt == 1,
                             [t_v[2 * u_ + kt], t_pt], [t_bo], tp=(0, hp * 64))
                    for kt in range(2):
                        P.mm(bo[ps_, 256:512], ones[:, 0:64], pt[:, kt * 256:(kt + 1) * 256],
                             kt == 0, kt == 1, [t_ones, t_pt], [t_bo], tp=(0, hp * 64))
                    if hp == 1:
                        P.add("dve", lambda e, bo=bo: e.reciprocal(rden[:], bo[:, 256:512]), [t_bo], [t_rden])
                        P.tt("dve", oraw[:, 2 + c, :], bo[:, 0:256], rden[:], ALU.mult, [t_bo, t_rden],
                             [t_or[2 + c]])
                else:
                    r, kts, nl, s0 = it["r"], it["kts"], it["nl"], it["s0"]
                    mml = []
                    for j_, kt in enumerate(kts):
                        if s0 % 2 == 1 and j_ == 0:
                            p0, p1 = 64, 128
                        elif s0 % 2 == 1 and j_ == nl - 1:
                            p0, p1 = 0, 64
                        else:
                            p0, p1 = 0, 128
                        mml.append((v_tok[p0:p1, kt, h * 64:(h + 1) * 64], ones[p0:p1, 0:64],
                                    pt[p0:p1, j_ * 64:(j_ + 1) * 64], p0, [t_v[kt]]))
                    for t in range(2):
                        mml.append((vc[:, t, h * 64:(h + 1) * 64], ones[:, 0:64],
                                    pt[:, (nl + t) * 64:(nl + t + 1) * 64], 0, [t_vc]))
                    for n_, (lv, lo_, rh, p0, rd) in enumerate(mml):
                        P.mm(bo[ps_, 0:64], lv, rh, n_ == 0, n_ == len(mml) - 1, rd + [t_pt], [t_bo],
                             tp=(p0, hp * 64))
                    for n_, (lv, lo_, rh, p0, rd) in enumerate(mml):
                        P.mm(bo[ps_, 64:128], lo_, rh, n_ == 0, n_ == len(mml) - 1, [t_ones, t_pt],
                             [t_bo], tp=(p0, hp * 64))
                    if hp == 1:
                        P.add("dve", lambda e, bo=bo: e.reciprocal(rden[:, 0:64], bo[:, 64:128]), [t_bo],
                              [t_rden])
                        P.tt("dve", oraw[:, 2 + c, (r % 4) * 64:(r % 4 + 1) * 64], bo[:, 0:64], rden[:, 0:64],
                             ALU.mult, [t_bo, t_rden], [t_or[2 + c]])

            for u in range(4):
                tok0 = u * 256
                ucols = slice(tok0, tok0 + 256)
                tt = u // 2
                _mark(" u%d-attn" % u)
                items = []
                for c in range(4):
                    if is_p:
                        for hp in range(2):
                            items.append({"c": c, "hp": hp, "h": 2 * c + hp, "u": u})
                    else:
                        for r in range(4 * u, 4 * u + 4):
                            s0 = min(max(r - 4, 0), 8)
                            kts = list(range(s0 // 2, (s0 + 7) // 2 + 1))
                            for hp in range(2):
                                items.append({"c": c, "hp": hp, "h": 2 * c + hp, "r": r, "s0": s0, "kts": kts,
                                              "nl": len(kts)})
                for it in items:
                    it["n"] = acnt["n"]
                    acnt["n"] += 1
                def pool_chunk():
                    _mark(" u%d-pool" % u)
                    for c in range(2):
                        for gh in range(2):
                            g4 = 2 * c + gh
                            for jj in range(2):
                                j = 2 * u + jj
                                lo, hi = seq_range(j)
                                ins = [i_ for i_ in (j - 1, j, j + 1) if lo <= i_ <= hi]
                                for n_, i_ in enumerate(ins):
                                    var = 3 if i_ == j - 1 else (4 if i_ == j + 1 else variant(j))
                                    P.mm(bank[4][gh * 64:(gh + 1) * 64, c * 256 + jj * 128:c * 256 + (jj + 1) * 128],
                                         za_tok[:, i_, g4 * 64:(g4 + 1) * 64], bands[:, g4 * 5 + var, :],
                                         n_ == 0, n_ == len(ins) - 1, [t_za[i_], t_bands], [t_bank[4]],
                                         tp=(0, gh * 64))
                        for jj in range(2):
                            P.tt("dve", pooledT[:, c, jj * 128:(jj + 1) * 128],
                                 bank[4][:, c * 256 + jj * 128:c * 256 + (jj + 1) * 128],
                                 invc[:, c * 3 + variant(2 * u + jj), :], ALU.mult, [t_bank[4], t_invc], [t_pl[c]])
                        P.mm(bank[5][:, c * 256:(c + 1) * 256], pwb[:, c, :], pooledT[:, c, :], True, True,
                             [t_pwb, t_pl[c]], [t_bank[5]])
                        P.actv(oraw[:, c, :], bank[5][:, c * 256:(c + 1) * 256], AF.Identity, [t_bank[5], t_vtT],
                               [t_or[c]], scale=vtT[:, l, 104 + c:105 + c])
                    for c in range(2):
                        for gh in range(2):
                            g4 = 2 * c + gh
                            for jj in range(2):
                                P.mm(bank[4][gh * 64:(gh + 1) * 64, c * 256 + jj * 128:c * 256 + (jj + 1) * 128],
                                     vg_tok[:, 2 * u + jj, g4 * 64:(g4 + 1) * 64], wsT[:, g4, :], True, True,
                                     [t_vg[2 * u + jj], t_wsT], [t_bank[4]], tp=(0, gh * 64))
                        P.tt("dve", gv[:, 0:256].rearrange("p (j n) -> p j n", j=2),
                             bank[4][:, c * 256:(c + 1) * 256].rearrange("p (j n) -> p j n", j=2),
                             bsb[:, c:c + 1, :].to_broadcast([128, 2, 128]), ALU.add, [t_bank[4], t_bsb], [t_gv])
                        P.tt("pool", oraw[:, 6 + c, :], gv[:, 0:256], uT[:, c, ucols], ALU.mult,
                             [t_gv, t_u[c][tt]], [t_or[6 + c]])

                def capture(fn):
                    rec = []
                    P.add = lambda eng, f, reads=(), writes=(), dma=None, is_out=False: rec.append(
                        (eng, f, reads, writes, dma, is_out))
                    try:
                        fn()
                    finally:
                        del P.add
                    return rec

                def emit_pair(fn, ita, itb):
                    ra = capture(lambda: fn(ita))
                    rb = capture(lambda: fn(itb))
                    pa = [x for x in ra if x[0] == "pe"]
                    pb = [x for x in rb if x[0] == "pe"]
                    for i_ in range(max(len(pa), len(pb))):
                        if i_ < len(pa):
                            P.add(*pa[i_])
                        if i_ < len(pb):
                            P.add(*pb[i_])
                    for x in ra + rb:
                        if x[0] != "pe":
                            P.add(*x)

                pairs = [(items[2 * j], items[2 * j + 1]) for j in range(len(items) // 2)]
                DEPTHP = 1
                for idx in range(len(pairs) + DEPTHP):
                    if idx == len(pairs) // 2:
                        pool_chunk()
                    if idx < len(pairs):
                        emit_pair(attn_S, *pairs[idx])
                    if idx - DEPTHP >= 0:
                        emit_pair(attn_PV, *pairs[idx - DEPTHP])
                _mark(" u%d-out" % u)
                segs = ((0, 2, 256.0), (2, 6, 512.0), (6, 8, 256.0))
                for si, (o0, o1, wd_) in enumerate(segs):
                    stb, t_stb = (bank[6], t_bank[6]) if si < 2 else (bank[5], t_bank[5])
                    scol = slice((si % 2) * 256, (si % 2) * 256 + 256)
                    for oc in range(o0, o1):
                        P.actv(sqo[:, 0:256], oraw[:, oc, :], AF.Square, [t_or[oc]], [t_sqo])
                        P.mm(stb[:, scol], ones[:], sqo[:, 0:256], oc == o0, oc == o1 - 1, [t_ones, t_sqo],
                             [t_stb])
                    P.actv(ms2, stb[:, scol], AF.Sqrt, [t_stb, t_chalf], [t_ms2], scale=1.0 / wd_,
                           bias=epsc[:, 0:1])
                    P.add("dve", lambda e, si=si: e.reciprocal(rs_ap[si], ms2), [t_ms2], [rs_t[si]])
                for oc in range(8):
                    si = 0 if oc < 2 else (1 if oc < 6 else 2)
                    P.stt(oTn[:, oc, :], oraw[:, oc, :], vtT[:, l, 96 + oc:97 + oc], rs_ap[si], ALU.mult, ALU.mult,
                          [t_or[oc], t_vtT, rs_t[si]], [t_aT[1][oc // 2]])
                for dc in range(8):
                    sl = sF if dc < 4 else sG
                    vw = _v3(wslot[sl], 8)
                    bw, t_bw = bank[2 + dc % 2], t_bank[2 + dc % 2]
                    for oc in range(8):
                        P.mm(bw[:, 0:256], vw[:, oc, (dc % 4) * 128:(dc % 4 + 1) * 128], oTn[:, oc, :],
                             oc == 0, oc == 7, [t_slot[sl], t_aT[1][oc // 2]], [t_bw])
                    P.stt(xT[:, dc, ucols], bw[:, 0:256], Gm[:, l, 1, dc, v:v + 1], xT[:, dc, ucols],
                          ALU.mult, ALU.add, [t_bw, t_x[dc][tt], t_mods], [t_x[dc][tt]])
                if u % 2 == 1:
                    after_tt(u // 2)
            S.done(pF)
            S.done(pG)

        class _Stop(Exception):
            pass
        build_program.Stop = _Stop
        try:
            for g in range(2):
                _mark("load%d" % g)
                if not EARLY_NORM:
                    nop = lambda tt: None
                    load_x(g, nop)
                    for l in range(L):
                        _mark("ffn%d%d0" % (g, l))
                        norm_tt(l, 0, g, 0); norm_tt(l, 0, g, 1)
                        ffn(g, l, 0, nop)
                        if g == 0 and l == 0:
                            ada_compute(1)
                        _mark("mix%d%d" % (g, l))
                        mixer_setup(g, l)
                        norm_tt(l, 1, g, 0); norm_tt(l, 1, g, 1)
                        mixer(g, l, nop)
                        _mark("ffn%d%d1" % (g, l))
                        norm_tt(l, 2, g, 0); norm_tt(l, 2, g, 1)
                        ffn(g, l, 1, nop)
                    _mark("store%d" % g)
                    store_y(g)
                    continue
                load_x(g, lambda tt, g=g: norm_tt(0, 0, g, tt))
                for l in range(L):
                    _mark("ffn%d%d0" % (g, l))
                    mixer_setup(g, l)
                    ffn(g, l, 0, lambda tt, g=g, l=l: norm_tt(l, 1, g, tt))
                    if g == 0 and l == 0:
                        ada_compute(1)
                    _mark("mix%d%d" % (g, l))
                    mixer(g, l, lambda tt, g=g, l=l: norm_tt(l, 2, g, tt))
                    _mark("ffn%d%d1" % (g, l))
                    if l + 1 < L:
                        ffn(g, l, 1, lambda tt, g=g, l=l: norm_tt(l + 1, 0, g, tt))
                    else:
                        ffn(g, l, 1, lambda tt: None)
                _mark("store%d" % g)
                store_y(g)
        except _Stop:
            pass
        _mark("end")
        build_program.phases = PH
        P.emit(st)
    return nc


def _consts():
    ident = np.eye(128, dtype=np.float32)
    bands = np.zeros((128, 20, 128), np.float32)
    invc = np.zeros((128, 6, 128), np.float32)
    for gi, w in enumerate(WINS):
        h = w // 2
        tp = np.arange(128)[:, None]
        t = np.arange(128)[None, :]
        inb = ((tp >= t - h) & (tp < t + h)).astype(np.float32)
        cnt_first = (np.minimum(t + h, 10 ** 6) - np.maximum(t - h, 0)).astype(np.float32)
        cnt_mid = np.full((1, 128), float(w), np.float32)
        cnt_last = ((128 - t) + h - np.maximum(0, 0)).astype(np.float32)
        cnt_last = np.minimum(cnt_last, w).astype(np.float32)
        eye = np.eye(128, dtype=np.float32)
        bands[:, gi * 5 + 0, :] = inb - eye * cnt_first
        bands[:, gi * 5 + 1, :] = inb - eye * cnt_mid
        bands[:, gi * 5 + 2, :] = inb - eye * cnt_last
        bands[:, gi * 5 + 3, :] = ((tp - 128) >= (t - h)).astype(np.float32)
        bands[:, gi * 5 + 4, :] = ((tp + 128) < (t + h)).astype(np.float32)
        c, half = gi // 2, gi % 2
        for vi, cn in enumerate((cnt_first, cnt_mid, cnt_last)):
            invc[half * 64:(half + 1) * 64, c * 3 + vi, :] = 1.0 / cn
    dw = np.zeros((31, 127), np.float32)
    for j in range(31):
        dw[j, j + 48] = 1.0
    cols = np.arange(64)
    cs = np.clip(cols - 8, 0, 48)
    ok = (cols[None, :] >= cs[:, None]) & (cols[None, :] < cs[:, None] + 16)
    colok = np.concatenate([ok.T, ok.T], 0).astype(np.float32)
    return ident, bands, invc, dw, colok


def make_in_maps(inp):
    f = lambda a: np.ascontiguousarray(np.asarray(a, dtype=np.float32))
    ident, bands, invc, dw, colok = _consts()
    vt = np.zeros((L, 128, 128), np.float32)
    gt = np.zeros((L, 384), np.float32)
    rpb = np.zeros((L, 128, 31), np.float32)
    for l in range(L):
        vt[l, 0:72] = f(inp["ada_b"])[l].reshape(72, 128)
        vt[l, 72:96] = f(inp["norm_g"])[l].reshape(24, 128)
        vt[l, 96:104] = f(inp["out_norm_g"])[l].reshape(8, 128)
        vt[l, 104:106] = f(inp["pool_scale"])[l].reshape(2, 128)
        gt[l, 0:64] = f(inp["q_norm_g"])[l]
        gt[l, 64:128] = f(inp["k_norm_g"])[l]
        gt[l, 128:384] = f(inp["sg_vnorm_g"])[l].reshape(256)
        rpb[l, 0:120] = f(inp["na_rpb"])[l].reshape(120, 31)
    shared = {
        "vt": vt, "gt": gt, "rpb": rpb,
        "poolw": f(inp["pool_w"]), "sgw": f(inp["sg_w"]), "sgb": f(inp["sg_b"]),
        "adaw": f(inp["ada_w"]), "wg": f(inp["ffn_w_gate"]), "wu": f(inp["ffn_w_up"]),
        "wd": f(inp["ffn_w_down"]), "win": f(inp["w_in"]), "wout": f(inp["w_out"]),
        "ident": ident, "bands": bands, "invc": invc, "dw": dw, "colok": colok,
    }
    xp = f(inp["x_prompt"]); xs = f(inp["x_sample"])
    ck = f(inp["cache_k"]); cv = f(inp["cache_v"]); c = f(inp["c"]); cc = f(inp["c_ctx"])
    maps = []
    for i in range(NCORES):
        x = np.stack([xp[4 * i:4 * i + 4].reshape(1024, D), xs[i]], 0)
        cvec = np.zeros((128, 128), np.float32)
        cvec[0:8] = cc.reshape(8, 128)
        cvec[8:16] = c[i].reshape(8, 128)
        m = dict(shared)
        m.update({"x": np.ascontiguousarray(x), "ck": np.ascontiguousarray(ck[i]),
                  "cv": np.ascontiguousarray(cv[i]), "cvec": cvec})
        maps.append(m)
    return maps


_NC_CACHE = {}


def kernel(**inputs):
    if "nc" not in _NC_CACHE:
        _NC_CACHE["nc"] = build_program()
    nc = _NC_CACHE["nc"]
    maps = make_in_maps(inputs)
    res = run_bass_kernel_spmd(nc, maps, core_ids=list(range(NCORES)))
    r = res.results
    yp = np.concatenate([r[i]["y"][0].reshape(4, 256, D) for i in range(NCORES)], 0)
    ys = np.stack([r[i]["y"][1] for i in range(NCORES)], 0)
    nk = np.concatenate([r[i]["nk"] for i in range(NCORES)], 0)
    nv = np.concatenate([r[i]["nv"] for i in range(NCORES)], 0)
    return (yp.astype(np.float32), ys.astype(np.float32), nk.astype(np.float32), nv.astype(np.float32))
```

```python
import numpy as np
from contextlib import ExitStack
import concourse.bass as bass
import concourse.mybir as mybir
from concourse.bass_utils import run_bass_kernel_spmd

F32 = mybir.dt.float32
BF16 = mybir.dt.bfloat16
AF = mybir.ActivationFunctionType
ALU = mybir.AluOpType
AX = mybir.AxisListType

D = 1024
DFF = 2816
L = 2
NH = 8
EPS = 1e-6
NCORES = 8
WINS = (2, 4, 8, 16)
EARLY_NORM = True


class Tl:
    __slots__ = ("name", "w", "r", "excl")

    def __init__(self, name="", excl=False):
        self.name = name
        self.w = None
        self.r = {}
        self.excl = excl


class Op:
    __slots__ = ("eng", "fn", "deps", "marked", "count", "dkey", "dcount", "is_dma", "uid")


class Prog:
    ENGS = ("pe", "act", "dve", "pool", "sp")

    def __init__(self, nc):
        self.nc = nc
        self.ops = {e: [] for e in self.ENGS}
        self.latest_dma = {}
        self.dma_counts = {}
        self.uid = 0
        self.out_keys = set()

    def add(self, eng, fn, reads=(), writes=(), dma=None, is_out=False):
        op = Op()
        op.eng = eng
        op.fn = fn
        op.marked = False
        op.count = None
        op.is_dma = dma is not None
        op.dkey = dma
        op.uid = self.uid
        self.uid += 1
        deps = {}
        for t in reads:
            if t.w is not None:
                deps[t.w.uid] = t.w
            if t.excl:
                for d in t.r.values():
                    if d.eng != eng:
                        deps[d.uid] = d
        for t in writes:
            if t.w is not None:
                deps[t.w.uid] = t.w
            for d in t.r.values():
                deps[d.uid] = d
        final = {}
        for d in deps.values():
            if d.is_dma:
                d = self.latest_dma[d.dkey]
            elif d.eng == "pe" and eng == "pe" and not op.is_dma:
                continue
            else:
                d.marked = True
            final[d.uid] = d
        op.deps = list(final.values())
        if op.is_dma:
            c = self.dma_counts.get(dma, 0) + 16
            self.dma_counts[dma] = c
            op.dcount = c
            self.latest_dma[dma] = op
            if is_out:
                self.out_keys.add(dma)
        rkey = ("dma", op.uid) if op.is_dma else eng
        for t in reads:
            t.r[rkey] = op
        for t in writes:
            t.w = op
            t.r = {}
        self.ops[eng].append(op)
        return op

    def dma(self, eng, key, out, in_, reads=(), writes=(), is_out=False):
        return self.add(eng, lambda e: e.dma_start(out=out, in_=in_), reads, writes,
                        dma=key, is_out=is_out)

    def mm(self, out, lhsT, rhs, start, stop, reads, writes, tp=None):
        if tp is None:
            return self.add("pe", lambda e: e.matmul(out, lhsT, rhs, start=start, stop=stop), reads, writes)
        return self.add("pe", lambda e: e.matmul(out, lhsT, rhs, start=start, stop=stop, tile_position=tp),
                        reads, writes)

    def tr(self, out, in_, ident, reads, writes):
        return self.add("pe", lambda e: e.transpose(out, in_, ident), reads, writes)

    def actv(self, out, in_, func, reads, writes, scale=None, bias=None):
        kw = {}
        if scale is not None:
            kw["scale"] = scale
        if bias is not None:
            kw["bias"] = bias
        return self.add("act", lambda e: e.activation(out, in_, func, **kw), reads, writes)

    def tt(self, eng, out, in0, in1, op, reads, writes):
        return self.add(eng, lambda e: e.tensor_tensor(out, in0, in1, op=op), reads, writes)

    def ts(self, eng, out, in0, s1, s2, op0, op1, reads, writes):
        if s2 is None:
            return self.add(eng, lambda e: e.tensor_scalar(out, in0, s1, None, op0=op0), reads, writes)
        return self.add(eng, lambda e: e.tensor_scalar(out, in0, s1, s2, op0=op0, op1=op1), reads, writes)

    def stt(self, out, in0, scalar, in1, op0, op1, reads, writes):
        return self.add("dve", lambda e: e.scalar_tensor_tensor(out, in0, scalar, in1, op0=op0, op1=op1),
                        reads, writes)

    def cp(self, eng, out, in_, reads, writes):
        if eng == "act":
            return self.add("act", lambda e: e.copy(out, in_), reads, writes)
        return self.add(eng, lambda e: e.tensor_copy(out, in_), reads, writes)

    def red(self, out, in_, reads, writes):
        return self.add("dve", lambda e: e.tensor_reduce(out, in_, axis=AX.X, op=ALU.add), reads, writes)

    def memset(self, eng, ap, val, writes):
        return self.add(eng, lambda e: e.memset(ap, val), (), writes)

    def emit(self, stack):
        nc = self.nc
        for e in self.ENGS:
            c = 0
            for op in self.ops[e]:
                if op.marked and not op.is_dma:
                    c += 1
                    op.count = c
        esem = {e: stack.enter_context(nc.semaphore("s_" + e)) for e in self.ENGS}
        dsem = {k: stack.enter_context(nc.semaphore("d_%d" % i)) for i, k in enumerate(self.dma_counts)}
        block = stack.enter_context(nc.Block())
        ops = self.ops
        out_final = [(dsem[k], self.dma_counts[k]) for k in sorted(self.out_keys, key=str)]

        def run(ename, e):
            known = {}
            for op in ops[ename]:
                for d in op.deps:
                    if d.is_dma:
                        s, v = dsem[d.dkey], d.dcount
                    else:
                        s, v = esem[d.eng], d.count
                    kid = id(s)
                    if known.get(kid, 0) >= v:
                        continue
                    known[kid] = v
                    e.wait_ge(s, v)
                ins = op.fn(e)
                if op.is_dma:
                    ins.then_inc(dsem[op.dkey], 16)
                elif op.marked:
                    ins.then_inc(esem[ename], 1)
            if ename == "sp":
                for s, v in out_final:
                    e.wait_ge(s, v)

        @block.sync
        def _(e):
            run("sp", e)

        @block.tensor
        def _(e):
            run("pe", e)

        @block.scalar
        def _(e):
            run("act", e)

        @block.vector
        def _(e):
            run("dve", e)

        @block.gpsimd
        def _(e):
            run("pool", e)


class Stream:
    NS = 6
    LOOK = 5

    def __init__(self, P, slots, t_slots):
        self.P = P
        self.slots = slots
        self.t_slots = t_slots
        self.pieces = []
        self.issued = 0
        self.freed = 0
        self.want = 0

    def plan(self, parts):
        self.pieces.append(parts)
        return len(self.pieces) - 1

    def pump(self):
        while self.issued <= self.want and self.issued < len(self.pieces) and self.issued < self.freed + self.NS:
            i = self.issued
            s = i % self.NS
            for dst_fn, src in self.pieces[i]:
                self.P.dma("pool", ("w", s), dst_fn(self.slots[s]), src, writes=[self.t_slots[s]])
            self.issued += 1

    def need(self, p):
        self.want = max(self.want, p + self.LOOK)
        self.pump()
        assert self.issued > p, (self.issued, p, self.freed)
        return p % self.NS

    def done(self, p):
        self.freed = max(self.freed, p + 1)
        self.pump()


def _v3(t, k):
    return t[:].rearrange("p (k n) -> p k n", k=k)


def build_program(stop_after=None):
    nc = bass.Bass("TRN2", target_bir_lowering=False)

    def din(name, shape):
        return nc.dram_tensor(name, shape, F32, kind="ExternalInput").ap()

    def dout(name, shape):
        return nc.dram_tensor(name, shape, F32, kind="ExternalOutput").ap()

    x_d = din("x", [2, 1024, D])
    ck_d = din("ck", [L, NH, 256, 64])
    cv_d = din("cv", [L, NH, 256, 64])
    cvec_d = din("cvec", [128, 128])
    vt_d = din("vt", [L, 128, 128])
    gt_d = din("gt", [L, 384])
    poolw_d = din("poolw", [L, 4, 64, 64])
    sgw_d = din("sgw", [L, 4, 128, 128])
    sgb_d = din("sgb", [L, 4, 128])
    rpb_d = din("rpb", [L, 128, 31])
    adaw_d = din("adaw", [L, D, 9 * D])
    wg_d = din("wg", [L, 2, D, DFF])
    wu_d = din("wu", [L, 2, D, DFF])
    wd_d = din("wd", [L, 2, DFF, D])
    win_d = din("win", [L, D, 2304])
    wout_d = din("wout", [L, D, D])
    ident_d = din("ident", [128, 128])
    bands_d = din("bands", [128, 20, 128])
    invc_d = din("invc", [128, 6, 128])
    dw_d = din("dw", [31, 127])
    colok_d = din("colok", [128, 64])
    y_d = dout("y", [2, 1024, D])
    nk_d = dout("nk", [4, L, NH, 256, 64])
    nv_d = dout("nv", [4, L, NH, 256, 64])

    P = Prog(nc)
    st = ExitStack()

    def sb(name, shape, dt=F32):
        return st.enter_context(nc.sbuf_tensor("s_" + name, shape, dt))

    with st:
        bank = [st.enter_context(nc.psum_tensor("bank%d" % i, [128, 512], F32)) for i in range(7)]
        bankb = st.enter_context(nc.psum_tensor("bankb", [128, 1024], BF16))
        t_bank = [Tl("bank%d" % i, excl=True) for i in range(7)]
        t_bankb = Tl("bankb", excl=True)

        ident = sb("ident", [128, 128]); t_ident = Tl()
        identb = sb("identb", [128, 128], BF16); t_identb = Tl()
        ones = sb("ones", [128, 128], BF16); t_ones = Tl()
        epsc = sb("epsc", [128, 1]); t_chalf = Tl()
        bands = sb("bands", [128, 20, 128], BF16); t_bands = Tl()
        invc = sb("invc", [128, 6, 128]); t_invc = Tl()
        dw = sb("dw", [31, 127]); t_dw = Tl()
        colok = sb("colok", [128, 64]); t_colok = Tl()
        P.dma("sp", "c0", ident[:], ident_d, writes=[t_ident])
        P.dma("sp", "c1", invc[:], invc_d, writes=[t_invc])
        P.dma("sp", "c2", dw[:], dw_d, writes=[t_dw])
        P.dma("sp", "c3", colok[:], colok_d, writes=[t_colok])
        P.dma("pool", "c4", bands[:], bands_d, writes=[t_bands])
        P.cp("dve", identb[:], ident[:], [t_ident], [t_identb])
        P.memset("pool", ones[:], 1.0, [t_ones])
        P.memset("pool", epsc[:], EPS, [t_chalf])

        xT = sb("xT", [128, 8, 1024]); t_x = [[Tl() for _ in range(2)] for _ in range(8)]
        hT = sb("hT", [128, 8, 1024], BF16); t_h = [[Tl() for _ in range(2)] for _ in range(8)]
        wslot = [sb("wslot%d" % i, [128, 4096], BF16) for i in range(Stream.NS)]
        t_slot = [Tl("slot%d" % i) for i in range(Stream.NS)]
        S = Stream(P, wslot, t_slot)
        aT = [sb("aT%d" % i, [128, 4, 512], BF16) for i in range(2)]
        t_aT = [[Tl() for _ in range(4)] for _ in range(2)]
        stmp = [sb("stmp%d" % i, [128, 512]) for i in range(2)]; t_stmp = [Tl(), Tl()]
        xio = sb("xio", [128, 2048]); t_or = [Tl() for _ in range(8)]
        oraw = xio[:].rearrange("p (c n) -> p c n", c=8)
        sq = [sb("sq%d" % i, [128, 512], BF16) for i in range(2)]; t_sq = [Tl(), Tl()]
        ms = sb("ms", [128, 512]); t_ms = Tl()
        rstd, t_rstd = ms, t_ms
        ntmp = [sb("ntmp%d" % i, [128, 512]) for i in range(2)]; t_ntmp = [Tl(), Tl()]
        mods = sb("mods", [128, L, 72, 2]); t_mods = Tl()
        Am = sb("Am", [128, L, 3, 8, 2])
        Gm = sb("Gm", [128, L, 3, 8, 2])
        vtT = sb("vtT", [128, L, 128]); t_vtT = Tl()
        scT = sb("scT", [128, 16], BF16); t_scT = Tl()
        vtr_t = sb("vtr", [128, 128]); t_vtr = Tl()
        vtr = vtr_t[:]
        za_tok = sb("za_tok", [128, 8, 256], BF16); t_za = [Tl() for _ in range(8)]
        vg_tok = sb("vg_tok", [128, 8, 256], BF16); t_vg = [Tl() for _ in range(8)]
        v_tok = sb("v_tok", [128, 8, 512], BF16); t_v = [Tl() for _ in range(8)]
        qT = sb("qT", [128, 4, 1024], BF16); t_q = [[Tl() for _ in range(8)] for _ in range(4)]
        kT = sb("kT", [128, 4, 1024], BF16); t_k = [[Tl() for _ in range(8)] for _ in range(4)]
        uT = sb("uT", [128, 2, 1024], BF16); t_u = [[Tl() for _ in range(2)] for _ in range(2)]
        kcT = sb("kcT", [128, 4, 256], BF16); t_kc = Tl()
        qkz = sb("qkz", [128, 1280]); t_qkz = Tl()
        vc = sb("vc", [128, 2, 512], BF16); t_vc = Tl()
        pooledT = sb("pooledT", [128, 2, 256], BF16); t_pl = [Tl(), Tl()]
        EB = sb("EB", [128, 8, 14, 64], BF16); t_EB = Tl()
        wsT = sb("wsT", [128, 4, 128], BF16); t_wsT = Tl()
        pwb = sb("pwb", [128, 2, 128], BF16); t_pwb = Tl()
        bsb = sb("bsb", [128, 2, 128]); t_bsb = Tl()
        gtb = sb("gtb", [128, 384]); t_gtb = Tl()
        rden = sb("rden", [128, 256]); t_rden = Tl()
        st8 = sb("st8", [128, 40]); t_st8 = Tl()

        pid = {}
        groups = [(0, 4), (4, 4), (8, 2), (10, 4), (14, 4), (18, 4)]

        def plan_ada(l):
            for pc in range(18):
                pid[("ada", l, pc)] = S.plan([(
                    lambda t: _v3(t, 8),
                    adaw_d[l].rearrange("(k p) n -> p k n", p=128)[:, :, pc * 512:(pc + 1) * 512])])

        def plan_ffn(g, l, j):
            for gi, (c0, nfc) in enumerate(groups):
                w = nfc * 128
                pid[("g", g, l, j, gi)] = S.plan([(
                    lambda t, w=w: _v3(t, 8)[:, :, 0:w],
                    wg_d[l, j].rearrange("(k p) n -> p k n", p=128)[:, :, c0 * 128:c0 * 128 + w])])
                pid[("u", g, l, j, gi)] = S.plan([(
                    lambda t, w=w: _v3(t, 8)[:, :, 0:w],
                    wu_d[l, j].rearrange("(k p) n -> p k n", p=128)[:, :, c0 * 128:c0 * 128 + w])])
                pid[("d", g, l, j, gi)] = S.plan([(
                    lambda t, nfc=nfc: _v3(t, 4)[:, 0:nfc, :],
                    wd_d[l, j][c0 * 128:c0 * 128 + w, :].rearrange("(f p) n -> p f n", p=128))])

        def plan_mixer(g, l):
            wv = win_d[l].rearrange("(k p) n -> p k n", p=128)
            pid[("A", g, l)] = S.plan([
                (lambda t: _v3(t, 8)[:, :, 0:256], wv[:, :, 0:256]),
                (lambda t: _v3(t, 8)[:, :, 256:512], wv[:, :, 2048:2304])])
            pid[("B", g, l)] = S.plan([(lambda t: _v3(t, 8), wv[:, :, 256:768])])
            pid[("C", g, l)] = S.plan([(lambda t: _v3(t, 8), wv[:, :, 768:1280])])
            pid[("D", g, l)] = S.plan([(lambda t: _v3(t, 8), wv[:, :, 1280:1792])])
            pid[("E", g, l)] = S.plan([(lambda t: _v3(t, 8)[:, :, 0:256], wv[:, :, 1792:2048])])
            wo = wout_d[l].rearrange("(k p) n -> p k n", p=128)
            pid[("F", g, l)] = S.plan([(lambda t: _v3(t, 8), wo[:, :, 0:512])])
            pid[("G", g, l)] = S.plan([(lambda t: _v3(t, 8), wo[:, :, 512:1024])])

        plan_ada(0)
        for g in range(2):
            for l in range(L):
                plan_ffn(g, l, 0)
                if g == 0 and l == 0:
                    plan_ada(1)
                plan_mixer(g, l)
                plan_ffn(g, l, 1)

        P.dma("sp", "v0", vtr, cvec_d, writes=[t_vtr])
        P.tr(bank[6][:, 0:128], vtr, ident[:], [t_vtr, t_ident], [t_bank[6]])
        P.actv(scT[:], bank[6][:, 0:16], AF.Silu, [t_bank[6]], [t_scT])
        for l in range(L):
            P.dma("sp", "v0", vtr, vt_d[l], writes=[t_vtr])
            P.tr(bank[6][:, 0:128], vtr, ident[:], [t_vtr, t_ident], [t_bank[6]])
            P.cp("dve", vtT[:, l, :], bank[6][:, 0:128], [t_bank[6]], [t_vtT])
        def ada_compute(l):
            for pc in range(18):
                p_ = pid[("ada", l, pc)]
                s = S.need(p_)
                wv = _v3(wslot[s], 8)
                for fc in range(4):
                    j = pc * 4 + fc
                    for k in range(8):
                        P.mm(bank[5][:, 2 * j:2 * j + 2], wv[:, k, fc * 128:(fc + 1) * 128],
                             scT[:, k:16:8], k == 0, k == 7, [t_slot[s], t_scT], [t_bank[5]])
                S.done(p_)
            P.tt("dve", mods[:, l, :, :], bank[5][:, 0:144].rearrange("p (j v) -> p j v", v=2),
                 vtT[:, l, 0:72].rearrange("p (j o) -> p j o", o=1).to_broadcast([128, 72, 2]),
                 ALU.add, [t_bank[5], t_vtT], [t_mods])
            for n in range(3):
                P.ts("dve", Am[:, l, n, :, :], mods[:, l, (3 * n + 1) * 8:(3 * n + 2) * 8, :], 1.0, None,
                     ALU.add, None, [t_mods], [t_mods])
                P.tt("dve", Am[:, l, n, :, :], Am[:, l, n, :, :],
                     vtT[:, l, 72 + n * 8:80 + n * 8].rearrange("p (j o) -> p j o", o=1).to_broadcast([128, 8, 2]),
                     ALU.mult, [t_mods, t_vtT], [t_mods])
                P.ts("dve", Gm[:, l, n, :, :], mods[:, l, (3 * n + 2) * 8:(3 * n + 3) * 8, :],
                     (1.0 if n == 1 else 0.5), None, ALU.mult, None, [t_mods], [t_mods])


        ADA0_DEFERRED = True

        PH = []
        _mark = lambda nm: PH.append((nm, len(P.ops["pe"])))
        def norm_tt(l, n, v, tt):
            cols = slice(tt * 512, (tt + 1) * 512)
            for k in range(8):
                P.actv(sq[k % 2][:], xT[:, k, cols], AF.Square, [t_x[k][tt]], [t_sq[k % 2]])
                P.mm(bank[6][:], ones[:], sq[k % 2][:], k == 0, k == 7, [t_sq[k % 2], t_ones], [t_bank[6]])
            P.actv(ms[:], bank[6][:], AF.Sqrt, [t_bank[6], t_chalf], [t_ms], scale=1.0 / D, bias=epsc[:, 0:1])
            P.add("dve", lambda e: e.reciprocal(rstd[:], ms[:]), [t_ms], [t_rstd])
            for k in range(8):
                P.tt("dve", ntmp[k % 2][:], xT[:, k, cols], rstd[:], ALU.mult,
                     [t_x[k][tt], t_rstd], [t_ntmp[k % 2]])
                P.actv(hT[:, k, cols], ntmp[k % 2][:], AF.Identity, [t_ntmp[k % 2], t_mods], [t_h[k][tt]],
                       scale=Am[:, l, n, k, v:v + 1], bias=mods[:, l, 3 * n * 8 + k, v:v + 1])

        cnt = {"up": 0, "dn": 0, "ab": 0}

        def ffn(g, l, j, after_tt):
            n = 0 if j == 0 else 2
            v = g
            pending = []
            last_gi = len(groups) - 1

            def flush():
                while pending:
                    pending.pop(0)()

            for gi, (c0, nfc) in enumerate(groups):
                pg_, pu_, pd_ = pid[("g", g, l, j, gi)], pid[("u", g, l, j, gi)], pid[("d", g, l, j, gi)]
                sg, su, sd = S.need(pg_), S.need(pu_), S.need(pd_)
                vg_, vu_, vd_ = _v3(wslot[sg], 8), _v3(wslot[su], 8), _v3(wslot[sd], 4)
                for tt in range(2):
                    cols = slice(tt * 512, (tt + 1) * 512)
                    ab = cnt["ab"] % 2
                    cnt["ab"] += 1
                    for f in range(nfc):
                        ib = cnt["up"] % 2
                        cnt["up"] += 1
                        for k in range(8):
                            P.mm(bank[ib][:], vg_[:, k, f * 128:(f + 1) * 128], hT[:, k, cols], k == 0, k == 7,
                                 [t_slot[sg], t_h[k][tt]], [t_bank[ib]])
                        for k in range(8):
                            P.mm(bank[2 + ib][:], vu_[:, k, f * 128:(f + 1) * 128], hT[:, k, cols], k == 0, k == 7,
                                 [t_slot[su], t_h[k][tt]], [t_bank[2 + ib]])
                        P.actv(stmp[ib][:], bank[ib][:], AF.Silu, [t_bank[ib]], [t_stmp[ib]])
                        P.tt("dve", aT[ab][:, f, :], stmp[ib][:], bank[2 + ib][:], ALU.mult,
                             [t_stmp[ib], t_bank[2 + ib]], [t_aT[ab][f]])

                    def down(ab=ab, nfc=nfc, sd=sd, vd_=vd_, cols=cols, tt=tt, gi=gi):
                        for dc in range(8):
                            ib2 = 4 + cnt["dn"] % 2
                            cnt["dn"] += 1
                            for f in range(nfc):
                                P.mm(bank[ib2][:], vd_[:, f, dc * 128:(dc + 1) * 128], aT[ab][:, f, :],
                                     f == 0, f == nfc - 1, [t_slot[sd], t_aT[ab][f]], [t_bank[ib2]])
                            P.stt(xT[:, dc, cols], bank[ib2][:], Gm[:, l, n, dc, v:v + 1], xT[:, dc, cols],
                                  ALU.mult, ALU.add, [t_bank[ib2], t_x[dc][tt], t_mods], [t_x[dc][tt]])
                        if gi == last_gi:
                            after_tt(tt)

                    flush()
                    if gi == last_gi:
                        down()
                    else:
                        pending.append(down)
                S.done(pg_)
                S.done(pu_)
                if gi > 0:
                    S.done(pid[("d", g, l, j, gi - 1)])
            flush()
            S.done(pid[("d", g, l, j, len(groups) - 1)])

        def load_x(g, after_tt):
            for i in range(8):
                xb = i % 2
                xs_ = xio[:, xb * 1024:(xb + 1) * 1024]
                tl_ = t_or[xb * 4:xb * 4 + 4]
                P.dma("sp", ("xio", xb), xs_, x_d[g, i * 128:(i + 1) * 128, :], writes=tl_)
                for hb in range(2):
                    for kk in range(4):
                        k = hb * 4 + kk
                        P.tr(bank[hb][:, kk * 128:(kk + 1) * 128], xs_[:, k * 128:(k + 1) * 128], ident[:],
                             tl_ + [t_ident], [t_bank[hb]])
                    P.cp("dve" if hb == 0 else "act", xT[:, hb * 4:hb * 4 + 4, i * 128:(i + 1) * 128],
                         bank[hb][:].rearrange("p (k n) -> p k n", k=4), [t_bank[hb]],
                         [t_x[k][i // 4] for k in range(hb * 4, hb * 4 + 4)])
                if i % 4 == 3:
                    after_tt(i // 4)

        def store_y(g):
            for i in range(8):
                xb = i % 2
                xs_ = xio[:, xb * 1024:(xb + 1) * 1024]
                tl_ = t_or[xb * 4:xb * 4 + 4]
                for hb in range(2):
                    for kk in range(4):
                        k = hb * 4 + kk
                        P.tr(bank[hb][:, kk * 128:(kk + 1) * 128], xT[:, k, i * 128:(i + 1) * 128], ident[:],
                             [t_x[k][i // 4], t_ident], [t_bank[hb]])
                    P.cp("dve" if hb == 0 else "act", xs_[:, hb * 512:(hb + 1) * 512], bank[hb][:],
                         [t_bank[hb]], tl_)
                P.dma("sp", ("y", xb), y_d[g, i * 128:(i + 1) * 128, :], xs_, reads=tl_, is_out=True)

        def grp_rms(src, t_src, nh):
            n = nh * 64
            s3 = src.rearrange("p (h d) -> p h d", d=64)
            sqs, t_sqs = sq[0], t_sq[0]
            P.tt("dve", sqs[:, 0:n], src, src, ALU.mult, [t_src], [t_sqs])
            P.red(st8[:, 0:nh], sqs[:, 0:n].rearrange("p (h d) -> p h d", d=64), [t_sqs], [t_st8])
            P.actv(st8[:, 0:nh], st8[:, 0:nh], AF.Sqrt, [t_st8, t_chalf], [t_st8], scale=1.0 / 64, bias=epsc[:, 0:1])
            P.add("dve", lambda e: e.reciprocal(st8[:, 8:8 + nh], st8[:, 0:nh]), [t_st8], [t_st8])
            P.tt("dve", s3, s3, st8[:, 8:8 + nh].rearrange("p (h o) -> p h o", o=1).to_broadcast([128, nh, 64]),
                 ALU.mult, [t_src, t_st8], [t_src])

        def mixer_setup(g, l):
            pwf = stmp[1][:, 0:256].rearrange("p (c n) -> p c n", c=2)
            t_pwf = t_stmp[1]
            RT = ntmp[0][0:31, 0:128]
            RT01 = ntmp[0][0:31, 128:352].rearrange("p (a n) -> p a n", a=2)
            t_RT = t_RT01 = t_ntmp[0]
            P.dma("sp", "ms0", gtb[:], gt_d[l:l + 1, :].to_broadcast([128, 384]), writes=[t_gtb])
            P.ts("dve", gtb[:, 0:64], gtb[:, 0:64], 0.125, None, ALU.mult, None, [t_gtb], [t_gtb])
            P.memset("pool", pwf, 0.0, [t_pwf])
            for c in range(2):
                for gh in range(2):
                    P.dma("sp", "ms1", pwf[gh * 64:(gh + 1) * 64, c, gh * 64:(gh + 1) * 64], poolw_d[l, 2 * c + gh],
                          writes=[t_pwf])
            P.cp("dve", pwb[:], pwf, [t_pwf], [t_pwb])
            for g4 in range(4):
                P.dma("sp", "ms2", stmp[0][:, 0:128], sgw_d[l, g4], writes=[t_stmp[0]])
                P.tr(bank[6][:, 0:128], stmp[0][:, 0:128], ident[:], [t_stmp[0], t_ident], [t_bank[6]])
                P.cp("dve", wsT[:, g4, :], bank[6][:, 0:128], [t_bank[6]], [t_wsT])
            for g4 in range(4):
                P.dma("sp", "ms3", bsb[(g4 % 2) * 64:(g4 % 2 + 1) * 64, g4 // 2, :],
                      sgb_d[l, g4:g4 + 1, :].to_broadcast([64, 128]), writes=[t_bsb])
            if g == 1:
                kctok = aT[1][:, 0:2, :]
                t_kctok = Tl()
                for t in range(2):
                    P.dma("pool", "ms4", kctok[:, t, :].rearrange("p (h d) -> p h d", h=8),
                          ck_d[l][:, t * 128:(t + 1) * 128, :].rearrange("h p d -> p h d"),
                          writes=[t_aT[1][0], t_aT[1][1]])
                    P.dma("pool", "ms5", vc[:, t, :].rearrange("p (h d) -> p h d", h=8),
                          cv_d[l][:, t * 128:(t + 1) * 128, :].rearrange("h p d -> p h d"), writes=[t_vc])
                for t in range(2):
                    for c in range(4):
                        P.tr(bankb[:, c * 128:(c + 1) * 128], kctok[:, t, c * 128:(c + 1) * 128], identb[:],
                             [t_aT[1][0], t_aT[1][1], t_identb], [t_bankb])
                    P.cp("dve", kcT[:, :, t * 128:(t + 1) * 128],
                         bankb[:, 0:512].rearrange("p (c n) -> p c n", c=4), [t_bankb], [t_kc])
                P.dma("sp", "ms6", stmp[1][:, 0:31], rpb_d[l], writes=[t_stmp[1]])
                P.tr(bank[6][0:31, 0:128], stmp[1][:, 0:31], ident[:], [t_stmp[1], t_ident], [t_bank[6]])
                P.cp("dve", RT, bank[6][0:31, 0:128], [t_bank[6]], [t_RT])
                rt3 = RT[:, 0:120].rearrange("p (h r) -> p h r", h=8)
                for half in range(2):
                    P.cp("dve", RT01[:, half, :].rearrange("p (h r) -> p h r", h=8), rt3[:, :, half:half + 14],
                         [t_RT], [t_RT01])
                for b in range(16):
                    bk = bank[b % 2]
                    for qi in range(4):
                        qc = b * 4 + qi
                        for half in range(2):
                            P.mm(bk[half * 64:(half + 1) * 64, qi * 112:(qi + 1) * 112],
                                 dw[0:31, 63 - qc:127 - qc], RT01[:, half, :], True, True,
                                 [t_dw, t_RT01], [t_bank[b % 2]], tp=(0, half * 64))
                    P.actv(EB[:, :, :, b * 4:(b + 1) * 4].rearrange("p h r q -> p q (h r)"),
                           bk[:, 0:448].rearrange("p (q n) -> p q n", q=4), AF.Exp, [t_bank[b % 2]], [t_EB])
                ebv = EB[:].rearrange("p h r q -> p (h r) q")
                P.tt("dve", ebv, ebv, colok[:, :].rearrange("p (o q) -> p o q", o=1).to_broadcast([128, 112, 64]),
                     ALU.mult, [t_EB, t_colok], [t_EB])

        acnt = {"n": 0, "bo": 0}
        SB_ = (0, 1, 4, 5)
        DEPTH = 3

        def mixer(g, l, after_tt):
            v = g
            is_p = (g == 0)
            if stop_after == "premix":
                raise build_program.Stop()
            pA, pB, pC, pD, pE, pF, pG = [pid[(nm, g, l)] for nm in "ABCDEFG"]
            sA, sB, sC, sD, sE = [S.need(p_) for p_ in (pA, pB, pC, pD, pE)]
            vA, vB, vC, vD, vE = [_v3(wslot[s_], 8) for s_ in (sA, sB, sC, sD, sE)]
            h3 = lambda ap: ap.rearrange("p (h d) -> p h d", d=64)
            SETS = (
                dict(qf=stmp[0][:], t_qf=[t_stmp[0]], kf=stmp[1][:], t_kf=[t_stmp[1]],
                     gv=ntmp[1][:, 0:256], t_gv=[t_ntmp[1]], vf=ntmp[0][:], t_vf=[t_ntmp[0]]),
                dict(qf=xio[:, 0:512], t_qf=[t_or[0], t_or[1]], kf=xio[:, 512:1024], t_kf=[t_or[2], t_or[3]],
                     gv=xio[:, 1024:1280], t_gv=[t_or[4]], vf=xio[:, 1536:2048], t_vf=[t_or[6], t_or[7]]),
            )
            sqq, t_sqq = sq[0][:], [t_sq[0]]
            sqk, t_sqk = aT[0][:, 1, :], [t_aT[0][1]]
            sqz, t_sqz = aT[0][:, 2, 0:256], [t_aT[0][2]]
            qb, t_qb = sq[1], t_sq[1]
            kb, t_kb = aT[0][:, 0, :], t_aT[0][0]
            bc = lambda ap, n: ap.rearrange("p (h o) -> p h o", o=1).to_broadcast([128, n, 64])
            gb = lambda ap: ap.rearrange("p (o d) -> p o d", o=1).to_broadcast([128, 8, 64])

            def proj_mm(i):
                tt = i // 4
                tok = slice(i * 128, (i + 1) * 128)
                for bi, (vw, sl) in enumerate(((vA, sA), (vB, sB), (vC, sC), (vD, sD))):
                    for k in range(8):
                        P.mm(bank[bi][:], hT[:, k, tok], vw[:, k, :], k == 0, k == 7,
                             [t_h[k][tt], t_slot[sl]], [t_bank[bi]])

            def proj_evac(i):
                B = SETS[i % 2]
                P.cp("act", B["qf"], bank[1][:], [t_bank[1]], B["t_qf"])
                P.cp("act", B["kf"], bank[2][:], [t_bank[2]], B["t_kf"])
                P.actv(B["gv"], bank[0][:, 256:512], AF.Gelu_apprx_tanh, [t_bank[0]], B["t_gv"])
                P.cp("act", za_tok[:, i, :], bank[0][:, 0:256], [t_bank[0]], [t_za[i]])
                P.cp("act", v_tok[:, i, :], bank[3][:], [t_bank[3]], [t_v[i]])
                if is_p:
                    b_, s0_ = i // 2, (i % 2) * 128
                    P.cp("dve", B["vf"], bank[3][:], [t_bank[3]], B["t_vf"])
                    P.dma("sp", "nv", nv_d[b_, l].rearrange("h s d -> s h d")[s0_:s0_ + 128], h3(B["vf"]),
                          reads=B["t_vf"], is_out=True)

            def proj_chain(i):
                B = SETS[i % 2]
                tok = slice(i * 128, (i + 1) * 128)
                qf, kf, gv = B["qf"], B["kf"], B["gv"]
                t_qf, t_kf, t_gv = B["t_qf"], B["t_kf"], B["t_gv"]
                P.tt("dve", sqq, qf, qf, ALU.mult, t_qf, t_sqq)
                P.tt("dve", sqk, kf, kf, ALU.mult, t_kf, t_sqk)
                P.tt("pool", sqz, gv, gv, ALU.mult, t_gv, t_sqz)
                P.red(st8[:, 0:8], h3(sqq), t_sqq, [t_st8])
                P.red(st8[:, 8:16], h3(sqk), t_sqk, [t_st8])
                P.red(st8[:, 16:20], h3(sqz), t_sqz, [t_st8])
                P.actv(st8[:, 0:20], st8[:, 0:20], AF.Sqrt, [t_st8, t_chalf], [t_st8], scale=1.0 / 64,
                       bias=epsc[:, 0:1])
                P.add("dve", lambda e: e.reciprocal(st8[:, 20:40], st8[:, 0:20]), [t_st8], [t_st8])
                P.tt("dve", h3(qf), h3(qf), bc(st8[:, 20:28], 8), ALU.mult, t_qf + [t_st8], t_qf)
                P.tt("dve", h3(kf), h3(kf), bc(st8[:, 28:36], 8), ALU.mult, t_kf + [t_st8], t_kf)
                P.tt("dve", h3(gv), h3(gv), bc(st8[:, 36:40], 4), ALU.mult, t_gv + [t_st8], t_gv)
                P.tt("pool", h3(qb[:]), h3(qf), gb(gtb[:, 0:64]), ALU.mult, t_qf + [t_gtb], [t_qb])
                P.tt("dve", h3(kf), h3(kf), gb(gtb[:, 64:128]), ALU.mult, t_kf + [t_gtb], t_kf)
                P.tt("pool", vg_tok[:, i, :], gv, gtb[:, 128:384], ALU.mult, t_gv + [t_gtb], [t_vg[i]])
                if is_p:
                    b_, s0_ = i // 2, (i % 2) * 128
                    P.dma("sp", "nk", nk_d[b_, l].rearrange("h s d -> s h d")[s0_:s0_ + 128], h3(kf),
                          reads=t_kf, is_out=True)
                P.cp("act", kb, kf, t_kf, [t_kb])
                for c in range(4):
                    P.tr(bankb[:, c * 128:(c + 1) * 128], qb[:, c * 128:(c + 1) * 128], identb[:],
                         [t_qb, t_identb], [t_bankb])
                for c in range(4):
                    P.tr(bankb[:, 512 + c * 128:512 + (c + 1) * 128], kb[:, c * 128:(c + 1) * 128], identb[:],
                         [t_kb, t_identb], [t_bankb])
                P.cp("dve", qT[:, :, tok], bankb[:, 0:512].rearrange("p (c n) -> p c n", c=4), [t_bankb],
                     [t_q[c][i] for c in range(4)])
                P.cp("dve", kT[:, :, tok], bankb[:, 512:1024].rearrange("p (c n) -> p c n", c=4), [t_bankb],
                     [t_k[c][i] for c in range(4)])

            proj_mm(0)
            proj_evac(0)
            for i in range(8):
                if i + 1 < 8:
                    proj_mm(i + 1)
                proj_chain(i)
                if i + 1 < 8:
                    proj_evac(i + 1)
            gv, t_gv = stmp[1], t_stmp[1]
            _mark(" zu")
            for tt in range(2):
                cols = slice(tt * 512, (tt + 1) * 512)
                for c in range(2):
                    for k in range(8):
                        P.mm(bank[4 + c][:], vE[:, k, c * 128:(c + 1) * 128], hT[:, k, cols], k == 0, k == 7,
                             [t_slot[sE], t_h[k][tt]], [t_bank[4 + c]])
                    P.actv(uT[:, c, cols], bank[4 + c][:], AF.Gelu_apprx_tanh, [t_bank[4 + c]], [t_u[c][tt]])
            for p_ in (pA, pB, pC, pD, pE):
                S.done(p_)
            sF, sG = S.need(pF), S.need(pG)
            if stop_after == "proj":
                raise build_program.Stop()

            def variant(j):
                if is_p:
                    return 0 if j % 2 == 0 else 2
                return 0 if j == 0 else (2 if j == 7 else 1)

            def seq_range(j):
                if is_p:
                    return (j // 2 * 2, j // 2 * 2 + 1)
                return (0, 7)

            sqo, t_sqo = sq[0], t_sq[0]
            rs_ap = (ms[:, 0:256], ms[:, 256:512], ntmp[0][:, 0:256])
            rs_t = (t_ms, t_ms, t_ntmp[0])
            ms2, t_ms2 = ntmp[0][:, 256:512], t_ntmp[0]
            oTn = aT[1][:].rearrange("p f n -> p (f n)").rearrange("p (c n) -> p c n", c=8)
            ETB = ((ms, t_ms), (ntmp[0], t_ntmp[0]), (ntmp[1], t_ntmp[1]))

            def attn_S(it):
                n = it["n"]
                c, hp, h = it["c"], it["hp"], it["h"]
                ps_ = slice(hp * 64, (hp + 1) * 64)
                bk, t_bk = bank[SB_[n % 4]], t_bank[SB_[n % 4]]
                pt, t_pt = aT[0][:, n % 4, :], t_aT[0][n % 4]
                if is_p:
                    u_, tok0_ = it["u"], it["u"] * 256
                    for kt in range(2):
                        P.mm(bk[:, kt * 256:(kt + 1) * 256],
                             kT[ps_, c, tok0_ + kt * 128:tok0_ + (kt + 1) * 128], qT[ps_, c, tok0_:tok0_ + 256],
                             True, True, [t_k[c][2 * u_ + kt], t_q[c][2 * u_], t_q[c][2 * u_ + 1]], [t_bk])
                    P.actv(pt, bk[:], AF.Exp, [t_bk], [t_pt])
                else:
                    r, kts, nl = it["r"], it["kts"], it["nl"]
                    et, t_et = ETB[n % 3]
                    qcols = slice(r * 64, (r + 1) * 64)
                    for j_, kt in enumerate(kts):
                        P.mm(bk[:, j_ * 64:(j_ + 1) * 64], kT[ps_, c, kt * 128:(kt + 1) * 128],
                             qT[ps_, c, qcols], True, True, [t_k[c][kt], t_q[c][r // 2]], [t_bk])
                    for t in range(2):
                        P.mm(bk[:, (nl + t) * 64:(nl + t + 1) * 64], kcT[ps_, c, t * 128:(t + 1) * 128],
                             qT[ps_, c, qcols], True, True, [t_kc, t_q[c][r // 2]], [t_bk])
                    P.actv(et[:, 0:nl * 64], bk[:, 0:nl * 64], AF.Exp, [t_bk], [t_et])
                    P.actv(pt[:, nl * 64:(nl + 2) * 64], bk[:, nl * 64:(nl + 2) * 64], AF.Exp, [t_bk], [t_pt])
                    i0 = 2 * kts[0] - r + 7
                    P.tt("dve", pt[:, 0:nl * 64].rearrange("p (j q) -> p j q", q=64),
                         et[:, 0:nl * 64].rearrange("p (j q) -> p j q", q=64),
                         EB[:, h, i0:i0 + 2 * nl - 1:2, :], ALU.mult, [t_et, t_EB], [t_pt])

            def attn_PV(it):
                n = it["n"]
                c, hp, h = it["c"], it["hp"], it["h"]
                ps_ = slice(hp * 64, (hp + 1) * 64)
                pt, t_pt = aT[0][:, n % 4, :], t_aT[0][n % 4]
                if hp == 0:
                    acnt["bo"] += 1
                bo, t_bo = bank[2 + acnt["bo"] % 2], t_bank[2 + acnt["bo"] % 2]
                if is_p:
                    u_ = it["u"]
                    for kt in range(2):
                        P.mm(bo[ps_, 0:256], v_tok[:, 2 * u_ + kt, h * 64:(h + 1) * 64],
                             pt[:, kt * 256:(kt + 1) * 256], kt == 0, kt == 1,
                             [t_v[2 * u_ + kt], t_pt], [t_bo], tp=(0, hp * 64))
                    for kt in range(2):
                        P.mm(bo[ps_, 256:512], ones[:, 0:64], pt[:, kt * 256:(kt + 1) * 256],
                             kt == 0, kt == 1, [t_ones, t_pt], [t_bo], tp=(0, hp * 64))
                    if hp == 1:
                        P.add("dve", lambda e, bo=bo: e.reciprocal(rden[:], bo[:, 256:512]), [t_bo], [t_rden])
                        P.tt("dve", oraw[:, 2 + c, :], bo[:, 0:256], rden[:], ALU.mult, [t_bo, t_rden],
                             [t_or[2 + c]])
                else:
                    r, kts, nl, s0 = it["r"], it["kts"], it["nl"], it["s0"]
                    mml = []
                    for j_, kt in enumerate(kts):
                        if s0 % 2 == 1 and j_ == 0:
                            p0, p1 = 64, 128
                        elif s0 % 2 == 1 and j_ == nl - 1:
                            p0, p1 = 0, 64
                        else:
                            p0, p1 = 0, 128
                        mml.append((v_tok[p0:p1, kt, h * 64:(h + 1) * 64], ones[p0:p1, 0:64],
                                    pt[p0:p1, j_ * 64:(j_ + 1) * 64], p0, [t_v[kt]]))
                    for t in range(2):
                        mml.append((vc[:, t, h * 64:(h + 1) * 64], ones[:, 0:64],
                                    pt[:, (nl + t) * 64:(nl + t + 1) * 64], 0, [t_vc]))
                    for n_, (lv, lo_, rh, p0, rd) in enumerate(mml):
                        P.mm(bo[ps_, 0:64], lv, rh, n_ == 0, n_ == len(mml) - 1, rd + [t_pt], [t_bo],
                             tp=(p0, hp * 64))
                    for n_, (lv, lo_, rh, p0, rd) in enumerate(mml):
                        P.mm(bo[ps_, 64:128], lo_, rh, n_ == 0, n_ == len(mml) - 1, [t_ones, t_pt],
                             [t_bo], tp=(p0, hp * 64))
                    if hp == 1:
                        P.add("dve", lambda e, bo=bo: e.reciprocal(rden[:, 0:64], bo[:, 64:128]), [t_bo],
                              [t_rden])
                        P.tt("dve", oraw[:, 2 + c, (r % 4) * 64:(r % 4 + 1) * 64], bo[:, 0:64], rden[:, 0:64],
                             ALU.mult, [t_bo, t_rden], [t_or[2 + c]])

            for u in range(4):
                tok0 = u * 256
                ucols = slice(tok0, tok0 + 256)
                tt = u // 2
                _mark(" u%d-attn" % u)
                items = []
                for c in range(4):
                    if is_p:
                        for hp in range(2):
                            items.append({"c": c, "hp": hp, "h": 2 * c + hp, "u": u})
                    else:
                        for r in range(4 * u, 4 * u + 4):
                            s0 = min(max(r - 4, 0), 8)
                            kts = list(range(s0 // 2, (s0 + 7) // 2 + 1))
                            for hp in range(2):
                                items.append({"c": c, "hp": hp, "h": 2 * c + hp, "r": r, "s0": s0, "kts": kts,
                                              "nl": len(kts)})
                for it in items:
                    it["n"] = acnt["n"]
                    acnt["n"] += 1
                def pool_chunk():
                    _mark(" u%d-pool" % u)
                    for c in range(2):
                        for gh in range(2):
                            g4 = 2 * c + gh
                            for jj in range(2):
                                j = 2 * u + jj
                                lo, hi = seq_range(j)
                                ins = [i_ for i_ in (j - 1, j, j + 1) if lo <= i_ <= hi]
                                for n_, i_ in enumerate(ins):
                                    var = 3 if i_ == j - 1 else (4 if i_ == j + 1 else variant(j))
                                    P.mm(bank[4][gh * 64:(gh + 1) * 64, c * 256 + jj * 128:c * 256 + (jj + 1) * 128],
                                         za_tok[:, i_, g4 * 64:(g4 + 1) * 64], bands[:, g4 * 5 + var, :],
                                         n_ == 0, n_ == len(ins) - 1, [t_za[i_], t_bands], [t_bank[4]],
                                         tp=(0, gh * 64))
                        for jj in range(2):
                            P.tt("dve", pooledT[:, c, jj * 128:(jj + 1) * 128],
                                 bank[4][:, c * 256 + jj * 128:c * 256 + (jj + 1) * 128],
                                 invc[:, c * 3 + variant(2 * u + jj), :], ALU.mult, [t_bank[4], t_invc], [t_pl[c]])
                        P.mm(bank[5][:, c * 256:(c + 1) * 256], pwb[:, c, :], pooledT[:, c, :], True, True,
                             [t_pwb, t_pl[c]], [t_bank[5]])
                        P.actv(oraw[:, c, :], bank[5][:, c * 256:(c + 1) * 256], AF.Identity, [t_bank[5], t_vtT],
                               [t_or[c]], scale=vtT[:, l, 104 + c:105 + c])
                    for c in range(2):
                        for gh in range(2):
                            g4 = 2 * c + gh
                            for jj in range(2):
                                P.mm(bank[4][gh * 64:(gh + 1) * 64, c * 256 + jj * 128:c * 256 + (jj + 1) * 128],
                                     vg_tok[:, 2 * u + jj, g4 * 64:(g4 + 1) * 64], wsT[:, g4, :], True, True,
                                     [t_vg[2 * u + jj], t_wsT], [t_bank[4]], tp=(0, gh * 64))
                        P.tt("dve", gv[:, 0:256].rearrange("p (j n) -> p j n", j=2),
                             bank[4][:, c * 256:(c + 1) * 256].rearrange("p (j n) -> p j n", j=2),
                             bsb[:, c:c + 1, :].to_broadcast([128, 2, 128]), ALU.add, [t_bank[4], t_bsb], [t_gv])
                        P.tt("pool", oraw[:, 6 + c, :], gv[:, 0:256], uT[:, c, ucols], ALU.mult,
                             [t_gv, t_u[c][tt]], [t_or[6 + c]])

                def capture(fn):
                    rec = []
                    P.add = lambda eng, f, reads=(), writes=(), dma=None, is_out=False: rec.append(
                        (eng, f, reads, writes, dma, is_out))
                    try:
                        fn()
                    finally:
                        del P.add
                    return rec

                def emit_pair(fn, ita, itb):
                    ra = capture(lambda: fn(ita))
                    rb = capture(lambda: fn(itb))
                    pa = [x for x in ra if x[0] == "pe"]
                    pb = [x for x in rb if x[0] == "pe"]
                    for i_ in range(max(len(pa), len(pb))):
                        if i_ < len(pa):
                            P.add(*pa[i_])
                        if i_ < len(pb):
                            P.add(*pb[i_])
                    for x in ra + rb:
                        if x[0] != "pe":
                            P.add(*x)

                pairs = [(items[2 * j], items[2 * j + 1]) for j in range(len(items) // 2)]
                DEPTHP = 1
                for idx in range(len(pairs) + DEPTHP):
                    if idx == len(pairs) // 2:
                        pool_chunk()
                    if idx < len(pairs):
                        emit_pair(attn_S, *pairs[idx])
                    if idx - DEPTHP >= 0:
                        emit_pair(attn_PV, *pairs[idx - DEPTHP])
                _mark(" u%d-out" % u)
                segs = ((0, 2, 256.0), (2, 6, 512.0), (6, 8, 256.0))
                for si, (o0, o1, wd_) in enumerate(segs):
                    stb, t_stb = (bank[6], t_bank[6]) if si < 2 else (bank[5], t_bank[5])
                    scol = slice((si % 2) * 256, (si % 2) * 256 + 256)
                    for oc in range(o0, o1):
                        sqo_, t_sqo_ = sq[oc % 2], t_sq[oc % 2]
                        P.actv(sqo_[:, 0:256], oraw[:, oc, :], AF.Square, [t_or[oc]], [t_sqo_])
                        P.mm(stb[:, scol], ones[:], sqo_[:, 0:256], oc == o0, oc == o1 - 1, [t_ones, t_sqo_],
                             [t_stb])
                    P.actv(ms2, stb[:, scol], AF.Sqrt, [t_stb, t_chalf], [t_ms2], scale=1.0 / wd_,
                           bias=epsc[:, 0:1])
                    P.add("dve", lambda e, si=si: e.reciprocal(rs_ap[si], ms2), [t_ms2], [rs_t[si]])
                for oc in range(8):
                    si = 0 if oc < 2 else (1 if oc < 6 else 2)
                    P.stt(oTn[:, oc, :], oraw[:, oc, :], vtT[:, l, 96 + oc:97 + oc], rs_ap[si], ALU.mult, ALU.mult,
                          [t_or[oc], t_vtT, rs_t[si]], [t_aT[1][oc // 2]])
                for dc in range(8):
                    sl = sF if dc < 4 else sG
                    vw = _v3(wslot[sl], 8)
                    bw, t_bw = bank[2 + dc % 2], t_bank[2 + dc % 2]
                    for oc in range(8):
                        P.mm(bw[:, 0:256], vw[:, oc, (dc % 4) * 128:(dc % 4 + 1) * 128], oTn[:, oc, :],
                             oc == 0, oc == 7, [t_slot[sl], t_aT[1][oc // 2]], [t_bw])
                    P.stt(xT[:, dc, ucols], bw[:, 0:256], Gm[:, l, 1, dc, v:v + 1], xT[:, dc, ucols],
                          ALU.mult, ALU.add, [t_bw, t_x[dc][tt], t_mods], [t_x[dc][tt]])
                if u % 2 == 1:
                    after_tt(u // 2)
            S.done(pF)
            S.done(pG)

        class _Stop(Exception):
            pass
        build_program.Stop = _Stop
        try:
            for g in range(2):
                _mark("load%d" % g)
                if not EARLY_NORM:
                    nop = lambda tt: None
                    load_x(g, nop)
                    if g == 0:
                        ada_compute(0)
                    for l in range(L):
                        _mark("ffn%d%d0" % (g, l))
                        norm_tt(l, 0, g, 0); norm_tt(l, 0, g, 1)
                        ffn(g, l, 0, nop)
                        if g == 0 and l == 0:
                            ada_compute(1)
                        _mark("mix%d%d" % (g, l))
                        mixer_setup(g, l)
                        norm_tt(l, 1, g, 0); norm_tt(l, 1, g, 1)
                        mixer(g, l, nop)
                        _mark("ffn%d%d1" % (g, l))
                        norm_tt(l, 2, g, 0); norm_tt(l, 2, g, 1)
                        ffn(g, l, 1, nop)
                    _mark("store%d" % g)
                    store_y(g)
                    continue
                if g == 0:
                    load_x(g, lambda tt: None)
                    ada_compute(0)
                    norm_tt(0, 0, 0, 0)
                    norm_tt(0, 0, 0, 1)
                else:
                    load_x(g, lambda tt, g=g: norm_tt(0, 0, g, tt))
                for l in range(L):
                    _mark("ffn%d%d0" % (g, l))
                    mixer_setup(g, l)
                    ffn(g, l, 0, lambda tt, g=g, l=l: norm_tt(l, 1, g, tt))
                    if g == 0 and l == 0:
                        ada_compute(1)
                    _mark("mix%d%d" % (g, l))
                    mixer(g, l, lambda tt, g=g, l=l: norm_tt(l, 2, g, tt))
                    _mark("ffn%d%d1" % (g, l))
                    if l + 1 < L:
                        ffn(g, l, 1, lambda tt, g=g, l=l: norm_tt(l + 1, 0, g, tt))
                    else:
                        ffn(g, l, 1, lambda tt: None)
                _mark("store%d" % g)
                store_y(g)
        except _Stop:
            pass
        _mark("end")
        build_program.phases = PH
        P.emit(st)
    return nc


def _consts():
    ident = np.eye(128, dtype=np.float32)
    bands = np.zeros((128, 20, 128), np.float32)
    invc = np.zeros((128, 6, 128), np.float32)
    for gi, w in enumerate(WINS):
        h = w // 2
        tp = np.arange(128)[:, None]
        t = np.arange(128)[None, :]
        inb = ((tp >= t - h) & (tp < t + h)).astype(np.float32)
        cnt_first = (np.minimum(t + h, 10 ** 6) - np.maximum(t - h, 0)).astype(np.float32)
        cnt_mid = np.full((1, 128), float(w), np.float32)
        cnt_last = ((128 - t) + h - np.maximum(0, 0)).astype(np.float32)
        cnt_last = np.minimum(cnt_last, w).astype(np.float32)
        eye = np.eye(128, dtype=np.float32)
        bands[:, gi * 5 + 0, :] = inb - eye * cnt_first
        bands[:, gi * 5 + 1, :] = inb - eye * cnt_mid
        bands[:, gi * 5 + 2, :] = inb - eye * cnt_last
        bands[:, gi * 5 + 3, :] = ((tp - 128) >= (t - h)).astype(np.float32)
        bands[:, gi * 5 + 4, :] = ((tp + 128) < (t + h)).astype(np.float32)
        c, half = gi // 2, gi % 2
        for vi, cn in enumerate((cnt_first, cnt_mid, cnt_last)):
            invc[half * 64:(half + 1) * 64, c * 3 + vi, :] = 1.0 / cn
    dw = np.zeros((31, 127), np.float32)
    for j in range(31):
        dw[j, j + 48] = 1.0
    cols = np.arange(64)
    cs = np.clip(cols - 8, 0, 48)
    ok = (cols[None, :] >= cs[:, None]) & (cols[None, :] < cs[:, None] + 16)
    colok = np.concatenate([ok.T, ok.T], 0).astype(np.float32)
    return ident, bands, invc, dw, colok


def make_in_maps(inp):
    f = lambda a: np.ascontiguousarray(np.asarray(a, dtype=np.float32))
    ident, bands, invc, dw, colok = _consts()
    vt = np.zeros((L, 128, 128), np.float32)
    gt = np.zeros((L, 384), np.float32)
    rpb = np.zeros((L, 128, 31), np.float32)
    for l in range(L):
        vt[l, 0:72] = f(inp["ada_b"])[l].reshape(72, 128)
        vt[l, 72:96] = f(inp["norm_g"])[l].reshape(24, 128)
        vt[l, 96:104] = f(inp["out_norm_g"])[l].reshape(8, 128)
        vt[l, 104:106] = f(inp["pool_scale"])[l].reshape(2, 128)
        gt[l, 0:64] = f(inp["q_norm_g"])[l]
        gt[l, 64:128] = f(inp["k_norm_g"])[l]
        gt[l, 128:384] = f(inp["sg_vnorm_g"])[l].reshape(256)
        rpb[l, 0:120] = f(inp["na_rpb"])[l].reshape(120, 31)
    shared = {
        "vt": vt, "gt": gt, "rpb": rpb,
        "poolw": f(inp["pool_w"]), "sgw": f(inp["sg_w"]), "sgb": f(inp["sg_b"]),
        "adaw": f(inp["ada_w"]), "wg": f(inp["ffn_w_gate"]), "wu": f(inp["ffn_w_up"]),
        "wd": f(inp["ffn_w_down"]), "win": f(inp["w_in"]), "wout": f(inp["w_out"]),
        "ident": ident, "bands": bands, "invc": invc, "dw": dw, "colok": colok,
    }
    xp = f(inp["x_prompt"]); xs = f(inp["x_sample"])
    ck = f(inp["cache_k"]); cv = f(inp["cache_v"]); c = f(inp["c"]); cc = f(inp["c_ctx"])
    maps = []
    for i in range(NCORES):
        x = np.stack([xp[4 * i:4 * i + 4].reshape(1024, D), xs[i]], 0)
        cvec = np.zeros((128, 128), np.float32)
        cvec[0:8] = cc.reshape(8, 128)
        cvec[8:16] = c[i].reshape(8, 128)
        m = dict(shared)
        m.update({"x": np.ascontiguousarray(x), "ck": np.ascontiguousarray(ck[i]),
                  "cv": np.ascontiguousarray(cv[i]), "cvec": cvec})
        maps.append(m)
    return maps


_NC_CACHE = {}


def kernel(**inputs):
    if "nc" not in _NC_CACHE:
        _NC_CACHE["nc"] = build_program()
    nc = _NC_CACHE["nc"]
    maps = make_in_maps(inputs)
    res = run_bass_kernel_spmd(nc, maps, core_ids=list(range(NCORES)))
    r = res.results
    yp = np.concatenate([r[i]["y"][0].reshape(4, 256, D) for i in range(NCORES)], 0)
    ys = np.stack([r[i]["y"][1] for i in range(NCORES)], 0)
    nk = np.concatenate([r[i]["nk"] for i in range(NCORES)], 0)
    nv = np.concatenate([r[i]["nv"] for i in range(NCORES)], 0)
    return (yp.astype(np.float32), ys.astype(np.float32), nk.astype(np.float32), nv.astype(np.float32))
```

```python
import numpy as np
from contextlib import ExitStack
import concourse.bass as bass
import concourse.mybir as mybir
from concourse.bass_utils import run_bass_kernel_spmd

F32 = mybir.dt.float32
BF16 = mybir.dt.bfloat16
AF = mybir.ActivationFunctionType
ALU = mybir.AluOpType
AX = mybir.AxisListType

D = 1024
DFF = 2816
L = 2
NH = 8
EPS = 1e-6
NCORES = 8
WINS = (2, 4, 8, 16)
EARLY_NORM = True


class Tl:
    __slots__ = ("name", "w", "r", "excl")

    def __init__(self, name="", excl=False):
        self.name = name
        self.w = None
        self.r = {}
        self.excl = excl


class Op:
    __slots__ = ("eng", "fn", "deps", "marked", "count", "dkey", "dcount", "is_dma", "uid")


class Prog:
    ENGS = ("pe", "act", "dve", "pool", "sp")

    def __init__(self, nc):
        self.nc = nc
        self.ops = {e: [] for e in self.ENGS}
        self.latest_dma = {}
        self.dma_counts = {}
        self.uid = 0
        self.out_keys = set()

    def add(self, eng, fn, reads=(), writes=(), dma=None, is_out=False):
        op = Op()
        op.eng = eng
        op.fn = fn
        op.marked = False
        op.count = None
        op.is_dma = dma is not None
        op.dkey = dma
        op.uid = self.uid
        self.uid += 1
        deps = {}
        for t in reads:
            if t.w is not None:
                deps[t.w.uid] = t.w
            if t.excl:
                for d in t.r.values():
                    if d.eng != eng:
                        deps[d.uid] = d
        for t in writes:
            if t.w is not None:
                deps[t.w.uid] = t.w
            for d in t.r.values():
                deps[d.uid] = d
        final = {}
        for d in deps.values():
            if d.is_dma:
                d = self.latest_dma[d.dkey]
            elif d.eng == "pe" and eng == "pe" and not op.is_dma:
                continue
            else:
                d.marked = True
            final[d.uid] = d
        op.deps = list(final.values())
        if op.is_dma:
            c = self.dma_counts.get(dma, 0) + 16
            self.dma_counts[dma] = c
            op.dcount = c
            self.latest_dma[dma] = op
            if is_out:
                self.out_keys.add(dma)
        rkey = ("dma", op.uid) if op.is_dma else eng
        for t in reads:
            t.r[rkey] = op
        for t in writes:
            t.w = op
            t.r = {}
        self.ops[eng].append(op)
        return op

    def dma(self, eng, key, out, in_, reads=(), writes=(), is_out=False):
        return self.add(eng, lambda e: e.dma_start(out=out, in_=in_), reads, writes,
                        dma=key, is_out=is_out)

    def mm(self, out, lhsT, rhs, start, stop, reads, writes, tp=None):
        if tp is None:
            return self.add("pe", lambda e: e.matmul(out, lhsT, rhs, start=start, stop=stop), reads, writes)
        return self.add("pe", lambda e: e.matmul(out, lhsT, rhs, start=start, stop=stop, tile_position=tp),
                        reads, writes)

    def tr(self, out, in_, ident, reads, writes):
        return self.add("pe", lambda e: e.transpose(out, in_, ident), reads, writes)

    def actv(self, out, in_, func, reads, writes, scale=None, bias=None):
        kw = {}
        if scale is not None:
            kw["scale"] = scale
        if bias is not None:
            kw["bias"] = bias
        return self.add("act", lambda e: e.activation(out, in_, func, **kw), reads, writes)

    def tt(self, eng, out, in0, in1, op, reads, writes):
        return self.add(eng, lambda e: e.tensor_tensor(out, in0, in1, op=op), reads, writes)

    def ts(self, eng, out, in0, s1, s2, op0, op1, reads, writes):
        if s2 is None:
            return self.add(eng, lambda e: e.tensor_scalar(out, in0, s1, None, op0=op0), reads, writes)
        return self.add(eng, lambda e: e.tensor_scalar(out, in0, s1, s2, op0=op0, op1=op1), reads, writes)

    def stt(self, out, in0, scalar, in1, op0, op1, reads, writes):
        return self.add("dve", lambda e: e.scalar_tensor_tensor(out, in0, scalar, in1, op0=op0, op1=op1),
                        reads, writes)

    def cp(self, eng, out, in_, reads, writes):
        if eng == "act":
            return self.add("act", lambda e: e.copy(out, in_), reads, writes)
        return self.add(eng, lambda e: e.tensor_copy(out, in_), reads, writes)

    def red(self, out, in_, reads, writes):
        return self.add("dve", lambda e: e.tensor_reduce(out, in_, axis=AX.X, op=ALU.add), reads, writes)

    def memset(self, eng, ap, val, writes):
        return self.add(eng, lambda e: e.memset(ap, val), (), writes)

    def emit(self, stack):
        nc = self.nc
        for e in self.ENGS:
            c = 0
            for op in self.ops[e]:
                if op.marked and not op.is_dma:
                    c += 1
                    op.count = c
        esem = {e: stack.enter_context(nc.semaphore("s_" + e)) for e in self.ENGS}
        dsem = {k: stack.enter_context(nc.semaphore("d_%d" % i)) for i, k in enumerate(self.dma_counts)}
        block = stack.enter_context(nc.Block())
        ops = self.ops
        out_final = [(dsem[k], self.dma_counts[k]) for k in sorted(self.out_keys, key=str)]

        def run(ename, e):
            known = {}
            for op in ops[ename]:
                for d in op.deps:
                    if d.is_dma:
                        s, v = dsem[d.dkey], d.dcount
                    else:
                        s, v = esem[d.eng], d.count
                    kid = id(s)
                    if known.get(kid, 0) >= v:
                        continue
                    known[kid] = v
                    e.wait_ge(s, v)
                ins = op.fn(e)
                if op.is_dma:
                    ins.then_inc(dsem[op.dkey], 16)
                elif op.marked:
                    ins.then_inc(esem[ename], 1)
            if ename == "sp":
                for s, v in out_final:
                    e.wait_ge(s, v)

        @block.sync
        def _(e):
            run("sp", e)

        @block.tensor
        def _(e):
            run("pe", e)

        @block.scalar
        def _(e):
            run("act", e)

        @block.vector
        def _(e):
            run("dve", e)

        @block.gpsimd
        def _(e):
            run("pool", e)


class Stream:
    NS = 6
    LOOK = 5

    def __init__(self, P, slots, t_slots):
        self.P = P
        self.slots = slots
        self.t_slots = t_slots
        self.pieces = []
        self.issued = 0
        self.freed = 0
        self.want = 0

    def plan(self, parts):
        self.pieces.append(parts)
        return len(self.pieces) - 1

    def pump(self):
        while self.issued <= self.want and self.issued < len(self.pieces) and self.issued < self.freed + self.NS:
            i = self.issued
            s = i % self.NS
            for dst_fn, src in self.pieces[i]:
                self.P.dma("pool", ("w", s), dst_fn(self.slots[s]), src, writes=[self.t_slots[s]])
            self.issued += 1

    def need(self, p):
        self.want = max(self.want, p + self.LOOK)
        self.pump()
        assert self.issued > p, (self.issued, p, self.freed)
        return p % self.NS

    def done(self, p):
        self.freed = max(self.freed, p + 1)
        self.pump()


def _v3(t, k):
    return t[:].rearrange("p (k n) -> p k n", k=k)


def build_program(stop_after=None):
    nc = bass.Bass("TRN2", target_bir_lowering=False)

    def din(name, shape):
        return nc.dram_tensor(name, shape, F32, kind="ExternalInput").ap()

    def dout(name, shape):
        return nc.dram_tensor(name, shape, F32, kind="ExternalOutput").ap()

    x_d = din("x", [2, 1024, D])
    ck_d = din("ck", [L, NH, 256, 64])
    cv_d = din("cv", [L, NH, 256, 64])
    cvec_d = din("cvec", [128, 128])
    vt_d = din("vt", [L, 128, 128])
    gt_d = din("gt", [L, 384])
    poolw_d = din("poolw", [L, 4, 64, 64])
    sgw_d = din("sgw", [L, 4, 128, 128])
    sgb_d = din("sgb", [L, 4, 128])
    rpb_d = din("rpb", [L, 128, 31])
    adaw_d = din("adaw", [L, D, 9 * D])
    wg_d = din("wg", [L, 2, D, DFF])
    wu_d = din("wu", [L, 2, D, DFF])
    wd_d = din("wd", [L, 2, DFF, D])
    win_d = din("win", [L, D, 2304])
    wout_d = din("wout", [L, D, D])
    ident_d = din("ident", [128, 128])
    bands_d = din("bands", [128, 20, 128])
    invc_d = din("invc", [128, 6, 128])
    dw_d = din("dw", [31, 127])
    colok_d = din("colok", [128, 64])
    y_d = dout("y", [2, 1024, D])
    nk_d = dout("nk", [4, L, NH, 256, 64])
    nv_d = dout("nv", [4, L, NH, 256, 64])

    P = Prog(nc)
    st = ExitStack()

    def sb(name, shape, dt=F32):
        return st.enter_context(nc.sbuf_tensor("s_" + name, shape, dt))

    with st:
        bank = [st.enter_context(nc.psum_tensor("bank%d" % i, [128, 512], F32)) for i in range(7)]
        bankb = st.enter_context(nc.psum_tensor("bankb", [128, 1024], BF16))
        t_bank = [Tl("bank%d" % i, excl=True) for i in range(7)]
        t_bankb = Tl("bankb", excl=True)

        ident = sb("ident", [128, 128]); t_ident = Tl()
        identb = sb("identb", [128, 128], BF16); t_identb = Tl()
        ones = sb("ones", [128, 128], BF16); t_ones = Tl()
        epsc = sb("epsc", [128, 1]); t_chalf = Tl()
        bands = sb("bands", [128, 20, 128], BF16); t_bands = Tl()
        invc = sb("invc", [128, 6, 128]); t_invc = Tl()
        dw = sb("dw", [31, 127]); t_dw = Tl()
        colok = sb("colok", [128, 64]); t_colok = Tl()
        P.dma("sp", "c0", ident[:], ident_d, writes=[t_ident])
        P.dma("sp", "c1", invc[:], invc_d, writes=[t_invc])
        P.dma("sp", "c2", dw[:], dw_d, writes=[t_dw])
        P.dma("sp", "c3", colok[:], colok_d, writes=[t_colok])
        P.dma("pool", "c4", bands[:], bands_d, writes=[t_bands])
        P.cp("dve", identb[:], ident[:], [t_ident], [t_identb])
        P.memset("pool", ones[:], 1.0, [t_ones])
        P.memset("pool", epsc[:], EPS, [t_chalf])

        xT = sb("xT", [128, 8, 1024]); t_x = [[Tl() for _ in range(2)] for _ in range(8)]
        hT = sb("hT", [128, 8, 1024], BF16); t_h = [[Tl() for _ in range(2)] for _ in range(8)]
        wslot = [sb("wslot%d" % i, [128, 4096], BF16) for i in range(Stream.NS)]
        t_slot = [Tl("slot%d" % i) for i in range(Stream.NS)]
        S = Stream(P, wslot, t_slot)
        aT = [sb("aT%d" % i, [128, 4, 512], BF16) for i in range(2)]
        t_aT = [[Tl() for _ in range(4)] for _ in range(2)]
        stmp = [sb("stmp%d" % i, [128, 512]) for i in range(2)]; t_stmp = [Tl(), Tl()]
        xio = sb("xio", [128, 2048]); t_or = [Tl() for _ in range(8)]
        oraw = xio[:].rearrange("p (c n) -> p c n", c=8)
        sq = [sb("sq%d" % i, [128, 512], BF16) for i in range(2)]; t_sq = [Tl(), Tl()]
        ms = sb("ms", [128, 512]); t_ms = Tl()
        rstd, t_rstd = ms, t_ms
        ntmp = [sb("ntmp%d" % i, [128, 512]) for i in range(2)]; t_ntmp = [Tl(), Tl()]
        mods = sb("mods", [128, L, 72, 2]); t_mods = Tl()
        Am = sb("Am", [128, L, 3, 8, 2])
        Gm = sb("Gm", [128, L, 3, 8, 2])
        vtT = sb("vtT", [128, L, 128]); t_vtT = Tl()
        scT = sb("scT", [128, 16], BF16); t_scT = Tl()
        vtr_t = sb("vtr", [128, 128]); t_vtr = Tl()
        vtr = vtr_t[:]
        za_tok = sb("za_tok", [128, 8, 256], BF16); t_za = [Tl() for _ in range(8)]
        vg_tok = sb("vg_tok", [128, 8, 256], BF16); t_vg = [Tl() for _ in range(8)]
        v_tok = sb("v_tok", [128, 8, 512], BF16); t_v = [Tl() for _ in range(8)]
        qT = sb("qT", [128, 4, 1024], BF16); t_q = [[Tl() for _ in range(8)] for _ in range(4)]
        kT = sb("kT", [128, 4, 1024], BF16); t_k = [[Tl() for _ in range(8)] for _ in range(4)]
        uT = sb("uT", [128, 2, 1024], BF16); t_u = [[Tl() for _ in range(2)] for _ in range(2)]
        kcT = sb("kcT", [128, 4, 256], BF16); t_kc = Tl()
        qkz = sb("qkz", [128, 1280]); t_qkz = Tl()
        vc = sb("vc", [128, 2, 512], BF16); t_vc = Tl()
        pooledT = sb("pooledT", [128, 2, 256], BF16); t_pl = [Tl(), Tl()]
        EB = sb("EB", [128, 8, 14, 64], BF16); t_EB = Tl()
        wsT = sb("wsT", [128, 4, 128], BF16); t_wsT = Tl()
        pwb = sb("pwb", [128, 2, 128], BF16); t_pwb = Tl()
        bsb = sb("bsb", [128, 2, 128]); t_bsb = Tl()
        gtb = sb("gtb", [128, 384]); t_gtb = Tl()
        rden = sb("rden", [128, 256]); t_rden = Tl()
        st8 = sb("st8", [128, 40]); t_st8 = Tl()

        pid = {}
        groups = [(0, 4), (4, 4), (8, 2), (10, 4), (14, 4), (18, 4)]

        def plan_ada(l):
            for pc in range(18):
                pid[("ada", l, pc)] = S.plan([(
                    lambda t: _v3(t, 8),
                    adaw_d[l].rearrange("(k p) n -> p k n", p=128)[:, :, pc * 512:(pc + 1) * 512])])

        def plan_ffn(g, l, j):
            for gi, (c0, nfc) in enumerate(groups):
                w = nfc * 128
                pid[("g", g, l, j, gi)] = S.plan([(
                    lambda t, w=w: _v3(t, 8)[:, :, 0:w],
                    wg_d[l, j].rearrange("(k p) n -> p k n", p=128)[:, :, c0 * 128:c0 * 128 + w])])
                pid[("u", g, l, j, gi)] = S.plan([(
                    lambda t, w=w: _v3(t, 8)[:, :, 0:w],
                    wu_d[l, j].rearrange("(k p) n -> p k n", p=128)[:, :, c0 * 128:c0 * 128 + w])])
                pid[("d", g, l, j, gi)] = S.plan([(
                    lambda t, nfc=nfc: _v3(t, 4)[:, 0:nfc, :],
                    wd_d[l, j][c0 * 128:c0 * 128 + w, :].rearrange("(f p) n -> p f n", p=128))])

        def plan_mixer(g, l):
            wv = win_d[l].rearrange("(k p) n -> p k n", p=128)
            pid[("A", g, l)] = S.plan([
                (lambda t: _v3(t, 8)[:, :, 0:256], wv[:, :, 0:256]),
                (lambda t: _v3(t, 8)[:, :, 256:512], wv[:, :, 2048:2304])])
            pid[("B", g, l)] = S.plan([(lambda t: _v3(t, 8), wv[:, :, 256:768])])
            pid[("C", g, l)] = S.plan([(lambda t: _v3(t, 8), wv[:, :, 768:1280])])
            pid[("D", g, l)] = S.plan([(lambda t: _v3(t, 8), wv[:, :, 1280:1792])])
            pid[("E", g, l)] = S.plan([(lambda t: _v3(t, 8)[:, :, 0:256], wv[:, :, 1792:2048])])
            wo = wout_d[l].rearrange("(k p) n -> p k n", p=128)
            pid[("F", g, l)] = S.plan([(lambda t: _v3(t, 8), wo[:, :, 0:512])])
            pid[("G", g, l)] = S.plan([(lambda t: _v3(t, 8), wo[:, :, 512:1024])])

        plan_ada(0)
        for g in range(2):
            for l in range(L):
                plan_ffn(g, l, 0)
                if g == 0 and l == 0:
                    plan_ada(1)
                plan_mixer(g, l)
                plan_ffn(g, l, 1)

        P.dma("sp", "v0", vtr, cvec_d, writes=[t_vtr])
        P.tr(bank[6][:, 0:128], vtr, ident[:], [t_vtr, t_ident], [t_bank[6]])
        P.actv(scT[:], bank[6][:, 0:16], AF.Silu, [t_bank[6]], [t_scT])
        for l in range(L):
            P.dma("sp", "v0", vtr, vt_d[l], writes=[t_vtr])
            P.tr(bank[6][:, 0:128], vtr, ident[:], [t_vtr, t_ident], [t_bank[6]])
            P.cp("dve", vtT[:, l, :], bank[6][:, 0:128], [t_bank[6]], [t_vtT])
        def ada_compute(l):
            for pc in range(18):
                p_ = pid[("ada", l, pc)]
                s = S.need(p_)
                wv = _v3(wslot[s], 8)
                for fc in range(4):
                    j = pc * 4 + fc
                    for k in range(8):
                        P.mm(bank[5][:, 2 * j:2 * j + 2], wv[:, k, fc * 128:(fc + 1) * 128],
                             scT[:, k:16:8], k == 0, k == 7, [t_slot[s], t_scT], [t_bank[5]])
                S.done(p_)
            P.tt("dve", mods[:, l, :, :], bank[5][:, 0:144].rearrange("p (j v) -> p j v", v=2),
                 vtT[:, l, 0:72].rearrange("p (j o) -> p j o", o=1).to_broadcast([128, 72, 2]),
                 ALU.add, [t_bank[5], t_vtT], [t_mods])
            for n in range(3):
                P.ts("dve", Am[:, l, n, :, :], mods[:, l, (3 * n + 1) * 8:(3 * n + 2) * 8, :], 1.0, None,
                     ALU.add, None, [t_mods], [t_mods])
                P.tt("dve", Am[:, l, n, :, :], Am[:, l, n, :, :],
                     vtT[:, l, 72 + n * 8:80 + n * 8].rearrange("p (j o) -> p j o", o=1).to_broadcast([128, 8, 2]),
                     ALU.mult, [t_mods, t_vtT], [t_mods])
                P.ts("dve", Gm[:, l, n, :, :], mods[:, l, (3 * n + 2) * 8:(3 * n + 3) * 8, :],
                     (1.0 if n == 1 else 0.5), None, ALU.mult, None, [t_mods], [t_mods])


        ADA0_DEFERRED = True

        PH = []
        _mark = lambda nm: PH.append((nm, len(P.ops["pe"])))
        def norm_tt(l, n, v, tt):
            cols = slice(tt * 512, (tt + 1) * 512)
            for k in range(8):
                P.actv(sq[k % 2][:], xT[:, k, cols], AF.Square, [t_x[k][tt]], [t_sq[k % 2]])
                P.mm(bank[6][:], ones[:], sq[k % 2][:], k == 0, k == 7, [t_sq[k % 2], t_ones], [t_bank[6]])
            P.actv(ms[:], bank[6][:], AF.Ln, [t_bank[6], t_chalf], [t_ms], scale=1.0 / D, bias=epsc[:, 0:1])
            P.actv(rstd[:], ms[:], AF.Exp, [t_ms], [t_rstd], scale=-0.5)
            for k in range(8):
                P.tt("dve", ntmp[k % 2][:], xT[:, k, cols], rstd[:], ALU.mult,
                     [t_x[k][tt], t_rstd], [t_ntmp[k % 2]])
                P.actv(hT[:, k, cols], ntmp[k % 2][:], AF.Identity, [t_ntmp[k % 2], t_mods], [t_h[k][tt]],
                       scale=Am[:, l, n, k, v:v + 1], bias=mods[:, l, 3 * n * 8 + k, v:v + 1])

        cnt = {"up": 0, "dn": 0, "ab": 0}

        def ffn(g, l, j, after_tt):
            n = 0 if j == 0 else 2
            v = g
            pending = []
            last_gi = len(groups) - 1

            def flush():
                while pending:
                    pending.pop(0)()

            for gi, (c0, nfc) in enumerate(groups):
                pg_, pu_, pd_ = pid[("g", g, l, j, gi)], pid[("u", g, l, j, gi)], pid[("d", g, l, j, gi)]
                sg, su, sd = S.need(pg_), S.need(pu_), S.need(pd_)
                vg_, vu_, vd_ = _v3(wslot[sg], 8), _v3(wslot[su], 8), _v3(wslot[sd], 4)
                for tt in range(2):
                    cols = slice(tt * 512, (tt + 1) * 512)
                    ab = cnt["ab"] % 2
                    cnt["ab"] += 1
                    for f in range(nfc):
                        ib = cnt["up"] % 2
                        cnt["up"] += 1
                        for k in range(8):
                            P.mm(bank[ib][:], vg_[:, k, f * 128:(f + 1) * 128], hT[:, k, cols], k == 0, k == 7,
                                 [t_slot[sg], t_h[k][tt]], [t_bank[ib]])
                        for k in range(8):
                            P.mm(bank[2 + ib][:], vu_[:, k, f * 128:(f + 1) * 128], hT[:, k, cols], k == 0, k == 7,
                                 [t_slot[su], t_h[k][tt]], [t_bank[2 + ib]])
                        P.actv(stmp[ib][:], bank[ib][:], AF.Silu, [t_bank[ib]], [t_stmp[ib]])
                        P.tt("dve", aT[ab][:, f, :], stmp[ib][:], bank[2 + ib][:], ALU.mult,
                             [t_stmp[ib], t_bank[2 + ib]], [t_aT[ab][f]])

                    def down(ab=ab, nfc=nfc, sd=sd, vd_=vd_, cols=cols, tt=tt, gi=gi):
                        for dc in range(8):
                            ib2 = 4 + cnt["dn"] % 2
                            cnt["dn"] += 1
                            for f in range(nfc):
                                P.mm(bank[ib2][:], vd_[:, f, dc * 128:(dc + 1) * 128], aT[ab][:, f, :],
                                     f == 0, f == nfc - 1, [t_slot[sd], t_aT[ab][f]], [t_bank[ib2]])
                            P.stt(xT[:, dc, cols], bank[ib2][:], Gm[:, l, n, dc, v:v + 1], xT[:, dc, cols],
                                  ALU.mult, ALU.add, [t_bank[ib2], t_x[dc][tt], t_mods], [t_x[dc][tt]])
                        if gi == last_gi:
                            after_tt(tt)

                    flush()
                    if gi == last_gi:
                        down()
                    else:
                        pending.append(down)
                S.done(pg_)
                S.done(pu_)
                if gi > 0:
                    S.done(pid[("d", g, l, j, gi - 1)])
            flush()
            S.done(pid[("d", g, l, j, len(groups) - 1)])

        def load_x(g, after_tt):
            for i in range(8):
                xb = i % 2
                xs_ = xio[:, xb * 1024:(xb + 1) * 1024]
                tl_ = t_or[xb * 4:xb * 4 + 4]
                P.dma("sp", ("xio", xb), xs_, x_d[g, i * 128:(i + 1) * 128, :], writes=tl_)
                for hb in range(2):
                    for kk in range(4):
                        k = hb * 4 + kk
                        P.tr(bank[hb][:, kk * 128:(kk + 1) * 128], xs_[:, k * 128:(k + 1) * 128], ident[:],
                             tl_ + [t_ident], [t_bank[hb]])
                    P.cp("dve" if hb == 0 else "act", xT[:, hb * 4:hb * 4 + 4, i * 128:(i + 1) * 128],
                         bank[hb][:].rearrange("p (k n) -> p k n", k=4), [t_bank[hb]],
                         [t_x[k][i // 4] for k in range(hb * 4, hb * 4 + 4)])
                if i % 4 == 3:
                    after_tt(i // 4)

        def store_y(g):
            for i in range(8):
                xb = i % 2
                xs_ = xio[:, xb * 1024:(xb + 1) * 1024]
                tl_ = t_or[xb * 4:xb * 4 + 4]
                for hb in range(2):
                    for kk in range(4):
                        k = hb * 4 + kk
                        P.tr(bank[hb][:, kk * 128:(kk + 1) * 128], xT[:, k, i * 128:(i + 1) * 128], ident[:],
                             [t_x[k][i // 4], t_ident], [t_bank[hb]])
                    P.cp("dve" if hb == 0 else "act", xs_[:, hb * 512:(hb + 1) * 512], bank[hb][:],
                         [t_bank[hb]], tl_)
                P.dma("sp", ("y", xb), y_d[g, i * 128:(i + 1) * 128, :], xs_, reads=tl_, is_out=True)

        def grp_rms(src, t_src, nh):
            n = nh * 64
            s3 = src.rearrange("p (h d) -> p h d", d=64)
            sqs, t_sqs = sq[0], t_sq[0]
            P.tt("dve", sqs[:, 0:n], src, src, ALU.mult, [t_src], [t_sqs])
            P.red(st8[:, 0:nh], sqs[:, 0:n].rearrange("p (h d) -> p h d", d=64), [t_sqs], [t_st8])
            P.actv(st8[:, 0:nh], st8[:, 0:nh], AF.Ln, [t_st8, t_chalf], [t_st8], scale=1.0 / 64, bias=epsc[:, 0:1])
            P.actv(st8[:, 8:8 + nh], st8[:, 0:nh], AF.Exp, [t_st8], [t_st8], scale=-0.5)
            P.tt("dve", s3, s3, st8[:, 8:8 + nh].rearrange("p (h o) -> p h o", o=1).to_broadcast([128, nh, 64]),
                 ALU.mult, [t_src, t_st8], [t_src])

        def mixer_setup(g, l):
            pwf = stmp[1][:, 0:256].rearrange("p (c n) -> p c n", c=2)
            t_pwf = t_stmp[1]
            RT = ntmp[0][0:31, 0:128]
            RT01 = ntmp[0][0:31, 128:352].rearrange("p (a n) -> p a n", a=2)
            t_RT = t_RT01 = t_ntmp[0]
            P.dma("sp", "ms0", gtb[:], gt_d[l:l + 1, :].to_broadcast([128, 384]), writes=[t_gtb])
            P.ts("dve", gtb[:, 0:64], gtb[:, 0:64], 0.125, None, ALU.mult, None, [t_gtb], [t_gtb])
            P.memset("pool", pwf, 0.0, [t_pwf])
            for c in range(2):
                for gh in range(2):
                    P.dma("sp", "ms1", pwf[gh * 64:(gh + 1) * 64, c, gh * 64:(gh + 1) * 64], poolw_d[l, 2 * c + gh],
                          writes=[t_pwf])
            P.cp("dve", pwb[:], pwf, [t_pwf], [t_pwb])
            for g4 in range(4):
                P.dma("sp", "ms2", stmp[0][:, 0:128], sgw_d[l, g4], writes=[t_stmp[0]])
                P.tr(bank[6][:, 0:128], stmp[0][:, 0:128], ident[:], [t_stmp[0], t_ident], [t_bank[6]])
                P.cp("dve", wsT[:, g4, :], bank[6][:, 0:128], [t_bank[6]], [t_wsT])
            for g4 in range(4):
                P.dma("sp", "ms3", bsb[(g4 % 2) * 64:(g4 % 2 + 1) * 64, g4 // 2, :],
                      sgb_d[l, g4:g4 + 1, :].to_broadcast([64, 128]), writes=[t_bsb])
            if g == 1:
                kctok = aT[1][:, 0:2, :]
                t_kctok = Tl()
                for t in range(2):
                    P.dma("pool", "ms4", kctok[:, t, :].rearrange("p (h d) -> p h d", h=8),
                          ck_d[l][:, t * 128:(t + 1) * 128, :].rearrange("h p d -> p h d"),
                          writes=[t_aT[1][0], t_aT[1][1]])
                    P.dma("pool", "ms5", vc[:, t, :].rearrange("p (h d) -> p h d", h=8),
                          cv_d[l][:, t * 128:(t + 1) * 128, :].rearrange("h p d -> p h d"), writes=[t_vc])
                for t in range(2):
                    for c in range(4):
                        P.tr(bankb[:, c * 128:(c + 1) * 128], kctok[:, t, c * 128:(c + 1) * 128], identb[:],
                             [t_aT[1][0], t_aT[1][1], t_identb], [t_bankb])
                    P.cp("dve", kcT[:, :, t * 128:(t + 1) * 128],
                         bankb[:, 0:512].rearrange("p (c n) -> p c n", c=4), [t_bankb], [t_kc])
                P.dma("sp", "ms6", stmp[1][:, 0:31], rpb_d[l], writes=[t_stmp[1]])
                P.tr(bank[6][0:31, 0:128], stmp[1][:, 0:31], ident[:], [t_stmp[1], t_ident], [t_bank[6]])
                P.cp("dve", RT, bank[6][0:31, 0:128], [t_bank[6]], [t_RT])
                rt3 = RT[:, 0:120].rearrange("p (h r) -> p h r", h=8)
                for half in range(2):
                    P.cp("dve", RT01[:, half, :].rearrange("p (h r) -> p h r", h=8), rt3[:, :, half:half + 14],
                         [t_RT], [t_RT01])
                for b in range(16):
                    bk = bank[b % 2]
                    for qi in range(4):
                        qc = b * 4 + qi
                        for half in range(2):
                            P.mm(bk[half * 64:(half + 1) * 64, qi * 112:(qi + 1) * 112],
                                 dw[0:31, 63 - qc:127 - qc], RT01[:, half, :], True, True,
                                 [t_dw, t_RT01], [t_bank[b % 2]], tp=(0, half * 64))
                    P.actv(EB[:, :, :, b * 4:(b + 1) * 4].rearrange("p h r q -> p q (h r)"),
                           bk[:, 0:448].rearrange("p (q n) -> p q n", q=4), AF.Exp, [t_bank[b % 2]], [t_EB])
                ebv = EB[:].rearrange("p h r q -> p (h r) q")
                P.tt("dve", ebv, ebv, colok[:, :].rearrange("p (o q) -> p o q", o=1).to_broadcast([128, 112, 64]),
                     ALU.mult, [t_EB, t_colok], [t_EB])

        acnt = {"n": 0, "bo": 0}
        SB_ = (0, 1, 4, 5)
        DEPTH = 3

        def mixer(g, l, after_tt):
            v = g
            is_p = (g == 0)
            if stop_after == "premix":
                raise build_program.Stop()
            pA, pB, pC, pD, pE, pF, pG = [pid[(nm, g, l)] for nm in "ABCDEFG"]
            sA, sB, sC, sD, sE = [S.need(p_) for p_ in (pA, pB, pC, pD, pE)]
            vA, vB, vC, vD, vE = [_v3(wslot[s_], 8) for s_ in (sA, sB, sC, sD, sE)]
            h3 = lambda ap: ap.rearrange("p (h d) -> p h d", d=64)
            SETS = (
                dict(qf=stmp[0][:], t_qf=[t_stmp[0]], kf=stmp[1][:], t_kf=[t_stmp[1]],
                     gv=ntmp[1][:, 0:256], t_gv=[t_ntmp[1]], vf=ntmp[0][:], t_vf=[t_ntmp[0]]),
                dict(qf=xio[:, 0:512], t_qf=[t_or[0], t_or[1]], kf=xio[:, 512:1024], t_kf=[t_or[2], t_or[3]],
                     gv=xio[:, 1024:1280], t_gv=[t_or[4]], vf=xio[:, 1536:2048], t_vf=[t_or[6], t_or[7]]),
            )
            sqq, t_sqq = sq[0][:], [t_sq[0]]
            sqk, t_sqk = aT[0][:, 1, :], [t_aT[0][1]]
            sqz, t_sqz = aT[0][:, 2, 0:256], [t_aT[0][2]]
            qb, t_qb = sq[1], t_sq[1]
            kb, t_kb = aT[0][:, 0, :], t_aT[0][0]
            bc = lambda ap, n: ap.rearrange("p (h o) -> p h o", o=1).to_broadcast([128, n, 64])
            gb = lambda ap: ap.rearrange("p (o d) -> p o d", o=1).to_broadcast([128, 8, 64])

            def proj_mm(i):
                tt = i // 4
                tok = slice(i * 128, (i + 1) * 128)
                for bi, (vw, sl) in enumerate(((vA, sA), (vB, sB), (vC, sC), (vD, sD))):
                    for k in range(8):
                        P.mm(bank[bi][:], hT[:, k, tok], vw[:, k, :], k == 0, k == 7,
                             [t_h[k][tt], t_slot[sl]], [t_bank[bi]])

            def proj_evac(i):
                B = SETS[i % 2]
                P.cp("act", B["qf"], bank[1][:], [t_bank[1]], B["t_qf"])
                P.cp("act", B["kf"], bank[2][:], [t_bank[2]], B["t_kf"])
                P.actv(B["gv"], bank[0][:, 256:512], AF.Gelu_apprx_tanh, [t_bank[0]], B["t_gv"])
                P.cp("act", za_tok[:, i, :], bank[0][:, 0:256], [t_bank[0]], [t_za[i]])
                P.cp("act", v_tok[:, i, :], bank[3][:], [t_bank[3]], [t_v[i]])
                if is_p:
                    b_, s0_ = i // 2, (i % 2) * 128
                    P.cp("dve", B["vf"], bank[3][:], [t_bank[3]], B["t_vf"])
                    P.dma("sp", "nv", nv_d[b_, l].rearrange("h s d -> s h d")[s0_:s0_ + 128], h3(B["vf"]),
                          reads=B["t_vf"], is_out=True)

            def proj_chain(i):
                B = SETS[i % 2]
                tok = slice(i * 128, (i + 1) * 128)
                qf, kf, gv = B["qf"], B["kf"], B["gv"]
                t_qf, t_kf, t_gv = B["t_qf"], B["t_kf"], B["t_gv"]
                P.tt("dve", sqq, qf, qf, ALU.mult, t_qf, t_sqq)
                P.tt("dve", sqk, kf, kf, ALU.mult, t_kf, t_sqk)
                P.tt("pool", sqz, gv, gv, ALU.mult, t_gv, t_sqz)
                P.red(st8[:, 0:8], h3(sqq), t_sqq, [t_st8])
                P.red(st8[:, 8:16], h3(sqk), t_sqk, [t_st8])
                P.red(st8[:, 16:20], h3(sqz), t_sqz, [t_st8])
                P.actv(st8[:, 0:20], st8[:, 0:20], AF.Ln, [t_st8, t_chalf], [t_st8], scale=1.0 / 64,
                       bias=epsc[:, 0:1])
                P.actv(st8[:, 20:40], st8[:, 0:20], AF.Exp, [t_st8], [t_st8], scale=-0.5)
                P.tt("dve", h3(qf), h3(qf), bc(st8[:, 20:28], 8), ALU.mult, t_qf + [t_st8], t_qf)
                P.tt("dve", h3(kf), h3(kf), bc(st8[:, 28:36], 8), ALU.mult, t_kf + [t_st8], t_kf)
                P.tt("dve", h3(gv), h3(gv), bc(st8[:, 36:40], 4), ALU.mult, t_gv + [t_st8], t_gv)
                P.tt("pool", h3(qb[:]), h3(qf), gb(gtb[:, 0:64]), ALU.mult, t_qf + [t_gtb], [t_qb])
                P.tt("dve", h3(kf), h3(kf), gb(gtb[:, 64:128]), ALU.mult, t_kf + [t_gtb], t_kf)
                P.tt("pool", vg_tok[:, i, :], gv, gtb[:, 128:384], ALU.mult, t_gv + [t_gtb], [t_vg[i]])
                if is_p:
                    b_, s0_ = i // 2, (i % 2) * 128
                    P.dma("sp", "nk", nk_d[b_, l].rearrange("h s d -> s h d")[s0_:s0_ + 128], h3(kf),
                          reads=t_kf, is_out=True)
                P.cp("act", kb, kf, t_kf, [t_kb])
                for c in range(4):
                    P.tr(bankb[:, c * 128:(c + 1) * 128], qb[:, c * 128:(c + 1) * 128], identb[:],
                         [t_qb, t_identb], [t_bankb])
                for c in range(4):
                    P.tr(bankb[:, 512 + c * 128:512 + (c + 1) * 128], kb[:, c * 128:(c + 1) * 128], identb[:],
                         [t_kb, t_identb], [t_bankb])
                P.cp("dve", qT[:, :, tok], bankb[:, 0:512].rearrange("p (c n) -> p c n", c=4), [t_bankb],
                     [t_q[c][i] for c in range(4)])
                P.cp("dve", kT[:, :, tok], bankb[:, 512:1024].rearrange("p (c n) -> p c n", c=4), [t_bankb],
                     [t_k[c][i] for c in range(4)])

            proj_mm(0)
            proj_evac(0)
            for i in range(8):
                if i + 1 < 8:
                    proj_mm(i + 1)
                proj_chain(i)
                if i + 1 < 8:
                    proj_evac(i + 1)
            gv, t_gv = stmp[1], t_stmp[1]
            _mark(" zu")
            for tt in range(2):
                cols = slice(tt * 512, (tt + 1) * 512)
                for c in range(2):
                    for k in range(8):
                        P.mm(bank[4 + c][:], vE[:, k, c * 128:(c + 1) * 128], hT[:, k, cols], k == 0, k == 7,
                             [t_slot[sE], t_h[k][tt]], [t_bank[4 + c]])
                    P.actv(uT[:, c, cols], bank[4 + c][:], AF.Gelu_apprx_tanh, [t_bank[4 + c]], [t_u[c][tt]])
            for p_ in (pA, pB, pC, pD, pE):
                S.done(p_)
            sF, sG = S.need(pF), S.need(pG)
            if stop_after == "proj":
                raise build_program.Stop()

            def variant(j):
                if is_p:
                    return 0 if j % 2 == 0 else 2
                return 0 if j == 0 else (2 if j == 7 else 1)

            def seq_range(j):
                if is_p:
                    return (j // 2 * 2, j // 2 * 2 + 1)
                return (0, 7)

            sqo, t_sqo = sq[0], t_sq[0]
            rs_ap = (ms[:, 0:256], ms[:, 256:512], ntmp[0][:, 0:256])
            rs_t = (t_ms, t_ms, t_ntmp[0])
            ms2, t_ms2 = ntmp[0][:, 256:512], t_ntmp[0]
            oTn = aT[1][:].rearrange("p f n -> p (f n)").rearrange("p (c n) -> p c n", c=8)
            ETB = ((ms, t_ms), (ntmp[0], t_ntmp[0]), (ntmp[1], t_ntmp[1]))

            def attn_S(it):
                n = it["n"]
                c, hp, h = it["c"], it["hp"], it["h"]
                ps_ = slice(hp * 64, (hp + 1) * 64)
                bk, t_bk = bank[SB_[n % 4]], t_bank[SB_[n % 4]]
                pt, t_pt = aT[0][:, n % 4, :], t_aT[0][n % 4]
                if is_p:
                    u_, tok0_ = it["u"], it["u"] * 256
                    for kt in range(2):
                        P.mm(bk[:, kt * 256:(kt + 1) * 256],
                             kT[ps_, c, tok0_ + kt * 128:tok0_ + (kt + 1) * 128], qT[ps_, c, tok0_:tok0_ + 256],
                             True, True, [t_k[c][2 * u_ + kt], t_q[c][2 * u_], t_q[c][2 * u_ + 1]], [t_bk])
                    P.actv(pt, bk[:], AF.Exp, [t_bk], [t_pt])
                else:
                    r, kts, nl = it["r"], it["kts"], it["nl"]
                    et, t_et = ETB[n % 3]
                    qcols = slice(r * 64, (r + 1) * 64)
                    for j_, kt in enumerate(kts):
                        P.mm(bk[:, j_ * 64:(j_ + 1) * 64], kT[ps_, c, kt * 128:(kt + 1) * 128],
                             qT[ps_, c, qcols], True, True, [t_k[c][kt], t_q[c][r // 2]], [t_bk])
                    for t in range(2):
                        P.mm(bk[:, (nl + t) * 64:(nl + t + 1) * 64], kcT[ps_, c, t * 128:(t + 1) * 128],
                             qT[ps_, c, qcols], True, True, [t_kc, t_q[c][r // 2]], [t_bk])
                    P.actv(et[:, 0:nl * 64], bk[:, 0:nl * 64], AF.Exp, [t_bk], [t_et])
                    P.actv(pt[:, nl * 64:(nl + 2) * 64], bk[:, nl * 64:(nl + 2) * 64], AF.Exp, [t_bk], [t_pt])
                    i0 = 2 * kts[0] - r + 7
                    P.tt("dve", pt[:, 0:nl * 64].rearrange("p (j q) -> p j q", q=64),
                         et[:, 0:nl * 64].rearrange("p (j q) -> p j q", q=64),
                         EB[:, h, i0:i0 + 2 * nl - 1:2, :], ALU.mult, [t_et, t_EB], [t_pt])

            def attn_PV(it):
                n = it["n"]
                c, hp, h = it["c"], it["hp"], it["h"]
                ps_ = slice(hp * 64, (hp + 1) * 64)
                pt, t_pt = aT[0][:, n % 4, :], t_aT[0][n % 4]
                if hp == 0:
                    acnt["bo"] += 1
                bo, t_bo = bank[2 + acnt["bo"] % 2], t_bank[2 + acnt["bo"] % 2]
                if is_p:
                    u_ = it["u"]
                    for kt in range(2):
                        P.mm(bo[ps_, 0:256], v_tok[:, 2 * u_ + kt, h * 64:(h + 1) * 64],
                             pt[:, kt * 256:(kt + 1) * 256], kt == 0, kt == 1,
                             [t_v[2 * u_ + kt], t_pt], [t_bo], tp=(0, hp * 64))
                    for kt in range(2):
                        P.mm(bo[ps_, 256:512], ones[:, 0:64], pt[:, kt * 256:(kt + 1) * 256],
                             kt == 0, kt == 1, [t_ones, t_pt], [t_bo], tp=(0, hp * 64))
                    if hp == 1:
                        P.add("dve", lambda e, bo=bo: e.reciprocal(rden[:], bo[:, 256:512]), [t_bo], [t_rden])
                        P.tt("dve", oraw[:, 2 + c, :], bo[:, 0:256], rden[:], ALU.mult, [t_bo, t_rden],
                             [t_or[2 + c]])
                else:
                    r, kts, nl, s0 = it["r"], it["kts"], it["nl"], it["s0"]
                    mml = []
                    for j_, kt in enumerate(kts):
                        if s0 % 2 == 1 and j_ == 0:
                            p0, p1 = 64, 128
                        elif s0 % 2 == 1 and j_ == nl - 1:
                            p0, p1 = 0, 64
                        else:
                            p0, p1 = 0, 128
                        mml.append((v_tok[p0:p1, kt, h * 64:(h + 1) * 64], ones[p0:p1, 0:64],
                                    pt[p0:p1, j_ * 64:(j_ + 1) * 64], p0, [t_v[kt]]))
                    for t in range(2):
                        mml.append((vc[:, t, h * 64:(h + 1) * 64], ones[:, 0:64],
                                    pt[:, (nl + t) * 64:(nl + t + 1) * 64], 0, [t_vc]))
                    for n_, (lv, lo_, rh, p0, rd) in enumerate(mml):
                        P.mm(bo[ps_, 0:64], lv, rh, n_ == 0, n_ == len(mml) - 1, rd + [t_pt], [t_bo],
                             tp=(p0, hp * 64))
                    for n_, (lv, lo_, rh, p0, rd) in enumerate(mml):
                        P.mm(bo[ps_, 64:128], lo_, rh, n_ == 0, n_ == len(mml) - 1, [t_ones, t_pt],
                             [t_bo], tp=(p0, hp * 64))
                    if hp == 1:
                        P.add("dve", lambda e, bo=bo: e.reciprocal(rden[:, 0:64], bo[:, 64:128]), [t_bo],
                              [t_rden])
                        P.tt("dve", oraw[:, 2 + c, (r % 4) * 64:(r % 4 + 1) * 64], bo[:, 0:64], rden[:, 0:64],
                             ALU.mult, [t_bo, t_rden], [t_or[2 + c]])

            for u in range(4):
                tok0 = u * 256
                ucols = slice(tok0, tok0 + 256)
                tt = u // 2
                _mark(" u%d-attn" % u)
                items = []
                for c in range(4):
                    if is_p:
                        for hp in range(2):
                            items.append({"c": c, "hp": hp, "h": 2 * c + hp, "u": u})
                    else:
                        for r in range(4 * u, 4 * u + 4):
                            s0 = min(max(r - 4, 0), 8)
                            kts = list(range(s0 // 2, (s0 + 7) // 2 + 1))
                            for hp in range(2):
                                items.append({"c": c, "hp": hp, "h": 2 * c + hp, "r": r, "s0": s0, "kts": kts,
                                              "nl": len(kts)})
                for it in items:
                    it["n"] = acnt["n"]
                    acnt["n"] += 1
                def pool_chunk():
                    _mark(" u%d-pool" % u)
                    for c in range(2):
                        for gh in range(2):
                            g4 = 2 * c + gh
                            for jj in range(2):
                                j = 2 * u + jj
                                lo, hi = seq_range(j)
                                ins = [i_ for i_ in (j - 1, j, j + 1) if lo <= i_ <= hi]
                                for n_, i_ in enumerate(ins):
                                    var = 3 if i_ == j - 1 else (4 if i_ == j + 1 else variant(j))
                                    P.mm(bank[4][gh * 64:(gh + 1) * 64, c * 256 + jj * 128:c * 256 + (jj + 1) * 128],
                                         za_tok[:, i_, g4 * 64:(g4 + 1) * 64], bands[:, g4 * 5 + var, :],
                                         n_ == 0, n_ == len(ins) - 1, [t_za[i_], t_bands], [t_bank[4]],
                                         tp=(0, gh * 64))
                        for jj in range(2):
                            P.tt("dve", pooledT[:, c, jj * 128:(jj + 1) * 128],
                                 bank[4][:, c * 256 + jj * 128:c * 256 + (jj + 1) * 128],
                                 invc[:, c * 3 + variant(2 * u + jj), :], ALU.mult, [t_bank[4], t_invc], [t_pl[c]])
                        P.mm(bank[5][:, c * 256:(c + 1) * 256], pwb[:, c, :], pooledT[:, c, :], True, True,
                             [t_pwb, t_pl[c]], [t_bank[5]])
                        P.actv(oraw[:, c, :], bank[5][:, c * 256:(c + 1) * 256], AF.Identity, [t_bank[5], t_vtT],
                               [t_or[c]], scale=vtT[:, l, 104 + c:105 + c])
                    for c in range(2):
                        for gh in range(2):
                            g4 = 2 * c + gh
                            for jj in range(2):
                                P.mm(bank[4][gh * 64:(gh + 1) * 64, c * 256 + jj * 128:c * 256 + (jj + 1) * 128],
                                     vg_tok[:, 2 * u + jj, g4 * 64:(g4 + 1) * 64], wsT[:, g4, :], True, True,
                                     [t_vg[2 * u + jj], t_wsT], [t_bank[4]], tp=(0, gh * 64))
                        P.tt("dve", gv[:, 0:256].rearrange("p (j n) -> p j n", j=2),
                             bank[4][:, c * 256:(c + 1) * 256].rearrange("p (j n) -> p j n", j=2),
                             bsb[:, c:c + 1, :].to_broadcast([128, 2, 128]), ALU.add, [t_bank[4], t_bsb], [t_gv])
                        P.tt("pool", oraw[:, 6 + c, :], gv[:, 0:256], uT[:, c, ucols], ALU.mult,
                             [t_gv, t_u[c][tt]], [t_or[6 + c]])

                def capture(fn):
                    rec = []
                    P.add = lambda eng, f, reads=(), writes=(), dma=None, is_out=False: rec.append(
                        (eng, f, reads, writes, dma, is_out))
                    try:
                        fn()
                    finally:
                        del P.add
                    return rec

                def emit_pair(fn, ita, itb):
                    ra = capture(lambda: fn(ita))
                    rb = capture(lambda: fn(itb))
                    pa = [x for x in ra if x[0] == "pe"]
                    pb = [x for x in rb if x[0] == "pe"]
                    for i_ in range(max(len(pa), len(pb))):
                        if i_ < len(pa):
                            P.add(*pa[i_])
                        if i_ < len(pb):
                            P.add(*pb[i_])
                    for x in ra + rb:
                        if x[0] != "pe":
                            P.add(*x)

                pairs = [(items[2 * j], items[2 * j + 1]) for j in range(len(items) // 2)]
                DEPTHP = 1
                for idx in range(len(pairs) + DEPTHP):
                    if idx == len(pairs) // 2:
                        pool_chunk()
                    if idx < len(pairs):
                        emit_pair(attn_S, *pairs[idx])
                    if idx - DEPTHP >= 0:
                        emit_pair(attn_PV, *pairs[idx - DEPTHP])
                _mark(" u%d-out" % u)
                segs = ((0, 2, 256.0), (2, 6, 512.0), (6, 8, 256.0))
                for si, (o0, o1, wd_) in enumerate(segs):
                    stb, t_stb = (bank[6], t_bank[6]) if si < 2 else (bank[5], t_bank[5])
                    scol = slice((si % 2) * 256, (si % 2) * 256 + 256)
                    for oc in range(o0, o1):
                        sqo_, t_sqo_ = sq[oc % 2], t_sq[oc % 2]
                        P.actv(sqo_[:, 0:256], oraw[:, oc, :], AF.Square, [t_or[oc]], [t_sqo_])
                        P.mm(stb[:, scol], ones[:], sqo_[:, 0:256], oc == o0, oc == o1 - 1, [t_ones, t_sqo_],
                             [t_stb])
                    P.actv(ms2, stb[:, scol], AF.Ln, [t_stb, t_chalf], [t_ms2], scale=1.0 / wd_,
                           bias=epsc[:, 0:1])
                    P.actv(rs_ap[si], ms2, AF.Exp, [t_ms2], [rs_t[si]], scale=-0.5)
                for oc in range(8):
                    si = 0 if oc < 2 else (1 if oc < 6 else 2)
                    P.stt(oTn[:, oc, :], oraw[:, oc, :], vtT[:, l, 96 + oc:97 + oc], rs_ap[si], ALU.mult, ALU.mult,
                          [t_or[oc], t_vtT, rs_t[si]], [t_aT[1][oc // 2]])
                for dc in range(8):
                    sl = sF if dc < 4 else sG
                    vw = _v3(wslot[sl], 8)
                    bw, t_bw = bank[2 + dc % 2], t_bank[2 + dc % 2]
                    for oc in range(8):
                        P.mm(bw[:, 0:256], vw[:, oc, (dc % 4) * 128:(dc % 4 + 1) * 128], oTn[:, oc, :],
                             oc == 0, oc == 7, [t_slot[sl], t_aT[1][oc // 2]], [t_bw])
                    P.stt(xT[:, dc, ucols], bw[:, 0:256], Gm[:, l, 1, dc, v:v + 1], xT[:, dc, ucols],
                          ALU.mult, ALU.add, [t_bw, t_x[dc][tt], t_mods], [t_x[dc][tt]])
                if u % 2 == 1:
                    after_tt(u // 2)
            S.done(pF)
            S.done(pG)

        class _Stop(Exception):
            pass
        build_program.Stop = _Stop
        try:
            for g in range(2):
                _mark("load%d" % g)
                if not EARLY_NORM:
                    nop = lambda tt: None
                    load_x(g, nop)
                    if g == 0:
                        ada_compute(0)
                    for l in range(L):
                        _mark("ffn%d%d0" % (g, l))
                        norm_tt(l, 0, g, 0); norm_tt(l, 0, g, 1)
                        ffn(g, l, 0, nop)
                        if g == 0 and l == 0:
                            ada_compute(1)
                        _mark("mix%d%d" % (g, l))
                        mixer_setup(g, l)
                        norm_tt(l, 1, g, 0); norm_tt(l, 1, g, 1)
                        mixer(g, l, nop)
                        _mark("ffn%d%d1" % (g, l))
                        norm_tt(l, 2, g, 0); norm_tt(l, 2, g, 1)
                        ffn(g, l, 1, nop)
                    _mark("store%d" % g)
                    store_y(g)
                    continue
                if g == 0:
                    load_x(g, lambda tt: None)
                    ada_compute(0)
                    norm_tt(0, 0, 0, 0)
                    norm_tt(0, 0, 0, 1)
                else:
                    load_x(g, lambda tt, g=g: norm_tt(0, 0, g, tt))
                for l in range(L):
                    _mark("ffn%d%d0" % (g, l))
                    mixer_setup(g, l)
                    ffn(g, l, 0, lambda tt, g=g, l=l: norm_tt(l, 1, g, tt))
                    if g == 0 and l == 0:
                        ada_compute(1)
                    _mark("mix%d%d" % (g, l))
                    mixer(g, l, lambda tt, g=g, l=l: norm_tt(l, 2, g, tt))
                    _mark("ffn%d%d1" % (g, l))
                    if l + 1 < L:
                        ffn(g, l, 1, lambda tt, g=g, l=l: norm_tt(l + 1, 0, g, tt))
                    else:
                        ffn(g, l, 1, lambda tt: None)
                _mark("store%d" % g)
                store_y(g)
        except _Stop:
            pass
        _mark("end")
        build_program.phases = PH
        P.emit(st)
    return nc


def _consts():
    ident = np.eye(128, dtype=np.float32)
    bands = np.zeros((128, 20, 128), np.float32)
    invc = np.zeros((128, 6, 128), np.float32)
    for gi, w in enumerate(WINS):
        h = w // 2
        tp = np.arange(128)[:, None]
        t = np.arange(128)[None, :]
        inb = ((tp >= t - h) & (tp < t + h)).astype(np.float32)
        cnt_first = (np.minimum(t + h, 10 ** 6) - np.maximum(t - h, 0)).astype(np.float32)
        cnt_mid = np.full((1, 128), float(w), np.float32)
        cnt_last = ((128 - t) + h - np.maximum(0, 0)).astype(np.float32)
        cnt_last = np.minimum(cnt_last, w).astype(np.float32)
        eye = np.eye(128, dtype=np.float32)
        bands[:, gi * 5 + 0, :] = inb - eye * cnt_first
        bands[:, gi * 5 + 1, :] = inb - eye * cnt_mid
        bands[:, gi * 5 + 2, :] = inb - eye * cnt_last
        bands[:, gi * 5 + 3, :] = ((tp - 128) >= (t - h)).astype(np.float32)
        bands[:, gi * 5 + 4, :] = ((tp + 128) < (t + h)).astype(np.float32)
        c, half = gi // 2, gi % 2
        for vi, cn in enumerate((cnt_first, cnt_mid, cnt_last)):
            invc[half * 64:(half + 1) * 64, c * 3 + vi, :] = 1.0 / cn
    dw = np.zeros((31, 127), np.float32)
    for j in range(31):
        dw[j, j + 48] = 1.0
    cols = np.arange(64)
    cs = np.clip(cols - 8, 0, 48)
    ok = (cols[None, :] >= cs[:, None]) & (cols[None, :] < cs[:, None] + 16)
    colok = np.concatenate([ok.T, ok.T], 0).astype(np.float32)
    return ident, bands, invc, dw, colok


def make_in_maps(inp):
    f = lambda a: np.ascontiguousarray(np.asarray(a, dtype=np.float32))
    ident, bands, invc, dw, colok = _consts()
    vt = np.zeros((L, 128, 128), np.float32)
    gt = np.zeros((L, 384), np.float32)
    rpb = np.zeros((L, 128, 31), np.float32)
    for l in range(L):
        vt[l, 0:72] = f(inp["ada_b"])[l].reshape(72, 128)
        vt[l, 72:96] = f(inp["norm_g"])[l].reshape(24, 128)
        vt[l, 96:104] = f(inp["out_norm_g"])[l].reshape(8, 128)
        vt[l, 104:106] = f(inp["pool_scale"])[l].reshape(2, 128)
        gt[l, 0:64] = f(inp["q_norm_g"])[l]
        gt[l, 64:128] = f(inp["k_norm_g"])[l]
        gt[l, 128:384] = f(inp["sg_vnorm_g"])[l].reshape(256)
        rpb[l, 0:120] = f(inp["na_rpb"])[l].reshape(120, 31)
    shared = {
        "vt": vt, "gt": gt, "rpb": rpb,
        "poolw": f(inp["pool_w"]), "sgw": f(inp["sg_w"]), "sgb": f(inp["sg_b"]),
        "adaw": f(inp["ada_w"]), "wg": f(inp["ffn_w_gate"]), "wu": f(inp["ffn_w_up"]),
        "wd": f(inp["ffn_w_down"]), "win": f(inp["w_in"]), "wout": f(inp["w_out"]),
        "ident": ident, "bands": bands, "invc": invc, "dw": dw, "colok": colok,
    }
    xp = f(inp["x_prompt"]); xs = f(inp["x_sample"])
    ck = f(inp["cache_k"]); cv = f(inp["cache_v"]); c = f(inp["c"]); cc = f(inp["c_ctx"])
    maps = []
    for i in range(NCORES):
        x = np.stack([xp[4 * i:4 * i + 4].reshape(1024, D), xs[i]], 0)
        cvec = np.zeros((128, 128), np.float32)
        cvec[0:8] = cc.reshape(8, 128)
        cvec[8:16] = c[i].reshape(8, 128)
        m = dict(shared)
        m.update({"x": np.ascontiguousarray(x), "ck": np.ascontiguousarray(ck[i]),
                  "cv": np.ascontiguousarray(cv[i]), "cvec": cvec})
        maps.append(m)
    return maps


_NC_CACHE = {}


def kernel(**inputs):
    if "nc" not in _NC_CACHE:
        _NC_CACHE["nc"] = build_program()
    nc = _NC_CACHE["nc"]
    maps = make_in_maps(inputs)
    res = run_bass_kernel_spmd(nc, maps, core_ids=list(range(NCORES)))
    r = res.results
    yp = np.concatenate([r[i]["y"][0].reshape(4, 256, D) for i in range(NCORES)], 0)
    ys = np.stack([r[i]["y"][1] for i in range(NCORES)], 0)
    nk = np.concatenate([r[i]["nk"] for i in range(NCORES)], 0)
    nv = np.concatenate([r[i]["nv"] for i in range(NCORES)], 0)
    return (yp.astype(np.float32), ys.astype(np.float32), nk.astype(np.float32), nv.astype(np.float32))
```

```python
import numpy as np
from contextlib import ExitStack
import concourse.bass as bass
import concourse.mybir as mybir
from concourse.bass_utils import run_bass_kernel_spmd

F32 = mybir.dt.float32
BF16 = mybir.dt.bfloat16
AF = mybir.ActivationFunctionType
ALU = mybir.AluOpType
AX = mybir.AxisListType

D = 1024
DFF = 2816
L = 2
NH = 8
EPS = 1e-6
NCORES = 8
WINS = (2, 4, 8, 16)
EARLY_NORM = True


class Tl:
    __slots__ = ("name", "w", "r", "excl")

    def __init__(self, name="", excl=False):
        self.name = name
        self.w = None
        self.r = {}
        self.excl = excl


class Op:
    __slots__ = ("eng", "fn", "deps", "marked", "count", "dkey", "dcount", "is_dma", "uid")


class Prog:
    ENGS = ("pe", "act", "dve", "pool", "sp")

    def __init__(self, nc):
        self.nc = nc
        self.ops = {e: [] for e in self.ENGS}
        self.latest_dma = {}
        self.dma_counts = {}
        self.uid = 0
        self.out_keys = set()

    def add(self, eng, fn, reads=(), writes=(), dma=None, is_out=False):
        op = Op()
        op.eng = eng
        op.fn = fn
        op.marked = False
        op.count = None
        op.is_dma = dma is not None
        op.dkey = dma
        op.uid = self.uid
        self.uid += 1
        deps = {}
        for t in reads:
            if t.w is not None:
                deps[t.w.uid] = t.w
            if t.excl:
                for d in t.r.values():
                    if d.eng != eng:
                        deps[d.uid] = d
        for t in writes:
            if t.w is not None:
                deps[t.w.uid] = t.w
            for d in t.r.values():
                deps[d.uid] = d
        final = {}
        for d in deps.values():
            if d.is_dma:
                d = self.latest_dma[d.dkey]
            elif d.eng == "pe" and eng == "pe" and not op.is_dma:
                continue
            else:
                d.marked = True
            final[d.uid] = d
        op.deps = list(final.values())
        if op.is_dma:
            c = self.dma_counts.get(dma, 0) + 16
            self.dma_counts[dma] = c
            op.dcount = c
            self.latest_dma[dma] = op
            if is_out:
                self.out_keys.add(dma)
        rkey = ("dma", op.uid) if op.is_dma else eng
        for t in reads:
            t.r[rkey] = op
        for t in writes:
            t.w = op
            t.r = {}
        self.ops[eng].append(op)
        return op

    def dma(self, eng, key, out, in_, reads=(), writes=(), is_out=False):
        return self.add(eng, lambda e: e.dma_start(out=out, in_=in_), reads, writes,
                        dma=key, is_out=is_out)

    def mm(self, out, lhsT, rhs, start, stop, reads, writes, tp=None):
        if tp is None:
            return self.add("pe", lambda e: e.matmul(out, lhsT, rhs, start=start, stop=stop), reads, writes)
        return self.add("pe", lambda e: e.matmul(out, lhsT, rhs, start=start, stop=stop, tile_position=tp),
                        reads, writes)

    def tr(self, out, in_, ident, reads, writes):
        return self.add("pe", lambda e: e.transpose(out, in_, ident), reads, writes)

    def actv(self, out, in_, func, reads, writes, scale=None, bias=None):
        kw = {}
        if scale is not None:
            kw["scale"] = scale
        if bias is not None:
            kw["bias"] = bias
        return self.add("act", lambda e: e.activation(out, in_, func, **kw), reads, writes)

    def tt(self, eng, out, in0, in1, op, reads, writes):
        return self.add(eng, lambda e: e.tensor_tensor(out, in0, in1, op=op), reads, writes)

    def ts(self, eng, out, in0, s1, s2, op0, op1, reads, writes):
        if s2 is None:
            return self.add(eng, lambda e: e.tensor_scalar(out, in0, s1, None, op0=op0), reads, writes)
        return self.add(eng, lambda e: e.tensor_scalar(out, in0, s1, s2, op0=op0, op1=op1), reads, writes)

    def stt(self, out, in0, scalar, in1, op0, op1, reads, writes):
        return self.add("dve", lambda e: e.scalar_tensor_tensor(out, in0, scalar, in1, op0=op0, op1=op1),
                        reads, writes)

    def cp(self, eng, out, in_, reads, writes):
        if eng == "act":
            return self.add("act", lambda e: e.copy(out, in_), reads, writes)
        return self.add(eng, lambda e: e.tensor_copy(out, in_), reads, writes)

    def red(self, out, in_, reads, writes):
        return self.add("dve", lambda e: e.tensor_reduce(out, in_, axis=AX.X, op=ALU.add), reads, writes)

    def memset(self, eng, ap, val, writes):
        return self.add(eng, lambda e: e.memset(ap, val), (), writes)

    def emit(self, stack):
        nc = self.nc
        for e in self.ENGS:
            c = 0
            for op in self.ops[e]:
                if op.marked and not op.is_dma:
                    c += 1
                    op.count = c
        esem = {e: stack.enter_context(nc.semaphore("s_" + e)) for e in self.ENGS}
        dsem = {k: stack.enter_context(nc.semaphore("d_%d" % i)) for i, k in enumerate(self.dma_counts)}
        block = stack.enter_context(nc.Block())
        ops = self.ops
        out_final = [(dsem[k], self.dma_counts[k]) for k in sorted(self.out_keys, key=str)]

        def run(ename, e):
            known = {}
            for op in ops[ename]:
                for d in op.deps:
                    if d.is_dma:
                        s, v = dsem[d.dkey], d.dcount
                    else:
                        s, v = esem[d.eng], d.count
                    kid = id(s)
                    if known.get(kid, 0) >= v:
                        continue
                    known[kid] = v
                    e.wait_ge(s, v)
                ins = op.fn(e)
                if op.is_dma:
                    ins.then_inc(dsem[op.dkey], 16)
                elif op.marked:
                    ins.then_inc(esem[ename], 1)
            if ename == "sp":
                for s, v in out_final:
                    e.wait_ge(s, v)

        @block.sync
        def _(e):
            run("sp", e)

        @block.tensor
        def _(e):
            run("pe", e)

        @block.scalar
        def _(e):
            run("act", e)

        @block.vector
        def _(e):
            run("dve", e)

        @block.gpsimd
        def _(e):
            run("pool", e)


class Stream:
    NS = 6
    LOOK = 5

    def __init__(self, P, slots, t_slots):
        self.P = P
        self.slots = slots
        self.t_slots = t_slots
        self.pieces = []
        self.issued = 0
        self.freed = 0
        self.want = 0

    def plan(self, parts):
        self.pieces.append(parts)
        return len(self.pieces) - 1

    def pump(self):
        while self.issued <= self.want and self.issued < len(self.pieces) and self.issued < self.freed + self.NS:
            i = self.issued
            s = i % self.NS
            for dst_fn, src in self.pieces[i]:
                self.P.dma("pool", ("w", s), dst_fn(self.slots[s]), src, writes=[self.t_slots[s]])
            self.issued += 1

    def need(self, p):
        self.want = max(self.want, p + self.LOOK)
        self.pump()
        assert self.issued > p, (self.issued, p, self.freed)
        return p % self.NS

    def done(self, p):
        self.freed = max(self.freed, p + 1)
        self.pump()


def _v3(t, k):
    return t[:].rearrange("p (k n) -> p k n", k=k)


def build_program(stop_after=None):
    nc = bass.Bass("TRN2", target_bir_lowering=False)

    def din(name, shape):
        return nc.dram_tensor(name, shape, F32, kind="ExternalInput").ap()

    def dout(name, shape):
        return nc.dram_tensor(name, shape, F32, kind="ExternalOutput").ap()

    x_d = din("x", [2, 1024, D])
    ck_d = din("ck", [L, NH, 256, 64])
    cv_d = din("cv", [L, NH, 256, 64])
    cvec_d = din("cvec", [128, 128])
    vt_d = din("vt", [L, 128, 128])
    gt_d = din("gt", [L, 384])
    poolw_d = din("poolw", [L, 4, 64, 64])
    sgw_d = din("sgw", [L, 4, 128, 128])
    sgb_d = din("sgb", [L, 4, 128])
    rpb_d = din("rpb", [L, 128, 31])
    adaw_d = din("adaw", [L, D, 9 * D])
    wg_d = din("wg", [L, 2, D, DFF])
    wu_d = din("wu", [L, 2, D, DFF])
    wd_d = din("wd", [L, 2, DFF, D])
    win_d = din("win", [L, D, 2304])
    wout_d = din("wout", [L, D, D])
    ident_d = din("ident", [128, 128])
    bands_d = din("bands", [128, 20, 128])
    invc_d = din("invc", [128, 6, 128])
    dw_d = din("dw", [31, 127])
    colok_d = din("colok", [128, 64])
    y_d = dout("y", [2, 1024, D])
    nk_d = dout("nk", [4, L, NH, 256, 64])
    nv_d = dout("nv", [4, L, NH, 256, 64])

    P = Prog(nc)
    st = ExitStack()

    def sb(name, shape, dt=F32):
        return st.enter_context(nc.sbuf_tensor("s_" + name, shape, dt))

    with st:
        bank = [st.enter_context(nc.psum_tensor("bank%d" % i, [128, 512], F32)) for i in range(7)]
        bankb = st.enter_context(nc.psum_tensor("bankb", [128, 1024], BF16))
        t_bank = [Tl("bank%d" % i, excl=True) for i in range(7)]
        t_bankb = Tl("bankb", excl=True)

        ident = sb("ident", [128, 128]); t_ident = Tl()
        identb = sb("identb", [128, 128], BF16); t_identb = Tl()
        ones = sb("ones", [128, 128], BF16); t_ones = Tl()
        epsc = sb("epsc", [128, 1]); t_chalf = Tl()
        nsh = sb("nsh", [128, 1])
        bands = sb("bands", [128, 20, 128], BF16); t_bands = Tl()
        invc = sb("invc", [128, 6, 128]); t_invc = Tl()
        dw = sb("dw", [31, 127]); t_dw = Tl()
        colok = sb("colok", [128, 64]); t_colok = Tl()
        P.dma("sp", "c0", ident[:], ident_d, writes=[t_ident])
        P.dma("sp", "c1", invc[:], invc_d, writes=[t_invc])
        P.dma("sp", "c2", dw[:], dw_d, writes=[t_dw])
        P.dma("sp", "c3", colok[:], colok_d, writes=[t_colok])
        P.dma("pool", "c4", bands[:], bands_d, writes=[t_bands])
        P.cp("dve", identb[:], ident[:], [t_ident], [t_identb])
        P.memset("pool", ones[:], 1.0, [t_ones])
        P.memset("pool", epsc[:], EPS, [t_chalf])
        P.memset("pool", nsh[:], -8.0, [t_chalf])

        xT = sb("xT", [128, 8, 1024]); t_x = [[Tl() for _ in range(2)] for _ in range(8)]
        hT = sb("hT", [128, 8, 1024], BF16); t_h = [[Tl() for _ in range(2)] for _ in range(8)]
        wslot = [sb("wslot%d" % i, [128, 4096], BF16) for i in range(Stream.NS)]
        t_slot = [Tl("slot%d" % i) for i in range(Stream.NS)]
        S = Stream(P, wslot, t_slot)
        aT = [sb("aT%d" % i, [128, 4, 512], BF16) for i in range(2)]
        t_aT = [[Tl() for _ in range(4)] for _ in range(2)]
        stmp = [sb("stmp%d" % i, [128, 512]) for i in range(2)]; t_stmp = [Tl(), Tl()]
        xio = sb("xio", [128, 2048]); t_or = [Tl() for _ in range(8)]
        oraw = xio[:].rearrange("p (c n) -> p c n", c=8)
        sq = [sb("sq%d" % i, [128, 512], BF16) for i in range(2)]; t_sq = [Tl(), Tl()]
        ms = sb("ms", [128, 512]); t_ms = Tl()
        rstd, t_rstd = ms, t_ms
        ntmp = [sb("ntmp%d" % i, [128, 512]) for i in range(2)]; t_ntmp = [Tl(), Tl()]
        mods = sb("mods", [128, L, 72, 2]); t_mods = Tl()
        Am = sb("Am", [128, L, 3, 8, 2])
        Gm = sb("Gm", [128, L, 3, 8, 2])
        vtT = sb("vtT", [128, L, 128]); t_vtT = Tl()
        scT = sb("scT", [128, 16], BF16); t_scT = Tl()
        vtr_t = sb("vtr", [128, 128]); t_vtr = Tl()
        vtr = vtr_t[:]
        za_tok = sb("za_tok", [128, 8, 256], BF16); t_za = [Tl() for _ in range(8)]
        vg_tok = sb("vg_tok", [128, 8, 256], BF16); t_vg = [Tl() for _ in range(8)]
        v_tok = sb("v_tok", [128, 8, 512], BF16); t_v = [Tl() for _ in range(8)]
        qT = sb("qT", [128, 4, 1024], BF16); t_q = [[Tl() for _ in range(8)] for _ in range(4)]
        kT = sb("kT", [128, 4, 1024], BF16); t_k = [[Tl() for _ in range(8)] for _ in range(4)]
        uT = sb("uT", [128, 2, 1024], BF16); t_u = [[Tl() for _ in range(2)] for _ in range(2)]
        kcT = sb("kcT", [128, 4, 256], BF16); t_kc = Tl()
        qkz = sb("qkz", [128, 1280]); t_qkz = Tl()
        vc = sb("vc", [128, 2, 512], BF16); t_vc = Tl()
        pooledT = sb("pooledT", [128, 2, 256], BF16); t_pl = [Tl(), Tl()]
        EB = sb("EB", [128, 8, 14, 64], BF16); t_EB = Tl()
        wsT = sb("wsT", [128, 4, 128], BF16); t_wsT = Tl()
        pwb = sb("pwb", [128, 2, 128], BF16); t_pwb = Tl()
        bsb = sb("bsb", [128, 2, 128]); t_bsb = Tl()
        gtb = sb("gtb", [128, 384]); t_gtb = Tl()
        rden = sb("rden", [128, 256]); t_rden = Tl()
        st8 = sb("st8", [128, 40]); t_st8 = Tl()

        pid = {}
        groups = [(0, 4), (4, 4), (8, 2), (10, 4), (14, 4), (18, 4)]

        def plan_ada(l):
            for pc in range(18):
                pid[("ada", l, pc)] = S.plan([(
                    lambda t: _v3(t, 8),
                    adaw_d[l].rearrange("(k p) n -> p k n", p=128)[:, :, pc * 512:(pc + 1) * 512])])

        def plan_ffn(g, l, j):
            for gi, (c0, nfc) in enumerate(groups):
                w = nfc * 128
                pid[("g", g, l, j, gi)] = S.plan([(
                    lambda t, w=w: _v3(t, 8)[:, :, 0:w],
                    wg_d[l, j].rearrange("(k p) n -> p k n", p=128)[:, :, c0 * 128:c0 * 128 + w])])
                pid[("u", g, l, j, gi)] = S.plan([(
                    lambda t, w=w: _v3(t, 8)[:, :, 0:w],
                    wu_d[l, j].rearrange("(k p) n -> p k n", p=128)[:, :, c0 * 128:c0 * 128 + w])])
                pid[("d", g, l, j, gi)] = S.plan([(
                    lambda t, nfc=nfc: _v3(t, 4)[:, 0:nfc, :],
                    wd_d[l, j][c0 * 128:c0 * 128 + w, :].rearrange("(f p) n -> p f n", p=128))])

        def plan_mixer(g, l):
            wv = win_d[l].rearrange("(k p) n -> p k n", p=128)
            pid[("A", g, l)] = S.plan([
                (lambda t: _v3(t, 8)[:, :, 0:256], wv[:, :, 0:256]),
                (lambda t: _v3(t, 8)[:, :, 256:512], wv[:, :, 2048:2304])])
            pid[("B", g, l)] = S.plan([(lambda t: _v3(t, 8), wv[:, :, 256:768])])
            pid[("C", g, l)] = S.plan([(lambda t: _v3(t, 8), wv[:, :, 768:1280])])
            pid[("D", g, l)] = S.plan([(lambda t: _v3(t, 8), wv[:, :, 1280:1792])])
            pid[("E", g, l)] = S.plan([(lambda t: _v3(t, 8)[:, :, 0:256], wv[:, :, 1792:2048])])
            wo = wout_d[l].rearrange("(k p) n -> p k n", p=128)
            pid[("F", g, l)] = S.plan([(lambda t: _v3(t, 8), wo[:, :, 0:512])])
            pid[("G", g, l)] = S.plan([(lambda t: _v3(t, 8), wo[:, :, 512:1024])])

        plan_ada(0)
        for g in range(2):
            for l in range(L):
                plan_ffn(g, l, 0)
                if g == 0 and l == 0:
                    plan_ada(1)
                plan_mixer(g, l)
                plan_ffn(g, l, 1)

        P.dma("sp", "v0", vtr, cvec_d, writes=[t_vtr])
        P.tr(bank[6][:, 0:128], vtr, ident[:], [t_vtr, t_ident], [t_bank[6]])
        P.actv(scT[:], bank[6][:, 0:16], AF.Silu, [t_bank[6]], [t_scT])
        for l in range(L):
            P.dma("sp", "v0", vtr, vt_d[l], writes=[t_vtr])
            P.tr(bank[6][:, 0:128], vtr, ident[:], [t_vtr, t_ident], [t_bank[6]])
            P.cp("dve", vtT[:, l, :], bank[6][:, 0:128], [t_bank[6]], [t_vtT])
        def ada_compute(l):
            for pc in range(18):
                p_ = pid[("ada", l, pc)]
                s = S.need(p_)
                wv = _v3(wslot[s], 8)
                for fc in range(4):
                    j = pc * 4 + fc
                    for k in range(8):
                        P.mm(bank[5][:, 2 * j:2 * j + 2], wv[:, k, fc * 128:(fc + 1) * 128],
                             scT[:, k:16:8], k == 0, k == 7, [t_slot[s], t_scT], [t_bank[5]])
                S.done(p_)
            P.tt("dve", mods[:, l, :, :], bank[5][:, 0:144].rearrange("p (j v) -> p j v", v=2),
                 vtT[:, l, 0:72].rearrange("p (j o) -> p j o", o=1).to_broadcast([128, 72, 2]),
                 ALU.add, [t_bank[5], t_vtT], [t_mods])
            for n in range(3):
                P.ts("dve", Am[:, l, n, :, :], mods[:, l, (3 * n + 1) * 8:(3 * n + 2) * 8, :], 1.0, None,
                     ALU.add, None, [t_mods], [t_mods])
                P.tt("dve", Am[:, l, n, :, :], Am[:, l, n, :, :],
                     vtT[:, l, 72 + n * 8:80 + n * 8].rearrange("p (j o) -> p j o", o=1).to_broadcast([128, 8, 2]),
                     ALU.mult, [t_mods, t_vtT], [t_mods])
                P.ts("dve", Gm[:, l, n, :, :], mods[:, l, (3 * n + 2) * 8:(3 * n + 3) * 8, :],
                     (1.0 if n == 1 else 0.5), None, ALU.mult, None, [t_mods], [t_mods])


        ADA0_DEFERRED = True

        PH = []
        _mark = lambda nm: PH.append((nm, len(P.ops["pe"])))
        def norm_tt(l, n, v, tt):
            cols = slice(tt * 512, (tt + 1) * 512)
            for k in range(8):
                P.actv(sq[k % 2][:], xT[:, k, cols], AF.Square, [t_x[k][tt]], [t_sq[k % 2]])
                P.mm(bank[6][:], ones[:], sq[k % 2][:], k == 0, k == 7, [t_sq[k % 2], t_ones], [t_bank[6]])
            P.actv(ms[:], bank[6][:], AF.Ln, [t_bank[6], t_chalf], [t_ms], scale=1.0 / D, bias=epsc[:, 0:1])
            P.actv(rstd[:], ms[:], AF.Exp, [t_ms], [t_rstd], scale=-0.5)
            for k in range(8):
                P.tt("dve", ntmp[k % 2][:], xT[:, k, cols], rstd[:], ALU.mult,
                     [t_x[k][tt], t_rstd], [t_ntmp[k % 2]])
                P.actv(hT[:, k, cols], ntmp[k % 2][:], AF.Identity, [t_ntmp[k % 2], t_mods], [t_h[k][tt]],
                       scale=Am[:, l, n, k, v:v + 1], bias=mods[:, l, 3 * n * 8 + k, v:v + 1])

        cnt = {"up": 0, "dn": 0, "ab": 0}

        def ffn(g, l, j, after_tt):
            n = 0 if j == 0 else 2
            v = g
            pending = []
            last_gi = len(groups) - 1

            def flush():
                while pending:
                    pending.pop(0)()

            for gi, (c0, nfc) in enumerate(groups):
                pg_, pu_, pd_ = pid[("g", g, l, j, gi)], pid[("u", g, l, j, gi)], pid[("d", g, l, j, gi)]
                sg, su, sd = S.need(pg_), S.need(pu_), S.need(pd_)
                vg_, vu_, vd_ = _v3(wslot[sg], 8), _v3(wslot[su], 8), _v3(wslot[sd], 4)
                for tt in range(2):
                    cols = slice(tt * 512, (tt + 1) * 512)
                    ab = cnt["ab"] % 2
                    cnt["ab"] += 1
                    for f in range(nfc):
                        ib = cnt["up"] % 2
                        cnt["up"] += 1
                        for k in range(8):
                            P.mm(bank[ib][:], vg_[:, k, f * 128:(f + 1) * 128], hT[:, k, cols], k == 0, k == 7,
                                 [t_slot[sg], t_h[k][tt]], [t_bank[ib]])
                        for k in range(8):
                            P.mm(bank[2 + ib][:], vu_[:, k, f * 128:(f + 1) * 128], hT[:, k, cols], k == 0, k == 7,
                                 [t_slot[su], t_h[k][tt]], [t_bank[2 + ib]])
                        P.actv(stmp[ib][:], bank[ib][:], AF.Silu, [t_bank[ib]], [t_stmp[ib]])
                        P.tt("dve", aT[ab][:, f, :], stmp[ib][:], bank[2 + ib][:], ALU.mult,
                             [t_stmp[ib], t_bank[2 + ib]], [t_aT[ab][f]])

                    def down(ab=ab, nfc=nfc, sd=sd, vd_=vd_, cols=cols, tt=tt, gi=gi):
                        for dc in range(8):
                            ib2 = 4 + cnt["dn"] % 2
                            cnt["dn"] += 1
                            for f in range(nfc):
                                P.mm(bank[ib2][:], vd_[:, f, dc * 128:(dc + 1) * 128], aT[ab][:, f, :],
                                     f == 0, f == nfc - 1, [t_slot[sd], t_aT[ab][f]], [t_bank[ib2]])
                            P.stt(xT[:, dc, cols], bank[ib2][:], Gm[:, l, n, dc, v:v + 1], xT[:, dc, cols],
                                  ALU.mult, ALU.add, [t_bank[ib2], t_x[dc][tt], t_mods], [t_x[dc][tt]])
                        if gi == last_gi:
                            after_tt(tt)

                    flush()
                    if gi == last_gi:
                        down()
                    else:
                        pending.append(down)
                S.done(pg_)
                S.done(pu_)
                if gi > 0:
                    S.done(pid[("d", g, l, j, gi - 1)])
            flush()
            S.done(pid[("d", g, l, j, len(groups) - 1)])

        def load_x(g, after_tt):
            for i in range(8):
                xb = i % 2
                xs_ = xio[:, xb * 1024:(xb + 1) * 1024]
                tl_ = t_or[xb * 4:xb * 4 + 4]
                P.dma("sp", ("xio", xb), xs_, x_d[g, i * 128:(i + 1) * 128, :], writes=tl_)
                for hb in range(2):
                    for kk in range(4):
                        k = hb * 4 + kk
                        P.tr(bank[hb][:, kk * 128:(kk + 1) * 128], xs_[:, k * 128:(k + 1) * 128], ident[:],
                             tl_ + [t_ident], [t_bank[hb]])
                    P.cp("dve" if hb == 0 else "act", xT[:, hb * 4:hb * 4 + 4, i * 128:(i + 1) * 128],
                         bank[hb][:].rearrange("p (k n) -> p k n", k=4), [t_bank[hb]],
                         [t_x[k][i // 4] for k in range(hb * 4, hb * 4 + 4)])
                if i % 4 == 3:
                    after_tt(i // 4)

        def store_y(g):
            for i in range(8):
                xb = i % 2
                xs_ = xio[:, xb * 1024:(xb + 1) * 1024]
                tl_ = t_or[xb * 4:xb * 4 + 4]
                for hb in range(2):
                    for kk in range(4):
                        k = hb * 4 + kk
                        P.tr(bank[hb][:, kk * 128:(kk + 1) * 128], xT[:, k, i * 128:(i + 1) * 128], ident[:],
                             [t_x[k][i // 4], t_ident], [t_bank[hb]])
                    P.cp("dve" if hb == 0 else "act", xs_[:, hb * 512:(hb + 1) * 512], bank[hb][:],
                         [t_bank[hb]], tl_)
                P.dma("sp", ("y", xb), y_d[g, i * 128:(i + 1) * 128, :], xs_, reads=tl_, is_out=True)

        def grp_rms(src, t_src, nh):
            n = nh * 64
            s3 = src.rearrange("p (h d) -> p h d", d=64)
            sqs, t_sqs = sq[0], t_sq[0]
            P.tt("dve", sqs[:, 0:n], src, src, ALU.mult, [t_src], [t_sqs])
            P.red(st8[:, 0:nh], sqs[:, 0:n].rearrange("p (h d) -> p h d", d=64), [t_sqs], [t_st8])
            P.actv(st8[:, 0:nh], st8[:, 0:nh], AF.Ln, [t_st8, t_chalf], [t_st8], scale=1.0 / 64, bias=epsc[:, 0:1])
            P.actv(st8[:, 8:8 + nh], st8[:, 0:nh], AF.Exp, [t_st8], [t_st8], scale=-0.5)
            P.tt("dve", s3, s3, st8[:, 8:8 + nh].rearrange("p (h o) -> p h o", o=1).to_broadcast([128, nh, 64]),
                 ALU.mult, [t_src, t_st8], [t_src])

        def mixer_setup(g, l):
            pwf = stmp[1][:, 0:256].rearrange("p (c n) -> p c n", c=2)
            t_pwf = t_stmp[1]
            RT = ntmp[0][0:31, 0:128]
            RT01 = ntmp[0][0:31, 128:352].rearrange("p (a n) -> p a n", a=2)
            t_RT = t_RT01 = t_ntmp[0]
            P.dma("sp", "ms0", gtb[:], gt_d[l:l + 1, :].to_broadcast([128, 384]), writes=[t_gtb])
            P.ts("dve", gtb[:, 0:64], gtb[:, 0:64], 0.125, None, ALU.mult, None, [t_gtb], [t_gtb])
            P.memset("pool", pwf, 0.0, [t_pwf])
            for c in range(2):
                for gh in range(2):
                    P.dma("sp", "ms1", pwf[gh * 64:(gh + 1) * 64, c, gh * 64:(gh + 1) * 64], poolw_d[l, 2 * c + gh],
                          writes=[t_pwf])
            P.cp("dve", pwb[:], pwf, [t_pwf], [t_pwb])
            for g4 in range(4):
                P.dma("sp", "ms2", stmp[0][:, 0:128], sgw_d[l, g4], writes=[t_stmp[0]])
                P.tr(bank[6][:, 0:128], stmp[0][:, 0:128], ident[:], [t_stmp[0], t_ident], [t_bank[6]])
                P.cp("dve", wsT[:, g4, :], bank[6][:, 0:128], [t_bank[6]], [t_wsT])
            for g4 in range(4):
                P.dma("sp", "ms3", bsb[(g4 % 2) * 64:(g4 % 2 + 1) * 64, g4 // 2, :],
                      sgb_d[l, g4:g4 + 1, :].to_broadcast([64, 128]), writes=[t_bsb])
            if g == 1:
                kctok = aT[1][:, 0:2, :]
                t_kctok = Tl()
                for t in range(2):
                    P.dma("pool", "ms4", kctok[:, t, :].rearrange("p (h d) -> p h d", h=8),
                          ck_d[l][:, t * 128:(t + 1) * 128, :].rearrange("h p d -> p h d"),
                          writes=[t_aT[1][0], t_aT[1][1]])
                    P.dma("pool", "ms5", vc[:, t, :].rearrange("p (h d) -> p h d", h=8),
                          cv_d[l][:, t * 128:(t + 1) * 128, :].rearrange("h p d -> p h d"), writes=[t_vc])
                for t in range(2):
                    for c in range(4):
                        P.tr(bankb[:, c * 128:(c + 1) * 128], kctok[:, t, c * 128:(c + 1) * 128], identb[:],
                             [t_aT[1][0], t_aT[1][1], t_identb], [t_bankb])
                    P.cp("dve", kcT[:, :, t * 128:(t + 1) * 128],
                         bankb[:, 0:512].rearrange("p (c n) -> p c n", c=4), [t_bankb], [t_kc])
                P.dma("sp", "ms6", stmp[1][:, 0:31], rpb_d[l], writes=[t_stmp[1]])
                P.tr(bank[6][0:31, 0:128], stmp[1][:, 0:31], ident[:], [t_stmp[1], t_ident], [t_bank[6]])
                P.cp("dve", RT, bank[6][0:31, 0:128], [t_bank[6]], [t_RT])
                rt3 = RT[:, 0:120].rearrange("p (h r) -> p h r", h=8)
                for half in range(2):
                    P.cp("dve", RT01[:, half, :].rearrange("p (h r) -> p h r", h=8), rt3[:, :, half:half + 14],
                         [t_RT], [t_RT01])
                for b in range(16):
                    bk = bank[b % 2]
                    for qi in range(4):
                        qc = b * 4 + qi
                        for half in range(2):
                            P.mm(bk[half * 64:(half + 1) * 64, qi * 112:(qi + 1) * 112],
                                 dw[0:31, 63 - qc:127 - qc], RT01[:, half, :], True, True,
                                 [t_dw, t_RT01], [t_bank[b % 2]], tp=(0, half * 64))
                    P.actv(EB[:, :, :, b * 4:(b + 1) * 4].rearrange("p h r q -> p q (h r)"),
                           bk[:, 0:448].rearrange("p (q n) -> p q n", q=4), AF.Exp, [t_bank[b % 2]], [t_EB])
                ebv = EB[:].rearrange("p h r q -> p (h r) q")
                P.tt("dve", ebv, ebv, colok[:, :].rearrange("p (o q) -> p o q", o=1).to_broadcast([128, 112, 64]),
                     ALU.mult, [t_EB, t_colok], [t_EB])

        acnt = {"n": 0, "bo": 0}
        SB_ = (0, 1, 4, 5)
        DEPTH = 3

        def mixer(g, l, after_tt):
            v = g
            is_p = (g == 0)
            if stop_after == "premix":
                raise build_program.Stop()
            pA, pB, pC, pD, pE, pF, pG = [pid[(nm, g, l)] for nm in "ABCDEFG"]
            sA, sB, sC, sD, sE = [S.need(p_) for p_ in (pA, pB, pC, pD, pE)]
            vA, vB, vC, vD, vE = [_v3(wslot[s_], 8) for s_ in (sA, sB, sC, sD, sE)]
            h3 = lambda ap: ap.rearrange("p (h d) -> p h d", d=64)
            SETS = (
                dict(qf=stmp[0][:], t_qf=[t_stmp[0]], kf=stmp[1][:], t_kf=[t_stmp[1]],
                     gv=ntmp[1][:, 0:256], t_gv=[t_ntmp[1]], vf=ntmp[0][:], t_vf=[t_ntmp[0]]),
                dict(qf=xio[:, 0:512], t_qf=[t_or[0], t_or[1]], kf=xio[:, 512:1024], t_kf=[t_or[2], t_or[3]],
                     gv=xio[:, 1024:1280], t_gv=[t_or[4]], vf=xio[:, 1536:2048], t_vf=[t_or[6], t_or[7]]),
            )
            sqq, t_sqq = sq[0][:], [t_sq[0]]
            sqk, t_sqk = aT[0][:, 1, :], [t_aT[0][1]]
            sqz, t_sqz = aT[0][:, 2, 0:256], [t_aT[0][2]]
            qb, t_qb = sq[1], t_sq[1]
            kb, t_kb = aT[0][:, 0, :], t_aT[0][0]
            bc = lambda ap, n: ap.rearrange("p (h o) -> p h o", o=1).to_broadcast([128, n, 64])
            gb = lambda ap: ap.rearrange("p (o d) -> p o d", o=1).to_broadcast([128, 8, 64])

            def proj_mm(i):
                tt = i // 4
                tok = slice(i * 128, (i + 1) * 128)
                for bi, (vw, sl) in enumerate(((vA, sA), (vB, sB), (vC, sC), (vD, sD))):
                    for k in range(8):
                        P.mm(bank[bi][:], hT[:, k, tok], vw[:, k, :], k == 0, k == 7,
                             [t_h[k][tt], t_slot[sl]], [t_bank[bi]])

            def proj_evac(i):
                B = SETS[i % 2]
                P.cp("act", B["qf"], bank[1][:], [t_bank[1]], B["t_qf"])
                P.cp("act", B["kf"], bank[2][:], [t_bank[2]], B["t_kf"])
                P.actv(B["gv"], bank[0][:, 256:512], AF.Gelu_apprx_tanh, [t_bank[0]], B["t_gv"])
                P.cp("act", za_tok[:, i, :], bank[0][:, 0:256], [t_bank[0]], [t_za[i]])
                P.cp("act", v_tok[:, i, :], bank[3][:], [t_bank[3]], [t_v[i]])
                if is_p:
                    b_, s0_ = i // 2, (i % 2) * 128
                    P.cp("dve", B["vf"], bank[3][:], [t_bank[3]], B["t_vf"])
                    P.dma("sp", "nv", nv_d[b_, l].rearrange("h s d -> s h d")[s0_:s0_ + 128], h3(B["vf"]),
                          reads=B["t_vf"], is_out=True)

            def proj_chain(i):
                B = SETS[i % 2]
                tok = slice(i * 128, (i + 1) * 128)
                qf, kf, gv = B["qf"], B["kf"], B["gv"]
                t_qf, t_kf, t_gv = B["t_qf"], B["t_kf"], B["t_gv"]
                P.tt("dve", sqq, qf, qf, ALU.mult, t_qf, t_sqq)
                P.tt("dve", sqk, kf, kf, ALU.mult, t_kf, t_sqk)
                P.tt("pool", sqz, gv, gv, ALU.mult, t_gv, t_sqz)
                P.red(st8[:, 0:8], h3(sqq), t_sqq, [t_st8])
                P.red(st8[:, 8:16], h3(sqk), t_sqk, [t_st8])
                P.red(st8[:, 16:20], h3(sqz), t_sqz, [t_st8])
                P.actv(st8[:, 0:20], st8[:, 0:20], AF.Ln, [t_st8, t_chalf], [t_st8], scale=1.0 / 64,
                       bias=epsc[:, 0:1])
                P.actv(st8[:, 20:40], st8[:, 0:20], AF.Exp, [t_st8], [t_st8], scale=-0.5)
                P.tt("dve", h3(qf), h3(qf), bc(st8[:, 20:28], 8), ALU.mult, t_qf + [t_st8], t_qf)
                P.tt("dve", h3(kf), h3(kf), bc(st8[:, 28:36], 8), ALU.mult, t_kf + [t_st8], t_kf)
                P.tt("dve", h3(gv), h3(gv), bc(st8[:, 36:40], 4), ALU.mult, t_gv + [t_st8], t_gv)
                P.tt("pool", h3(qb[:]), h3(qf), gb(gtb[:, 0:64]), ALU.mult, t_qf + [t_gtb], [t_qb])
                P.tt("dve", h3(kf), h3(kf), gb(gtb[:, 64:128]), ALU.mult, t_kf + [t_gtb], t_kf)
                P.tt("pool", vg_tok[:, i, :], gv, gtb[:, 128:384], ALU.mult, t_gv + [t_gtb], [t_vg[i]])
                if is_p:
                    b_, s0_ = i // 2, (i % 2) * 128
                    P.dma("sp", "nk", nk_d[b_, l].rearrange("h s d -> s h d")[s0_:s0_ + 128], h3(kf),
                          reads=t_kf, is_out=True)
                P.cp("act", kb, kf, t_kf, [t_kb])
                for c in range(4):
                    P.tr(bankb[:, c * 128:(c + 1) * 128], qb[:, c * 128:(c + 1) * 128], identb[:],
                         [t_qb, t_identb], [t_bankb])
                for c in range(4):
                    P.tr(bankb[:, 512 + c * 128:512 + (c + 1) * 128], kb[:, c * 128:(c + 1) * 128], identb[:],
                         [t_kb, t_identb], [t_bankb])
                P.cp("dve", qT[:, :, tok], bankb[:, 0:512].rearrange("p (c n) -> p c n", c=4), [t_bankb],
                     [t_q[c][i] for c in range(4)])
                P.cp("dve", kT[:, :, tok], bankb[:, 512:1024].rearrange("p (c n) -> p c n", c=4), [t_bankb],
                     [t_k[c][i] for c in range(4)])

            proj_mm(0)
            proj_evac(0)
            for i in range(8):
                if i + 1 < 8:
                    proj_mm(i + 1)
                proj_chain(i)
                if i + 1 < 8:
                    proj_evac(i + 1)
            gv, t_gv = stmp[1], t_stmp[1]
            _mark(" zu")
            for tt in range(2):
                cols = slice(tt * 512, (tt + 1) * 512)
                for c in range(2):
                    for k in range(8):
                        P.mm(bank[4 + c][:], vE[:, k, c * 128:(c + 1) * 128], hT[:, k, cols], k == 0, k == 7,
                             [t_slot[sE], t_h[k][tt]], [t_bank[4 + c]])
                    P.actv(uT[:, c, cols], bank[4 + c][:], AF.Gelu_apprx_tanh, [t_bank[4 + c]], [t_u[c][tt]])
            for p_ in (pA, pB, pC, pD, pE):
                S.done(p_)
            sF, sG = S.need(pF), S.need(pG)
            if stop_after == "proj":
                raise build_program.Stop()

            def variant(j):
                if is_p:
                    return 0 if j % 2 == 0 else 2
                return 0 if j == 0 else (2 if j == 7 else 1)

            def seq_range(j):
                if is_p:
                    return (j // 2 * 2, j // 2 * 2 + 1)
                return (0, 7)

            sqo, t_sqo = sq[0], t_sq[0]
            rs_ap = (ms[:, 0:256], ms[:, 256:512], ntmp[0][:, 0:256])
            rs_t = (t_ms, t_ms, t_ntmp[0])
            ms2, t_ms2 = ntmp[0][:, 256:512], t_ntmp[0]
            oTn = aT[1][:].rearrange("p f n -> p (f n)").rearrange("p (c n) -> p c n", c=8)
            ETB = ((ms, t_ms), (ntmp[0], t_ntmp[0]), (ntmp[1], t_ntmp[1]))

            def attn_S(it):
                n = it["n"]
                c, hp, h = it["c"], it["hp"], it["h"]
                ps_ = slice(hp * 64, (hp + 1) * 64)
                bk, t_bk = bank[SB_[n % 4]], t_bank[SB_[n % 4]]
                pt, t_pt = aT[0][:, n % 4, :], t_aT[0][n % 4]
                if is_p:
                    u_, tok0_ = it["u"], it["u"] * 256
                    for kt in range(2):
                        P.mm(bk[:, kt * 256:(kt + 1) * 256],
                             kT[ps_, c, tok0_ + kt * 128:tok0_ + (kt + 1) * 128], qT[ps_, c, tok0_:tok0_ + 256],
                             True, True, [t_k[c][2 * u_ + kt], t_q[c][2 * u_], t_q[c][2 * u_ + 1]], [t_bk])
                    P.actv(pt, bk[:], AF.Exp, [t_bk, t_chalf], [t_pt], bias=nsh[:, 0:1])
                else:
                    r, kts, nl = it["r"], it["kts"], it["nl"]
                    et, t_et = ETB[n % 3]
                    qcols = slice(r * 64, (r + 1) * 64)
                    for j_, kt in enumerate(kts):
                        P.mm(bk[:, j_ * 64:(j_ + 1) * 64], kT[ps_, c, kt * 128:(kt + 1) * 128],
                             qT[ps_, c, qcols], True, True, [t_k[c][kt], t_q[c][r // 2]], [t_bk])
                    for t in range(2):
                        P.mm(bk[:, (nl + t) * 64:(nl + t + 1) * 64], kcT[ps_, c, t * 128:(t + 1) * 128],
                             qT[ps_, c, qcols], True, True, [t_kc, t_q[c][r // 2]], [t_bk])
                    P.actv(et[:, 0:nl * 64], bk[:, 0:nl * 64], AF.Exp, [t_bk, t_chalf], [t_et], bias=nsh[:, 0:1])
                    P.actv(pt[:, nl * 64:(nl + 2) * 64], bk[:, nl * 64:(nl + 2) * 64], AF.Exp, [t_bk, t_chalf], [t_pt],
                           bias=nsh[:, 0:1])
                    i0 = 2 * kts[0] - r + 7
                    P.tt("dve", pt[:, 0:nl * 64].rearrange("p (j q) -> p j q", q=64),
                         et[:, 0:nl * 64].rearrange("p (j q) -> p j q", q=64),
                         EB[:, h, i0:i0 + 2 * nl - 1:2, :], ALU.mult, [t_et, t_EB], [t_pt])

            def attn_PV(it):
                n = it["n"]
                c, hp, h = it["c"], it["hp"], it["h"]
                ps_ = slice(hp * 64, (hp + 1) * 64)
                pt, t_pt = aT[0][:, n % 4, :], t_aT[0][n % 4]
                if hp == 0:
                    acnt["bo"] += 1
                bo, t_bo = bank[2 + acnt["bo"] % 2], t_bank[2 + acnt["bo"] % 2]
                if is_p:
                    u_ = it["u"]
                    for kt in range(2):
                        P.mm(bo[ps_, 0:256], v_tok[:, 2 * u_ + kt, h * 64:(h + 1) * 64],
                             pt[:, kt * 256:(kt + 1) * 256], kt == 0, kt == 1,
                             [t_v[2 * u_ + kt], t_pt], [t_bo], tp=(0, hp * 64))
                    for kt in range(2):
                        P.mm(bo[ps_, 256:512], ones[:, 0:64], pt[:, kt * 256:(kt + 1) * 256],
                             kt == 0, kt == 1, [t_ones, t_pt], [t_bo], tp=(0, hp * 64))
                    if hp == 1:
                        P.add("dve", lambda e, bo=bo: e.reciprocal(rden[:], bo[:, 256:512]), [t_bo], [t_rden])
                        P.tt("dve", oraw[:, 2 + c, :], bo[:, 0:256], rden[:], ALU.mult, [t_bo, t_rden],
                             [t_or[2 + c]])
                else:
                    r, kts, nl, s0 = it["r"], it["kts"], it["nl"], it["s0"]
                    mml = []
                    for j_, kt in enumerate(kts):
                        if s0 % 2 == 1 and j_ == 0:
                            p0, p1 = 64, 128
                        elif s0 % 2 == 1 and j_ == nl - 1:
                            p0, p1 = 0, 64
                        else:
                            p0, p1 = 0, 128
                        mml.append((v_tok[p0:p1, kt, h * 64:(h + 1) * 64], ones[p0:p1, 0:64],
                                    pt[p0:p1, j_ * 64:(j_ + 1) * 64], p0, [t_v[kt]]))
                    for t in range(2):
                        mml.append((vc[:, t, h * 64:(h + 1) * 64], ones[:, 0:64],
                                    pt[:, (nl + t) * 64:(nl + t + 1) * 64], 0, [t_vc]))
                    for n_, (lv, lo_, rh, p0, rd) in enumerate(mml):
                        P.mm(bo[ps_, 0:64], lv, rh, n_ == 0, n_ == len(mml) - 1, rd + [t_pt], [t_bo],
                             tp=(p0, hp * 64))
                    for n_, (lv, lo_, rh, p0, rd) in enumerate(mml):
                        P.mm(bo[ps_, 64:128], lo_, rh, n_ == 0, n_ == len(mml) - 1, [t_ones, t_pt],
                             [t_bo], tp=(p0, hp * 64))
                    if hp == 1:
                        P.add("dve", lambda e, bo=bo: e.reciprocal(rden[:, 0:64], bo[:, 64:128]), [t_bo],
                              [t_rden])
                        P.tt("dve", oraw[:, 2 + c, (r % 4) * 64:(r % 4 + 1) * 64], bo[:, 0:64], rden[:, 0:64],
                             ALU.mult, [t_bo, t_rden], [t_or[2 + c]])

            for u in range(4):
                tok0 = u * 256
                ucols = slice(tok0, tok0 + 256)
                tt = u // 2
                _mark(" u%d-attn" % u)
                items = []
                for c in range(4):
                    if is_p:
                        for hp in range(2):
                            items.append({"c": c, "hp": hp, "h": 2 * c + hp, "u": u})
                    else:
                        for r in range(4 * u, 4 * u + 4):
                            s0 = min(max(r - 4, 0), 8)
                            kts = list(range(s0 // 2, (s0 + 7) // 2 + 1))
                            for hp in range(2):
                                items.append({"c": c, "hp": hp, "h": 2 * c + hp, "r": r, "s0": s0, "kts": kts,
                                              "nl": len(kts)})
                for it in items:
                    it["n"] = acnt["n"]
                    acnt["n"] += 1
                def pool_chunk():
                    _mark(" u%d-pool" % u)
                    for c in range(2):
                        for gh in range(2):
                            g4 = 2 * c + gh
                            for jj in range(2):
                                j = 2 * u + jj
                                lo, hi = seq_range(j)
                                ins = [i_ for i_ in (j - 1, j, j + 1) if lo <= i_ <= hi]
                                for n_, i_ in enumerate(ins):
                                    var = 3 if i_ == j - 1 else (4 if i_ == j + 1 else variant(j))
                                    P.mm(bank[4][gh * 64:(gh + 1) * 64, c * 256 + jj * 128:c * 256 + (jj + 1) * 128],
                                         za_tok[:, i_, g4 * 64:(g4 + 1) * 64], bands[:, g4 * 5 + var, :],
                                         n_ == 0, n_ == len(ins) - 1, [t_za[i_], t_bands], [t_bank[4]],
                                         tp=(0, gh * 64))
                        for jj in range(2):
                            P.tt("dve", pooledT[:, c, jj * 128:(jj + 1) * 128],
                                 bank[4][:, c * 256 + jj * 128:c * 256 + (jj + 1) * 128],
                                 invc[:, c * 3 + variant(2 * u + jj), :], ALU.mult, [t_bank[4], t_invc], [t_pl[c]])
                        P.mm(bank[5][:, c * 256:(c + 1) * 256], pwb[:, c, :], pooledT[:, c, :], True, True,
                             [t_pwb, t_pl[c]], [t_bank[5]])
                        P.actv(oraw[:, c, :], bank[5][:, c * 256:(c + 1) * 256], AF.Identity, [t_bank[5], t_vtT],
                               [t_or[c]], scale=vtT[:, l, 104 + c:105 + c])
                    for c in range(2):
                        for gh in range(2):
                            g4 = 2 * c + gh
                            for jj in range(2):
                                P.mm(bank[4][gh * 64:(gh + 1) * 64, c * 256 + jj * 128:c * 256 + (jj + 1) * 128],
                                     vg_tok[:, 2 * u + jj, g4 * 64:(g4 + 1) * 64], wsT[:, g4, :], True, True,
                                     [t_vg[2 * u + jj], t_wsT], [t_bank[4]], tp=(0, gh * 64))
                        P.tt("dve", gv[:, 0:256].rearrange("p (j n) -> p j n", j=2),
                             bank[4][:, c * 256:(c + 1) * 256].rearrange("p (j n) -> p j n", j=2),
                             bsb[:, c:c + 1, :].to_broadcast([128, 2, 128]), ALU.add, [t_bank[4], t_bsb], [t_gv])
                        P.tt("pool", oraw[:, 6 + c, :], gv[:, 0:256], uT[:, c, ucols], ALU.mult,
                             [t_gv, t_u[c][tt]], [t_or[6 + c]])

                def capture(fn):
                    rec = []
                    P.add = lambda eng, f, reads=(), writes=(), dma=None, is_out=False: rec.append(
                        (eng, f, reads, writes, dma, is_out))
                    try:
                        fn()
                    finally:
                        del P.add
                    return rec

                def emit_pair(fn, ita, itb):
                    ra = capture(lambda: fn(ita))
                    rb = capture(lambda: fn(itb))
                    pa = [x for x in ra if x[0] == "pe"]
                    pb = [x for x in rb if x[0] == "pe"]
                    for i_ in range(max(len(pa), len(pb))):
                        if i_ < len(pa):
                            P.add(*pa[i_])
                        if i_ < len(pb):
                            P.add(*pb[i_])
                    for x in ra + rb:
                        if x[0] != "pe":
                            P.add(*x)

                pairs = [(items[2 * j], items[2 * j + 1]) for j in range(len(items) // 2)]
                DEPTHP = 1
                for idx in range(len(pairs) + DEPTHP):
                    if idx == len(pairs) // 2:
                        pool_chunk()
                    if idx < len(pairs):
                        emit_pair(attn_S, *pairs[idx])
                    if idx - DEPTHP >= 0:
                        emit_pair(attn_PV, *pairs[idx - DEPTHP])
                _mark(" u%d-out" % u)
                segs = ((0, 2, 256.0), (2, 6, 512.0), (6, 8, 256.0))
                for si, (o0, o1, wd_) in enumerate(segs):
                    stb, t_stb = (bank[6], t_bank[6]) if si < 2 else (bank[5], t_bank[5])
                    scol = slice((si % 2) * 256, (si % 2) * 256 + 256)
                    for oc in range(o0, o1):
                        sqo_, t_sqo_ = sq[oc % 2], t_sq[oc % 2]
                        P.actv(sqo_[:, 0:256], oraw[:, oc, :], AF.Square, [t_or[oc]], [t_sqo_])
                        P.mm(stb[:, scol], ones[:], sqo_[:, 0:256], oc == o0, oc == o1 - 1, [t_ones, t_sqo_],
                             [t_stb])
                    P.actv(ms2, stb[:, scol], AF.Ln, [t_stb, t_chalf], [t_ms2], scale=1.0 / wd_,
                           bias=epsc[:, 0:1])
                    P.actv(rs_ap[si], ms2, AF.Exp, [t_ms2], [rs_t[si]], scale=-0.5)
                for oc in range(8):
                    si = 0 if oc < 2 else (1 if oc < 6 else 2)
                    P.stt(oTn[:, oc, :], oraw[:, oc, :], vtT[:, l, 96 + oc:97 + oc], rs_ap[si], ALU.mult, ALU.mult,
                          [t_or[oc], t_vtT, rs_t[si]], [t_aT[1][oc // 2]])
                for dc in range(8):
                    sl = sF if dc < 4 else sG
                    vw = _v3(wslot[sl], 8)
                    bw, t_bw = bank[2 + dc % 2], t_bank[2 + dc % 2]
                    for oc in range(8):
                        P.mm(bw[:, 0:256], vw[:, oc, (dc % 4) * 128:(dc % 4 + 1) * 128], oTn[:, oc, :],
                             oc == 0, oc == 7, [t_slot[sl], t_aT[1][oc // 2]], [t_bw])
                    P.stt(xT[:, dc, ucols], bw[:, 0:256], Gm[:, l, 1, dc, v:v + 1], xT[:, dc, ucols],
                          ALU.mult, ALU.add, [t_bw, t_x[dc][tt], t_mods], [t_x[dc][tt]])
                if u % 2 == 1:
                    after_tt(u // 2)
            S.done(pF)
            S.done(pG)

        class _Stop(Exception):
            pass
        build_program.Stop = _Stop
        try:
            for g in range(2):
                _mark("load%d" % g)
                if not EARLY_NORM:
                    nop = lambda tt: None
                    load_x(g, nop)
                    if g == 0:
                        ada_compute(0)
                    for l in range(L):
                        _mark("ffn%d%d0" % (g, l))
                        norm_tt(l, 0, g, 0); norm_tt(l, 0, g, 1)
                        ffn(g, l, 0, nop)
                        if g == 0 and l == 0:
                            ada_compute(1)
                        _mark("mix%d%d" % (g, l))
                        mixer_setup(g, l)
                        norm_tt(l, 1, g, 0); norm_tt(l, 1, g, 1)
                        mixer(g, l, nop)
                        _mark("ffn%d%d1" % (g, l))
                        norm_tt(l, 2, g, 0); norm_tt(l, 2, g, 1)
                        ffn(g, l, 1, nop)
                    _mark("store%d" % g)
                    store_y(g)
                    continue
                if g == 0:
                    load_x(g, lambda tt: None)
                    ada_compute(0)
                    norm_tt(0, 0, 0, 0)
                    norm_tt(0, 0, 0, 1)
                else:
                    load_x(g, lambda tt, g=g: norm_tt(0, 0, g, tt))
                for l in range(L):
                    _mark("ffn%d%d0" % (g, l))
                    mixer_setup(g, l)
                    ffn(g, l, 0, lambda tt, g=g, l=l: norm_tt(l, 1, g, tt))
                    if g == 0 and l == 0:
                        ada_compute(1)
                    _mark("mix%d%d" % (g, l))
                    mixer(g, l, lambda tt, g=g, l=l: norm_tt(l, 2, g, tt))
                    _mark("ffn%d%d1" % (g, l))
                    if l + 1 < L:
                        ffn(g, l, 1, lambda tt, g=g, l=l: norm_tt(l + 1, 0, g, tt))
                    else:
                        ffn(g, l, 1, lambda tt: None)
                _mark("store%d" % g)
                store_y(g)
        except _Stop:
            pass
        _mark("end")
        build_program.phases = PH
        P.emit(st)
    return nc


def _consts():
    ident = np.eye(128, dtype=np.float32)
    bands = np.zeros((128, 20, 128), np.float32)
    invc = np.zeros((128, 6, 128), np.float32)
    for gi, w in enumerate(WINS):
        h = w // 2
        tp = np.arange(128)[:, None]
        t = np.arange(128)[None, :]
        inb = ((tp >= t - h) & (tp < t + h)).astype(np.float32)
        cnt_first = (np.minimum(t + h, 10 ** 6) - np.maximum(t - h, 0)).astype(np.float32)
        cnt_mid = np.full((1, 128), float(w), np.float32)
        cnt_last = ((128 - t) + h - np.maximum(0, 0)).astype(np.float32)
        cnt_last = np.minimum(cnt_last, w).astype(np.float32)
        eye = np.eye(128, dtype=np.float32)
        bands[:, gi * 5 + 0, :] = inb - eye * cnt_first
        bands[:, gi * 5 + 1, :] = inb - eye * cnt_mid
        bands[:, gi * 5 + 2, :] = inb - eye * cnt_last
        bands[:, gi * 5 + 3, :] = ((tp - 128) >= (t - h)).astype(np.float32)
        bands[:, gi * 5 + 4, :] = ((tp + 128) < (t + h)).astype(np.float32)
        c, half = gi // 2, gi % 2
        for vi, cn in enumerate((cnt_first, cnt_mid, cnt_last)):
            invc[half * 64:(half + 1) * 64, c * 3 + vi, :] = 1.0 / cn
    dw = np.zeros((31, 127), np.float32)
    for j in range(31):
        dw[j, j + 48] = 1.0
    cols = np.arange(64)
    cs = np.clip(cols - 8, 0, 48)
    ok = (cols[None, :] >= cs[:, None]) & (cols[None, :] < cs[:, None] + 16)
    colok = np.concatenate([ok.T, ok.T], 0).astype(np.float32)
    return ident, bands, invc, dw, colok


def make_in_maps(inp):
    f = lambda a: np.ascontiguousarray(np.asarray(a, dtype=np.float32))
    ident, bands, invc, dw, colok = _consts()
    vt = np.zeros((L, 128, 128), np.float32)
    gt = np.zeros((L, 384), np.float32)
    rpb = np.zeros((L, 128, 31), np.float32)
    for l in range(L):
        vt[l, 0:72] = f(inp["ada_b"])[l].reshape(72, 128)
        vt[l, 72:96] = f(inp["norm_g"])[l].reshape(24, 128)
        vt[l, 96:104] = f(inp["out_norm_g"])[l].reshape(8, 128)
        vt[l, 104:106] = f(inp["pool_scale"])[l].reshape(2, 128)
        gt[l, 0:64] = f(inp["q_norm_g"])[l]
        gt[l, 64:128] = f(inp["k_norm_g"])[l]
        gt[l, 128:384] = f(inp["sg_vnorm_g"])[l].reshape(256)
        rpb[l, 0:120] = f(inp["na_rpb"])[l].reshape(120, 31)
    shared = {
        "vt": vt, "gt": gt, "rpb": rpb,
        "poolw": f(inp["pool_w"]), "sgw": f(inp["sg_w"]), "sgb": f(inp["sg_b"]),
        "adaw": f(inp["ada_w"]), "wg": f(inp["ffn_w_gate"]), "wu": f(inp["ffn_w_up"]),
        "wd": f(inp["ffn_w_down"]), "win": f(inp["w_in"]), "wout": f(inp["w_out"]),
        "ident": ident, "bands": bands, "invc": invc, "dw": dw, "colok": colok,
    }
    xp = f(inp["x_prompt"]); xs = f(inp["x_sample"])
    ck = f(inp["cache_k"]); cv = f(inp["cache_v"]); c = f(inp["c"]); cc = f(inp["c_ctx"])
    maps = []
    for i in range(NCORES):
        x = np.stack([xp[4 * i:4 * i + 4].reshape(1024, D), xs[i]], 0)
        cvec = np.zeros((128, 128), np.float32)
        cvec[0:8] = cc.reshape(8, 128)
        cvec[8:16] = c[i].reshape(8, 128)
        m = dict(shared)
        m.update({"x": np.ascontiguousarray(x), "ck": np.ascontiguousarray(ck[i]),
                  "cv": np.ascontiguousarray(cv[i]), "cvec": cvec})
        maps.append(m)
    return maps


_NC_CACHE = {}


def kernel(**inputs):
    if "nc" not in _NC_CACHE:
        _NC_CACHE["nc"] = build_program()
    nc = _NC_CACHE["nc"]
    maps = make_in_maps(inputs)
    res = run_bass_kernel_spmd(nc, maps, core_ids=list(range(NCORES)))
    r = res.results
    yp = np.concatenate([r[i]["y"][0].reshape(4, 256, D) for i in range(NCORES)], 0)
    ys = np.stack([r[i]["y"][1] for i in range(NCORES)], 0)
    nk = np.concatenate([r[i]["nk"] for i in range(NCORES)], 0)
    nv = np.concatenate([r[i]["nv"] for i in range(NCORES)], 0)
    return (yp.astype(np.float32), ys.astype(np.float32), nk.astype(np.float32), nv.astype(np.float32))
```

```python
import numpy as np
from contextlib import ExitStack
import concourse.bass as bass
import concourse.mybir as mybir
from concourse.bass_utils import run_bass_kernel_spmd

F32 = mybir.dt.float32
BF16 = mybir.dt.bfloat16
AF = mybir.ActivationFunctionType
ALU = mybir.AluOpType
AX = mybir.AxisListType

D = 1024
DFF = 2816
L = 2
NH = 8
EPS = 1e-6
NCORES = 8
WINS = (2, 4, 8, 16)
EARLY_NORM = True


class Tl:
    __slots__ = ("name", "w", "r", "excl")

    def __init__(self, name="", excl=False):
        self.name = name
        self.w = None
        self.r = {}
        self.excl = excl


class Op:
    __slots__ = ("eng", "fn", "deps", "marked", "count", "dkey", "dcount", "is_dma", "uid")


class Prog:
    ENGS = ("pe", "act", "dve", "pool", "sp")

    def __init__(self, nc):
        self.nc = nc
        self.ops = {e: [] for e in self.ENGS}
        self.latest_dma = {}
        self.dma_counts = {}
        self.uid = 0
        self.out_keys = set()

    def add(self, eng, fn, reads=(), writes=(), dma=None, is_out=False):
        op = Op()
        op.eng = eng
        op.fn = fn
        op.marked = False
        op.count = None
        op.is_dma = dma is not None
        op.dkey = dma
        op.uid = self.uid
        self.uid += 1
        deps = {}
        for t in reads:
            if t.w is not None:
                deps[t.w.uid] = t.w
            if t.excl:
                for d in t.r.values():
                    if d.eng != eng:
                        deps[d.uid] = d
        for t in writes:
            if t.w is not None:
                deps[t.w.uid] = t.w
            for d in t.r.values():
                deps[d.uid] = d
        final = {}
        for d in deps.values():
            if d.is_dma:
                d = self.latest_dma[d.dkey]
            elif d.eng == "pe" and eng == "pe" and not op.is_dma:
                continue
            else:
                d.marked = True
            final[d.uid] = d
        op.deps = list(final.values())
        if op.is_dma:
            c = self.dma_counts.get(dma, 0) + 16
            self.dma_counts[dma] = c
            op.dcount = c
            self.latest_dma[dma] = op
            if is_out:
                self.out_keys.add(dma)
        rkey = ("dma", op.uid) if op.is_dma else eng
        for t in reads:
            t.r[rkey] = op
        for t in writes:
            t.w = op
            t.r = {}
        self.ops[eng].append(op)
        return op

    def dma(self, eng, key, out, in_, reads=(), writes=(), is_out=False):
        return self.add(eng, lambda e: e.dma_start(out=out, in_=in_), reads, writes,
                        dma=key, is_out=is_out)

    def mm(self, out, lhsT, rhs, start, stop, reads, writes, tp=None):
        if tp is None:
            return self.add("pe", lambda e: e.matmul(out, lhsT, rhs, start=start, stop=stop), reads, writes)
        return self.add("pe", lambda e: e.matmul(out, lhsT, rhs, start=start, stop=stop, tile_position=tp),
                        reads, writes)

    def tr(self, out, in_, ident, reads, writes):
        return self.add("pe", lambda e: e.transpose(out, in_, ident), reads, writes)

    def actv(self, out, in_, func, reads, writes, scale=None, bias=None):
        kw = {}
        if scale is not None:
            kw["scale"] = scale
        if bias is not None:
            kw["bias"] = bias
        return self.add("act", lambda e: e.activation(out, in_, func, **kw), reads, writes)

    def tt(self, eng, out, in0, in1, op, reads, writes):
        return self.add(eng, lambda e: e.tensor_tensor(out, in0, in1, op=op), reads, writes)

    def ts(self, eng, out, in0, s1, s2, op0, op1, reads, writes):
        if s2 is None:
            return self.add(eng, lambda e: e.tensor_scalar(out, in0, s1, None, op0=op0), reads, writes)
        return self.add(eng, lambda e: e.tensor_scalar(out, in0, s1, s2, op0=op0, op1=op1), reads, writes)

    def stt(self, out, in0, scalar, in1, op0, op1, reads, writes):
        return self.add("dve", lambda e: e.scalar_tensor_tensor(out, in0, scalar, in1, op0=op0, op1=op1),
                        reads, writes)

    def cp(self, eng, out, in_, reads, writes):
        if eng == "act":
            return self.add("act", lambda e: e.copy(out, in_), reads, writes)
        return self.add(eng, lambda e: e.tensor_copy(out, in_), reads, writes)

    def red(self, out, in_, reads, writes):
        return self.add("dve", lambda e: e.tensor_reduce(out, in_, axis=AX.X, op=ALU.add), reads, writes)

    def memset(self, eng, ap, val, writes):
        return self.add(eng, lambda e: e.memset(ap, val), (), writes)

    def emit(self, stack):
        nc = self.nc
        for e in self.ENGS:
            c = 0
            for op in self.ops[e]:
                if op.marked and not op.is_dma:
                    c += 1
                    op.count = c
        esem = {e: stack.enter_context(nc.semaphore("s_" + e)) for e in self.ENGS}
        dsem = {k: stack.enter_context(nc.semaphore("d_%d" % i)) for i, k in enumerate(self.dma_counts)}
        block = stack.enter_context(nc.Block())
        ops = self.ops
        out_final = [(dsem[k], self.dma_counts[k]) for k in sorted(self.out_keys, key=str)]

        def run(ename, e):
            known = {}
            for op in ops[ename]:
                for d in op.deps:
                    if d.is_dma:
                        s, v = dsem[d.dkey], d.dcount
                    else:
                        s, v = esem[d.eng], d.count
                    kid = id(s)
                    if known.get(kid, 0) >= v:
                        continue
                    known[kid] = v
                    e.wait_ge(s, v)
                ins = op.fn(e)
                if op.is_dma:
                    ins.then_inc(dsem[op.dkey], 16)
                elif op.marked:
                    ins.then_inc(esem[ename], 1)
            if ename == "sp":
                for s, v in out_final:
                    e.wait_ge(s, v)

        @block.sync
        def _(e):
            run("sp", e)

        @block.tensor
        def _(e):
            run("pe", e)

        @block.scalar
        def _(e):
            run("act", e)

        @block.vector
        def _(e):
            run("dve", e)

        @block.gpsimd
        def _(e):
            run("pool", e)


class Stream:
    NS = 6
    LOOK = 5

    def __init__(self, P, slots, t_slots):
        self.P = P
        self.slots = slots
        self.t_slots = t_slots
        self.pieces = []
        self.issued = 0
        self.freed = 0
        self.want = 0

    def plan(self, parts):
        self.pieces.append(parts)
        return len(self.pieces) - 1

    def pump(self):
        while self.issued <= self.want and self.issued < len(self.pieces) and self.issued < self.freed + self.NS:
            i = self.issued
            s = i % self.NS
            for dst_fn, src in self.pieces[i]:
                self.P.dma("pool", ("w", s), dst_fn(self.slots[s]), src, writes=[self.t_slots[s]])
            self.issued += 1

    def need(self, p):
        self.want = max(self.want, p + self.LOOK)
        self.pump()
        assert self.issued > p, (self.issued, p, self.freed)
        return p % self.NS

    def done(self, p):
        self.freed = max(self.freed, p + 1)
        self.pump()


def _v3(t, k):
    return t[:].rearrange("p (k n) -> p k n", k=k)


def build_program(stop_after=None):
    nc = bass.Bass("TRN2", target_bir_lowering=False)

    def din(name, shape):
        return nc.dram_tensor(name, shape, F32, kind="ExternalInput").ap()

    def dout(name, shape):
        return nc.dram_tensor(name, shape, F32, kind="ExternalOutput").ap()

    x_d = din("x", [2, 1024, D])
    ck_d = din("ck", [L, NH, 256, 64])
    cv_d = din("cv", [L, NH, 256, 64])
    cvec_d = din("cvec", [128, 128])
    vt_d = din("vt", [L, 128, 128])
    gt_d = din("gt", [L, 384])
    poolw_d = din("poolw", [L, 4, 64, 64])
    sgw_d = din("sgw", [L, 4, 128, 128])
    sgb_d = din("sgb", [L, 4, 128])
    rpb_d = din("rpb", [L, 128, 31])
    adaw_d = din("adaw", [L, D, 9 * D])
    wg_d = din("wg", [L, 2, D, DFF])
    wu_d = din("wu", [L, 2, D, DFF])
    wd_d = din("wd", [L, 2, DFF, D])
    win_d = din("win", [L, D, 2304])
    wout_d = din("wout", [L, D, D])
    ident_d = din("ident", [128, 128])
    bands_d = din("bands", [128, 20, 128])
    invc_d = din("invc", [128, 6, 128])
    dw_d = din("dw", [31, 127])
    colok_d = din("colok", [128, 64])
    y_d = dout("y", [2, 1024, D])
    nk_d = dout("nk", [4, L, NH, 256, 64])
    nv_d = dout("nv", [4, L, NH, 256, 64])

    P = Prog(nc)
    st = ExitStack()

    def sb(name, shape, dt=F32):
        return st.enter_context(nc.sbuf_tensor("s_" + name, shape, dt))

    with st:
        bank = [st.enter_context(nc.psum_tensor("bank%d" % i, [128, 512], F32)) for i in range(7)]
        bankb = st.enter_context(nc.psum_tensor("bankb", [128, 1024], BF16))
        t_bank = [Tl("bank%d" % i, excl=True) for i in range(7)]
        t_bankb = Tl("bankb", excl=True)

        ident = sb("ident", [128, 128]); t_ident = Tl()
        identb = sb("identb", [128, 128], BF16); t_identb = Tl()
        ones = sb("ones", [128, 128], BF16); t_ones = Tl()
        epsc = sb("epsc", [128, 1]); t_chalf = Tl()
        nsh = sb("nsh", [128, 1])
        bands = sb("bands", [128, 20, 128], BF16); t_bands = Tl()
        invc = sb("invc", [128, 6, 128]); t_invc = Tl()
        dw = sb("dw", [31, 127]); t_dw = Tl()
        colok = sb("colok", [128, 64]); t_colok = Tl()
        P.dma("sp", "c0", ident[:], ident_d, writes=[t_ident])
        P.dma("sp", "c1", invc[:], invc_d, writes=[t_invc])
        P.dma("sp", "c2", dw[:], dw_d, writes=[t_dw])
        P.dma("sp", "c3", colok[:], colok_d, writes=[t_colok])
        P.dma("pool", "c4", bands[:], bands_d, writes=[t_bands])
        P.cp("dve", identb[:], ident[:], [t_ident], [t_identb])
        P.memset("pool", ones[:], 1.0, [t_ones])
        P.memset("pool", epsc[:], EPS, [t_chalf])
        P.memset("pool", nsh[:], -8.0, [t_chalf])

        xT = sb("xT", [128, 8, 1024]); t_x = [[Tl() for _ in range(2)] for _ in range(8)]
        hT = sb("hT", [128, 8, 1024], BF16); t_h = [[Tl() for _ in range(2)] for _ in range(8)]
        wslot = [sb("wslot%d" % i, [128, 4096], BF16) for i in range(Stream.NS)]
        t_slot = [Tl("slot%d" % i) for i in range(Stream.NS)]
        S = Stream(P, wslot, t_slot)
        aT = [sb("aT%d" % i, [128, 4, 512], BF16) for i in range(2)]
        t_aT = [[Tl() for _ in range(4)] for _ in range(2)]
        stmp = [sb("stmp%d" % i, [128, 512]) for i in range(2)]; t_stmp = [Tl(), Tl()]
        xio = sb("xio", [128, 2048]); t_or = [Tl() for _ in range(8)]
        oraw = xio[:].rearrange("p (c n) -> p c n", c=8)
        sq = [sb("sq%d" % i, [128, 512], BF16) for i in range(2)]; t_sq = [Tl(), Tl()]
        ms = sb("ms", [128, 512]); t_ms = Tl()
        rstd, t_rstd = ms, t_ms
        ntmp = [sb("ntmp%d" % i, [128, 512]) for i in range(2)]; t_ntmp = [Tl(), Tl()]
        mods = sb("mods", [128, L, 72, 2]); t_mods = Tl()
        Am = sb("Am", [128, L, 3, 8, 2])
        Gm = sb("Gm", [128, L, 3, 8, 2])
        vtT = sb("vtT", [128, L, 128]); t_vtT = Tl()
        scT = sb("scT", [128, 16], BF16); t_scT = Tl()
        vtr_t = sb("vtr", [128, 128]); t_vtr = Tl()
        vtr = vtr_t[:]
        za_tok = sb("za_tok", [128, 8, 256], BF16); t_za = [Tl() for _ in range(8)]
        vg_tok = sb("vg_tok", [128, 8, 256], BF16); t_vg = [Tl() for _ in range(8)]
        v_tok = sb("v_tok", [128, 8, 512], BF16); t_v = [Tl() for _ in range(8)]
        qT = sb("qT", [128, 4, 1024], BF16); t_q = [[Tl() for _ in range(8)] for _ in range(4)]
        kT = sb("kT", [128, 4, 1024], BF16); t_k = [[Tl() for _ in range(8)] for _ in range(4)]
        uT = sb("uT", [128, 2, 1024], BF16); t_u = [[Tl() for _ in range(2)] for _ in range(2)]
        kcT = sb("kcT", [128, 4, 256], BF16); t_kc = Tl()
        qkz = sb("qkz", [128, 1280]); t_qkz = Tl()
        vc = sb("vc", [128, 2, 512], BF16); t_vc = Tl()
        pooledT = sb("pooledT", [128, 2, 256], BF16); t_pl = [Tl(), Tl()]
        EB = sb("EB", [128, 8, 14, 64], BF16); t_EB = Tl()
        wsT = sb("wsT", [128, 4, 128], BF16); t_wsT = Tl()
        pwb = sb("pwb", [128, 2, 128], BF16); t_pwb = Tl()
        bsb = sb("bsb", [128, 2, 128]); t_bsb = Tl()
        gtb = sb("gtb", [128, 384]); t_gtb = Tl()
        rden = sb("rden", [128, 256]); t_rden = Tl()
        st8 = sb("st8", [128, 40]); t_st8 = Tl()

        pid = {}
        groups = [(0, 4), (4, 4), (8, 2), (10, 4), (14, 4), (18, 4)]

        def plan_ada(l):
            for pc in range(18):
                pid[("ada", l, pc)] = S.plan([(
                    lambda t: _v3(t, 8),
                    adaw_d[l].rearrange("(k p) n -> p k n", p=128)[:, :, pc * 512:(pc + 1) * 512])])

        def plan_ffn(g, l, j):
            for gi, (c0, nfc) in enumerate(groups):
                w = nfc * 128
                pid[("g", g, l, j, gi)] = S.plan([(
                    lambda t, w=w: _v3(t, 8)[:, :, 0:w],
                    wg_d[l, j].rearrange("(k p) n -> p k n", p=128)[:, :, c0 * 128:c0 * 128 + w])])
                pid[("u", g, l, j, gi)] = S.plan([(
                    lambda t, w=w: _v3(t, 8)[:, :, 0:w],
                    wu_d[l, j].rearrange("(k p) n -> p k n", p=128)[:, :, c0 * 128:c0 * 128 + w])])
                pid[("d", g, l, j, gi)] = S.plan([(
                    lambda t, nfc=nfc: _v3(t, 4)[:, 0:nfc, :],
                    wd_d[l, j][c0 * 128:c0 * 128 + w, :].rearrange("(f p) n -> p f n", p=128))])

        def plan_mixer(g, l):
            wv = win_d[l].rearrange("(k p) n -> p k n", p=128)
            pid[("A", g, l)] = S.plan([
                (lambda t: _v3(t, 8)[:, :, 0:256], wv[:, :, 0:256]),
                (lambda t: _v3(t, 8)[:, :, 256:512], wv[:, :, 2048:2304])])
            pid[("B", g, l)] = S.plan([(lambda t: _v3(t, 8), wv[:, :, 256:768])])
            pid[("C", g, l)] = S.plan([(lambda t: _v3(t, 8), wv[:, :, 768:1280])])
            pid[("D", g, l)] = S.plan([(lambda t: _v3(t, 8), wv[:, :, 1280:1792])])
            pid[("E", g, l)] = S.plan([(lambda t: _v3(t, 8)[:, :, 0:256], wv[:, :, 1792:2048])])
            wo = wout_d[l].rearrange("(k p) n -> p k n", p=128)
            pid[("F", g, l)] = S.plan([(lambda t: _v3(t, 8), wo[:, :, 0:512])])
            pid[("G", g, l)] = S.plan([(lambda t: _v3(t, 8), wo[:, :, 512:1024])])

        for g in range(2):
            for l in range(L):
                plan_ffn(g, l, 0)
                plan_mixer(g, l)
                plan_ffn(g, l, 1)

        P.dma("sp", "v0", vtr, cvec_d, writes=[t_vtr])
        P.tr(bank[6][:, 0:128], vtr, ident[:], [t_vtr, t_ident], [t_bank[6]])
        P.actv(scT[:], bank[6][:, 0:16], AF.Silu, [t_bank[6]], [t_scT])
        for l in range(L):
            P.dma("sp", "v0", vtr, vt_d[l], writes=[t_vtr])
            P.tr(bank[6][:, 0:128], vtr, ident[:], [t_vtr, t_ident], [t_bank[6]])
            P.cp("dve", vtT[:, l, :], bank[6][:, 0:128], [t_bank[6]], [t_vtT])
        def ada_compute(l):
            for pc in range(18):
                p_ = pid[("ada", l, pc)]
                s = S.need(p_)
                wv = _v3(wslot[s], 8)
                for fc in range(4):
                    j = pc * 4 + fc
                    for k in range(8):
                        P.mm(bank[5][:, 2 * j:2 * j + 2], wv[:, k, fc * 128:(fc + 1) * 128],
                             scT[:, k:16:8], k == 0, k == 7, [t_slot[s], t_scT], [t_bank[5]])
                S.done(p_)
            P.tt("dve", mods[:, l, :, :], bank[5][:, 0:144].rearrange("p (j v) -> p j v", v=2),
                 vtT[:, l, 0:72].rearrange("p (j o) -> p j o", o=1).to_broadcast([128, 72, 2]),
                 ALU.add, [t_bank[5], t_vtT], [t_mods])
            for n in range(3):
                P.ts("dve", Am[:, l, n, :, :], mods[:, l, (3 * n + 1) * 8:(3 * n + 2) * 8, :], 1.0, None,
                     ALU.add, None, [t_mods], [t_mods])
                P.tt("dve", Am[:, l, n, :, :], Am[:, l, n, :, :],
                     vtT[:, l, 72 + n * 8:80 + n * 8].rearrange("p (j o) -> p j o", o=1).to_broadcast([128, 8, 2]),
                     ALU.mult, [t_mods, t_vtT], [t_mods])
                P.ts("dve", Gm[:, l, n, :, :], mods[:, l, (3 * n + 2) * 8:(3 * n + 3) * 8, :],
                     (1.0 if n == 1 else 0.5), None, ALU.mult, None, [t_mods], [t_mods])


        ADA0_DEFERRED = True
        ebflat = EB[:].rearrange("p h r q -> p (h r q)")
        adas = [ebflat[:, j * 2048:(j + 1) * 2048].rearrange("p (k n) -> p k n", k=8) for j in range(3)]
        t_adas = [Tl() for _ in range(3)]
        ada_q = {"jobs": [], "issued": 0, "done": 0}

        def ada_add_jobs(l, hp0, hp1, jbase):
            for hpi in range(hp0, hp1):
                ada_q["jobs"].append((l, hpi, jbase))

        def ada_issue_upto(n):
            while ada_q["issued"] < min(n, len(ada_q["jobs"])):
                i_ = ada_q["issued"]
                l_, hpi, _ = ada_q["jobs"][i_]
                P.dma("pool", ("ada", i_ % 3), adas[i_ % 3],
                      adaw_d[l_].rearrange("(k p) n -> p k n", p=128)[:, :, hpi * 256:(hpi + 1) * 256],
                      writes=[t_adas[i_ % 3]])
                ada_q["issued"] += 1

        def ada_consume(n):
            for _ in range(n):
                i_ = ada_q["done"]
                if i_ >= len(ada_q["jobs"]):
                    return
                ada_issue_upto(i_ + 1)
                l_, hpi, jbase = ada_q["jobs"][i_]
                for fc in range(2):
                    col = (hpi * 2 + fc - jbase) * 2
                    for k in range(8):
                        P.mm(bank[6][:, col:col + 2], adas[i_ % 3][:, k, fc * 128:(fc + 1) * 128],
                             scT[:, k:16:8], k == 0, k == 7, [t_adas[i_ % 3], t_scT], [t_bank[6]])
                ada_q["done"] += 1
                ada_issue_upto(ada_q["done"] + 2)

        def ada_finalize(l, m0, m1):
            j0, j1 = m0 * 8, m1 * 8
            P.tt("dve", mods[:, l, j0:j1, :],
                 bank[6][:, 0:(j1 - j0) * 2].rearrange("p (j v) -> p j v", v=2),
                 vtT[:, l, j0:j1].rearrange("p (j o) -> p j o", o=1).to_broadcast([128, j1 - j0, 2]),
                 ALU.add, [t_bank[6], t_vtT], [t_mods])
            for n in range(m0 // 3, m1 // 3):
                P.ts("dve", Am[:, l, n, :, :], mods[:, l, (3 * n + 1) * 8:(3 * n + 2) * 8, :], 1.0, None,
                     ALU.add, None, [t_mods], [t_mods])
                P.tt("dve", Am[:, l, n, :, :], Am[:, l, n, :, :],
                     vtT[:, l, 72 + n * 8:80 + n * 8].rearrange("p (j o) -> p j o", o=1).to_broadcast([128, 8, 2]),
                     ALU.mult, [t_mods, t_vtT], [t_mods])
                P.ts("dve", Gm[:, l, n, :, :], mods[:, l, (3 * n + 2) * 8:(3 * n + 3) * 8, :],
                     (1.0 if n == 1 else 0.5), None, ALU.mult, None, [t_mods], [t_mods])

        def make_ada_hook(counts, fin):
            def hook(gi, tt, f):
                sl_ = gi * 2 + tt
                if sl_ < len(counts):
                    if f < counts[sl_]:
                        ada_consume(1)
                elif sl_ == len(counts) and f == 0:
                    fin()
            return hook

        PH = []
        _mark = lambda nm: PH.append((nm, len(P.ops["pe"])))
        def norm_tt(l, n, v, tt):
            cols = slice(tt * 512, (tt + 1) * 512)
            for k in range(8):
                P.actv(sq[k % 2][:], xT[:, k, cols], AF.Square, [t_x[k][tt]], [t_sq[k % 2]])
                P.mm(bank[6][:], ones[:], sq[k % 2][:], k == 0, k == 7, [t_sq[k % 2], t_ones], [t_bank[6]])
            P.actv(ms[:], bank[6][:], AF.Ln, [t_bank[6], t_chalf], [t_ms], scale=1.0 / D, bias=epsc[:, 0:1])
            P.actv(rstd[:], ms[:], AF.Exp, [t_ms], [t_rstd], scale=-0.5)
            for k in range(8):
                P.tt("dve", ntmp[k % 2][:], xT[:, k, cols], rstd[:], ALU.mult,
                     [t_x[k][tt], t_rstd], [t_ntmp[k % 2]])
                P.actv(hT[:, k, cols], ntmp[k % 2][:], AF.Identity, [t_ntmp[k % 2], t_mods], [t_h[k][tt]],
                       scale=Am[:, l, n, k, v:v + 1], bias=mods[:, l, 3 * n * 8 + k, v:v + 1])

        cnt = {"up": 0, "dn": 0, "ab": 0}

        def ffn(g, l, j, after_tt, hook=None):
            n = 0 if j == 0 else 2
            v = g
            pending = []
            last_gi = len(groups) - 1

            def flush():
                while pending:
                    pending.pop(0)()

            for gi, (c0, nfc) in enumerate(groups):
                pg_, pu_, pd_ = pid[("g", g, l, j, gi)], pid[("u", g, l, j, gi)], pid[("d", g, l, j, gi)]
                sg, su, sd = S.need(pg_), S.need(pu_), S.need(pd_)
                vg_, vu_, vd_ = _v3(wslot[sg], 8), _v3(wslot[su], 8), _v3(wslot[sd], 4)
                for tt in range(2):
                    cols = slice(tt * 512, (tt + 1) * 512)
                    ab = cnt["ab"] % 2
                    cnt["ab"] += 1
                    for f in range(nfc):
                        ib = cnt["up"] % 2
                        cnt["up"] += 1
                        for k in range(8):
                            P.mm(bank[ib][:], vg_[:, k, f * 128:(f + 1) * 128], hT[:, k, cols], k == 0, k == 7,
                                 [t_slot[sg], t_h[k][tt]], [t_bank[ib]])
                        for k in range(8):
                            P.mm(bank[2 + ib][:], vu_[:, k, f * 128:(f + 1) * 128], hT[:, k, cols], k == 0, k == 7,
                                 [t_slot[su], t_h[k][tt]], [t_bank[2 + ib]])
                        P.actv(stmp[ib][:], bank[ib][:], AF.Silu, [t_bank[ib]], [t_stmp[ib]])
                        P.tt("dve", aT[ab][:, f, :], stmp[ib][:], bank[2 + ib][:], ALU.mult,
                             [t_stmp[ib], t_bank[2 + ib]], [t_aT[ab][f]])
                        if hook is not None:
                            hook(gi, tt, f)

                    def down(ab=ab, nfc=nfc, sd=sd, vd_=vd_, cols=cols, tt=tt, gi=gi):
                        for dc in range(8):
                            ib2 = 4 + cnt["dn"] % 2
                            cnt["dn"] += 1
                            for f in range(nfc):
                                P.mm(bank[ib2][:], vd_[:, f, dc * 128:(dc + 1) * 128], aT[ab][:, f, :],
                                     f == 0, f == nfc - 1, [t_slot[sd], t_aT[ab][f]], [t_bank[ib2]])
                            P.stt(xT[:, dc, cols], bank[ib2][:], Gm[:, l, n, dc, v:v + 1], xT[:, dc, cols],
                                  ALU.mult, ALU.add, [t_bank[ib2], t_x[dc][tt], t_mods], [t_x[dc][tt]])
                        if gi == last_gi:
                            after_tt(tt)

                    flush()
                    if gi == last_gi:
                        down()
                    else:
                        pending.append(down)
                S.done(pg_)
                S.done(pu_)
                if gi > 0:
                    S.done(pid[("d", g, l, j, gi - 1)])
            flush()
            S.done(pid[("d", g, l, j, len(groups) - 1)])

        def load_x(g, after_tt):
            for i in range(8):
                xb = i % 2
                xs_ = xio[:, xb * 1024:(xb + 1) * 1024]
                tl_ = t_or[xb * 4:xb * 4 + 4]
                P.dma("sp", ("xio", xb), xs_, x_d[g, i * 128:(i + 1) * 128, :], writes=tl_)
                for hb in range(2):
                    for kk in range(4):
                        k = hb * 4 + kk
                        P.tr(bank[hb][:, kk * 128:(kk + 1) * 128], xs_[:, k * 128:(k + 1) * 128], ident[:],
                             tl_ + [t_ident], [t_bank[hb]])
                    P.cp("dve" if hb == 0 else "act", xT[:, hb * 4:hb * 4 + 4, i * 128:(i + 1) * 128],
                         bank[hb][:].rearrange("p (k n) -> p k n", k=4), [t_bank[hb]],
                         [t_x[k][i // 4] for k in range(hb * 4, hb * 4 + 4)])
                if i % 4 == 3:
                    after_tt(i // 4)

        def store_y(g):
            for i in range(8):
                xb = i % 2
                xs_ = xio[:, xb * 1024:(xb + 1) * 1024]
                tl_ = t_or[xb * 4:xb * 4 + 4]
                for hb in range(2):
                    for kk in range(4):
                        k = hb * 4 + kk
                        P.tr(bank[hb][:, kk * 128:(kk + 1) * 128], xT[:, k, i * 128:(i + 1) * 128], ident[:],
                             [t_x[k][i // 4], t_ident], [t_bank[hb]])
                    P.cp("dve" if hb == 0 else "act", xs_[:, hb * 512:(hb + 1) * 512], bank[hb][:],
                         [t_bank[hb]], tl_)
                P.dma("sp", ("y", xb), y_d[g, i * 128:(i + 1) * 128, :], xs_, reads=tl_, is_out=True)

        def grp_rms(src, t_src, nh):
            n = nh * 64
            s3 = src.rearrange("p (h d) -> p h d", d=64)
            sqs, t_sqs = sq[0], t_sq[0]
            P.tt("dve", sqs[:, 0:n], src, src, ALU.mult, [t_src], [t_sqs])
            P.red(st8[:, 0:nh], sqs[:, 0:n].rearrange("p (h d) -> p h d", d=64), [t_sqs], [t_st8])
            P.actv(st8[:, 0:nh], st8[:, 0:nh], AF.Ln, [t_st8, t_chalf], [t_st8], scale=1.0 / 64, bias=epsc[:, 0:1])
            P.actv(st8[:, 8:8 + nh], st8[:, 0:nh], AF.Exp, [t_st8], [t_st8], scale=-0.5)
            P.tt("dve", s3, s3, st8[:, 8:8 + nh].rearrange("p (h o) -> p h o", o=1).to_broadcast([128, nh, 64]),
                 ALU.mult, [t_src, t_st8], [t_src])

        def mixer_setup(g, l):
            pwf = stmp[1][:, 0:256].rearrange("p (c n) -> p c n", c=2)
            t_pwf = t_stmp[1]
            RT = ntmp[0][0:31, 0:128]
            RT01 = ntmp[0][0:31, 128:352].rearrange("p (a n) -> p a n", a=2)
            t_RT = t_RT01 = t_ntmp[0]
            P.dma("sp", "ms0", gtb[:], gt_d[l:l + 1, :].to_broadcast([128, 384]), writes=[t_gtb])
            P.ts("dve", gtb[:, 0:64], gtb[:, 0:64], 0.125, None, ALU.mult, None, [t_gtb], [t_gtb])
            P.memset("pool", pwf, 0.0, [t_pwf])
            for c in range(2):
                for gh in range(2):
                    P.dma("sp", "ms1", pwf[gh * 64:(gh + 1) * 64, c, gh * 64:(gh + 1) * 64], poolw_d[l, 2 * c + gh],
                          writes=[t_pwf])
            P.cp("dve", pwb[:], pwf, [t_pwf], [t_pwb])
            for g4 in range(4):
                P.dma("sp", "ms2", stmp[0][:, 0:128], sgw_d[l, g4], writes=[t_stmp[0]])
                P.tr(bank[6][:, 0:128], stmp[0][:, 0:128], ident[:], [t_stmp[0], t_ident], [t_bank[6]])
                P.cp("dve", wsT[:, g4, :], bank[6][:, 0:128], [t_bank[6]], [t_wsT])
            for g4 in range(4):
                P.dma("sp", "ms3", bsb[(g4 % 2) * 64:(g4 % 2 + 1) * 64, g4 // 2, :],
                      sgb_d[l, g4:g4 + 1, :].to_broadcast([64, 128]), writes=[t_bsb])
            if g == 1:
                kctok = aT[1][:, 0:2, :]
                t_kctok = Tl()
                for t in range(2):
                    P.dma("pool", "ms4", kctok[:, t, :].rearrange("p (h d) -> p h d", h=8),
                          ck_d[l][:, t * 128:(t + 1) * 128, :].rearrange("h p d -> p h d"),
                          writes=[t_aT[1][0], t_aT[1][1]])
                    P.dma("pool", "ms5", vc[:, t, :].rearrange("p (h d) -> p h d", h=8),
                          cv_d[l][:, t * 128:(t + 1) * 128, :].rearrange("h p d -> p h d"), writes=[t_vc])
                for t in range(2):
                    for c in range(4):
                        P.tr(bankb[:, c * 128:(c + 1) * 128], kctok[:, t, c * 128:(c + 1) * 128], identb[:],
                             [t_aT[1][0], t_aT[1][1], t_identb], [t_bankb])
                    P.cp("dve", kcT[:, :, t * 128:(t + 1) * 128],
                         bankb[:, 0:512].rearrange("p (c n) -> p c n", c=4), [t_bankb], [t_kc])
                P.dma("sp", "ms6", stmp[1][:, 0:31], rpb_d[l], writes=[t_stmp[1]])
                P.tr(bank[6][0:31, 0:128], stmp[1][:, 0:31], ident[:], [t_stmp[1], t_ident], [t_bank[6]])
                P.cp("dve", RT, bank[6][0:31, 0:128], [t_bank[6]], [t_RT])
                rt3 = RT[:, 0:120].rearrange("p (h r) -> p h r", h=8)
                for half in range(2):
                    P.cp("dve", RT01[:, half, :].rearrange("p (h r) -> p h r", h=8), rt3[:, :, half:half + 14],
                         [t_RT], [t_RT01])
                for b in range(16):
                    bk = bank[b % 2]
                    for qi in range(4):
                        qc = b * 4 + qi
                        for half in range(2):
                            P.mm(bk[half * 64:(half + 1) * 64, qi * 112:(qi + 1) * 112],
                                 dw[0:31, 63 - qc:127 - qc], RT01[:, half, :], True, True,
                                 [t_dw, t_RT01], [t_bank[b % 2]], tp=(0, half * 64))
                    P.actv(EB[:, :, :, b * 4:(b + 1) * 4].rearrange("p h r q -> p q (h r)"),
                           bk[:, 0:448].rearrange("p (q n) -> p q n", q=4), AF.Exp, [t_bank[b % 2]], [t_EB] + t_adas)
                ebv = EB[:].rearrange("p h r q -> p (h r) q")
                P.tt("dve", ebv, ebv, colok[:, :].rearrange("p (o q) -> p o q", o=1).to_broadcast([128, 112, 64]),
                     ALU.mult, [t_EB, t_colok], [t_EB])

        acnt = {"n": 0, "bo": 0}
        SB_ = (0, 1, 4, 5)
        DEPTH = 3

        def mixer(g, l, after_tt):
            v = g
            is_p = (g == 0)
            if stop_after == "premix":
                raise build_program.Stop()
            pA, pB, pC, pD, pE, pF, pG = [pid[(nm, g, l)] for nm in "ABCDEFG"]
            sA, sB, sC, sD, sE = [S.need(p_) for p_ in (pA, pB, pC, pD, pE)]
            vA, vB, vC, vD, vE = [_v3(wslot[s_], 8) for s_ in (sA, sB, sC, sD, sE)]
            h3 = lambda ap: ap.rearrange("p (h d) -> p h d", d=64)
            SETS = (
                dict(qf=stmp[0][:], t_qf=[t_stmp[0]], kf=stmp[1][:], t_kf=[t_stmp[1]],
                     gv=ntmp[1][:, 0:256], t_gv=[t_ntmp[1]], vf=ntmp[0][:], t_vf=[t_ntmp[0]]),
                dict(qf=xio[:, 0:512], t_qf=[t_or[0], t_or[1]], kf=xio[:, 512:1024], t_kf=[t_or[2], t_or[3]],
                     gv=xio[:, 1024:1280], t_gv=[t_or[4]], vf=xio[:, 1536:2048], t_vf=[t_or[6], t_or[7]]),
            )
            sqq, t_sqq = sq[0][:], [t_sq[0]]
            sqk, t_sqk = aT[0][:, 1, :], [t_aT[0][1]]
            sqz, t_sqz = aT[0][:, 2, 0:256], [t_aT[0][2]]
            qb, t_qb = sq[1], t_sq[1]
            kb, t_kb = aT[0][:, 0, :], t_aT[0][0]
            bc = lambda ap, n: ap.rearrange("p (h o) -> p h o", o=1).to_broadcast([128, n, 64])
            gb = lambda ap: ap.rearrange("p (o d) -> p o d", o=1).to_broadcast([128, 8, 64])

            def proj_mm(i):
                tt = i // 4
                tok = slice(i * 128, (i + 1) * 128)
                for bi, (vw, sl) in enumerate(((vA, sA), (vB, sB), (vC, sC), (vD, sD))):
                    for k in range(8):
                        P.mm(bank[bi][:], hT[:, k, tok], vw[:, k, :], k == 0, k == 7,
                             [t_h[k][tt], t_slot[sl]], [t_bank[bi]])

            def proj_evac(i):
                B = SETS[i % 2]
                P.cp("act", B["qf"], bank[1][:], [t_bank[1]], B["t_qf"])
                P.cp("act", B["kf"], bank[2][:], [t_bank[2]], B["t_kf"])
                P.actv(B["gv"], bank[0][:, 256:512], AF.Gelu_apprx_tanh, [t_bank[0]], B["t_gv"])
                P.cp("act", za_tok[:, i, :], bank[0][:, 0:256], [t_bank[0]], [t_za[i]])
                P.cp("act", v_tok[:, i, :], bank[3][:], [t_bank[3]], [t_v[i]])
                if is_p:
                    b_, s0_ = i // 2, (i % 2) * 128
                    P.cp("dve", B["vf"], bank[3][:], [t_bank[3]], B["t_vf"])
                    P.dma("sp", "nv", nv_d[b_, l].rearrange("h s d -> s h d")[s0_:s0_ + 128], h3(B["vf"]),
                          reads=B["t_vf"], is_out=True)

            def proj_chain(i):
                B = SETS[i % 2]
                tok = slice(i * 128, (i + 1) * 128)
                qf, kf, gv = B["qf"], B["kf"], B["gv"]
                t_qf, t_kf, t_gv = B["t_qf"], B["t_kf"], B["t_gv"]
                P.tt("dve", sqq, qf, qf, ALU.mult, t_qf, t_sqq)
                P.tt("dve", sqk, kf, kf, ALU.mult, t_kf, t_sqk)
                P.tt("pool", sqz, gv, gv, ALU.mult, t_gv, t_sqz)
                P.red(st8[:, 0:8], h3(sqq), t_sqq, [t_st8])
                P.red(st8[:, 8:16], h3(sqk), t_sqk, [t_st8])
                P.red(st8[:, 16:20], h3(sqz), t_sqz, [t_st8])
                P.actv(st8[:, 0:20], st8[:, 0:20], AF.Ln, [t_st8, t_chalf], [t_st8], scale=1.0 / 64,
                       bias=epsc[:, 0:1])
                P.actv(st8[:, 20:40], st8[:, 0:20], AF.Exp, [t_st8], [t_st8], scale=-0.5)
                P.tt("dve", h3(qf), h3(qf), bc(st8[:, 20:28], 8), ALU.mult, t_qf + [t_st8], t_qf)
                P.tt("dve", h3(kf), h3(kf), bc(st8[:, 28:36], 8), ALU.mult, t_kf + [t_st8], t_kf)
                P.tt("dve", h3(gv), h3(gv), bc(st8[:, 36:40], 4), ALU.mult, t_gv + [t_st8], t_gv)
                P.tt("pool", h3(qb[:]), h3(qf), gb(gtb[:, 0:64]), ALU.mult, t_qf + [t_gtb], [t_qb])
                P.tt("dve", h3(kf), h3(kf), gb(gtb[:, 64:128]), ALU.mult, t_kf + [t_gtb], t_kf)
                P.tt("pool", vg_tok[:, i, :], gv, gtb[:, 128:384], ALU.mult, t_gv + [t_gtb], [t_vg[i]])
                if is_p:
                    b_, s0_ = i // 2, (i % 2) * 128
                    P.dma("sp", "nk", nk_d[b_, l].rearrange("h s d -> s h d")[s0_:s0_ + 128], h3(kf),
                          reads=t_kf, is_out=True)
                P.cp("act", kb, kf, t_kf, [t_kb])
                for c in range(4):
                    P.tr(bankb[:, c * 128:(c + 1) * 128], qb[:, c * 128:(c + 1) * 128], identb[:],
                         [t_qb, t_identb], [t_bankb])
                for c in range(4):
                    P.tr(bankb[:, 512 + c * 128:512 + (c + 1) * 128], kb[:, c * 128:(c + 1) * 128], identb[:],
                         [t_kb, t_identb], [t_bankb])
                P.cp("dve", qT[:, :, tok], bankb[:, 0:512].rearrange("p (c n) -> p c n", c=4), [t_bankb],
                     [t_q[c][i] for c in range(4)])
                P.cp("dve", kT[:, :, tok], bankb[:, 512:1024].rearrange("p (c n) -> p c n", c=4), [t_bankb],
                     [t_k[c][i] for c in range(4)])

            proj_mm(0)
            proj_evac(0)
            for i in range(8):
                if i + 1 < 8:
                    proj_mm(i + 1)
                proj_chain(i)
                if i + 1 < 8:
                    proj_evac(i + 1)
            gv, t_gv = stmp[1], t_stmp[1]
            _mark(" zu")
            for tt in range(2):
                cols = slice(tt * 512, (tt + 1) * 512)
                for c in range(2):
                    for k in range(8):
                        P.mm(bank[4 + c][:], vE[:, k, c * 128:(c + 1) * 128], hT[:, k, cols], k == 0, k == 7,
                             [t_slot[sE], t_h[k][tt]], [t_bank[4 + c]])
                    P.actv(uT[:, c, cols], bank[4 + c][:], AF.Gelu_apprx_tanh, [t_bank[4 + c]], [t_u[c][tt]])
            for p_ in (pA, pB, pC, pD, pE):
                S.done(p_)
            sF, sG = S.need(pF), S.need(pG)
            if stop_after == "proj":
                raise build_program.Stop()

            def variant(j):
                if is_p:
                    return 0 if j % 2 == 0 else 2
                return 0 if j == 0 else (2 if j == 7 else 1)

            def seq_range(j):
                if is_p:
                    return (j // 2 * 2, j // 2 * 2 + 1)
                return (0, 7)

            sqo, t_sqo = sq[0], t_sq[0]
            rs_ap = (ms[:, 0:256], ms[:, 256:512], ntmp[0][:, 0:256])
            rs_t = (t_ms, t_ms, t_ntmp[0])
            ms2, t_ms2 = ntmp[0][:, 256:512], t_ntmp[0]
            oTn = aT[1][:].rearrange("p f n -> p (f n)").rearrange("p (c n) -> p c n", c=8)
            ETB = ((ms, t_ms), (ntmp[0], t_ntmp[0]), (ntmp[1], t_ntmp[1]))

            def attn_S(it):
                n = it["n"]
                c, hp, h = it["c"], it["hp"], it["h"]
                ps_ = slice(hp * 64, (hp + 1) * 64)
                bk, t_bk = bank[SB_[n % 4]], t_bank[SB_[n % 4]]
                pt, t_pt = aT[0][:, n % 4, :], t_aT[0][n % 4]
                if is_p:
                    u_, tok0_ = it["u"], it["u"] * 256
                    for kt in range(2):
                        P.mm(bk[:, kt * 256:(kt + 1) * 256],
                             kT[ps_, c, tok0_ + kt * 128:tok0_ + (kt + 1) * 128], qT[ps_, c, tok0_:tok0_ + 256],
                             True, True, [t_k[c][2 * u_ + kt], t_q[c][2 * u_], t_q[c][2 * u_ + 1]], [t_bk])
                    P.actv(pt, bk[:], AF.Exp, [t_bk, t_chalf], [t_pt], bias=nsh[:, 0:1])
                else:
                    r, kts, nl = it["r"], it["kts"], it["nl"]
                    et, t_et = ETB[n % 3]
                    qcols = slice(r * 64, (r + 1) * 64)
                    for j_, kt in enumerate(kts):
                        P.mm(bk[:, j_ * 64:(j_ + 1) * 64], kT[ps_, c, kt * 128:(kt + 1) * 128],
                             qT[ps_, c, qcols], True, True, [t_k[c][kt], t_q[c][r // 2]], [t_bk])
                    for t in range(2):
                        P.mm(bk[:, (nl + t) * 64:(nl + t + 1) * 64], kcT[ps_, c, t * 128:(t + 1) * 128],
                             qT[ps_, c, qcols], True, True, [t_kc, t_q[c][r // 2]], [t_bk])
                    P.actv(et[:, 0:nl * 64], bk[:, 0:nl * 64], AF.Exp, [t_bk, t_chalf], [t_et], bias=nsh[:, 0:1])
                    P.actv(pt[:, nl * 64:(nl + 2) * 64], bk[:, nl * 64:(nl + 2) * 64], AF.Exp, [t_bk, t_chalf], [t_pt],
                           bias=nsh[:, 0:1])
                    i0 = 2 * kts[0] - r + 7
                    P.tt("dve", pt[:, 0:nl * 64].rearrange("p (j q) -> p j q", q=64),
                         et[:, 0:nl * 64].rearrange("p (j q) -> p j q", q=64),
                         EB[:, h, i0:i0 + 2 * nl - 1:2, :], ALU.mult, [t_et, t_EB], [t_pt])

            def attn_PV(it):
                n = it["n"]
                c, hp, h = it["c"], it["hp"], it["h"]
                ps_ = slice(hp * 64, (hp + 1) * 64)
                pt, t_pt = aT[0][:, n % 4, :], t_aT[0][n % 4]
                if hp == 0:
                    acnt["bo"] += 1
                bo, t_bo = bank[2 + acnt["bo"] % 2], t_bank[2 + acnt["bo"] % 2]
                if is_p:
                    u_ = it["u"]
                    for kt in range(2):
                        P.mm(bo[ps_, 0:256], v_tok[:, 2 * u_ + kt, h * 64:(h + 1) * 64],
                             pt[:, kt * 256:(kt + 1) * 256], kt == 0, kt == 1,
                             [t_v[2 * u_ + kt], t_pt], [t_bo], tp=(0, hp * 64))
                    for kt in range(2):
                        P.mm(bo[ps_, 256:512], ones[:, 0:64], pt[:, kt * 256:(kt + 1) * 256],
                             kt == 0, kt == 1, [t_ones, t_pt], [t_bo], tp=(0, hp * 64))
                    if hp == 1:
                        P.add("dve", lambda e, bo=bo: e.reciprocal(rden[:], bo[:, 256:512]), [t_bo], [t_rden])
                        P.tt("dve", oraw[:, 2 + c, :], bo[:, 0:256], rden[:], ALU.mult, [t_bo, t_rden],
                             [t_or[2 + c]])
                else:
                    r, kts, nl, s0 = it["r"], it["kts"], it["nl"], it["s0"]
                    mml = []
                    for j_, kt in enumerate(kts):
                        if s0 % 2 == 1 and j_ == 0:
                            p0, p1 = 64, 128
                        elif s0 % 2 == 1 and j_ == nl - 1:
                            p0, p1 = 0, 64
                        else:
                            p0, p1 = 0, 128
                        mml.append((v_tok[p0:p1, kt, h * 64:(h + 1) * 64], ones[p0:p1, 0:64],
                                    pt[p0:p1, j_ * 64:(j_ + 1) * 64], p0, [t_v[kt]]))
                    for t in range(2):
                        mml.append((vc[:, t, h * 64:(h + 1) * 64], ones[:, 0:64],
                                    pt[:, (nl + t) * 64:(nl + t + 1) * 64], 0, [t_vc]))
                    for n_, (lv, lo_, rh, p0, rd) in enumerate(mml):
                        P.mm(bo[ps_, 0:64], lv, rh, n_ == 0, n_ == len(mml) - 1, rd + [t_pt], [t_bo],
                             tp=(p0, hp * 64))
                    for n_, (lv, lo_, rh, p0, rd) in enumerate(mml):
                        P.mm(bo[ps_, 64:128], lo_, rh, n_ == 0, n_ == len(mml) - 1, [t_ones, t_pt],
                             [t_bo], tp=(p0, hp * 64))
                    if hp == 1:
                        P.add("dve", lambda e, bo=bo: e.reciprocal(rden[:, 0:64], bo[:, 64:128]), [t_bo],
                              [t_rden])
                        P.tt("dve", oraw[:, 2 + c, (r % 4) * 64:(r % 4 + 1) * 64], bo[:, 0:64], rden[:, 0:64],
                             ALU.mult, [t_bo, t_rden], [t_or[2 + c]])

            for u in range(4):
                tok0 = u * 256
                ucols = slice(tok0, tok0 + 256)
                tt = u // 2
                _mark(" u%d-attn" % u)
                items = []
                for c in range(4):
                    if is_p:
                        for hp in range(2):
                            items.append({"c": c, "hp": hp, "h": 2 * c + hp, "u": u})
                    else:
                        for r in range(4 * u, 4 * u + 4):
                            s0 = min(max(r - 4, 0), 8)
                            kts = list(range(s0 // 2, (s0 + 7) // 2 + 1))
                            for hp in range(2):
                                items.append({"c": c, "hp": hp, "h": 2 * c + hp, "r": r, "s0": s0, "kts": kts,
                                              "nl": len(kts)})
                for it in items:
                    it["n"] = acnt["n"]
                    acnt["n"] += 1
                def pool_chunk():
                    _mark(" u%d-pool" % u)
                    for c in range(2):
                        for gh in range(2):
                            g4 = 2 * c + gh
                            for jj in range(2):
                                j = 2 * u + jj
                                lo, hi = seq_range(j)
                                ins = [i_ for i_ in (j - 1, j, j + 1) if lo <= i_ <= hi]
                                for n_, i_ in enumerate(ins):
                                    var = 3 if i_ == j - 1 else (4 if i_ == j + 1 else variant(j))
                                    P.mm(bank[4][gh * 64:(gh + 1) * 64, c * 256 + jj * 128:c * 256 + (jj + 1) * 128],
                                         za_tok[:, i_, g4 * 64:(g4 + 1) * 64], bands[:, g4 * 5 + var, :],
                                         n_ == 0, n_ == len(ins) - 1, [t_za[i_], t_bands], [t_bank[4]],
                                         tp=(0, gh * 64))
                        for jj in range(2):
                            P.tt("dve", pooledT[:, c, jj * 128:(jj + 1) * 128],
                                 bank[4][:, c * 256 + jj * 128:c * 256 + (jj + 1) * 128],
                                 invc[:, c * 3 + variant(2 * u + jj), :], ALU.mult, [t_bank[4], t_invc], [t_pl[c]])
                        P.mm(bank[5][:, c * 256:(c + 1) * 256], pwb[:, c, :], pooledT[:, c, :], True, True,
                             [t_pwb, t_pl[c]], [t_bank[5]])
                        P.actv(oraw[:, c, :], bank[5][:, c * 256:(c + 1) * 256], AF.Identity, [t_bank[5], t_vtT],
                               [t_or[c]], scale=vtT[:, l, 104 + c:105 + c])
                    for c in range(2):
                        for gh in range(2):
                            g4 = 2 * c + gh
                            for jj in range(2):
                                P.mm(bank[4][gh * 64:(gh + 1) * 64, c * 256 + jj * 128:c * 256 + (jj + 1) * 128],
                                     vg_tok[:, 2 * u + jj, g4 * 64:(g4 + 1) * 64], wsT[:, g4, :], True, True,
                                     [t_vg[2 * u + jj], t_wsT], [t_bank[4]], tp=(0, gh * 64))
                        P.tt("dve", gv[:, 0:256].rearrange("p (j n) -> p j n", j=2),
                             bank[4][:, c * 256:(c + 1) * 256].rearrange("p (j n) -> p j n", j=2),
                             bsb[:, c:c + 1, :].to_broadcast([128, 2, 128]), ALU.add, [t_bank[4], t_bsb], [t_gv])
                        P.tt("pool", oraw[:, 6 + c, :], gv[:, 0:256], uT[:, c, ucols], ALU.mult,
                             [t_gv, t_u[c][tt]], [t_or[6 + c]])

                def capture(fn):
                    rec = []
                    P.add = lambda eng, f, reads=(), writes=(), dma=None, is_out=False: rec.append(
                        (eng, f, reads, writes, dma, is_out))
                    try:
                        fn()
                    finally:
                        del P.add
                    return rec

                def emit_pair(fn, ita, itb):
                    ra = capture(lambda: fn(ita))
                    rb = capture(lambda: fn(itb))
                    pa = [x for x in ra if x[0] == "pe"]
                    pb = [x for x in rb if x[0] == "pe"]
                    for i_ in range(max(len(pa), len(pb))):
                        if i_ < len(pa):
                            P.add(*pa[i_])
                        if i_ < len(pb):
                            P.add(*pb[i_])
                    for x in ra + rb:
                        if x[0] != "pe":
                            P.add(*x)

                pairs = [(items[2 * j], items[2 * j + 1]) for j in range(len(items) // 2)]
                DEPTHP = 1
                for idx in range(len(pairs) + DEPTHP):
                    if idx == len(pairs) // 2:
                        pool_chunk()
                    if idx < len(pairs):
                        emit_pair(attn_S, *pairs[idx])
                    if idx - DEPTHP >= 0:
                        emit_pair(attn_PV, *pairs[idx - DEPTHP])
                _mark(" u%d-out" % u)
                segs = ((0, 2, 256.0), (2, 6, 512.0), (6, 8, 256.0))
                for si, (o0, o1, wd_) in enumerate(segs):
                    stb, t_stb = (bank[6], t_bank[6]) if si < 2 else (bank[5], t_bank[5])
                    scol = slice((si % 2) * 256, (si % 2) * 256 + 256)
                    for oc in range(o0, o1):
                        sqo_, t_sqo_ = sq[oc % 2], t_sq[oc % 2]
                        P.actv(sqo_[:, 0:256], oraw[:, oc, :], AF.Square, [t_or[oc]], [t_sqo_])
                        P.mm(stb[:, scol], ones[:], sqo_[:, 0:256], oc == o0, oc == o1 - 1, [t_ones, t_sqo_],
                             [t_stb])
                    P.actv(ms2, stb[:, scol], AF.Ln, [t_stb, t_chalf], [t_ms2], scale=1.0 / wd_,
                           bias=epsc[:, 0:1])
                    P.actv(rs_ap[si], ms2, AF.Exp, [t_ms2], [rs_t[si]], scale=-0.5)
                for oc in range(8):
                    si = 0 if oc < 2 else (1 if oc < 6 else 2)
                    P.stt(oTn[:, oc, :], oraw[:, oc, :], vtT[:, l, 96 + oc:97 + oc], rs_ap[si], ALU.mult, ALU.mult,
                          [t_or[oc], t_vtT, rs_t[si]], [t_aT[1][oc // 2]])
                for dc in range(8):
                    sl = sF if dc < 4 else sG
                    vw = _v3(wslot[sl], 8)
                    bw, t_bw = bank[2 + dc % 2], t_bank[2 + dc % 2]
                    for oc in range(8):
                        P.mm(bw[:, 0:256], vw[:, oc, (dc % 4) * 128:(dc % 4 + 1) * 128], oTn[:, oc, :],
                             oc == 0, oc == 7, [t_slot[sl], t_aT[1][oc // 2]], [t_bw])
                    P.stt(xT[:, dc, ucols], bw[:, 0:256], Gm[:, l, 1, dc, v:v + 1], xT[:, dc, ucols],
                          ALU.mult, ALU.add, [t_bw, t_x[dc][tt], t_mods], [t_x[dc][tt]])
                if u % 2 == 1:
                    after_tt(u // 2)
            S.done(pF)
            S.done(pG)

        class _Stop(Exception):
            pass
        build_program.Stop = _Stop
        try:
            for g in range(2):
                _mark("load%d" % g)
                if not EARLY_NORM:
                    nop = lambda tt: None
                    load_x(g, nop)
                    if g == 0:
                        ada_compute(0)
                    for l in range(L):
                        _mark("ffn%d%d0" % (g, l))
                        norm_tt(l, 0, g, 0); norm_tt(l, 0, g, 1)
                        ffn(g, l, 0, nop)
                        if g == 0 and l == 0:
                            ada_compute(1)
                        _mark("mix%d%d" % (g, l))
                        mixer_setup(g, l)
                        norm_tt(l, 1, g, 0); norm_tt(l, 1, g, 1)
                        mixer(g, l, nop)
                        _mark("ffn%d%d1" % (g, l))
                        norm_tt(l, 2, g, 0); norm_tt(l, 2, g, 1)
                        ffn(g, l, 1, nop)
                    _mark("store%d" % g)
                    store_y(g)
                    continue
                if g == 0:
                    load_x(g, lambda tt: None)
                    ada_add_jobs(0, 0, 12, 0)
                    ada_issue_upto(3)
                    ada_consume(12)
                    ada_finalize(0, 0, 3)
                    ada_add_jobs(0, 12, 36, 24)
                    ada_issue_upto(ada_q["done"] + 2)
                    norm_tt(0, 0, 0, 0)
                    norm_tt(0, 0, 0, 1)
                else:
                    load_x(g, lambda tt, g=g: norm_tt(0, 0, g, tt))
                for l in range(L):
                    _mark("ffn%d%d0" % (g, l))
                    mixer_setup(g, l)
                    hk = None
                    if g == 0 and l == 0:
                        hk = make_ada_hook([3, 3, 3, 3, 2, 2, 2, 2, 2, 2], lambda: ada_finalize(0, 3, 9))
                    ffn(g, l, 0, lambda tt, g=g, l=l: norm_tt(l, 1, g, tt), hook=hk)
                    _mark("mix%d%d" % (g, l))
                    mixer(g, l, lambda tt, g=g, l=l: norm_tt(l, 2, g, tt))
                    _mark("ffn%d%d1" % (g, l))
                    if l + 1 < L:
                        hk = None
                        if g == 0:
                            ada_add_jobs(l + 1, 0, 36, 0)
                            ada_issue_upto(ada_q["done"] + 2)
                            hk = make_ada_hook([4, 4, 4, 4, 2, 2, 4, 4, 4, 4],
                                               lambda l=l: ada_finalize(l + 1, 0, 9))
                        ffn(g, l, 1, lambda tt, g=g, l=l: norm_tt(l + 1, 0, g, tt), hook=hk)
                    else:
                        ffn(g, l, 1, lambda tt: None)
                _mark("store%d" % g)
                store_y(g)
        except _Stop:
            pass
        _mark("end")
        build_program.phases = PH
        P.emit(st)
    return nc


def _consts():
    ident = np.eye(128, dtype=np.float32)
    bands = np.zeros((128, 20, 128), np.float32)
    invc = np.zeros((128, 6, 128), np.float32)
    for gi, w in enumerate(WINS):
        h = w // 2
        tp = np.arange(128)[:, None]
        t = np.arange(128)[None, :]
        inb = ((tp >= t - h) & (tp < t + h)).astype(np.float32)
        cnt_first = (np.minimum(t + h, 10 ** 6) - np.maximum(t - h, 0)).astype(np.float32)
        cnt_mid = np.full((1, 128), float(w), np.float32)
        cnt_last = ((128 - t) + h - np.maximum(0, 0)).astype(np.float32)
        cnt_last = np.minimum(cnt_last, w).astype(np.float32)
        eye = np.eye(128, dtype=np.float32)
        bands[:, gi * 5 + 0, :] = inb - eye * cnt_first
        bands[:, gi * 5 + 1, :] = inb - eye * cnt_mid
        bands[:, gi * 5 + 2, :] = inb - eye * cnt_last
        bands[:, gi * 5 + 3, :] = ((tp - 128) >= (t - h)).astype(np.float32)
        bands[:, gi * 5 + 4, :] = ((tp + 128) < (t + h)).astype(np.float32)
        c, half = gi // 2, gi % 2
        for vi, cn in enumerate((cnt_first, cnt_mid, cnt_last)):
            invc[half * 64:(half + 1) * 64, c * 3 + vi, :] = 1.0 / cn
    dw = np.zeros((31, 127), np.float32)
    for j in range(31):
        dw[j, j + 48] = 1.0
    cols = np.arange(64)
    cs = np.clip(cols - 8, 0, 48)
    ok = (cols[None, :] >= cs[:, None]) & (cols[None, :] < cs[:, None] + 16)
    colok = np.concatenate([ok.T, ok.T], 0).astype(np.float32)
    return ident, bands, invc, dw, colok


def make_in_maps(inp):
    f = lambda a: np.ascontiguousarray(np.asarray(a, dtype=np.float32))
    ident, bands, invc, dw, colok = _consts()
    vt = np.zeros((L, 128, 128), np.float32)
    gt = np.zeros((L, 384), np.float32)
    rpb = np.zeros((L, 128, 31), np.float32)
    for l in range(L):
        vt[l, 0:72] = f(inp["ada_b"])[l].reshape(72, 128)
        vt[l, 72:96] = f(inp["norm_g"])[l].reshape(24, 128)
        vt[l, 96:104] = f(inp["out_norm_g"])[l].reshape(8, 128)
        vt[l, 104:106] = f(inp["pool_scale"])[l].reshape(2, 128)
        gt[l, 0:64] = f(inp["q_norm_g"])[l]
        gt[l, 64:128] = f(inp["k_norm_g"])[l]
        gt[l, 128:384] = f(inp["sg_vnorm_g"])[l].reshape(256)
        rpb[l, 0:120] = f(inp["na_rpb"])[l].reshape(120, 31)
    shared = {
        "vt": vt, "gt": gt, "rpb": rpb,
        "poolw": f(inp["pool_w"]), "sgw": f(inp["sg_w"]), "sgb": f(inp["sg_b"]),
        "adaw": f(inp["ada_w"]), "wg": f(inp["ffn_w_gate"]), "wu": f(inp["ffn_w_up"]),
        "wd": f(inp["ffn_w_down"]), "win": f(inp["w_in"]), "wout": f(inp["w_out"]),
        "ident": ident, "bands": bands, "invc": invc, "dw": dw, "colok": colok,
    }
    xp = f(inp["x_prompt"]); xs = f(inp["x_sample"])
    ck = f(inp["cache_k"]); cv = f(inp["cache_v"]); c = f(inp["c"]); cc = f(inp["c_ctx"])
    maps = []
    for i in range(NCORES):
        x = np.stack([xp[4 * i:4 * i + 4].reshape(1024, D), xs[i]], 0)
        cvec = np.zeros((128, 128), np.float32)
        cvec[0:8] = cc.reshape(8, 128)
        cvec[8:16] = c[i].reshape(8, 128)
        m = dict(shared)
        m.update({"x": np.ascontiguousarray(x), "ck": np.ascontiguousarray(ck[i]),
                  "cv": np.ascontiguousarray(cv[i]), "cvec": cvec})
        maps.append(m)
    return maps


_NC_CACHE = {}


def kernel(**inputs):
    if "nc" not in _NC_CACHE:
        _NC_CACHE["nc"] = build_program()
    nc = _NC_CACHE["nc"]
    maps = make_in_maps(inputs)
    res = run_bass_kernel_spmd(nc, maps, core_ids=list(range(NCORES)))
    r = res.results
    yp = np.concatenate([r[i]["y"][0].reshape(4, 256, D) for i in range(NCORES)], 0)
    ys = np.stack([r[i]["y"][1] for i in range(NCORES)], 0)
    nk = np.concatenate([r[i]["nk"] for i in range(NCORES)], 0)
    nv = np.concatenate([r[i]["nv"] for i in range(NCORES)], 0)
    return (yp.astype(np.float32), ys.astype(np.float32), nk.astype(np.float32), nv.astype(np.float32))
```
